# Optimizing a Trainium2 kernel written in Bass

```python
import math
import jax, jax.numpy as jnp
from jax import lax
import numpy as np

D_MODEL = 2048
BATCH = 16
SEQ = 256
DEPTH = 4
DEC_BATCH = 2
DEC_SEQ = 1024
PAST_LEN = 256

GRID_W = 64
N_MIXERS = 3
N_SSM_LAYERS = (DEPTH + 2) // 3
N_DIFF_LAYERS = (DEPTH + 1) // 3
N_WIN_LAYERS = DEPTH // 3
N_MOD = 9
D_FF = 5632
FFN_RES = 0.5
EPS = 1e-6
ROPE_BASE = 10000.0
QBLK = 128

SSM_EXPAND = 2
D_INNER = SSM_EXPAND * D_MODEL
SSM_HEAD_P = 64
SSM_HEADS = D_INNER // SSM_HEAD_P
SSM_GROUPS = 8
SSM_HPG = SSM_HEADS // SSM_GROUPS
D_STATE = 128
D_CONV = 3
CONV_CH = D_INNER + 2 * SSM_GROUPS * D_STATE
SSM_IN = D_INNER + CONV_CH + 2 * SSM_HEADS
CHUNK = 128

DIFF_HEADS = 8
DIFF_DH = D_MODEL // (2 * DIFF_HEADS)
DIFF_QKV = 4 * DIFF_HEADS * DIFF_DH + DIFF_HEADS * 2 * DIFF_DH

WIN_HEADS = 16
WIN_KV = 4
WIN_QPK = WIN_HEADS // WIN_KV
WIN_DH = D_MODEL // WIN_HEADS
WINDOW = 128
WIN_QKV = (WIN_HEADS + 2 * WIN_KV) * WIN_DH

kernel_name = "hybrid_flow_ssd_diff_swa_step"


def _rmsnorm(x, g):
    xf = x.astype(jnp.float32)
    y = xf * lax.rsqrt(jnp.mean(xf * xf, axis=-1, keepdims=True) + EPS)
    return (y * g.astype(jnp.float32)).astype(x.dtype)


def _swiglu(h, w_in, w_out):
    gate, up = jnp.split(h @ w_in, 2, axis=-1)
    return (jax.nn.silu(gate) * up) @ w_out


def _modulation(cvec, w, b):
    m = jax.nn.silu(cvec) @ w + b
    return [t[:, None, :] for t in jnp.split(m, N_MOD, axis=-1)]


def _modnorm(x, g, shift, scale):
    return _rmsnorm(x, g) * (1.0 + scale) + shift


def _ffn_sub(x, shift, scale, gate, g, w_in, w_out):
    return x + FFN_RES * gate * _swiglu(_modnorm(x, g, shift, scale), w_in, w_out)


def _to_blocks(t, blk):
    b, L = t.shape[:2]
    return jnp.moveaxis(t.reshape((b, L // blk, blk) + t.shape[2:]), 1, 0)


def _from_blocks(t):
    t = jnp.moveaxis(t, 0, 1)
    return t.reshape((t.shape[0], t.shape[1] * t.shape[2]) + t.shape[3:])


def _axial_angles(L, d):
    rows = L // GRID_W
    row = jnp.repeat(jnp.arange(rows, dtype=jnp.float32), GRID_W)
    col = jnp.tile(jnp.arange(GRID_W, dtype=jnp.float32), rows)
    nf = d // 4
    inv = ROPE_BASE ** (-jnp.arange(nf, dtype=jnp.float32) / nf)
    return jnp.concatenate([row[:, None] * inv, col[:, None] * inv], axis=-1)


def _apply_axial_rope(x, ang):
    d = x.shape[-1]
    h, qd = d // 2, d // 4
    ang = ang.reshape((ang.shape[0],) + (1,) * (x.ndim - 3) + (h,))
    cos, sin = jnp.cos(ang), jnp.sin(ang)
    xf = x.astype(jnp.float32)
    parts = []
    for a in range(2):
        xa = xf[..., a * h:(a + 1) * h]
        ca, sa = cos[..., a * qd:(a + 1) * qd], sin[..., a * qd:(a + 1) * qd]
        x1, x2 = xa[..., :qd], xa[..., qd:]
        parts += [x1 * ca - x2 * sa, x2 * ca + x1 * sa]
    return jnp.concatenate(parts, axis=-1).astype(x.dtype)


def _dwconv(x, w):
    C = x.shape[-1]
    return lax.conv_general_dilated(x, w[:, None, :], window_strides=(1,),
                                    padding=[(D_CONV // 2, D_CONV // 2)],
                                    dimension_numbers=('NWC', 'WIO', 'NWC'),
                                    feature_group_count=C)


def _ssd(x, dt, a, bm, cm, h0):
    b, L, G, R, P = x.shape
    N = bm.shape[-1]
    nc = L // CHUNK
    x = x.reshape(b, nc, CHUNK, G, R, P)
    dt = dt.reshape(b, nc, CHUNK, G, R)
    bm = bm.reshape(b, nc, CHUNK, G, N)
    cm = cm.reshape(b, nc, CHUNK, G, N)
    acum = jnp.cumsum(dt * a, axis=2)
    seg = acum[:, :, :, None] - acum[:, :, None, :]
    causal = jnp.tril(jnp.ones((CHUNK, CHUNK), dtype=bool))[:, :, None, None]
    decay = jnp.exp(jnp.where(causal, seg, -jnp.inf))
    xdt = x * dt[..., None]
    cb = jnp.einsum('bcign,bcjgn->bcijg', cm, bm)
    y_diag = jnp.einsum('bcijg,bcijgr,bcjgrp->bcigrp', cb, decay, xdt)
    decay_end = jnp.exp(acum[:, :, -1:] - acum)
    chunk_states = jnp.einsum('bcjgn,bcjgr,bcjgrp->bcgrpn', bm, decay_end, xdt)
    chunk_decay = jnp.exp(acum[:, :, -1])

    def step(h, inp):
        s, dcy = inp
        return h * dcy[..., None, None] + s, h

    h_final, h_prev = lax.scan(step, h0, (jnp.moveaxis(chunk_states, 1, 0),
                                          jnp.moveaxis(chunk_decay, 1, 0)))
    h_prev = jnp.moveaxis(h_prev, 0, 1)
    y_off = jnp.einsum('bcign,bcigr,bcgrpn->bcigrp', cm, jnp.exp(acum), h_prev)
    return (y_diag + y_off).reshape(b, L, G, R, P), h_final


def _ssm_mixer(h, h0_f, h0_b, w_in, conv_w, conv_b, dt_bias, a_log, d_skip, norm_g, w_out):
    f32 = jnp.float32
    b, L, _ = h.shape
    z, xbc, dtr = jnp.split(h @ w_in, [D_INNER, D_INNER + CONV_CH], axis=-1)
    xbc = jax.nn.silu(_dwconv(xbc, conv_w) + conv_b)
    xs, bm, cm = jnp.split(xbc, [D_INNER, D_INNER + SSM_GROUPS * D_STATE], axis=-1)
    xs = xs.reshape(b, L, SSM_GROUPS, SSM_HPG, SSM_HEAD_P).astype(f32)
    bm = bm.reshape(b, L, SSM_GROUPS, D_STATE).astype(f32)
    cm = cm.reshape(b, L, SSM_GROUPS, D_STATE).astype(f32)
    dt = jax.nn.softplus(dtr.reshape(b, L, 2, SSM_GROUPS, SSM_HPG).astype(f32)
                         + dt_bias.astype(f32).reshape(2, SSM_GROUPS, SSM_HPG))
    a = -jnp.exp(a_log.astype(f32)).reshape(2, SSM_GROUPS, SSM_HPG)
    shp = (b, SSM_GROUPS, SSM_HPG, SSM_HEAD_P, D_STATE)
    flip = lambda t: jnp.flip(t, axis=1)
    y_f, hf = _ssd(xs, dt[:, :, 0], a[0], bm, cm, h0_f.astype(f32).reshape(shp))
    y_b, hb = _ssd(flip(xs), flip(dt[:, :, 1]), a[1], flip(bm), flip(cm),
                   h0_b.astype(f32).reshape(shp))
    y = y_f + flip(y_b) + d_skip.astype(f32).reshape(SSM_GROUPS, SSM_HPG)[:, :, None] * xs
    y = y.reshape(b, L, D_INNER).astype(h.dtype)
    y = _rmsnorm(y * jax.nn.silu(z), norm_g)
    st = (b, SSM_HEADS, SSM_HEAD_P, D_STATE)
    return y @ w_out, hf.reshape(st).astype(h.dtype), hb.reshape(st).astype(h.dtype)


def _diff_qkv(h, w_qkv, ang):
    b, L, _ = h.shape
    q, k, v = jnp.split(h @ w_qkv, [2 * DIFF_HEADS * DIFF_DH, 4 * DIFF_HEADS * DIFF_DH], axis=-1)
    q = q.reshape(b, L, 2, DIFF_HEADS, DIFF_DH)
    k = k.reshape(b, L, 2, DIFF_HEADS, DIFF_DH)
    v = v.reshape(b, L, DIFF_HEADS, 2 * DIFF_DH)
    if ang is not None:
        q, k = _apply_axial_rope(q, ang), _apply_axial_rope(k, ang)
    return q, k, v


def _diff_attend(q, k_all, v_all, lam):
    scale = DIFF_DH ** -0.5

    def blk(qb):
        s = jnp.einsum('bqmhd,bkmhd->bmhqk', qb, k_all).astype(jnp.float32) * scale
        p = jax.nn.softmax(s, axis=-1)
        p = p[:, 0] - lam * p[:, 1]
        return jnp.einsum('bhqk,bkhe->bqhe', p.astype(v_all.dtype), v_all)

    return _from_blocks(lax.map(blk, _to_blocks(q, QBLK)))


def _diff_out(o, subln_g, w_out, lambda_init):
    o = _rmsnorm(o, subln_g) * (1.0 - lambda_init)
    b, L = o.shape[:2]
    return o.reshape(b, L, DIFF_HEADS * 2 * DIFF_DH) @ w_out


def _win_qkv(h, w_qkv, ang):
    b, L, _ = h.shape
    q, k, v = jnp.split(h @ w_qkv, [WIN_HEADS * WIN_DH, (WIN_HEADS + WIN_KV) * WIN_DH], axis=-1)
    q = q.reshape(b, L, WIN_KV, WIN_QPK, WIN_DH)
    k = k.reshape(b, L, WIN_KV, WIN_DH)
    v = v.reshape(b, L, WIN_KV, WIN_DH)
    if ang is not None:
        q, k = _apply_axial_rope(q, ang), _apply_axial_rope(k, ang)
    return q, k, v


def _sink_attend_block(qb, k_all, v_all, valid, sink):
    s = jnp.einsum('bqgrd,bkgd->bgrqk', qb, k_all).astype(jnp.float32) * (WIN_DH ** -0.5)
    if valid is not None:
        s = jnp.where(valid, s, -jnp.inf)
    sk = jnp.broadcast_to(sink.astype(jnp.float32)[None, :, :, None, None], s.shape[:-1] + (1,))
    p = jax.nn.softmax(jnp.concatenate([s, sk], axis=-1), axis=-1)[..., :-1]
    return jnp.einsum('bgrqk,bkgd->bqgrd', p.astype(v_all.dtype), v_all)


def _win_context(q, kc, vc, sink):
    blk = lambda qb: _sink_attend_block(qb, kc, vc, None, sink)
    return _from_blocks(lax.map(blk, _to_blocks(q, QBLK)))


def _win_latent(q, k, v, kc, vc, sink):
    b, L = q.shape[:2]
    nb = L // WINDOW
    Lc = kc.shape[1]

    def band(t):
        tp = jnp.pad(t, ((0, 0), (WINDOW, WINDOW), (0, 0), (0, 0)))
        tp = tp.reshape((b, nb + 2, WINDOW) + t.shape[2:])
        return jnp.moveaxis(jnp.concatenate([tp[:, :-2], tp[:, 1:-1], tp[:, 2:]], axis=2), 1, 0)

    n = jnp.arange(nb)[:, None, None]
    qpos = n * WINDOW + jnp.arange(WINDOW)[None, :, None]
    kpos = (n - 1) * WINDOW + jnp.arange(3 * WINDOW)[None, None, :]
    band_ok = (jnp.abs(qpos - kpos) <= WINDOW) & (kpos >= 0) & (kpos < L)
    valid = jnp.concatenate([jnp.ones((nb, WINDOW, Lc), dtype=bool), band_ok], axis=-1)

    def blk(args):
        qb, kb, vb, ok = args
        return _sink_attend_block(qb, jnp.concatenate([kc, kb], axis=1),
                                  jnp.concatenate([vc, vb], axis=1), ok, sink)

    return _from_blocks(lax.map(blk, (_to_blocks(q, WINDOW), band(k), band(v), valid)))


def setup_inputs(seed: int = 0) -> dict:
    key = jax.random.key(seed)
    ks = iter(jax.random.split(key, 48))
    f32 = jnp.float32
    D = D_MODEL
    nrm = lambda shape, s: jax.random.normal(next(ks), shape, f32) * s
    gain = lambda shape: 1.0 + nrm(shape, 0.05)
    unif = lambda shape, lo, hi: jax.random.uniform(next(ks), shape, f32, lo, hi)
    dt0 = jnp.exp(unif((N_SSM_LAYERS, 2, SSM_HEADS), math.log(1e-3), math.log(1e-1)))
    return {
        "x_prompt": nrm((BATCH, SEQ, D), 1.0),
        "x_sample": nrm((DEC_BATCH, DEC_SEQ, D), 1.0),
        "state_l0_fwd": nrm((DEC_BATCH, SSM_HEADS, SSM_HEAD_P, D_STATE), 0.5),
        "state_l0_bwd": nrm((DEC_BATCH, SSM_HEADS, SSM_HEAD_P, D_STATE), 0.5),
        "cache_l1_k": nrm((DEC_BATCH, PAST_LEN, 2, DIFF_HEADS, DIFF_DH), 1.0),
        "cache_l1_v": nrm((DEC_BATCH, PAST_LEN, DIFF_HEADS, 2 * DIFF_DH), 1.0),
        "cache_l2_k": nrm((DEC_BATCH, PAST_LEN, WIN_KV, WIN_DH), 1.0),
        "cache_l2_v": nrm((DEC_BATCH, PAST_LEN, WIN_KV, WIN_DH), 1.0),
        "state_l3_fwd": nrm((DEC_BATCH, SSM_HEADS, SSM_HEAD_P, D_STATE), 0.5),
        "state_l3_bwd": nrm((DEC_BATCH, SSM_HEADS, SSM_HEAD_P, D_STATE), 0.5),
        "c": nrm((DEC_BATCH, D), 1.0),
        "c_ctx": nrm((D,), 1.0),
        "norm_g": gain((DEPTH, 3, D)),
        "w_ada": nrm((DEPTH, D, N_MOD * D), 0.5 * D ** -0.5),
        "b_ada": nrm((DEPTH, N_MOD * D), 0.02),
        "ffn1_w_in": nrm((DEPTH, D, 2 * D_FF), D ** -0.5),
        "ffn1_w_out": nrm((DEPTH, D_FF, D), D_FF ** -0.5),
        "ffn2_w_in": nrm((DEPTH, D, 2 * D_FF), D ** -0.5),
        "ffn2_w_out": nrm((DEPTH, D_FF, D), D_FF ** -0.5),
        "ssm_w_in": nrm((N_SSM_LAYERS, D, SSM_IN), D ** -0.5),
        "ssm_conv_w": nrm((N_SSM_LAYERS, D_CONV, CONV_CH), D_CONV ** -0.5),
        "ssm_conv_b": nrm((N_SSM_LAYERS, CONV_CH), 0.02),
        "ssm_dt_bias": dt0 + jnp.log(-jnp.expm1(-dt0)),
        "ssm_a_log": jnp.log(unif((N_SSM_LAYERS, 2, SSM_HEADS), 1.0, 16.0)),
        "ssm_d": gain((N_SSM_LAYERS, SSM_HEADS)),
        "ssm_norm_g": gain((N_SSM_LAYERS, D_INNER)),
        "ssm_w_out": nrm((N_SSM_LAYERS, D_INNER, D), D_INNER ** -0.5),
        "diff_w_qkv": nrm((N_DIFF_LAYERS, D, DIFF_QKV), D ** -0.5),
        "diff_lambda": nrm((N_DIFF_LAYERS, 4, DIFF_DH), 0.1),
        "diff_subln_g": gain((N_DIFF_LAYERS, 2 * DIFF_DH)),
        "diff_w_out": nrm((N_DIFF_LAYERS, DIFF_HEADS * 2 * DIFF_DH, D), (DIFF_HEADS * 2 * DIFF_DH) ** -0.5),
        "win_w_qkv": nrm((N_WIN_LAYERS, D, WIN_QKV), D ** -0.5),
        "win_sink": nrm((N_WIN_LAYERS, WIN_HEADS), 0.5),
        "win_w_out": nrm((N_WIN_LAYERS, WIN_HEADS * WIN_DH, D), (WIN_HEADS * WIN_DH) ** -0.5),
        "final_norm_g": gain((D,)),
    }


def reference(x_prompt, x_sample, state_l0_fwd, state_l0_bwd, cache_l1_k, cache_l1_v,
              cache_l2_k, cache_l2_v, state_l3_fwd, state_l3_bwd, c, c_ctx,
              norm_g, w_ada, b_ada, ffn1_w_in, ffn1_w_out, ffn2_w_in, ffn2_w_out,
              ssm_w_in, ssm_conv_w, ssm_conv_b, ssm_dt_bias, ssm_a_log, ssm_d, ssm_norm_g, ssm_w_out,
              diff_w_qkv, diff_lambda, diff_subln_g, diff_w_out,
              win_w_qkv, win_sink, win_w_out, final_norm_g):
    caches = [(state_l0_fwd, state_l0_bwd), (cache_l1_k, cache_l1_v),
              (cache_l2_k, cache_l2_v), (state_l3_fwd, state_l3_bwd)]
    L_lat = x_sample.shape[1]
    ang_diff = _axial_angles(L_lat, DIFF_DH)
    ang_win = _axial_angles(L_lat, WIN_DH)
    xp, xs = x_prompt, x_sample
    cctx = c_ctx[None, :]
    new_state = []
    for i in range(DEPTH):
        m, j = i % N_MIXERS, i // N_MIXERS
        mp = _modulation(cctx, w_ada[i], b_ada[i])
        ms = _modulation(c, w_ada[i], b_ada[i])
        xp = _ffn_sub(xp, mp[0], mp[1], mp[2], norm_g[i, 0], ffn1_w_in[i], ffn1_w_out[i])
        xs = _ffn_sub(xs, ms[0], ms[1], ms[2], norm_g[i, 0], ffn1_w_in[i], ffn1_w_out[i])
        hp = _modnorm(xp, norm_g[i, 1], mp[3], mp[4])
        hs = _modnorm(xs, norm_g[i, 1], ms[3], ms[4])
        cache_a, cache_b = caches[i]
        if m == 0:
            prm = (ssm_w_in[j], ssm_conv_w[j], ssm_conv_b[j], ssm_dt_bias[j], ssm_a_log[j],
                   ssm_d[j], ssm_norm_g[j], ssm_w_out[j])
            z0 = jnp.zeros((hp.shape[0], SSM_HEADS, SSM_HEAD_P, D_STATE), hp.dtype)
            yp, hf, hb = _ssm_mixer(hp, z0, z0, *prm)
            ys, _, _ = _ssm_mixer(hs, cache_a, cache_b, *prm)
            new_state += [hf, hb]
        elif m == 1:
            lambda_init = 0.8 - 0.6 * math.exp(-0.3 * i)
            lv = diff_lambda[j].astype(jnp.float32)
            lam = (jnp.exp(jnp.sum(lv[0] * lv[1])) - jnp.exp(jnp.sum(lv[2] * lv[3]))
                   + lambda_init)
            qp, kp, vp = _diff_qkv(hp, diff_w_qkv[j], None)
            yp = _diff_out(_diff_attend(qp, kp, vp, lam), diff_subln_g[j], diff_w_out[j], lambda_init)
            ql, kl, vl = _diff_qkv(hs, diff_w_qkv[j], ang_diff)
            k_all = jnp.concatenate([cache_a.astype(kl.dtype), kl], axis=1)
            v_all = jnp.concatenate([cache_b.astype(vl.dtype), vl], axis=1)
            ys = _diff_out(_diff_attend(ql, k_all, v_all, lam), diff_subln_g[j], diff_w_out[j], lambda_init)
            new_state += [kp, vp]
        else:
            sink = win_sink[j].reshape(WIN_KV, WIN_QPK)
            qp, kp, vp = _win_qkv(hp, win_w_qkv[j], None)
            op = _win_context(qp, kp, vp, sink)
            yp = op.reshape(op.shape[0], op.shape[1], WIN_HEADS * WIN_DH) @ win_w_out[j]
            ql, kl, vl = _win_qkv(hs, win_w_qkv[j], ang_win)
            ol = _win_latent(ql, kl, vl, cache_a.astype(kl.dtype), cache_b.astype(vl.dtype), sink)
            ys = ol.reshape(ol.shape[0], ol.shape[1], WIN_HEADS * WIN_DH) @ win_w_out[j]
            new_state += [kp, vp]
        xp = xp + mp[5] * yp
        xs = xs + ms[5] * ys
        xp = _ffn_sub(xp, mp[6], mp[7], mp[8], norm_g[i, 2], ffn2_w_in[i], ffn2_w_out[i])
        xs = _ffn_sub(xs, ms[6], ms[7], ms[8], norm_g[i, 2], ffn2_w_in[i], ffn2_w_out[i])
    y_prompt = _rmsnorm(xp, final_norm_g)
    y_sample = _rmsnorm(xs, final_norm_g)
    s0f, s0b, k1, v1, k2, v2, s3f, s3b = new_state
    return (y_prompt, y_sample, s0f, s0b, k1, v1, k2, v2, s3f, s3b)
```

```python
import contextlib
import math
import os
import numpy as np
import concourse.bass as bass
import concourse.mybir as mybir
from concourse.bass_utils import run_bass_kernel_spmd

F32 = mybir.dt.float32
BF16 = mybir.dt.bfloat16
AF = mybir.ActivationFunctionType
ALU = mybir.AluOpType
AX = mybir.AxisListType

D = 2048
DC = 16
NP_SEQ = 2
LP = 256
LS = 1024
T = NP_SEQ * LP + LS
NTT = T // 512
DEPTH = 4
D_FF = 5632
FC = D_FF // 128
N_MOD = 9
EPS = 1e-6
N_CORES = 8

SAME_ENGINE_SYNC = os.environ.get('KSYNC', '1') == '1'
KSTOP = int(os.environ.get('KSTOP', '9'))
KPART = os.environ.get('KPART', 'kv')
KSKIP = os.environ.get('KSKIP', '')


class Res:
    __slots__ = ("name", "w", "r", "dsem", "dval")

    def __init__(self, name):
        self.name = name
        self.w = None
        self.r = {}
        self.dsem = None
        self.dval = 0


class Eng:
    def __init__(self, name, handle, sem):
        self.name = name
        self.h = handle
        self.sem = sem
        self.count = 0
        self.waited = {}
        self.pend_r = []
        self.pend_w = []


class Sched:
    def __init__(self, nc, es):
        self.nc = nc
        self.es = es
        self.engs = {}
        for name, h in (("pe", nc.tensor), ("act", nc.scalar), ("dve", nc.vector),
                        ("pool", nc.gpsimd), ("sp", nc.sync)):
            sem = es.enter_context(nc.semaphore("sem_" + name))
            self.engs[name] = Eng(name, h, sem)
        self.sem_ids = {}
        self.store_toks = {}
        self.n_ops = 0

    def _wait(self, E, deps):
        for (sem, val) in deps:
            if sem is E.sem and not SAME_ENGINE_SYNC:
                continue
            k = id(sem)
            if E.waited.get(k, 0) >= val:
                continue
            E.h.wait_ge(sem, val)
            E.waited[k] = val

    @staticmethod
    def _deps(reads, writes):
        deps = []
        for r in reads:
            if r.w is not None:
                deps.append(r.w)
        for w in writes:
            if w.w is not None:
                deps.append(w.w)
            for sem_k, (sem, val) in w.r.items():
                deps.append((sem, val))
        return deps

    def op(self, eng, fn, reads=(), writes=(), inc=True):
        E = self.engs[eng]
        self._wait(E, self._deps(reads, writes))
        ins = fn(E.h)
        self.n_ops += 1
        E.pend_r.extend(reads)
        E.pend_w.extend(writes)
        if inc:
            E.count += 1
            ins.then_inc(E.sem, 1)
            tok = (E.sem, E.count)
            for r in E.pend_r:
                r.r[id(E.sem)] = tok
            for w in E.pend_w:
                w.w = tok
                w.r = {}
            E.pend_r = []
            E.pend_w = []
        return ins

    def dma(self, eng, out, in_, reads=(), writes=(), store=False):
        E = self.engs[eng]
        self._wait(E, self._deps(reads, writes))
        res = (list(writes) + list(reads))[0]
        if res.dsem is None:
            if res.name not in self.sem_ids:
                self.sem_ids[res.name] = [self.es.enter_context(self.nc.semaphore("dsem_" + res.name)), 0]
            res.dsem, res.dval = self.sem_ids[res.name]
        res.dval += 16
        self.sem_ids[res.name][1] = res.dval
        E.h.dma_start(out=out, in_=in_).then_inc(res.dsem, 16)
        self.n_ops += 1
        tok = (res.dsem, res.dval)
        for r in reads:
            r.r[id(res.dsem)] = tok
        for w in writes:
            w.w = tok
            w.r = {}
        if store:
            self.store_toks[id(res.dsem)] = tok

    def barrier(self):
        toks = [(E.sem, E.count) for E in self.engs.values() if E.count > 0]
        toks += list(self.store_toks.values())
        for E in self.engs.values():
            assert not E.pend_r and not E.pend_w, "pending ops at barrier"
            self._wait(E, toks)

    def finish(self):
        E = self.engs["sp"]
        self._wait(E, list(self.store_toks.values()))
        toks = [(e.sem, e.count) for e in self.engs.values() if e.count > 0]
        self._wait(E, toks)


DI = 4096
NG = 8
N_SSM = 2
SSM_IN = 10368
QBL = 128
CONST_SPEC = (("norm_g", DEPTH * 3 * DC), ("final_g", DC), ("ident", 128), ("ones", 128),
              ("Ule", 128), ("Uge", 128), ("SL", 128), ("SU", 128), ("RT", 128),
              ("conv_w", N_SSM * 3 * 48), ("conv_b", N_SSM * 48), ("ssm_norm_g", N_SSM * 32),
              ("dt_bias", N_SSM * 128), ("a_log", N_SSM * 128), ("ssm_d", N_SSM * 64),
              ("subln", 256), ("lam", 512), ("sink", 16))


def fm(vec):
    v = np.asarray(vec, np.float32).reshape(-1, 128)
    return np.ascontiguousarray(v.T)


def bc(vec):
    v = np.asarray(vec, np.float32).reshape(1, -1)
    return np.ascontiguousarray(np.broadcast_to(v, (128, v.shape[1])))


def const_layout():
    lay = {}
    n = 0
    for name, w in CONST_SPEC:
        lay[name] = (n, w)
        n += w
    return lay, n


def pack_consts(inp):
    k = np.arange(128)
    parts = {
        "norm_g": fm(inp["norm_g"].reshape(-1)),
        "final_g": fm(inp["final_norm_g"]),
        "ident": np.eye(128, dtype=np.float32),
        "ones": np.ones((128, 128), np.float32),
        "Ule": (k[:, None] <= k[None, :]).astype(np.float32),
        "Uge": (k[:, None] >= k[None, :]).astype(np.float32),
        "SL": (k[:, None] > k[None, :]).astype(np.float32),
        "SU": (k[:, None] < k[None, :]).astype(np.float32),
    }
    R = np.zeros((128, 128), np.float32)
    for d in range(128):
        if (d // 32) % 2 == 0:
            R[d, d + 32] = -1.0
        else:
            R[d, d - 32] = 1.0
    parts["RT"] = np.ascontiguousarray(R.T)
    parts["conv_w"] = fm(inp["ssm_conv_w"].reshape(-1))
    parts["conv_b"] = fm(inp["ssm_conv_b"].reshape(-1))
    parts["ssm_norm_g"] = fm(inp["ssm_norm_g"].reshape(-1))
    parts["dt_bias"] = bc(inp["ssm_dt_bias"].reshape(-1))
    parts["a_log"] = bc(inp["ssm_a_log"].reshape(-1))
    parts["ssm_d"] = bc(inp["ssm_d"].reshape(-1))
    parts["subln"] = bc(inp["diff_subln_g"].reshape(-1))
    parts["lam"] = bc(inp["diff_lambda"].reshape(-1))
    parts["sink"] = bc(inp["win_sink"].reshape(-1))
    lay, n = const_layout()
    arrs = []
    for name, w in CONST_SPEC:
        a = parts[name]
        assert a.shape == (128, w), (name, a.shape, w)
        arrs.append(a)
    return np.ascontiguousarray(np.concatenate(arrs, axis=1))


def rope_tables():
    L, GW, nf = LS, 64, 32
    rows = L // GW
    row = np.repeat(np.arange(rows, dtype=np.float32), GW)
    col = np.tile(np.arange(GW, dtype=np.float32), rows)
    inv = (np.float32(10000.0) ** (-np.arange(nf, dtype=np.float32) / np.float32(nf))).astype(np.float32)
    ar = (row[:, None] * inv).astype(np.float32)
    ac = (col[:, None] * inv).astype(np.float32)
    ang = np.concatenate([ar, ar, ac, ac], axis=1)
    cs = np.stack([np.cos(ang).T, np.sin(ang).T], axis=1).astype(np.float32)
    return np.ascontiguousarray(cs)


def win_mask():
    qi = np.arange(128)[:, None]
    kj = np.arange(384)[None, :]
    ok = np.abs(qi + 128 - kj) <= 128
    return np.where(ok, 0.0, -30000.0).astype(np.float32)


TPM = 1024


class Builder:
    def __init__(self, layers=None, passes=("A", "B"), ffn=True, mix=True):
        self.layers = list(range(DEPTH)) if layers is None else layers
        self.pass_names = passes
        self.do_ffn = ffn
        self.do_mix = mix
        self.nc = bass.Bass("TRN2", target_bir_lowering=False)

    def dram_in(self, name, shape, dt=F32):
        return self.nc.dram_tensor(name, list(shape), dt, kind="ExternalInput").ap()

    def dram_out(self, name, shape, dt=F32):
        return self.nc.dram_tensor(name, list(shape), dt, kind="ExternalOutput").ap()

    def sb(self, name, shape, dt, es=None):
        self.uid = getattr(self, "uid", 0) + 1
        return (es or self.es).enter_context(self.nc.sbuf_tensor(f"{name}_{self.uid}", list(shape), dt))

    def psum(self):
        i = self.ps_next
        self.ps_next = (i + 1) % 8
        return self.ps_t[i], self.ps_r[i]

    def wslot(self):
        i = self.w_next
        self.w_next = (i + 1) % self.NW
        return self.w_t[i], self.w_r[i]

    def load_cols(self, W, col0, ncols=128, rows=D):
        t, r = self.wslot()
        kc = rows // 128
        assert kc * ncols <= 4096
        view = t[:, 0:kc * ncols].rearrange("p (c n) -> p c n", c=kc)
        src = W[:, col0:col0 + ncols].rearrange("(c p) n -> p c n", p=128)
        self.S.dma("pool", view, src, writes=[r])
        return view, r

    def load_rows(self, W, row0, ncols=D, col0=0):
        t, r = self.wslot()
        view = t[:, 0:ncols]
        self.S.dma("pool", view, W[row0:row0 + 128, col0:col0 + ncols], writes=[r])
        return view, r

    def scratch32(self):
        i = self.sc_next
        self.sc_next = (i + 1) % self.NSC
        return self.sc32[i], self.sc32_r[i]

    def cs(self, name, a=0, b=None):
        off, w = self.lay[name]
        if b is None:
            b = w
        return self.cst[:, off + a:off + b]

    def build(self):
        nc = self.nc
        with contextlib.ExitStack() as es:
            self.es = es
            self.S = S = Sched(nc, es)
            self.declare_io()
            self.alloc()
            self.load_consts()
            self.modulation_all()
            for pn in self.pass_names:
                self.set_pass(pn)
                self.load_x()
                for i in self.layers:
                    self.layer(i)
                self.final()
                S.barrier()
            S.finish()
        return nc

    def set_pass(self, pn):
        if pn == "A":
            self.tok0, self.TP, self.nseq, self.L, self.sample, self.v = 0, 512, 2, 256, False, 0
        else:
            self.tok0, self.TP, self.nseq, self.L, self.sample, self.v = 512, 1024, 1, 1024, True, 1
        self.TT = self.TP // 512
        self.NT = self.TP // 128

    def declare_io(self):
        lay, ncst = const_layout()
        self.lay = lay
        di = self.dram_in
        self.xT_in = di("xT", [D, T])
        self.cvec_in = di("cvec", [128, DC * 2])
        self.consts_in = di("consts", [128, ncst])
        self.bada_in = di("b_ada_fm", [128, DEPTH * N_MOD * DC])
        self.rope_in = di("rope_cs", [128, 2, LS])
        self.wmask_in = di("wmask", [128, 384])
        self.st_in = di("st_in", [2, 2, 128, DI])
        self.kc1T_in = di("kc1T", [D, 256])
        self.vc1_in = di("vc1", [256, D])
        self.kc2T_in = di("kc2T", [512, 256])
        self.vc2_in = di("vc2", [256, 512])
        self.w_ada = di("w_ada", [DEPTH, D, N_MOD * D])
        if self.do_ffn:
            self.ffn_w_in = [di("ffn1_w_in", [DEPTH, D, 2 * D_FF]), di("ffn2_w_in", [DEPTH, D, 2 * D_FF])]
            self.ffn_w_out = [di("ffn1_w_out", [DEPTH, D_FF, D]), di("ffn2_w_out", [DEPTH, D_FF, D])]
        kinds = {i % 3 for i in self.layers} if self.do_mix else set()
        if 0 in kinds:
            self.ssm_w_in = di("ssm_w_in", [N_SSM, D, SSM_IN])
            self.ssm_w_out = di("ssm_w_out", [N_SSM, DI, D])
        if 1 in kinds:
            self.diff_w_qkv = di("diff_w_qkv", [1, D, 6144])
            self.diff_w_out = di("diff_w_out", [1, D, D])
        if 2 in kinds:
            self.win_w_qkv = di("win_w_qkv", [1, D, 3072])
            self.win_w_out = di("win_w_out", [1, D, D])
        do = self.dram_out
        self.yT_out = do("yT", [D, T])
        self.st_out = do("st_out", [2, 2, 2, 128, DI])
        self.k1T_out = do("k1T", [D, 512])
        self.v1_out = do("v1", [512, D])
        self.k2T_out = do("k2T", [512, 512])
        self.v2_out = do("v2", [512, 512])
        self.yz_scr = self.nc.dram_tensor("yz_scr", [32, 128, TPM], BF16, kind="Internal").ap()

    def alloc(self):
        nc = self.nc
        lay, ncst = const_layout()
        self.x = self.sb("x", [128, DC, TPM], F32)
        self.x_r = [[Res(f"x{c}_{t}") for t in range(2)] for c in range(DC)]
        self.h = self.sb("h", [128, DC, TPM], BF16)
        self.h_r = [[Res(f"h{c}_{t}") for t in range(2)] for c in range(DC)]
        self.cst = self.sb("cst", [128, ncst], F32)
        self.cst_r = Res("cst")
        self.ident_bf = self.sb("ident_bf", [128, 128], BF16)
        self.ones_bf = self.sb("ones_bf", [128, 128], BF16)
        self.RT_bf = self.sb("RT_bf", [128, 128], BF16)
        self.cbf_r = Res("cbf")
        self.mods = self.sb("mods", [128, DEPTH * N_MOD * DC, 2], F32)
        self.mods_r = Res("mods")
        self.ab = self.sb("ab", [128, 3, DC], F32)
        self.ab_r = Res("ab")
        self.rstd = self.sb("rstd", [128, TPM], F32)
        self.rstd_r = [Res(f"rstd{t}") for t in range(2)]
        self.ps_t = [self.es.enter_context(nc.psum_tensor(f"ps{i}", [128, 512], F32)) for i in range(8)]
        self.ps_r = [Res(f"ps{i}") for i in range(8)]
        self.ps_next = 0
        self.NW = 4
        self.w_t = [self.sb(f"w{i}", [128, 4096], BF16) for i in range(self.NW)]
        self.w_r = [Res(f"w{i}") for i in range(self.NW)]
        self.w_next = 0
        self.NSC = 3
        self.sc32 = [self.sb(f"sc32_{i}", [128, 512], F32) for i in range(self.NSC)]
        self.sc32_r = [Res(f"sc32_{i}") for i in range(self.NSC)]
        self.sc_next = 0
        self.sq = [self.sb(f"sq{i}", [128, 512], BF16) for i in range(2)]
        self.sq_r = [Res(f"sq{i}") for i in range(2)]
        self.sq_next = 0
        self.NA = 2
        self.a_t = [self.sb(f"a{i}", [128, TPM], BF16) for i in range(self.NA)]
        self.a_r = [[Res(f"a{i}_{t}") for t in range(2)] for i in range(self.NA)]
        self.a_next = 0

    def load_consts(self):
        S = self.S
        S.dma("sp", self.cst[:], self.consts_in, writes=[self.cst_r])
        for dst, name in ((self.ident_bf, "ident"), (self.ones_bf, "ones"), (self.RT_bf, "RT")):
            S.op("dve", lambda e, dst=dst, name=name: e.tensor_copy(out=dst[:], in_=self.cs(name)),
                 reads=[self.cst_r], writes=[self.cbf_r])

    def load_x(self):
        S = self.S
        xin = self.xT_in.rearrange("(c p) t -> p c t", p=128)
        for c in range(DC):
            for t in range(self.TT):
                S.dma("sp", self.x[:, c, t * 512:(t + 1) * 512],
                      xin[:, c, self.tok0 + t * 512:self.tok0 + (t + 1) * 512], writes=[self.x_r[c][t]])

    def modulation_all(self):
        S = self.S
        NCC = N_MOD * DC
        with contextlib.ExitStack() as ms:
            cv = self.sb("cv", [128, DC * 2], F32, ms)
            cv_r = Res("cv")
            csil = self.sb("csil", [128, DC, 2], BF16, ms)
            csil_r = Res("csil")
            bada = self.sb("bada", [128, DEPTH * NCC], F32, ms)
            bada_r = Res("bada")
            S.dma("sp", cv[:], self.cvec_in, writes=[cv_r])
            S.dma("sp", bada[:], self.bada_in, writes=[bada_r])
            S.op("act", lambda e: e.activation(out=csil[:].rearrange("p c v -> p (c v)"), in_=cv[:], func=AF.Silu),
                 reads=[cv_r], writes=[csil_r])
            for i in self.layers:
                W = self.w_ada[i]
                ps, ps_r = self.psum()
                for cc in range(NCC):
                    wv, wr = self.load_cols(W, cc * 128)
                    for k in range(DC):
                        S.op("pe", lambda e, k=k, wv=wv, cc=cc, ps=ps: e.matmul(
                            ps[:, cc * 2:cc * 2 + 2], wv[:, k, :], csil[:, k, :],
                            start=(k == 0), stop=(k == DC - 1)),
                            reads=[wr, csil_r], writes=[ps_r], inc=(k == DC - 1))
                S.op("dve", lambda e, i=i, ps=ps: e.tensor_tensor(
                    out=self.mods[:, i * NCC:(i + 1) * NCC, :],
                    in0=ps[:, 0:2 * NCC].rearrange("p (c v) -> p c v", v=2),
                    in1=bada[:, i * NCC:(i + 1) * NCC].unsqueeze(2).to_broadcast([128, NCC, 2]), op=ALU.add),
                    reads=[ps_r, bada_r], writes=[self.mods_r])
            S.barrier()

    def modvec(self, i, j):
        b = (i * N_MOD + j) * DC
        return self.mods[:, b:b + DC, self.v]

    def modcol(self, i, j, c):
        b = (i * N_MOD + j) * DC + c
        return self.mods[:, b, self.v:self.v + 1]

    def rms_stats(self):
        S = self.S
        for t in range(self.TT):
            ts = slice(t * 512, (t + 1) * 512)
            ps, ps_r = self.psum()
            for c in range(DC):
                i = self.sq_next
                self.sq_next = (i + 1) % 2
                sq, sq_r = self.sq[i], self.sq_r[i]
                S.op("act", lambda e, c=c, ts=ts, sq=sq: e.activation(out=sq[:], in_=self.x[:, c, ts],
                                                                      func=AF.Square),
                     reads=[self.x_r[c][t]], writes=[sq_r])
                S.op("pe", lambda e, c=c, sq=sq, ps=ps: e.matmul(ps[:], self.ones_bf[:], sq[:],
                                                                 start=(c == 0), stop=(c == DC - 1)),
                     reads=[sq_r, self.cbf_r], writes=[ps_r], inc=True)
            S.op("dve", lambda e, ts=ts, ps=ps: e.tensor_scalar(
                out=self.rstd[:, ts], in0=ps[:], scalar1=1.0 / D, scalar2=EPS, op0=ALU.mult, op1=ALU.add),
                reads=[ps_r], writes=[self.rstd_r[t]])
            S.op("act", lambda e, ts=ts: e.sqrt(out=self.rstd[:, ts], in_=self.rstd[:, ts]),
                 reads=[self.rstd_r[t]], writes=[self.rstd_r[t]])
            S.op("dve", lambda e, ts=ts: e.reciprocal(out=self.rstd[:, ts], in_=self.rstd[:, ts]),
                 reads=[self.rstd_r[t]], writes=[self.rstd_r[t]])

    def modnorm(self, i, sub, j_shift, j_scale):
        S = self.S
        off, _ = self.lay["norm_g"]
        g = self.cst[:, off + (i * 3 + sub) * DC: off + (i * 3 + sub + 1) * DC]
        S.op("dve", lambda e: e.scalar_tensor_tensor(
            out=self.ab[:, 0, :], in0=self.modvec(i, j_scale), scalar=1.0, in1=g, op0=ALU.add, op1=ALU.mult),
            reads=[self.mods_r, self.cst_r], writes=[self.ab_r])
        self.rms_stats()
        for t in range(self.TT):
            ts = slice(t * 512, (t + 1) * 512)
            for c in range(DC):
                sc, sc_r = self.scratch32()
                S.op("dve", lambda e, c=c, ts=ts, sc=sc: e.scalar_tensor_tensor(
                    out=sc[:], in0=self.x[:, c, ts], scalar=self.ab[:, 0, c:c + 1], in1=self.rstd[:, ts],
                    op0=ALU.mult, op1=ALU.mult),
                    reads=[self.x_r[c][t], self.ab_r, self.rstd_r[t]], writes=[sc_r])
                S.op("act", lambda e, c=c, ts=ts, sc=sc: e.activation(
                    out=self.h[:, c, ts], in_=sc[:], func=AF.Identity,
                    bias=self.modcol(i, j_shift, c), scale=1.0),
                    reads=[sc_r, self.mods_r], writes=[self.h_r[c][t]])

    def ffn(self, i, which, j_gate):
        S = self.S
        W_in = self.ffn_w_in[which][i]
        W_out = self.ffn_w_out[which][i]
        S.op("dve", lambda e: e.tensor_scalar(out=self.ab[:, 2, :], in0=self.modvec(i, j_gate), scalar1=0.5,
                                              scalar2=None, op0=ALU.mult), reads=[self.mods_r], writes=[self.ab_r])
        for f in range(FC):
            wg, wg_r = self.load_cols(W_in, f * 128)
            wu, wu_r = self.load_cols(W_in, D_FF + f * 128)
            wo, wo_r = self.load_rows(W_out, f * 128)
            ai = self.a_next
            self.a_next = (ai + 1) % self.NA
            a = self.a_t[ai]
            for t in range(self.TT):
                ts = slice(t * 512, (t + 1) * 512)
                pg, pg_r = self.psum()
                for k in range(DC):
                    S.op("pe", lambda e, k=k, pg=pg, wg=wg, ts=ts: e.matmul(
                        pg[:], wg[:, k, :], self.h[:, k, ts], start=(k == 0), stop=(k == DC - 1)),
                        reads=[wg_r, self.h_r[k][t]], writes=[pg_r], inc=(k == DC - 1))
                pu, pu_r = self.psum()
                for k in range(DC):
                    S.op("pe", lambda e, k=k, pu=pu, wu=wu, ts=ts: e.matmul(
                        pu[:], wu[:, k, :], self.h[:, k, ts], start=(k == 0), stop=(k == DC - 1)),
                        reads=[wu_r, self.h_r[k][t]], writes=[pu_r], inc=(k == DC - 1))
                sc, sc_r = self.scratch32()
                S.op("act", lambda e, sc=sc, pg=pg: e.activation(out=sc[:], in_=pg[:], func=AF.Silu),
                     reads=[pg_r], writes=[sc_r])
                S.op("dve", lambda e, sc=sc, pu=pu, a=a, ts=ts: e.tensor_tensor(
                    out=a[:, ts], in0=sc[:], in1=pu[:], op=ALU.mult),
                    reads=[sc_r, pu_r], writes=[self.a_r[ai][t]])
            for oc in range(DC):
                for t in range(self.TT):
                    ts = slice(t * 512, (t + 1) * 512)
                    po, po_r = self.psum()
                    S.op("pe", lambda e, po=po, wo=wo, oc=oc, a=a, ts=ts: e.matmul(
                        po[:], wo[:, oc * 128:(oc + 1) * 128], a[:, ts], start=True, stop=True),
                        reads=[wo_r, self.a_r[ai][t]], writes=[po_r])
                    self.x_accum(po, po_r, self.ab[:, 2, oc:oc + 1], self.ab_r, oc, t)

    def x_accum(self, po, po_r, scal, scal_r, oc, t, n=512, off=0):
        ts = slice(t * 512 + off, t * 512 + off + n)
        self.S.op("dve", lambda e: e.scalar_tensor_tensor(
            out=self.x[:, oc, ts], in0=po[:, 0:n], scalar=scal, in1=self.x[:, oc, ts],
            op0=ALU.mult, op1=ALU.add),
            reads=[po_r, scal_r, self.x_r[oc][t]], writes=[self.x_r[oc][t]])

    def layer(self, i):
        if self.do_ffn:
            self.modnorm(i, 0, 0, 1)
            self.ffn(i, 0, 2)
        if self.do_mix:
            self.modnorm(i, 1, 3, 4)
            self.S.barrier()
            m = i % 3
            if m == 0:
                self.ssm_mixer(i)
            elif m == 1:
                self.diff_mixer(i)
            else:
                self.win_mixer(i)
            self.S.barrier()
        if self.do_ffn:
            self.modnorm(i, 2, 6, 7)
            self.ffn(i, 1, 8)

    def final(self):
        S = self.S
        self.rms_stats()
        off, _ = self.lay["final_g"]
        yout = self.yT_out.rearrange("(c p) t -> p c t", p=128)
        for t in range(self.TT):
            ts = slice(t * 512, (t + 1) * 512)
            for c in range(DC):
                sc, sc_r = self.scratch32()
                S.op("dve", lambda e, c=c, ts=ts, sc=sc: e.scalar_tensor_tensor(
                    out=sc[:], in0=self.x[:, c, ts], scalar=self.cst[:, off + c:off + c + 1],
                    in1=self.rstd[:, ts], op0=ALU.mult, op1=ALU.mult),
                    reads=[self.x_r[c][t], self.cst_r, self.rstd_r[t]], writes=[sc_r])
                S.dma("sp", yout[:, c, self.tok0 + t * 512:self.tok0 + (t + 1) * 512], sc[:],
                      reads=[sc_r], store=True)

    def proj_fm(self, wv, wr, t, ps, ps_r, ncol=128, wcol0=0):
        ts = slice(t * 512, (t + 1) * 512)
        for k in range(DC):
            self.S.op("pe", lambda e, k=k: e.matmul(ps[0:ncol, :], wv[:, k, wcol0:wcol0 + ncol], self.h[:, k, ts],
                                                    start=(k == 0), stop=(k == DC - 1)),
                      reads=[wr, self.h_r[k][t]], writes=[ps_r], inc=(k == DC - 1))

    def proj_tm(self, wv, wr, nt, ps, ps_r, ncol, pcol0=0, wcol0=0):
        t = nt // 4
        tk = slice(nt * 128, (nt + 1) * 128)
        for k in range(DC):
            self.S.op("pe", lambda e, k=k: e.matmul(ps[:, pcol0:pcol0 + ncol], self.h[:, k, tk],
                                                    wv[:, k, wcol0:wcol0 + ncol],
                                                    start=(k == 0), stop=(k == DC - 1)),
                      reads=[wr, self.h_r[k][t]], writes=[ps_r], inc=(k == DC - 1))

    def rope_evac(self, ps, ps_r, dst, dst_r, t, cs_t, cs_r):
        S = self.S
        ts = slice(t * 512, (t + 1) * 512)
        xb = self.sq[self.sq_next]
        xb_r = self.sq_r[self.sq_next]
        self.sq_next = (self.sq_next + 1) % 2
        S.op("dve", lambda e: e.tensor_copy(out=xb[:], in_=ps[:]), reads=[ps_r], writes=[xb_r])
        pr, pr_r = self.psum()
        S.op("pe", lambda e: e.matmul(pr[:], self.RT_bf[:], xb[:], start=True, stop=True),
             reads=[xb_r, self.cbf_r], writes=[pr_r])
        s1, s1_r = self.scratch32()
        s2, s2_r = self.scratch32()
        S.op("dve", lambda e: e.tensor_tensor(out=s1[:], in0=ps[:], in1=cs_t[:, 0, ts], op=ALU.mult),
             reads=[ps_r, cs_r], writes=[s1_r])
        S.op("dve", lambda e: e.tensor_tensor(out=s2[:], in0=pr[:], in1=cs_t[:, 1, ts], op=ALU.mult),
             reads=[pr_r, cs_r], writes=[s2_r])
        S.op("dve", lambda e: e.tensor_tensor(out=dst, in0=s1[:], in1=s2[:], op=ALU.add),
             reads=[s1_r, s2_r], writes=[dst_r])

    def attn_tile(self, qT_ap, kparts, vparts, scale, sink_ap, W):
        S = self.S
        P, P_r = W["P"], W["P_r"]
        PT, PT_r = W["PT"], W["PT_r"]
        st, st_r = W["st"], W["st_r"]
        dv = W["dv"]
        srcs = []
        col = 0
        for j, (kT_ap, n, mask_ap) in enumerate(kparts):
            ps, ps_r = self.psum()
            S.op("pe", lambda e, ps=ps, kT_ap=kT_ap, n=n: e.matmul(ps[:, 0:n], qT_ap, kT_ap, start=True, stop=True),
                 reads=W["q_reads"] + W["k_reads"], writes=[ps_r])
            if mask_ap is not None:
                sc, sc_r = self.scratch32()
                S.op("dve", lambda e, sc=sc, ps=ps, n=n, mask_ap=mask_ap: e.tensor_tensor(
                    out=sc[:, 0:n], in0=ps[:, 0:n], in1=mask_ap, op=ALU.add),
                    reads=[ps_r, W["mask_r"]], writes=[sc_r])
                src, src_r = sc, sc_r
            else:
                sc, sc_r = self.scratch32()
                S.op("dve", lambda e, sc=sc, ps=ps, n=n: e.tensor_copy(out=sc[:, 0:n], in_=ps[:, 0:n]),
                     reads=[ps_r], writes=[sc_r])
                src, src_r = sc, sc_r
            S.op("dve", lambda e, src=src, n=n, j=j: e.reduce_max(out=st[:, j:j + 1], in_=src[:, 0:n], axis=AX.X),
                 reads=[src_r], writes=[st_r])
            srcs.append((src, src_r, n, col))
            col += n
        nk = col
        for j in range(1, len(kparts)):
            S.op("dve", lambda e, j=j: e.tensor_tensor(out=st[:, 0:1], in0=st[:, 0:1], in1=st[:, j:j + 1], op=ALU.max),
                 reads=[st_r], writes=[st_r])
        if sink_ap is None:
            S.op("dve", lambda e: e.tensor_scalar(out=st[:, 4:5], in0=st[:, 0:1], scalar1=-scale, scalar2=None,
                                                  op0=ALU.mult), reads=[st_r], writes=[st_r])
        else:
            S.op("dve", lambda e: e.tensor_scalar(out=st[:, 4:5], in0=st[:, 0:1], scalar1=-scale,
                                                  scalar2=W["nsink_ap"], op0=ALU.mult, op1=ALU.min),
                 reads=[st_r, W["nsink_r"]], writes=[st_r])
        S.op("dve", lambda e: e.memset(st[:, 8:8 + len(kparts) + 1], 0.0), writes=[st_r])
        for j, (src, src_r, n, c0) in enumerate(srcs):
            S.op("act", lambda e, src=src, n=n, c0=c0, j=j: e.activation(
                out=P[:, c0:c0 + n], in_=src[:, 0:n], func=AF.Exp, bias=st[:, 4:5], scale=scale,
                accum_out=st[:, 8 + j:9 + j]),
                reads=[src_r, st_r], writes=[P_r, st_r])
        ns = len(kparts)
        if sink_ap is not None:
            S.op("act", lambda e: e.activation(out=st[:, 8 + ns:9 + ns], in_=sink_ap, func=AF.Exp,
                                               bias=st[:, 4:5], scale=1.0),
                 reads=[st_r, W["nsink_r"]], writes=[st_r])
            ns += 1
        S.op("dve", lambda e: e.reduce_sum(out=st[:, 5:6], in_=st[:, 8:8 + ns], axis=AX.X),
             reads=[st_r], writes=[st_r])
        S.op("dve", lambda e: e.reciprocal(out=st[:, 6:7], in_=st[:, 5:6]), reads=[st_r], writes=[st_r])
        nkt = nk // 128
        for b0 in range(0, nkt, 4):
            nb = min(4, nkt - b0)
            pt, pt_r = self.psum()
            for jj in range(nb):
                kt = b0 + jj
                S.op("pe", lambda e, pt=pt, jj=jj, kt=kt: e.matmul(
                    pt[:, jj * 128:(jj + 1) * 128], P[:, kt * 128:(kt + 1) * 128], self.ident_bf[:],
                    start=True, stop=True), reads=[P_r, self.cbf_r], writes=[pt_r], inc=(jj == nb - 1))
            S.op("act", lambda e, pt=pt, b0=b0, nb=nb: e.copy(
                out=PT[:, b0 * 128:(b0 + nb) * 128], in_=pt[:, 0:nb * 128]),
                reads=[pt_r], writes=[PT_r])
        po, po_r = self.psum()
        for kt in range(nkt):
            S.op("pe", lambda e, kt=kt: e.matmul(po[:, 0:dv], PT[:, kt * 128:(kt + 1) * 128], vparts[kt],
                                                 start=(kt == 0), stop=(kt == nkt - 1)),
                 reads=[PT_r] + W["v_reads"], writes=[po_r], inc=(kt == nkt - 1))
        return po, po_r

    def transpose_to(self, src_ap, src_reads, dst_ap, dst_r, ncols=128, scale_ap=None, scale_r=None):
        S = self.S
        pt, pt_r = self.psum()
        S.op("pe", lambda e: e.matmul(pt[0:ncols, 0:128], src_ap, self.ident_bf[:], start=True, stop=True),
             reads=list(src_reads) + [self.cbf_r], writes=[pt_r])
        if scale_ap is None:
            S.op("act", lambda e: e.copy(out=dst_ap, in_=pt[0:ncols, 0:128]), reads=[pt_r], writes=[dst_r])
        else:
            S.op("dve", lambda e: e.tensor_scalar(out=dst_ap, in0=pt[0:ncols, 0:128], scalar1=scale_ap, scalar2=None,
                                                  op0=ALU.mult), reads=[pt_r, scale_r], writes=[dst_r])

    def win_mixer(self, i):
        S = self.S
        Wq = self.win_w_qkv[0]
        Wo = self.win_w_out[0]
        TP, NT, L = self.TP, self.NT, self.L
        scale = 128 ** -0.5
        koff = 256 if self.sample else 0
        with contextlib.ExitStack() as ms:
            sb = lambda n, s, d: self.sb("wn_" + n, s, d, ms)
            kT = sb("kT", [128, koff + TP], BF16); kT_r = Res("wkT")
            V = sb("V", [128, (koff + TP) // 128, 128], BF16); V_r = Res("wV")
            qT = sb("qT", [128, TP], BF16); qT_r = Res("wqT")
            oT = sb("oT", [128, TP], BF16); oT_r = Res("woT")
            W = {"P": sb("P", [128, 640], BF16), "P_r": Res("wP"), "PT": sb("PT", [128, 640], BF16),
                 "PT_r": Res("wPT"), "st": sb("st", [128, 16], F32), "st_r": Res("wst"), "dv": 128,
                 "q_reads": [qT_r], "k_reads": [kT_r], "v_reads": [V_r]}
            o_tm = sb("o_tm", [128, 128], BF16); o_r = Res("wo_tm")
            nsink = sb("nsink", [128, 16], F32); nsink_r = Res("wnsink")
            W["nsink_r"] = nsink_r
            stage = sb("stage", [128, 512], F32); stage_r = Res("wstage")
            S.op("dve", lambda e: e.tensor_scalar(out=nsink[:], in0=self.cs("sink"), scalar1=-1.0, scalar2=None,
                                                  op0=ALU.mult), reads=[self.cst_r], writes=[nsink_r])
            if self.sample:
                cs_t = sb("cs", [128, 2, LS], F32); cs_r = Res("wcs")
                mask = sb("mask", [128, 384], F32); mask_r = Res("wmask")
                W["mask_r"] = mask_r
                S.dma("sp", cs_t[:], self.rope_in, writes=[cs_r])
                S.dma("sp", mask[:], self.wmask_in, writes=[mask_r])
            for g in range(4):
                if KSTOP < 1:
                    break
                wk, wk_r = self.load_cols(Wq, 2048 + g * 128)
                if self.sample:
                    S.dma("pool", kT[:, 0:256], self.kc2T_in[g * 128:(g + 1) * 128, :], writes=[kT_r])
                for t in range(self.TT if 'k' in KPART else 0):
                    ps, ps_r = self.psum()
                    self.proj_fm(wk, wk_r, t, ps, ps_r)
                    dst = kT[:, koff + t * 512: koff + (t + 1) * 512]
                    if self.sample:
                        self.rope_evac(ps, ps_r, dst, kT_r, t, cs_t, cs_r)
                    else:
                        S.op("act", lambda e, ps=ps: e.copy(out=stage[:], in_=ps[:]), reads=[ps_r], writes=[stage_r])
                        S.op("dve", lambda e, dst=dst: e.tensor_copy(out=dst, in_=stage[:]),
                             reads=[stage_r], writes=[kT_r])
                        if 's' not in KSKIP:
                            S.dma("sp", self.k2T_out[g * 128:(g + 1) * 128, t * 512:(t + 1) * 512], stage[:],
                                  reads=[stage_r], store=True)
                wv, wv_r = self.load_cols(Wq, 2560 + g * 128)
                if self.sample:
                    S.dma("pool", V[:, 0:2, :],
                          self.vc2_in[:, g * 128:(g + 1) * 128].rearrange("(n p) d -> p n d", p=128), writes=[V_r])
                for nt in range(NT if 'v' in KPART else 0):
                    ps, ps_r = self.psum()
                    self.proj_tm(wv, wv_r, nt, ps, ps_r, 128)
                    if self.sample:
                        S.op("act", lambda e, ps=ps, nt=nt: e.copy(out=V[:, koff // 128 + nt, :], in_=ps[:, 0:128]),
                             reads=[ps_r], writes=[V_r])
                    else:
                        S.op("act", lambda e, ps=ps: e.copy(out=stage[:, 0:128], in_=ps[:, 0:128]),
                             reads=[ps_r], writes=[stage_r])
                        S.op("dve", lambda e, nt=nt: e.tensor_copy(out=V[:, koff // 128 + nt, :], in_=stage[:, 0:128]),
                             reads=[stage_r], writes=[V_r])
                        S.dma("sp", self.v2_out[nt * 128:(nt + 1) * 128, g * 128:(g + 1) * 128], stage[:, 0:128],
                              reads=[stage_r], store=True)
                for r in range(4):
                    if KSTOP < 2:
                        break
                    hd = g * 4 + r
                    wq, wq_r = self.load_cols(Wq, hd * 128)
                    W["nsink_ap"] = nsink[:, hd:hd + 1]
                    for t in range(self.TT):
                        ps, ps_r = self.psum()
                        self.proj_fm(wq, wq_r, t, ps, ps_r)
                        dst = qT[:, t * 512:(t + 1) * 512]
                        if self.sample:
                            self.rope_evac(ps, ps_r, dst, qT_r, t, cs_t, cs_r)
                        else:
                            S.op("act", lambda e, ps=ps, dst=dst: e.copy(out=dst, in_=ps[:]),
                                 reads=[ps_r], writes=[qT_r])
                    for s in range(self.nseq):
                        if KSTOP < 3:
                            break
                        for qb in range(L // 128):
                            q0 = s * L + qb * 128
                            if self.sample:
                                b_lo, b_hi = max(qb - 1, 0), min(qb + 1, L // 128 - 1)
                                nb = (b_hi - b_lo + 1) * 128
                                m0 = (b_lo - (qb - 1)) * 128
                                kparts = [(kT[:, 0:256], 256, None),
                                          (kT[:, 256 + b_lo * 128: 256 + b_lo * 128 + nb], nb, mask[:, m0:m0 + nb])]
                                vparts = [V[:, 0, :], V[:, 1, :]] + [V[:, 2 + b, :] for b in range(b_lo, b_hi + 1)]
                            else:
                                kparts = [(kT[:, s * L:(s + 1) * L], L, None)]
                                vparts = [V[:, s * (L // 128) + b, :] for b in range(L // 128)]
                            po, po_r = self.attn_tile(qT[:, q0:q0 + 128], kparts, vparts, scale,
                                                      self.cs("sink", hd, hd + 1), W)
                            S.op("dve", lambda e, po=po: e.tensor_scalar(
                                out=o_tm[:], in0=po[:, 0:128], scalar1=W["st"][:, 6:7], scalar2=None, op0=ALU.mult),
                                reads=[po_r, W["st_r"]], writes=[o_r])
                            self.transpose_to(o_tm[:], [o_r], oT[:, q0:q0 + 128], oT_r)
                    if KSTOP < 4:
                        continue
                    wo, wo_r = self.load_rows(Wo, hd * 128)
                    for oc in range(DC):
                        for t in range(self.TT):
                            po, po_r = self.psum()
                            S.op("pe", lambda e, po=po, oc=oc, t=t: e.matmul(
                                po[:], wo[:, oc * 128:(oc + 1) * 128], oT[:, t * 512:(t + 1) * 512],
                                start=True, stop=True), reads=[wo_r, oT_r], writes=[po_r])
                            self.x_accum(po, po_r, self.modcol(i, 5, oc), self.mods_r, oc, t)
            S.barrier()

    def diff_mixer(self, i):
        S = self.S
        Wq = self.diff_w_qkv[0]
        Wo = self.diff_w_out[0]
        TP, NT, L = self.TP, self.NT, self.L
        scale = 128 ** -0.5
        lambda_init = 0.8 - 0.6 * math.exp(-0.3 * i)
        koff = 256 if self.sample else 0
        NK = koff + L
        with contextlib.ExitStack() as ms:
            sb = lambda n, s, d: self.sb("df_" + n, s, d, ms)
            kT = sb("kT", [128, 2, koff + TP], BF16); kT_r = Res("dkT")
            V = sb("V", [128, (koff + TP) // 128, 256], BF16); V_r = Res("dV")
            qT = sb("qT", [128, 2, TP], BF16); qT_r = Res("dqT")
            oT = sb("oT", [128, 2, TP], BF16); oT_r = Res("doT")
            Ws = []
            for m in range(2):
                Ws.append({"P": sb(f"P{m}", [128, NK], BF16), "P_r": Res(f"dP{m}"),
                           "PT": sb(f"PT{m}", [128, NK], BF16), "PT_r": Res(f"dPT{m}"),
                           "st": sb(f"st{m}", [128, 16], F32), "st_r": Res(f"dst{m}"), "dv": 256,
                           "q_reads": [qT_r], "k_reads": [kT_r], "v_reads": [V_r]})
            o32 = sb("o32", [128, 256], F32); o32_r = Res("do32")
            o_tm = sb("o_tm", [128, 256], BF16); o_r = Res("do_tm")
            lam = sb("lam", [128, 8], F32); lam_r = Res("dlam")
            junk = sb("junk", [128, 256], F32); junk_r = Res("djunk")
            stage = sb("stage", [128, 512], F32); stage_r = Res("dstage")
            lv = self.cs("lam")
            S.op("dve", lambda e: e.tensor_tensor(out=junk[:, 0:128], in0=lv[:, 0:128], in1=lv[:, 128:256], op=ALU.mult),
                 reads=[self.cst_r], writes=[junk_r])
            S.op("dve", lambda e: e.reduce_sum(out=lam[:, 0:1], in_=junk[:, 0:128], axis=AX.X),
                 reads=[junk_r], writes=[lam_r])
            S.op("dve", lambda e: e.tensor_tensor(out=junk[:, 128:256], in0=lv[:, 256:384], in1=lv[:, 384:512],
                                                  op=ALU.mult), reads=[self.cst_r], writes=[junk_r])
            S.op("dve", lambda e: e.reduce_sum(out=lam[:, 1:2], in_=junk[:, 128:256], axis=AX.X),
                 reads=[junk_r], writes=[lam_r])
            S.op("act", lambda e: e.activation(out=lam[:, 2:4], in_=lam[:, 0:2], func=AF.Exp),
                 reads=[lam_r], writes=[lam_r])
            S.op("dve", lambda e: e.tensor_tensor(out=lam[:, 4:5], in0=lam[:, 3:4], in1=lam[:, 2:3], op=ALU.subtract),
                 reads=[lam_r], writes=[lam_r])
            S.op("dve", lambda e: e.tensor_scalar(out=lam[:, 4:5], in0=lam[:, 4:5], scalar1=-lambda_init, scalar2=None,
                                                  op0=ALU.add), reads=[lam_r], writes=[lam_r])
            if self.sample:
                cs_t = sb("cs", [128, 2, LS], F32); cs_r = Res("dcs")
                S.dma("sp", cs_t[:], self.rope_in, writes=[cs_r])
            for hd in range(8):
                for m in range(2):
                    col = m * 1024 + hd * 128
                    wk, wk_r = self.load_cols(Wq, 2048 + col)
                    if self.sample:
                        S.dma("pool", kT[:, m, 0:256], self.kc1T_in[col:col + 128, :], writes=[kT_r])
                    for t in range(self.TT):
                        ps, ps_r = self.psum()
                        self.proj_fm(wk, wk_r, t, ps, ps_r)
                        dst = kT[:, m, koff + t * 512: koff + (t + 1) * 512]
                        if self.sample:
                            self.rope_evac(ps, ps_r, dst, kT_r, t, cs_t, cs_r)
                        else:
                            S.op("act", lambda e, ps=ps: e.copy(out=stage[:], in_=ps[:]),
                                 reads=[ps_r], writes=[stage_r])
                            S.op("dve", lambda e, dst=dst: e.tensor_copy(out=dst, in_=stage[:]),
                                 reads=[stage_r], writes=[kT_r])
                            S.dma("sp", self.k1T_out[col:col + 128, t * 512:(t + 1) * 512], stage[:],
                                  reads=[stage_r], store=True)
                    wq, wq_r = self.load_cols(Wq, col)
                    for t in range(self.TT):
                        ps, ps_r = self.psum()
                        self.proj_fm(wq, wq_r, t, ps, ps_r)
                        dst = qT[:, m, t * 512:(t + 1) * 512]
                        if self.sample:
                            self.rope_evac(ps, ps_r, dst, qT_r, t, cs_t, cs_r)
                        else:
                            S.op("act", lambda e, ps=ps, dst=dst: e.copy(out=dst, in_=ps[:]),
                                 reads=[ps_r], writes=[qT_r])
                wv, wv_r = self.load_cols(Wq, 4096 + hd * 256, ncols=256)
                if self.sample:
                    S.dma("pool", V[:, 0:2, :],
                          self.vc1_in[:, hd * 256:(hd + 1) * 256].rearrange("(n p) d -> p n d", p=128), writes=[V_r])
                for nt in range(NT):
                    ps, ps_r = self.psum()
                    self.proj_tm(wv, wv_r, nt, ps, ps_r, 256)
                    if self.sample:
                        S.op("act", lambda e, ps=ps, nt=nt: e.copy(out=V[:, koff // 128 + nt, :], in_=ps[:, 0:256]),
                             reads=[ps_r], writes=[V_r])
                    else:
                        S.op("act", lambda e, ps=ps: e.copy(out=stage[:, 0:256], in_=ps[:, 0:256]),
                             reads=[ps_r], writes=[stage_r])
                        S.op("dve", lambda e, nt=nt: e.tensor_copy(out=V[:, koff // 128 + nt, :], in_=stage[:, 0:256]),
                             reads=[stage_r], writes=[V_r])
                        S.dma("sp", self.v1_out[nt * 128:(nt + 1) * 128, hd * 256:(hd + 1) * 256], stage[:, 0:256],
                              reads=[stage_r], store=True)
                for s in range(self.nseq):
                    k0 = s * L if not self.sample else 0
                    vparts = [V[:, (k0 // 128) + b, :] for b in range(NK // 128)]
                    for qb in range(L // 128):
                        q0 = s * L + qb * 128
                        pos = []
                        for m in range(2):
                            kparts = []
                            c = 0
                            while c < NK:
                                n = min(512, NK - c)
                                kparts.append((kT[:, m, k0 + c:k0 + c + n], n, None))
                                c += n
                            pos.append(self.attn_tile(qT[:, m, q0:q0 + 128], kparts, vparts, scale, None, Ws[m]))
                        st0, st1 = Ws[0]["st"], Ws[1]["st"]
                        S.op("dve", lambda e: e.tensor_tensor(out=st1[:, 7:8], in0=st1[:, 6:7], in1=lam[:, 4:5],
                                                              op=ALU.mult),
                             reads=[Ws[1]["st_r"], lam_r], writes=[Ws[1]["st_r"]])
                        (p0, p0_r), (p1, p1_r) = pos
                        S.op("dve", lambda e, p0=p0: e.tensor_scalar(out=o32[:], in0=p0[:, 0:256], scalar1=st0[:, 6:7],
                                                                     scalar2=None, op0=ALU.mult),
                             reads=[p0_r, Ws[0]["st_r"]], writes=[o32_r])
                        S.op("dve", lambda e, p1=p1: e.scalar_tensor_tensor(
                            out=o32[:], in0=p1[:, 0:256], scalar=st1[:, 7:8], in1=o32[:], op0=ALU.mult, op1=ALU.add),
                            reads=[p1_r, Ws[1]["st_r"], o32_r], writes=[o32_r])
                        S.op("dve", lambda e: e.memset(st0[:, 12:13], 0.0), writes=[Ws[0]["st_r"]])
                        S.op("act", lambda e: e.activation(out=junk[:], in_=o32[:], func=AF.Square,
                                                           accum_out=st0[:, 12:13]),
                             reads=[o32_r, Ws[0]["st_r"]], writes=[junk_r, Ws[0]["st_r"]])
                        S.op("dve", lambda e: e.tensor_scalar(out=st0[:, 13:14], in0=st0[:, 12:13], scalar1=1.0 / 256,
                                                              scalar2=EPS, op0=ALU.mult, op1=ALU.add),
                             reads=[Ws[0]["st_r"]], writes=[Ws[0]["st_r"]])
                        S.op("act", lambda e: e.sqrt(out=st0[:, 13:14], in_=st0[:, 13:14]),
                             reads=[Ws[0]["st_r"]], writes=[Ws[0]["st_r"]])
                        S.op("dve", lambda e: e.reciprocal(out=st0[:, 14:15], in_=st0[:, 13:14]),
                             reads=[Ws[0]["st_r"]], writes=[Ws[0]["st_r"]])
                        S.op("dve", lambda e: e.tensor_scalar(out=st0[:, 14:15], in0=st0[:, 14:15],
                                                              scalar1=1.0 - lambda_init, scalar2=None, op0=ALU.mult),
                             reads=[Ws[0]["st_r"]], writes=[Ws[0]["st_r"]])
                        S.op("dve", lambda e: e.scalar_tensor_tensor(
                            out=o_tm[:], in0=o32[:], scalar=st0[:, 14:15], in1=self.cs("subln"),
                            op0=ALU.mult, op1=ALU.mult), reads=[o32_r, Ws[0]["st_r"], self.cst_r], writes=[o_r])
                        for eh in range(2):
                            self.transpose_to(o_tm[:, eh * 128:(eh + 1) * 128], [o_r], oT[:, eh, q0:q0 + 128], oT_r)
                wos = [self.load_rows(Wo, hd * 256 + eh * 128) for eh in range(2)]
                for oc in range(DC):
                    for t in range(self.TT):
                        po, po_r = self.psum()
                        for eh in range(2):
                            wo, wo_r = wos[eh]
                            S.op("pe", lambda e, po=po, oc=oc, t=t, eh=eh, wo=wo: e.matmul(
                                po[:], wo[:, oc * 128:(oc + 1) * 128], oT[:, eh, t * 512:(t + 1) * 512],
                                start=(eh == 0), stop=(eh == 1)), reads=[wo_r, oT_r], writes=[po_r], inc=(eh == 1))
                        self.x_accum(po, po_r, self.modcol(i, 5, oc), self.mods_r, oc, t)
            S.barrier()

    def ssm_mixer(self, i):
        S = self.S
        j = i // 3
        Win = self.ssm_w_in[j]
        Wout = self.ssm_w_out[j]
        TP, NT, L, nseq = self.TP, self.NT, self.L, self.nseq
        nch = L // 128
        U = {0: self.cs("Ule"), 1: self.cs("Uge")}
        SLU = {0: self.cs("SL"), 1: self.cs("SU")}
        with contextlib.ExitStack() as ms:
            sb = lambda n, s, d: self.sb("ss_" + n, s, d, ms)
            ssq = sb("ssq", [128, NT, NG], F32); ssq_r = Res("sssq")
            S.op("dve", lambda e: e.memset(ssq[:].rearrange("p a b -> p (a b)"), 0.0), writes=[ssq_r])
            with contextlib.ExitStack() as gs:
                gb = lambda n, s, d: self.sb("sg_" + n, s, d, gs)
                z_tm = gb("z", [128, NT, 512], BF16); z_r = Res("sz")
                yf = gb("yf", [128, NT, 512], BF16); yf_r = Res("syf")
                xc = gb("xc", [128, 6, TP], BF16); xc_r = Res("sxc")
                pre = gb("pre", [128, TP], F32); pre_r = Res("spre")
                acc = self.rstd; acc_r = Res("sacc")
                dt = gb("dt", [128, NT, 16], F32); dt_r = Res("sdt")
                dtA = gb("dtA", [128, NT, 16], F32); dtA_r = Res("sdtA")
                eac = gb("eac", [128, NT, 16], F32); eac_r = Res("seac")
                cd = gb("cd", [128, NT, 16], F32); cd_r = Res("scd")
                abc = gb("abc", [128, 2, 128], F32); abc_r = Res("sabc")
                xdt = gb("xdt", [128, 512], BF16); xdt_r = Res("sxdt")
                xdtd = gb("xdtd", [128, 512], BF16); xdtd_r = Res("sxdtd")
                B_tm = gb("B_tm", [128, 128], BF16); Btm_r = Res("sBtm")
                mCB = gb("mCB", [128, 128], F32); mCB_r = Res("smCB")
                MT = gb("MT", [128, 8, 128], BF16); MT_r = Res("sMT")
                hT = gb("hT", [128, 512], F32); hT_r = Res("shT")
                hTb = gb("hTb", [128, 512], BF16); hTb_r = Res("shTb")
                y32 = gb("y32", [128, 512], F32); y32_r = Res("sy32")
                yz = gb("yz", [128, 512], BF16); yz_r = Res("syz")
                yzT = self.a_t[1][:, 0:512].rearrange("p (q t) -> p q t", q=4); yzT_r = Res("syzT")
                junk = self.a_t[0]; junk_r = Res("sjunk")
                off, _ = self.lay["a_log"]
                S.op("act", lambda e: e.activation(out=abc[:, 0, :], in_=self.cst[:, off + j * 128: off + (j + 1) * 128],
                                                   func=AF.Exp), reads=[self.cst_r], writes=[abc_r])
                S.op("dve", lambda e: e.tensor_scalar(out=abc[:, 0, :], in0=abc[:, 0, :], scalar1=-1.0, scalar2=None,
                                                      op0=ALU.mult), reads=[abc_r], writes=[abc_r])
                offb, _ = self.lay["dt_bias"]
                offd, _ = self.lay["ssm_d"]
                offcw, _ = self.lay["conv_w"]
                offcb, _ = self.lay["conv_b"]
                offng, _ = self.lay["ssm_norm_g"]
                for g in range(NG):
                    for d_ in range(2):
                        wd, wd_r = self.load_cols(Win, DI + 6144 + d_ * 64 + g * 8, ncols=8)
                        for nt in range(NT):
                            ps, ps_r = self.psum()
                            self.proj_tm(wd, wd_r, nt, ps, ps_r, 8)
                            hsl = slice(d_ * 64 + g * 8, d_ * 64 + g * 8 + 8)
                            bsl = slice(offb + j * 128 + d_ * 64 + g * 8, offb + j * 128 + d_ * 64 + g * 8 + 8)
                            sc, sc_r = self.scratch32()
                            S.op("dve", lambda e, sc=sc, ps=ps, bsl=bsl: e.tensor_tensor(
                                out=sc[:, 0:8], in0=ps[:, 0:8], in1=self.cst[:, bsl], op=ALU.add),
                                reads=[ps_r, self.cst_r], writes=[sc_r])
                            S.op("act", lambda e, sc=sc: e.activation(out=sc[:, 0:8], in_=sc[:, 0:8], func=AF.Exp),
                                 reads=[sc_r], writes=[sc_r])
                            S.op("dve", lambda e, sc=sc: e.tensor_scalar(out=sc[:, 0:8], in0=sc[:, 0:8], scalar1=1.0,
                                                                         scalar2=None, op0=ALU.add),
                                 reads=[sc_r], writes=[sc_r])
                            S.op("act", lambda e, sc=sc, nt=nt, d_=d_: e.activation(
                                out=dt[:, nt, d_ * 8:(d_ + 1) * 8], in_=sc[:, 0:8], func=AF.Ln),
                                reads=[sc_r], writes=[dt_r])
                            S.op("dve", lambda e, nt=nt, d_=d_, hsl=hsl: e.tensor_tensor(
                                out=dtA[:, nt, d_ * 8:(d_ + 1) * 8], in0=dt[:, nt, d_ * 8:(d_ + 1) * 8],
                                in1=abc[:, 0, hsl], op=ALU.mult), reads=[dt_r, abc_r], writes=[dtA_r])
                    for nt in range(NT):
                        ps, ps_r = self.psum()
                        S.op("pe", lambda e, ps=ps, nt=nt: e.matmul(ps[:, 0:8], U[0], dtA[:, nt, 0:8],
                                                                    start=True, stop=True),
                             reads=[self.cst_r, dtA_r], writes=[ps_r])
                        S.op("pe", lambda e, ps=ps, nt=nt: e.matmul(ps[:, 8:16], U[1], dtA[:, nt, 8:16],
                                                                    start=True, stop=True),
                             reads=[self.cst_r, dtA_r], writes=[ps_r])
                        S.op("pe", lambda e, ps=ps, nt=nt: e.matmul(ps[:, 16:32], self.cs("ones"), dtA[:, nt, :],
                                                                    start=True, stop=True),
                             reads=[self.cst_r, dtA_r], writes=[ps_r])
                        S.op("act", lambda e, ps=ps, nt=nt: e.activation(out=eac[:, nt, :], in_=ps[:, 0:16], func=AF.Exp),
                             reads=[ps_r], writes=[eac_r])
                        S.op("act", lambda e, ps=ps, nt=nt: e.activation(out=cd[:, nt, :], in_=ps[:, 16:32], func=AF.Exp),
                             reads=[ps_r], writes=[cd_r])
                    cols = [DI + g * 512 + q * 128 for q in range(4)] + [DI + DI + g * 128, DI + DI + 1024 + g * 128]
                    for ci, col in enumerate(cols):
                        cc = (col - DI) // 128
                        wv, wv_r = self.load_cols(Win, col)
                        for t in range(self.TT):
                            ps, ps_r = self.psum()
                            self.proj_fm(wv, wv_r, t, ps, ps_r)
                            S.op("act", lambda e, ps=ps, t=t: e.copy(out=pre[:, t * 512:(t + 1) * 512], in_=ps[:]),
                                 reads=[ps_r], writes=[pre_r])
                        w0 = self.cst[:, offcw + (j * 3 + 0) * 48 + cc: offcw + (j * 3 + 0) * 48 + cc + 1]
                        w1 = self.cst[:, offcw + (j * 3 + 1) * 48 + cc: offcw + (j * 3 + 1) * 48 + cc + 1]
                        w2 = self.cst[:, offcw + (j * 3 + 2) * 48 + cc: offcw + (j * 3 + 2) * 48 + cc + 1]
                        cb = self.cst[:, offcb + j * 48 + cc: offcb + j * 48 + cc + 1]
                        S.op("dve", lambda e, w1=w1, cb=cb: e.tensor_scalar(
                            out=acc[:, 0:TP], in0=pre[:], scalar1=w1, scalar2=cb, op0=ALU.mult, op1=ALU.add),
                            reads=[pre_r, self.cst_r], writes=[acc_r])
                        for s in range(nseq):
                            a0, a1 = s * L, (s + 1) * L
                            S.op("dve", lambda e, w0=w0, a0=a0, a1=a1: e.scalar_tensor_tensor(
                                out=acc[:, a0 + 1:a1], in0=pre[:, a0:a1 - 1], scalar=w0, in1=acc[:, a0 + 1:a1],
                                op0=ALU.mult, op1=ALU.add), reads=[pre_r, acc_r, self.cst_r], writes=[acc_r])
                            S.op("dve", lambda e, w2=w2, a0=a0, a1=a1: e.scalar_tensor_tensor(
                                out=acc[:, a0:a1 - 1], in0=pre[:, a0 + 1:a1], scalar=w2, in1=acc[:, a0:a1 - 1],
                                op0=ALU.mult, op1=ALU.add), reads=[pre_r, acc_r, self.cst_r], writes=[acc_r])
                        S.op("act", lambda e, ci=ci: e.activation(out=xc[:, ci, :], in_=acc[:, 0:TP], func=AF.Silu),
                             reads=[acc_r], writes=[xc_r])
                    for half in range(2):
                        wz, wz_r = self.load_cols(Win, g * 512 + half * 256, ncols=256)
                        for nt in range(NT):
                            ps, ps_r = self.psum()
                            self.proj_tm(wz, wz_r, nt, ps, ps_r, 256)
                            S.op("act", lambda e, ps=ps, nt=nt, half=half: e.activation(
                                out=z_tm[:, nt, half * 256:(half + 1) * 256], in_=ps[:, 0:256], func=AF.Silu),
                                reads=[ps_r], writes=[z_r])
                    for s in range(nseq):
                        for d_ in range(2):
                            if self.sample:
                                S.dma("sp", hT[:], self.st_in[j, d_, :, g * 512:(g + 1) * 512], writes=[hT_r])
                            else:
                                S.op("dve", lambda e: e.memset(hT[:], 0.0), writes=[hT_r])
                            S.op("act", lambda e: e.copy(out=hTb[:], in_=hT[:]), reads=[hT_r], writes=[hTb_r])
                            order = range(nch) if d_ == 0 else range(nch - 1, -1, -1)
                            for c in order:
                                nt = s * nch + c
                                tk = slice(nt * 128, (nt + 1) * 128)
                                dsl = slice(d_ * 8, d_ * 8 + 8)
                                px, px_r = self.psum()
                                for q in range(4):
                                    S.op("pe", lambda e, q=q, px=px: e.matmul(
                                        px[:, q * 128:(q + 1) * 128], xc[:, q, tk], self.ident_bf[:],
                                        start=True, stop=True), reads=[xc_r, self.cbf_r], writes=[px_r], inc=(q == 3))
                                S.op("dve", lambda e, px=px: e.tensor_tensor(
                                    out=xdt[:].rearrange("p (h q) -> p h q", h=8),
                                    in0=px[:].rearrange("p (h q) -> p h q", h=8),
                                    in1=dt[:, nt, dsl].unsqueeze(2).to_broadcast([128, 8, 64]), op=ALU.mult),
                                    reads=[px_r, dt_r], writes=[xdt_r])
                                if d_ == 0:
                                    dsk = self.cst[:, offd + j * 64 + g * 8: offd + j * 64 + g * 8 + 8]
                                    S.op("dve", lambda e, px=px, dsk=dsk: e.tensor_tensor(
                                        out=y32[:].rearrange("p (h q) -> p h q", h=8),
                                        in0=px[:].rearrange("p (h q) -> p h q", h=8),
                                        in1=dsk.unsqueeze(2).to_broadcast([128, 8, 64]), op=ALU.mult),
                                        reads=[px_r, self.cst_r], writes=[y32_r])
                                else:
                                    S.op("act", lambda e: e.copy(out=y32[:], in_=yf[:, nt, :]),
                                         reads=[yf_r], writes=[y32_r])
                                pb, pb_r = self.psum()
                                S.op("pe", lambda e, pb=pb: e.matmul(pb[:, 0:128], xc[:, 4, tk], self.ident_bf[:],
                                                                     start=True, stop=True),
                                     reads=[xc_r, self.cbf_r], writes=[pb_r])
                                S.op("act", lambda e, pb=pb: e.copy(out=B_tm[:], in_=pb[:, 0:128]),
                                     reads=[pb_r], writes=[Btm_r])
                                pc, pc_r = self.psum()
                                S.op("pe", lambda e, pc=pc: e.matmul(pc[:, 0:128], xc[:, 4, tk], xc[:, 5, tk],
                                                                     start=True, stop=True),
                                     reads=[xc_r], writes=[pc_r])
                                S.op("dve", lambda e, pc=pc: e.tensor_tensor(out=mCB[:], in0=pc[:, 0:128], in1=U[d_],
                                                                             op=ALU.mult),
                                     reads=[pc_r, self.cst_r], writes=[mCB_r])
                                iend = 127 if d_ == 0 else 0
                                for quad in range(2):
                                    hs = slice(d_ * 8 + quad * 4, d_ * 8 + quad * 4 + 4)
                                    rseg, rseg_r = self.scratch32()
                                    Lt, Lt_r = self.scratch32()
                                    S.op("dve", lambda e, hs=hs, rseg=rseg: e.tensor_tensor(
                                        out=rseg[:].rearrange("p (h q) -> p h q", h=4),
                                        in0=U[d_].unsqueeze(1).to_broadcast([128, 4, 128]),
                                        in1=dtA[:, nt, hs].unsqueeze(2).to_broadcast([128, 4, 128]), op=ALU.mult),
                                        reads=[self.cst_r, dtA_r], writes=[rseg_r])
                                    pl, pl_r = self.psum()
                                    S.op("pe", lambda e, pl=pl, rseg=rseg: e.matmul(pl[:], SLU[d_], rseg[:], start=True, stop=True),
                                         reads=[self.cst_r, rseg_r], writes=[pl_r])
                                    S.op("act", lambda e, pl=pl, Lt=Lt: e.activation(out=Lt[:], in_=pl[:], func=AF.Exp),
                                         reads=[pl_r], writes=[Lt_r])
                                    S.op("dve", lambda e, quad=quad, Lt=Lt: e.tensor_tensor(
                                        out=MT[:, quad * 4:(quad + 1) * 4, :],
                                        in0=Lt[:].rearrange("p (h q) -> p h q", h=4),
                                        in1=mCB[:].unsqueeze(1).to_broadcast([128, 4, 128]), op=ALU.mult),
                                        reads=[Lt_r, mCB_r], writes=[MT_r])
                                    S.op("dve", lambda e, quad=quad, Lt=Lt: e.tensor_tensor(
                                        out=xdtd[:, quad * 256:(quad + 1) * 256].rearrange("p (h q) -> p h q", h=4),
                                        in0=xdt[:, quad * 256:(quad + 1) * 256].rearrange("p (h q) -> p h q", h=4),
                                        in1=Lt[:].rearrange("p (h q) -> p h q", h=4)[:, :, iend:iend + 1]
                                        .to_broadcast([128, 4, 64]), op=ALU.mult),
                                        reads=[xdt_r, Lt_r], writes=[xdtd_r])
                                py, py_r = self.psum()
                                for hh in range(8):
                                    S.op("pe", lambda e, hh=hh, py=py: e.matmul(
                                        py[:, hh * 64:(hh + 1) * 64], MT[:, hh, :], xdt[:, hh * 64:(hh + 1) * 64],
                                        start=True, stop=True), reads=[MT_r, xdt_r], writes=[py_r], inc=(hh == 7))
                                pf, pf_r = self.psum()
                                S.op("pe", lambda e, pf=pf: e.matmul(pf[:], xc[:, 5, tk], hTb[:], start=True, stop=True),
                                     reads=[xc_r, hTb_r], writes=[pf_r])
                                S.op("dve", lambda e, py=py: e.tensor_tensor(out=y32[:], in0=py[:], in1=y32[:], op=ALU.add),
                                     reads=[py_r, y32_r], writes=[y32_r])
                                sc, sc_r = self.scratch32()
                                S.op("dve", lambda e, pf=pf, sc=sc: e.tensor_tensor(
                                    out=sc[:].rearrange("p (h q) -> p h q", h=8),
                                    in0=pf[:].rearrange("p (h q) -> p h q", h=8),
                                    in1=eac[:, nt, dsl].unsqueeze(2).to_broadcast([128, 8, 64]), op=ALU.mult),
                                    reads=[pf_r, eac_r], writes=[sc_r])
                                if d_ == 0:
                                    S.op("dve", lambda e, sc=sc: e.tensor_tensor(out=yf[:, nt, :], in0=sc[:], in1=y32[:],
                                                                                op=ALU.add),
                                         reads=[sc_r, y32_r], writes=[yf_r])
                                else:
                                    S.op("dve", lambda e, sc=sc: e.tensor_tensor(out=y32[:], in0=sc[:], in1=y32[:],
                                                                                op=ALU.add),
                                         reads=[sc_r, y32_r], writes=[y32_r])
                                    S.op("dve", lambda e: e.tensor_tensor(out=yz[:], in0=y32[:], in1=z_tm[:, nt, :],
                                                                          op=ALU.mult),
                                         reads=[y32_r, z_r], writes=[yz_r])
                                    S.op("act", lambda e: e.activation(out=junk[:, 0:512], in_=yz[:], func=AF.Square,
                                                                       accum_out=ssq[:, nt, g:g + 1]),
                                         reads=[yz_r, ssq_r], writes=[junk_r, ssq_r])
                                    for q in range(4):
                                        kc = g * 4 + q
                                        self.transpose_to(yz[:, q * 128:(q + 1) * 128], [yz_r], yzT[:, q, :], yzT_r,
                                                          scale_ap=self.cst[:, offng + j * 32 + kc: offng + j * 32 + kc + 1],
                                                          scale_r=self.cst_r)
                                    S.dma("sp", self.yz_scr[g * 4:(g + 1) * 4, :, nt * 128:(nt + 1) * 128]
                                          .rearrange("q p t -> p q t"), yzT, reads=[yzT_r], store=True)
                                pS, pS_r = self.psum()
                                S.op("pe", lambda e, pS=pS: e.matmul(pS[:], B_tm[:], xdtd[:], start=True, stop=True),
                                     reads=[Btm_r, xdtd_r], writes=[pS_r])
                                S.op("dve", lambda e: e.tensor_tensor(
                                    out=hT[:].rearrange("p (h q) -> p h q", h=8),
                                    in0=hT[:].rearrange("p (h q) -> p h q", h=8),
                                    in1=cd[:, nt, dsl].unsqueeze(2).to_broadcast([128, 8, 64]), op=ALU.mult),
                                    reads=[hT_r, cd_r], writes=[hT_r])
                                S.op("dve", lambda e, pS=pS: e.tensor_tensor(out=hT[:], in0=hT[:], in1=pS[:], op=ALU.add),
                                     reads=[hT_r, pS_r], writes=[hT_r])
                                S.op("act", lambda e: e.copy(out=hTb[:], in_=hT[:]), reads=[hT_r], writes=[hTb_r])
                            if not self.sample:
                                S.dma("sp", self.st_out[j, d_, s, :, g * 512:(g + 1) * 512], hT[:],
                                      reads=[hT_r], store=True)
                S.barrier()
            with contextlib.ExitStack() as os_:
                ob = lambda n, s, d: self.sb("so_" + n, s, d, os_)
                yzt = ob("yzt", [128, 32, 512], BF16); yzt_r = Res("syzt")
                rs = ob("rs", [128, NT], F32); rs_r = Res("srs")
                dg = ob("dg", [128, 128], F32); dg_r = Res("sdg")
                S._wait(S.engs["sp"], list(S.store_toks.values()))
                S.op("dve", lambda e: e.reduce_sum(out=rs[:], in_=ssq[:], axis=AX.X), reads=[ssq_r], writes=[rs_r])
                S.op("dve", lambda e: e.tensor_scalar(out=rs[:], in0=rs[:], scalar1=1.0 / DI, scalar2=EPS,
                                                      op0=ALU.mult, op1=ALU.add), reads=[rs_r], writes=[rs_r])
                S.op("act", lambda e: e.sqrt(out=rs[:], in_=rs[:]), reads=[rs_r], writes=[rs_r])
                S.op("dve", lambda e: e.reciprocal(out=rs[:], in_=rs[:]), reads=[rs_r], writes=[rs_r])
                for nt in range(NT):
                    S.op("dve", lambda e, nt=nt: e.tensor_scalar(out=dg[:], in0=self.cs("ident"), scalar1=rs[:, nt:nt + 1],
                                                                 scalar2=None, op0=ALU.mult),
                         reads=[self.cst_r, rs_r], writes=[dg_r])
                    ps, ps_r = self.psum()
                    S.op("pe", lambda e, ps=ps: e.matmul(ps[:, 0:128], self.cs("ones"), dg[:], start=True, stop=True),
                         reads=[self.cst_r, dg_r], writes=[ps_r])
                    S.op("act", lambda e, ps=ps, nt=nt: e.copy(out=self.rstd[:, nt * 128:(nt + 1) * 128], in_=ps[:, 0:128]),
                         reads=[ps_r], writes=[self.rstd_r[nt // 4]])
                for t in range(self.TT):
                    S.dma("sp", yzt[:], self.yz_scr[:, :, t * 512:(t + 1) * 512].rearrange("k p t -> p k t"),
                          writes=[yzt_r])
                    for oc in range(DC):
                        wo, wo_r = self.load_cols(Wout, oc * 128, rows=DI)
                        po, po_r = self.psum()
                        for kc in range(32):
                            S.op("pe", lambda e, kc=kc, po=po, wo=wo: e.matmul(po[:], wo[:, kc, :], yzt[:, kc, :],
                                                                               start=(kc == 0), stop=(kc == 31)),
                                 reads=[wo_r, yzt_r], writes=[po_r], inc=(kc == 31))
                        sc, sc_r = self.scratch32()
                        S.op("dve", lambda e, po=po, sc=sc, oc=oc, t=t: e.scalar_tensor_tensor(
                            out=sc[:], in0=po[:], scalar=self.modcol(i, 5, oc), in1=self.rstd[:, t * 512:(t + 1) * 512],
                            op0=ALU.mult, op1=ALU.mult), reads=[po_r, self.mods_r, self.rstd_r[t]], writes=[sc_r])
                        S.op("dve", lambda e, sc=sc, oc=oc, t=t: e.tensor_tensor(
                            out=self.x[:, oc, t * 512:(t + 1) * 512], in0=sc[:], in1=self.x[:, oc, t * 512:(t + 1) * 512],
                            op=ALU.add), reads=[sc_r, self.x_r[oc][t]], writes=[self.x_r[oc][t]])
                S.barrier()


WEIGHT_KEYS = ("w_ada", "ffn1_w_in", "ffn2_w_in", "ffn1_w_out", "ffn2_w_out", "ssm_w_in", "ssm_w_out",
               "diff_w_qkv", "diff_w_out", "win_w_qkv", "win_w_out")


def make_in_maps(inp, cores=range(N_CORES)):
    consts = pack_consts(inp)
    bada = fm(inp["b_ada"].reshape(-1))
    rope = rope_tables()
    wm = win_mask()
    maps = []
    for core in cores:
        b = core // 4
        xs = np.concatenate([inp["x_prompt"][2 * core], inp["x_prompt"][2 * core + 1], inp["x_sample"][b]], axis=0)
        cv = np.stack([fm(inp["c_ctx"]), fm(inp["c"][b])], axis=2).reshape(128, DC * 2)
        st = np.stack([np.stack([inp[f"state_l{l}_{d}"][b].reshape(DI, 128).T for d in ("fwd", "bwd")])
                       for l in (0, 3)])
        m = {"xT": np.ascontiguousarray(xs.T), "cvec": np.ascontiguousarray(cv), "consts": consts,
             "b_ada_fm": bada, "rope_cs": rope, "wmask": wm, "st_in": np.ascontiguousarray(st),
             "kc1T": np.ascontiguousarray(inp["cache_l1_k"][b].reshape(256, D).T),
             "vc1": np.ascontiguousarray(inp["cache_l1_v"][b].reshape(256, D)),
             "kc2T": np.ascontiguousarray(inp["cache_l2_k"][b].reshape(256, 512).T),
             "vc2": np.ascontiguousarray(inp["cache_l2_v"][b].reshape(256, 512))}
        for k in WEIGHT_KEYS:
            m[k] = inp[k]
        maps.append(m)
    return maps


def assemble(results):
    B, S_ = 16, 256
    y_prompt = np.zeros((B, S_, D), np.float32)
    y_sample = np.zeros((2, LS, D), np.float32)
    st = [np.zeros((B, 64, 64, 128), np.float32) for _ in range(4)]
    k1 = np.zeros((B, S_, 2, 8, 128), np.float32)
    v1 = np.zeros((B, S_, 8, 256), np.float32)
    k2 = np.zeros((B, S_, 4, 128), np.float32)
    v2 = np.zeros((B, S_, 4, 128), np.float32)
    for core, r in enumerate(results):
        y = r["yT"].T
        y_prompt[2 * core] = y[0:256]
        y_prompt[2 * core + 1] = y[256:512]
        if core % 4 == 0:
            y_sample[core // 4] = y[512:]
        so = r["st_out"]
        for l in range(2):
            for d in range(2):
                for s in range(2):
                    st[l * 2 + d][2 * core + s] = so[l, d, s].T.reshape(64, 64, 128)
        k1t = r["k1T"].T
        k2t = r["k2T"].T
        for s in range(2):
            k1[2 * core + s] = k1t[s * 256:(s + 1) * 256].reshape(256, 2, 8, 128)
            v1[2 * core + s] = r["v1"][s * 256:(s + 1) * 256].reshape(256, 8, 256)
            k2[2 * core + s] = k2t[s * 256:(s + 1) * 256].reshape(256, 4, 128)
            v2[2 * core + s] = r["v2"][s * 256:(s + 1) * 256].reshape(256, 4, 128)
    return (y_prompt, y_sample, st[0], st[1], k1, v1, k2, v2, st[2], st[3])


def kernel(**inp):
    inp = {k: np.asarray(v) for k, v in inp.items()}
    nc = Builder().build()
    maps = make_in_maps(inp)
    res = run_bass_kernel_spmd(nc, maps, core_ids=list(range(N_CORES)))
    return assemble(res.results)
```

```python
import contextlib
import math
import os
import numpy as np
import concourse.bass as bass
import concourse.mybir as mybir
from concourse.bass_utils import run_bass_kernel_spmd

F32 = mybir.dt.float32
BF16 = mybir.dt.bfloat16
AF = mybir.ActivationFunctionType
ALU = mybir.AluOpType
AX = mybir.AxisListType

D = 2048
DC = 16
NP_SEQ = 2
LP = 256
LS = 1024
T = NP_SEQ * LP + LS
NTT = T // 512
DEPTH = 4
D_FF = 5632
FC = D_FF // 128
N_MOD = 9
EPS = 1e-6
N_CORES = 8

SAME_ENGINE_SYNC = os.environ.get('KSYNC', '1') == '1'
KSTOP = int(os.environ.get('KSTOP', '9'))
KPART = os.environ.get('KPART', 'kv')
KSKIP = os.environ.get('KSKIP', '')


class Res:
    __slots__ = ("name", "w", "r", "dsem", "dval")

    def __init__(self, name):
        self.name = name
        self.w = None
        self.r = {}
        self.dsem = None
        self.dval = 0


class Eng:
    def __init__(self, name, handle, sem):
        self.name = name
        self.h = handle
        self.sem = sem
        self.count = 0
        self.waited = {}
        self.pend_r = []
        self.pend_w = []


class Sched:
    def __init__(self, nc, es):
        self.nc = nc
        self.es = es
        self.engs = {}
        for name, h in (("pe", nc.tensor), ("act", nc.scalar), ("dve", nc.vector),
                        ("pool", nc.gpsimd), ("sp", nc.sync)):
            sem = es.enter_context(nc.semaphore("sem_" + name))
            self.engs[name] = Eng(name, h, sem)
        self.sem_ids = {}
        self.store_toks = {}
        self.n_ops = 0

    def _wait(self, E, deps):
        for (sem, val) in deps:
            if sem is E.sem and not SAME_ENGINE_SYNC:
                continue
            k = id(sem)
            if E.waited.get(k, 0) >= val:
                continue
            E.h.wait_ge(sem, val)
            E.waited[k] = val

    @staticmethod
    def _deps(reads, writes):
        deps = []
        for r in reads:
            if r.w is not None:
                deps.append(r.w)
        for w in writes:
            if w.w is not None:
                deps.append(w.w)
            for sem_k, (sem, val) in w.r.items():
                deps.append((sem, val))
        return deps

    def op(self, eng, fn, reads=(), writes=(), inc=True):
        E = self.engs[eng]
        self._wait(E, self._deps(reads, writes))
        ins = fn(E.h)
        self.n_ops += 1
        E.pend_r.extend(reads)
        E.pend_w.extend(writes)
        if inc:
            E.count += 1
            ins.then_inc(E.sem, 1)
            tok = (E.sem, E.count)
            for r in E.pend_r:
                r.r[id(E.sem)] = tok
            for w in E.pend_w:
                w.w = tok
                w.r = {}
            E.pend_r = []
            E.pend_w = []
        return ins

    def dma(self, eng, out, in_, reads=(), writes=(), store=False):
        E = self.engs[eng]
        self._wait(E, self._deps(reads, writes))
        res = (list(writes) + list(reads))[0]
        if res.dsem is None:
            if res.name not in self.sem_ids:
                self.sem_ids[res.name] = [self.es.enter_context(self.nc.semaphore("dsem_" + res.name)), 0]
            res.dsem, res.dval = self.sem_ids[res.name]
        res.dval += 16
        self.sem_ids[res.name][1] = res.dval
        E.h.dma_start(out=out, in_=in_).then_inc(res.dsem, 16)
        self.n_ops += 1
        tok = (res.dsem, res.dval)
        for r in reads:
            r.r[id(res.dsem)] = tok
        for w in writes:
            w.w = tok
            w.r = {}
        if store:
            self.store_toks[id(res.dsem)] = tok

    def barrier(self):
        toks = [(E.sem, E.count) for E in self.engs.values() if E.count > 0]
        toks += list(self.store_toks.values())
        for E in self.engs.values():
            assert not E.pend_r and not E.pend_w, "pending ops at barrier"
            self._wait(E, toks)

    def finish(self):
        E = self.engs["sp"]
        self._wait(E, list(self.store_toks.values()))
        toks = [(e.sem, e.count) for e in self.engs.values() if e.count > 0]
        self._wait(E, toks)


DI = 4096
NG = 8
N_SSM = 2
SSM_IN = 10368
QBL = 128
CONST_SPEC = (("norm_g", DEPTH * 3 * DC), ("final_g", DC), ("ident", 128), ("ones", 128),
              ("Ule", 128), ("Uge", 128), ("SL", 128), ("SU", 128), ("RT", 128),
              ("conv_w", N_SSM * 3 * 48), ("conv_b", N_SSM * 48), ("ssm_norm_g", N_SSM * 32),
              ("dt_bias", N_SSM * 128), ("a_log", N_SSM * 128), ("ssm_d", N_SSM * 64),
              ("subln", 256), ("lam", 512), ("sink", 16))


def fm(vec):
    v = np.asarray(vec, np.float32).reshape(-1, 128)
    return np.ascontiguousarray(v.T)


def bc(vec):
    v = np.asarray(vec, np.float32).reshape(1, -1)
    return np.ascontiguousarray(np.broadcast_to(v, (128, v.shape[1])))


def const_layout():
    lay = {}
    n = 0
    for name, w in CONST_SPEC:
        lay[name] = (n, w)
        n += w
    return lay, n


def pack_consts(inp):
    k = np.arange(128)
    parts = {
        "norm_g": fm(inp["norm_g"].reshape(-1)),
        "final_g": fm(inp["final_norm_g"]),
        "ident": np.eye(128, dtype=np.float32),
        "ones": np.ones((128, 128), np.float32),
        "Ule": (k[:, None] <= k[None, :]).astype(np.float32),
        "Uge": (k[:, None] >= k[None, :]).astype(np.float32),
        "SL": (k[:, None] > k[None, :]).astype(np.float32),
        "SU": (k[:, None] < k[None, :]).astype(np.float32),
    }
    R = np.zeros((128, 128), np.float32)
    for d in range(128):
        if (d // 32) % 2 == 0:
            R[d, d + 32] = -1.0
        else:
            R[d, d - 32] = 1.0
    parts["RT"] = np.ascontiguousarray(R.T)
    parts["conv_w"] = fm(inp["ssm_conv_w"].reshape(-1))
    parts["conv_b"] = fm(inp["ssm_conv_b"].reshape(-1))
    parts["ssm_norm_g"] = fm(inp["ssm_norm_g"].reshape(-1))
    parts["dt_bias"] = bc(inp["ssm_dt_bias"].reshape(-1))
    parts["a_log"] = bc(inp["ssm_a_log"].reshape(-1))
    parts["ssm_d"] = bc(inp["ssm_d"].reshape(-1))
    parts["subln"] = bc(inp["diff_subln_g"].reshape(-1))
    parts["lam"] = bc(inp["diff_lambda"].reshape(-1))
    parts["sink"] = bc(inp["win_sink"].reshape(-1))
    lay, n = const_layout()
    arrs = []
    for name, w in CONST_SPEC:
        a = parts[name]
        assert a.shape == (128, w), (name, a.shape, w)
        arrs.append(a)
    return np.ascontiguousarray(np.concatenate(arrs, axis=1))


def rope_tables():
    L, GW, nf = LS, 64, 32
    rows = L // GW
    row = np.repeat(np.arange(rows, dtype=np.float32), GW)
    col = np.tile(np.arange(GW, dtype=np.float32), rows)
    inv = (np.float32(10000.0) ** (-np.arange(nf, dtype=np.float32) / np.float32(nf))).astype(np.float32)
    ar = (row[:, None] * inv).astype(np.float32)
    ac = (col[:, None] * inv).astype(np.float32)
    ang = np.concatenate([ar, ar, ac, ac], axis=1)
    cs = np.stack([np.cos(ang).T, np.sin(ang).T], axis=1).astype(np.float32)
    return np.ascontiguousarray(cs)


def win_mask():
    qi = np.arange(128)[:, None]
    kj = np.arange(384)[None, :]
    ok = np.abs(qi + 128 - kj) <= 128
    return np.where(ok, 0.0, -30000.0).astype(np.float32)


TPM = 1024


class Builder:
    def __init__(self, layers=None, passes=("A", "B"), ffn=True, mix=True):
        self.layers = list(range(DEPTH)) if layers is None else layers
        self.pass_names = passes
        self.do_ffn = ffn
        self.do_mix = mix
        self.nc = bass.Bass("TRN2", target_bir_lowering=False)

    def dram_in(self, name, shape, dt=F32):
        return self.nc.dram_tensor(name, list(shape), dt, kind="ExternalInput").ap()

    def dram_out(self, name, shape, dt=F32):
        return self.nc.dram_tensor(name, list(shape), dt, kind="ExternalOutput").ap()

    def sb(self, name, shape, dt, es=None):
        self.uid = getattr(self, "uid", 0) + 1
        return (es or self.es).enter_context(self.nc.sbuf_tensor(f"{name}_{self.uid}", list(shape), dt))

    def psum(self):
        i = self.ps_next
        self.ps_next = (i + 1) % 8
        return self.ps_t[i], self.ps_r[i]

    def wslot(self):
        i = self.w_next
        self.w_next = (i + 1) % self.NW
        return self.w_t[i], self.w_r[i]

    def load_cols(self, W, col0, ncols=128, rows=D):
        t, r = self.wslot()
        kc = rows // 128
        assert kc * ncols <= 4096
        view = t[:, 0:kc * ncols].rearrange("p (c n) -> p c n", c=kc)
        src = W[:, col0:col0 + ncols].rearrange("(c p) n -> p c n", p=128)
        self.S.dma("pool", view, src, writes=[r])
        return view, r

    def load_rows(self, W, row0, ncols=D, col0=0):
        t, r = self.wslot()
        view = t[:, 0:ncols]
        self.S.dma("pool", view, W[row0:row0 + 128, col0:col0 + ncols], writes=[r])
        return view, r

    def scratch32(self):
        i = self.sc_next
        self.sc_next = (i + 1) % self.NSC
        return self.sc32[i], self.sc32_r[i]

    def cs(self, name, a=0, b=None):
        off, w = self.lay[name]
        if b is None:
            b = w
        return self.cst[:, off + a:off + b]

    def build(self):
        nc = self.nc
        with contextlib.ExitStack() as es:
            self.es = es
            self.S = S = Sched(nc, es)
            self.declare_io()
            self.alloc()
            self.load_consts()
            self.modulation_all()
            for pn in self.pass_names:
                self.set_pass(pn)
                self.load_x()
                for i in self.layers:
                    self.layer(i)
                self.final()
                S.barrier()
            S.finish()
        return nc

    def set_pass(self, pn):
        if pn == "A":
            self.tok0, self.TP, self.nseq, self.L, self.sample, self.v = 0, 512, 2, 256, False, 0
        else:
            self.tok0, self.TP, self.nseq, self.L, self.sample, self.v = 512, 1024, 1, 1024, True, 1
        self.TT = self.TP // 512
        self.NT = self.TP // 128

    def declare_io(self):
        lay, ncst = const_layout()
        self.lay = lay
        di = self.dram_in
        self.xT_in = di("xT", [D, T])
        self.cvec_in = di("cvec", [128, DC * 2])
        self.consts_in = di("consts", [128, ncst])
        self.bada_in = di("b_ada_fm", [128, DEPTH * N_MOD * DC])
        self.rope_in = di("rope_cs", [128, 2, LS])
        self.wmask_in = di("wmask", [128, 384])
        self.st_in = di("st_in", [2, 2, 128, DI])
        self.kc1T_in = di("kc1T", [D, 256])
        self.vc1_in = di("vc1", [256, D])
        self.kc2T_in = di("kc2T", [512, 256])
        self.vc2_in = di("vc2", [256, 512])
        self.w_ada = di("w_ada", [DEPTH, D, N_MOD * D])
        if self.do_ffn:
            self.ffn_w_in = [di("ffn1_w_in", [DEPTH, D, 2 * D_FF]), di("ffn2_w_in", [DEPTH, D, 2 * D_FF])]
            self.ffn_w_out = [di("ffn1_w_out", [DEPTH, D_FF, D]), di("ffn2_w_out", [DEPTH, D_FF, D])]
        kinds = {i % 3 for i in self.layers} if self.do_mix else set()
        if 0 in kinds:
            self.ssm_w_in = di("ssm_w_in", [N_SSM, D, SSM_IN])
            self.ssm_w_out = di("ssm_w_out", [N_SSM, DI, D])
        if 1 in kinds:
            self.diff_w_qkv = di("diff_w_qkv", [1, D, 6144])
            self.diff_w_out = di("diff_w_out", [1, D, D])
        if 2 in kinds:
            self.win_w_qkv = di("win_w_qkv", [1, D, 3072])
            self.win_w_out = di("win_w_out", [1, D, D])
        do = self.dram_out
        self.yT_out = do("yT", [D, T])
        self.st_out = do("st_out", [2, 2, 2, 128, DI])
        self.k1T_out = do("k1T", [D, 512])
        self.v1_out = do("v1", [512, D])
        self.k2T_out = do("k2T", [512, 512])
        self.v2_out = do("v2", [512, 512])
        self.yz_scr = self.nc.dram_tensor("yz_scr", [32, 128, TPM], BF16, kind="Internal").ap()

    def alloc(self):
        nc = self.nc
        lay, ncst = const_layout()
        self.x = self.sb("x", [128, DC, TPM], F32)
        self.x_r = [[Res(f"x{c}_{t}") for t in range(2)] for c in range(DC)]
        self.h = self.sb("h", [128, DC, TPM], BF16)
        self.h_r = [[Res(f"h{c}_{t}") for t in range(2)] for c in range(DC)]
        self.cst = self.sb("cst", [128, ncst], F32)
        self.cst_r = Res("cst")
        self.ident_bf = self.sb("ident_bf", [128, 128], BF16)
        self.ones_bf = self.sb("ones_bf", [128, 128], BF16)
        self.RT_bf = self.sb("RT_bf", [128, 128], BF16)
        self.cbf_r = Res("cbf")
        self.mods = self.sb("mods", [128, DEPTH * N_MOD * DC, 2], F32)
        self.mods_r = Res("mods")
        self.ab = self.sb("ab", [128, 3, DC], F32)
        self.ab_r = Res("ab")
        self.rstd = self.sb("rstd", [128, TPM], F32)
        self.rstd_r = [Res(f"rstd{t}") for t in range(2)]
        self.ps_t = [self.es.enter_context(nc.psum_tensor(f"ps{i}", [128, 512], F32)) for i in range(8)]
        self.ps_r = [Res(f"ps{i}") for i in range(8)]
        self.ps_next = 0
        self.NW = 4
        self.w_t = [self.sb(f"w{i}", [128, 4096], BF16) for i in range(self.NW)]
        self.w_r = [Res(f"w{i}") for i in range(self.NW)]
        self.w_next = 0
        self.NSC = 3
        self.sc32 = [self.sb(f"sc32_{i}", [128, 512], F32) for i in range(self.NSC)]
        self.sc32_r = [Res(f"sc32_{i}") for i in range(self.NSC)]
        self.sc_next = 0
        self.sq = [self.sb(f"sq{i}", [128, 512], BF16) for i in range(2)]
        self.sq_r = [Res(f"sq{i}") for i in range(2)]
        self.sq_next = 0

    def load_consts(self):
        S = self.S
        S.dma("sp", self.cst[:], self.consts_in, writes=[self.cst_r])
        for dst, name in ((self.ident_bf, "ident"), (self.ones_bf, "ones"), (self.RT_bf, "RT")):
            S.op("dve", lambda e, dst=dst, name=name: e.tensor_copy(out=dst[:], in_=self.cs(name)),
                 reads=[self.cst_r], writes=[self.cbf_r])

    def load_x(self):
        S = self.S
        xin = self.xT_in.rearrange("(c p) t -> p c t", p=128)
        for c in range(DC):
            for t in range(self.TT):
                S.dma("sp", self.x[:, c, t * 512:(t + 1) * 512],
                      xin[:, c, self.tok0 + t * 512:self.tok0 + (t + 1) * 512], writes=[self.x_r[c][t]])

    def modulation_all(self):
        S = self.S
        NCC = N_MOD * DC
        with contextlib.ExitStack() as ms:
            cv = self.sb("cv", [128, DC * 2], F32, ms)
            cv_r = Res("cv")
            csil = self.sb("csil", [128, DC, 2], BF16, ms)
            csil_r = Res("csil")
            bada = self.sb("bada", [128, DEPTH * NCC], F32, ms)
            bada_r = Res("bada")
            S.dma("sp", cv[:], self.cvec_in, writes=[cv_r])
            S.dma("sp", bada[:], self.bada_in, writes=[bada_r])
            S.op("act", lambda e: e.activation(out=csil[:].rearrange("p c v -> p (c v)"), in_=cv[:], func=AF.Silu),
                 reads=[cv_r], writes=[csil_r])
            for i in self.layers:
                W = self.w_ada[i]
                ps, ps_r = self.psum()
                for cc in range(NCC):
                    if cc % 2 == 0:
                        wv, wr = self.load_cols(W, cc * 128, ncols=256)
                    hf = (cc % 2) * 128
                    for k in range(DC):
                        S.op("pe", lambda e, k=k, wv=wv, cc=cc, ps=ps, hf=hf: e.matmul(
                            ps[:, cc * 2:cc * 2 + 2], wv[:, k, hf:hf + 128], csil[:, k, :],
                            start=(k == 0), stop=(k == DC - 1)),
                            reads=[wr, csil_r], writes=[ps_r], inc=(k == DC - 1))
                S.op("dve", lambda e, i=i, ps=ps: e.tensor_tensor(
                    out=self.mods[:, i * NCC:(i + 1) * NCC, :],
                    in0=ps[:, 0:2 * NCC].rearrange("p (c v) -> p c v", v=2),
                    in1=bada[:, i * NCC:(i + 1) * NCC].unsqueeze(2).to_broadcast([128, NCC, 2]), op=ALU.add),
                    reads=[ps_r, bada_r], writes=[self.mods_r])
            S.barrier()

    def modvec(self, i, j):
        b = (i * N_MOD + j) * DC
        return self.mods[:, b:b + DC, self.v]

    def modcol(self, i, j, c):
        b = (i * N_MOD + j) * DC + c
        return self.mods[:, b, self.v:self.v + 1]

    def rms_stats(self):
        S = self.S
        for t in range(self.TT):
            ts = slice(t * 512, (t + 1) * 512)
            ps, ps_r = self.psum()
            for c in range(DC):
                i = self.sq_next
                self.sq_next = (i + 1) % 2
                sq, sq_r = self.sq[i], self.sq_r[i]
                S.op("act", lambda e, c=c, ts=ts, sq=sq: e.activation(out=sq[:], in_=self.x[:, c, ts],
                                                                      func=AF.Square),
                     reads=[self.x_r[c][t]], writes=[sq_r])
                S.op("pe", lambda e, c=c, sq=sq, ps=ps: e.matmul(ps[:], self.ones_bf[:], sq[:],
                                                                 start=(c == 0), stop=(c == DC - 1)),
                     reads=[sq_r, self.cbf_r], writes=[ps_r], inc=True)
            S.op("dve", lambda e, ts=ts, ps=ps: e.tensor_scalar(
                out=self.rstd[:, ts], in0=ps[:], scalar1=1.0 / D, scalar2=EPS, op0=ALU.mult, op1=ALU.add),
                reads=[ps_r], writes=[self.rstd_r[t]])
            S.op("act", lambda e, ts=ts: e.sqrt(out=self.rstd[:, ts], in_=self.rstd[:, ts]),
                 reads=[self.rstd_r[t]], writes=[self.rstd_r[t]])
            S.op("dve", lambda e, ts=ts: e.reciprocal(out=self.rstd[:, ts], in_=self.rstd[:, ts]),
                 reads=[self.rstd_r[t]], writes=[self.rstd_r[t]])

    def modnorm(self, i, sub, j_shift, j_scale):
        S = self.S
        off, _ = self.lay["norm_g"]
        g = self.cst[:, off + (i * 3 + sub) * DC: off + (i * 3 + sub + 1) * DC]
        S.op("dve", lambda e: e.scalar_tensor_tensor(
            out=self.ab[:, 0, :], in0=self.modvec(i, j_scale), scalar=1.0, in1=g, op0=ALU.add, op1=ALU.mult),
            reads=[self.mods_r, self.cst_r], writes=[self.ab_r])
        self.rms_stats()
        for t in range(self.TT):
            ts = slice(t * 512, (t + 1) * 512)
            for c in range(DC):
                sc, sc_r = self.scratch32()
                S.op("dve", lambda e, c=c, ts=ts, sc=sc: e.scalar_tensor_tensor(
                    out=sc[:], in0=self.x[:, c, ts], scalar=self.ab[:, 0, c:c + 1], in1=self.rstd[:, ts],
                    op0=ALU.mult, op1=ALU.mult),
                    reads=[self.x_r[c][t], self.ab_r, self.rstd_r[t]], writes=[sc_r])
                S.op("act", lambda e, c=c, ts=ts, sc=sc: e.activation(
                    out=self.h[:, c, ts], in_=sc[:], func=AF.Identity,
                    bias=self.modcol(i, j_shift, c), scale=1.0),
                    reads=[sc_r, self.mods_r], writes=[self.h_r[c][t]])

    def ffn(self, i, which, j_gate):
        S = self.S
        W_in = self.ffn_w_in[which][i]
        W_out = self.ffn_w_out[which][i]
        S.op("dve", lambda e: e.tensor_scalar(out=self.ab[:, 2, :], in0=self.modvec(i, j_gate), scalar1=0.5,
                                              scalar2=None, op0=ALU.mult), reads=[self.mods_r], writes=[self.ab_r])
        NX = 5
        with contextlib.ExitStack() as fs:
            extra_t = [self.sb(f"wx{k}", [128, 4096], BF16, fs) for k in range(NX)]
            extra_r = [Res(f"wx{k}") for k in range(NX)]
            a_t = [self.sb(f"fa{k}", [128, 2, TPM], BF16, fs) for k in range(2)]
            a_r = [[[Res(f"fa{k}_{j}_{t}") for t in range(2)] for j in range(2)] for k in range(2)]
            save = (self.w_t, self.w_r, self.NW, self.w_next)
            self.w_t = self.w_t + extra_t
            self.w_r = self.w_r + extra_r
            self.NW = len(self.w_t)
            for f in range(FC // 2):
                wg, wg_r = self.load_cols(W_in, f * 256, ncols=256)
                wu, wu_r = self.load_cols(W_in, D_FF + f * 256, ncols=256)
                wt_, wo_r = self.wslot()
                wo = wt_[:, 0:2 * D].rearrange("p (c n) -> p c n", c=2)
                S.dma("pool", wo, W_out[f * 256:(f + 1) * 256, :].rearrange("(c p) n -> p c n", p=128), writes=[wo_r])
                ai = f % 2
                a = a_t[ai]
                for j in range(2):
                    for t in range(self.TT):
                        ts = slice(t * 512, (t + 1) * 512)
                        pg, pg_r = self.psum()
                        for k in range(DC):
                            S.op("pe", lambda e, k=k: e.matmul(
                                pg[:], wg[:, k, j * 128:(j + 1) * 128], self.h[:, k, ts], start=(k == 0),
                                stop=(k == DC - 1)),
                                reads=[wg_r, self.h_r[k][t]], writes=[pg_r], inc=(k == DC - 1))
                        pu, pu_r = self.psum()
                        for k in range(DC):
                            S.op("pe", lambda e, k=k: e.matmul(
                                pu[:], wu[:, k, j * 128:(j + 1) * 128], self.h[:, k, ts], start=(k == 0),
                                stop=(k == DC - 1)),
                                reads=[wu_r, self.h_r[k][t]], writes=[pu_r], inc=(k == DC - 1))
                        sc, sc_r = self.scratch32()
                        S.op("act", lambda e: e.activation(out=sc[:], in_=pg[:], func=AF.Silu),
                             reads=[pg_r], writes=[sc_r])
                        S.op("dve", lambda e: e.tensor_tensor(out=a[:, j, ts], in0=sc[:], in1=pu[:], op=ALU.mult),
                             reads=[sc_r, pu_r], writes=[a_r[ai][j][t]])
                for oc in range(DC):
                    for t in range(self.TT):
                        ts = slice(t * 512, (t + 1) * 512)
                        po, po_r = self.psum()
                        for j in range(2):
                            S.op("pe", lambda e, j=j: e.matmul(
                                po[:], wo[:, j, oc * 128:(oc + 1) * 128], a[:, j, ts], start=(j == 0), stop=(j == 1)),
                                reads=[wo_r, a_r[ai][j][t]], writes=[po_r], inc=(j == 1))
                        self.x_accum(po, po_r, self.ab[:, 2, oc:oc + 1], self.ab_r, oc, t)
            S.barrier()
            self.w_t, self.w_r, self.NW, self.w_next = save

    def x_accum(self, po, po_r, scal, scal_r, oc, t, n=512, off=0):
        ts = slice(t * 512 + off, t * 512 + off + n)
        self.S.op("dve", lambda e: e.scalar_tensor_tensor(
            out=self.x[:, oc, ts], in0=po[:, 0:n], scalar=scal, in1=self.x[:, oc, ts],
            op0=ALU.mult, op1=ALU.add),
            reads=[po_r, scal_r, self.x_r[oc][t]], writes=[self.x_r[oc][t]])

    def layer(self, i):
        if self.do_ffn:
            self.modnorm(i, 0, 0, 1)
            self.ffn(i, 0, 2)
        if self.do_mix:
            self.modnorm(i, 1, 3, 4)
            self.S.barrier()
            m = i % 3
            if m == 0:
                self.ssm_mixer(i)
            elif m == 1:
                self.diff_mixer(i)
            else:
                self.win_mixer(i)
            self.S.barrier()
        if self.do_ffn:
            self.modnorm(i, 2, 6, 7)
            self.ffn(i, 1, 8)

    def final(self):
        S = self.S
        self.rms_stats()
        off, _ = self.lay["final_g"]
        yout = self.yT_out.rearrange("(c p) t -> p c t", p=128)
        for t in range(self.TT):
            ts = slice(t * 512, (t + 1) * 512)
            for c in range(DC):
                sc, sc_r = self.scratch32()
                S.op("dve", lambda e, c=c, ts=ts, sc=sc: e.scalar_tensor_tensor(
                    out=sc[:], in0=self.x[:, c, ts], scalar=self.cst[:, off + c:off + c + 1],
                    in1=self.rstd[:, ts], op0=ALU.mult, op1=ALU.mult),
                    reads=[self.x_r[c][t], self.cst_r, self.rstd_r[t]], writes=[sc_r])
                S.dma("sp", yout[:, c, self.tok0 + t * 512:self.tok0 + (t + 1) * 512], sc[:],
                      reads=[sc_r], store=True)

    def proj_fm(self, wv, wr, t, ps, ps_r, ncol=128, wcol0=0):
        ts = slice(t * 512, (t + 1) * 512)
        for k in range(DC):
            self.S.op("pe", lambda e, k=k: e.matmul(ps[0:ncol, :], wv[:, k, wcol0:wcol0 + ncol], self.h[:, k, ts],
                                                    start=(k == 0), stop=(k == DC - 1)),
                      reads=[wr, self.h_r[k][t]], writes=[ps_r], inc=(k == DC - 1))

    def proj_tm(self, wv, wr, nt, ps, ps_r, ncol, pcol0=0, wcol0=0):
        t = nt // 4
        tk = slice(nt * 128, (nt + 1) * 128)
        for k in range(DC):
            self.S.op("pe", lambda e, k=k: e.matmul(ps[:, pcol0:pcol0 + ncol], self.h[:, k, tk],
                                                    wv[:, k, wcol0:wcol0 + ncol],
                                                    start=(k == 0), stop=(k == DC - 1)),
                      reads=[wr, self.h_r[k][t]], writes=[ps_r], inc=(k == DC - 1))

    def rope_evac(self, ps, ps_r, dst, dst_r, t, cs_t, cs_r):
        S = self.S
        ts = slice(t * 512, (t + 1) * 512)
        xb = self.sq[self.sq_next]
        xb_r = self.sq_r[self.sq_next]
        self.sq_next = (self.sq_next + 1) % 2
        S.op("dve", lambda e: e.tensor_copy(out=xb[:], in_=ps[:]), reads=[ps_r], writes=[xb_r])
        pr, pr_r = self.psum()
        S.op("pe", lambda e: e.matmul(pr[:], self.RT_bf[:], xb[:], start=True, stop=True),
             reads=[xb_r, self.cbf_r], writes=[pr_r])
        s1, s1_r = self.scratch32()
        s2, s2_r = self.scratch32()
        S.op("dve", lambda e: e.tensor_tensor(out=s1[:], in0=ps[:], in1=cs_t[:, 0, ts], op=ALU.mult),
             reads=[ps_r, cs_r], writes=[s1_r])
        S.op("dve", lambda e: e.tensor_tensor(out=s2[:], in0=pr[:], in1=cs_t[:, 1, ts], op=ALU.mult),
             reads=[pr_r, cs_r], writes=[s2_r])
        S.op("dve", lambda e: e.tensor_tensor(out=dst, in0=s1[:], in1=s2[:], op=ALU.add),
             reads=[s1_r, s2_r], writes=[dst_r])

    def attn_tile(self, qT_ap, kparts, vparts, scale, sink_ap, W):
        S = self.S
        P, P_r = W["P"], W["P_r"]
        PT, PT_r = W["PT"], W["PT_r"]
        st, st_r = W["st"], W["st_r"]
        dv = W["dv"]
        srcs = []
        col = 0
        for j, (kT_ap, n, mask_ap) in enumerate(kparts):
            ps, ps_r = self.psum()
            S.op("pe", lambda e, ps=ps, kT_ap=kT_ap, n=n: e.matmul(ps[:, 0:n], qT_ap, kT_ap, start=True, stop=True),
                 reads=W["q_reads"] + W["k_reads"], writes=[ps_r])
            if mask_ap is not None:
                sc, sc_r = self.scratch32()
                S.op("dve", lambda e, sc=sc, ps=ps, n=n, mask_ap=mask_ap: e.tensor_tensor(
                    out=sc[:, 0:n], in0=ps[:, 0:n], in1=mask_ap, op=ALU.add),
                    reads=[ps_r, W["mask_r"]], writes=[sc_r])
                src, src_r = sc, sc_r
            else:
                sc, sc_r = self.scratch32()
                S.op("dve", lambda e, sc=sc, ps=ps, n=n: e.tensor_copy(out=sc[:, 0:n], in_=ps[:, 0:n]),
                     reads=[ps_r], writes=[sc_r])
                src, src_r = sc, sc_r
            S.op("dve", lambda e, src=src, n=n, j=j: e.reduce_max(out=st[:, j:j + 1], in_=src[:, 0:n], axis=AX.X),
                 reads=[src_r], writes=[st_r])
            srcs.append((src, src_r, n, col))
            col += n
        nk = col
        for j in range(1, len(kparts)):
            S.op("dve", lambda e, j=j: e.tensor_tensor(out=st[:, 0:1], in0=st[:, 0:1], in1=st[:, j:j + 1], op=ALU.max),
                 reads=[st_r], writes=[st_r])
        if sink_ap is None:
            S.op("dve", lambda e: e.tensor_scalar(out=st[:, 4:5], in0=st[:, 0:1], scalar1=-scale, scalar2=None,
                                                  op0=ALU.mult), reads=[st_r], writes=[st_r])
        else:
            S.op("dve", lambda e: e.tensor_scalar(out=st[:, 4:5], in0=st[:, 0:1], scalar1=-scale,
                                                  scalar2=W["nsink_ap"], op0=ALU.mult, op1=ALU.min),
                 reads=[st_r, W["nsink_r"]], writes=[st_r])
        S.op("dve", lambda e: e.memset(st[:, 8:8 + len(kparts) + 1], 0.0), writes=[st_r])
        for j, (src, src_r, n, c0) in enumerate(srcs):
            S.op("act", lambda e, src=src, n=n, c0=c0, j=j: e.activation(
                out=P[:, c0:c0 + n], in_=src[:, 0:n], func=AF.Exp, bias=st[:, 4:5], scale=scale,
                accum_out=st[:, 8 + j:9 + j]),
                reads=[src_r, st_r], writes=[P_r, st_r])
        ns = len(kparts)
        if sink_ap is not None:
            S.op("act", lambda e: e.activation(out=st[:, 8 + ns:9 + ns], in_=sink_ap, func=AF.Exp,
                                               bias=st[:, 4:5], scale=1.0),
                 reads=[st_r, W["nsink_r"]], writes=[st_r])
            ns += 1
        S.op("dve", lambda e: e.reduce_sum(out=st[:, 5:6], in_=st[:, 8:8 + ns], axis=AX.X),
             reads=[st_r], writes=[st_r])
        S.op("dve", lambda e: e.reciprocal(out=st[:, 6:7], in_=st[:, 5:6]), reads=[st_r], writes=[st_r])
        nkt = nk // 128
        for b0 in range(0, nkt, 4):
            nb = min(4, nkt - b0)
            pt, pt_r = self.psum()
            for jj in range(nb):
                kt = b0 + jj
                S.op("pe", lambda e, pt=pt, jj=jj, kt=kt: e.matmul(
                    pt[:, jj * 128:(jj + 1) * 128], P[:, kt * 128:(kt + 1) * 128], self.ident_bf[:],
                    start=True, stop=True), reads=[P_r, self.cbf_r], writes=[pt_r], inc=(jj == nb - 1))
            S.op("act", lambda e, pt=pt, b0=b0, nb=nb: e.copy(
                out=PT[:, b0 * 128:(b0 + nb) * 128], in_=pt[:, 0:nb * 128]),
                reads=[pt_r], writes=[PT_r])
        po, po_r = self.psum()
        for kt in range(nkt):
            S.op("pe", lambda e, kt=kt: e.matmul(po[:, 0:dv], PT[:, kt * 128:(kt + 1) * 128], vparts[kt],
                                                 start=(kt == 0), stop=(kt == nkt - 1)),
                 reads=[PT_r] + W["v_reads"], writes=[po_r], inc=(kt == nkt - 1))
        return po, po_r

    def transpose_to(self, src_ap, src_reads, dst_ap, dst_r, ncols=128, scale_ap=None, scale_r=None):
        S = self.S
        pt, pt_r = self.psum()
        S.op("pe", lambda e: e.matmul(pt[0:ncols, 0:128], src_ap, self.ident_bf[:], start=True, stop=True),
             reads=list(src_reads) + [self.cbf_r], writes=[pt_r])
        if scale_ap is None:
            S.op("act", lambda e: e.copy(out=dst_ap, in_=pt[0:ncols, 0:128]), reads=[pt_r], writes=[dst_r])
        else:
            S.op("dve", lambda e: e.tensor_scalar(out=dst_ap, in0=pt[0:ncols, 0:128], scalar1=scale_ap, scalar2=None,
                                                  op0=ALU.mult), reads=[pt_r, scale_r], writes=[dst_r])

    def win_mixer(self, i):
        S = self.S
        Wq = self.win_w_qkv[0]
        Wo = self.win_w_out[0]
        TP, NT, L = self.TP, self.NT, self.L
        scale = 128 ** -0.5
        koff = 256 if self.sample else 0
        with contextlib.ExitStack() as ms:
            sb = lambda n, s, d: self.sb("wn_" + n, s, d, ms)
            kT = sb("kT", [128, koff + TP], BF16); kT_r = Res("wkT")
            V = sb("V", [128, (koff + TP) // 128, 128], BF16); V_r = Res("wV")
            qT = sb("qT", [128, TP], BF16); qT_r = Res("wqT")
            oT = sb("oT", [128, TP], BF16); oT_r = Res("woT")
            W = {"P": sb("P", [128, 640], BF16), "P_r": Res("wP"), "PT": sb("PT", [128, 640], BF16),
                 "PT_r": Res("wPT"), "st": sb("st", [128, 16], F32), "st_r": Res("wst"), "dv": 128,
                 "q_reads": [qT_r], "k_reads": [kT_r], "v_reads": [V_r]}
            o_tm = sb("o_tm", [128, 128], BF16); o_r = Res("wo_tm")
            nsink = sb("nsink", [128, 16], F32); nsink_r = Res("wnsink")
            W["nsink_r"] = nsink_r
            stage = sb("stage", [128, 512], F32); stage_r = Res("wstage")
            S.op("dve", lambda e: e.tensor_scalar(out=nsink[:], in0=self.cs("sink"), scalar1=-1.0, scalar2=None,
                                                  op0=ALU.mult), reads=[self.cst_r], writes=[nsink_r])
            if self.sample:
                cs_t = sb("cs", [128, 2, LS], F32); cs_r = Res("wcs")
                mask = sb("mask", [128, 384], F32); mask_r = Res("wmask")
                W["mask_r"] = mask_r
                S.dma("sp", cs_t[:], self.rope_in, writes=[cs_r])
                S.dma("sp", mask[:], self.wmask_in, writes=[mask_r])
            for g in range(4):
                if KSTOP < 1:
                    break
                wk, wk_r = self.load_cols(Wq, 2048 + g * 128)
                if self.sample:
                    S.dma("pool", kT[:, 0:256], self.kc2T_in[g * 128:(g + 1) * 128, :], writes=[kT_r])
                for t in range(self.TT if 'k' in KPART else 0):
                    ps, ps_r = self.psum()
                    self.proj_fm(wk, wk_r, t, ps, ps_r)
                    dst = kT[:, koff + t * 512: koff + (t + 1) * 512]
                    if self.sample:
                        self.rope_evac(ps, ps_r, dst, kT_r, t, cs_t, cs_r)
                    else:
                        S.op("act", lambda e, ps=ps: e.copy(out=stage[:], in_=ps[:]), reads=[ps_r], writes=[stage_r])
                        S.op("dve", lambda e, dst=dst: e.tensor_copy(out=dst, in_=stage[:]),
                             reads=[stage_r], writes=[kT_r])
                        if 's' not in KSKIP:
                            S.dma("sp", self.k2T_out[g * 128:(g + 1) * 128, t * 512:(t + 1) * 512], stage[:],
                                  reads=[stage_r], store=True)
                wv, wv_r = self.load_cols(Wq, 2560 + g * 128)
                if self.sample:
                    S.dma("pool", V[:, 0:2, :],
                          self.vc2_in[:, g * 128:(g + 1) * 128].rearrange("(n p) d -> p n d", p=128), writes=[V_r])
                for nt in range(NT if 'v' in KPART else 0):
                    ps, ps_r = self.psum()
                    self.proj_tm(wv, wv_r, nt, ps, ps_r, 128)
                    if self.sample:
                        S.op("act", lambda e, ps=ps, nt=nt: e.copy(out=V[:, koff // 128 + nt, :], in_=ps[:, 0:128]),
                             reads=[ps_r], writes=[V_r])
                    else:
                        S.op("act", lambda e, ps=ps: e.copy(out=stage[:, 0:128], in_=ps[:, 0:128]),
                             reads=[ps_r], writes=[stage_r])
                        S.op("dve", lambda e, nt=nt: e.tensor_copy(out=V[:, koff // 128 + nt, :], in_=stage[:, 0:128]),
                             reads=[stage_r], writes=[V_r])
                        S.dma("sp", self.v2_out[nt * 128:(nt + 1) * 128, g * 128:(g + 1) * 128], stage[:, 0:128],
                              reads=[stage_r], store=True)
                for r in range(4):
                    if KSTOP < 2:
                        break
                    hd = g * 4 + r
                    wq, wq_r = self.load_cols(Wq, hd * 128)
                    W["nsink_ap"] = nsink[:, hd:hd + 1]
                    for t in range(self.TT):
                        ps, ps_r = self.psum()
                        self.proj_fm(wq, wq_r, t, ps, ps_r)
                        dst = qT[:, t * 512:(t + 1) * 512]
                        if self.sample:
                            self.rope_evac(ps, ps_r, dst, qT_r, t, cs_t, cs_r)
                        else:
                            S.op("act", lambda e, ps=ps, dst=dst: e.copy(out=dst, in_=ps[:]),
                                 reads=[ps_r], writes=[qT_r])
                    for s in range(self.nseq):
                        if KSTOP < 3:
                            break
                        for qb in range(L // 128):
                            q0 = s * L + qb * 128
                            if self.sample:
                                b_lo, b_hi = max(qb - 1, 0), min(qb + 1, L // 128 - 1)
                                nb = (b_hi - b_lo + 1) * 128
                                m0 = (b_lo - (qb - 1)) * 128
                                kparts = [(kT[:, 0:256], 256, None),
                                          (kT[:, 256 + b_lo * 128: 256 + b_lo * 128 + nb], nb, mask[:, m0:m0 + nb])]
                                vparts = [V[:, 0, :], V[:, 1, :]] + [V[:, 2 + b, :] for b in range(b_lo, b_hi + 1)]
                            else:
                                kparts = [(kT[:, s * L:(s + 1) * L], L, None)]
                                vparts = [V[:, s * (L // 128) + b, :] for b in range(L // 128)]
                            po, po_r = self.attn_tile(qT[:, q0:q0 + 128], kparts, vparts, scale,
                                                      self.cs("sink", hd, hd + 1), W)
                            S.op("dve", lambda e, po=po: e.tensor_scalar(
                                out=o_tm[:], in0=po[:, 0:128], scalar1=W["st"][:, 6:7], scalar2=None, op0=ALU.mult),
                                reads=[po_r, W["st_r"]], writes=[o_r])
                            self.transpose_to(o_tm[:], [o_r], oT[:, q0:q0 + 128], oT_r)
                    if KSTOP < 4:
                        continue
                    wo, wo_r = self.load_rows(Wo, hd * 128)
                    for oc in range(DC):
                        for t in range(self.TT):
                            po, po_r = self.psum()
                            S.op("pe", lambda e, po=po, oc=oc, t=t: e.matmul(
                                po[:], wo[:, oc * 128:(oc + 1) * 128], oT[:, t * 512:(t + 1) * 512],
                                start=True, stop=True), reads=[wo_r, oT_r], writes=[po_r])
                            self.x_accum(po, po_r, self.modcol(i, 5, oc), self.mods_r, oc, t)
            S.barrier()

    def diff_mixer(self, i):
        S = self.S
        Wq = self.diff_w_qkv[0]
        Wo = self.diff_w_out[0]
        TP, NT, L = self.TP, self.NT, self.L
        scale = 128 ** -0.5
        lambda_init = 0.8 - 0.6 * math.exp(-0.3 * i)
        koff = 256 if self.sample else 0
        NK = koff + L
        with contextlib.ExitStack() as ms:
            sb = lambda n, s, d: self.sb("df_" + n, s, d, ms)
            kT = sb("kT", [128, 2, koff + TP], BF16); kT_r = Res("dkT")
            V = sb("V", [128, (koff + TP) // 128, 256], BF16); V_r = Res("dV")
            qT = sb("qT", [128, 2, TP], BF16); qT_r = Res("dqT")
            oT = sb("oT", [128, 2, TP], BF16); oT_r = Res("doT")
            Ws = []
            for m in range(2):
                Ws.append({"P": sb(f"P{m}", [128, NK], BF16), "P_r": Res(f"dP{m}"),
                           "PT": sb(f"PT{m}", [128, NK], BF16), "PT_r": Res(f"dPT{m}"),
                           "st": sb(f"st{m}", [128, 16], F32), "st_r": Res(f"dst{m}"), "dv": 256,
                           "q_reads": [qT_r], "k_reads": [kT_r], "v_reads": [V_r]})
            o32 = sb("o32", [128, 256], F32); o32_r = Res("do32")
            o_tm = sb("o_tm", [128, 256], BF16); o_r = Res("do_tm")
            lam = sb("lam", [128, 8], F32); lam_r = Res("dlam")
            junk = sb("junk", [128, 256], F32); junk_r = Res("djunk")
            stage = sb("stage", [128, 512], F32); stage_r = Res("dstage")
            lv = self.cs("lam")
            S.op("dve", lambda e: e.tensor_tensor(out=junk[:, 0:128], in0=lv[:, 0:128], in1=lv[:, 128:256], op=ALU.mult),
                 reads=[self.cst_r], writes=[junk_r])
            S.op("dve", lambda e: e.reduce_sum(out=lam[:, 0:1], in_=junk[:, 0:128], axis=AX.X),
                 reads=[junk_r], writes=[lam_r])
            S.op("dve", lambda e: e.tensor_tensor(out=junk[:, 128:256], in0=lv[:, 256:384], in1=lv[:, 384:512],
                                                  op=ALU.mult), reads=[self.cst_r], writes=[junk_r])
            S.op("dve", lambda e: e.reduce_sum(out=lam[:, 1:2], in_=junk[:, 128:256], axis=AX.X),
                 reads=[junk_r], writes=[lam_r])
            S.op("act", lambda e: e.activation(out=lam[:, 2:4], in_=lam[:, 0:2], func=AF.Exp),
                 reads=[lam_r], writes=[lam_r])
            S.op("dve", lambda e: e.tensor_tensor(out=lam[:, 4:5], in0=lam[:, 3:4], in1=lam[:, 2:3], op=ALU.subtract),
                 reads=[lam_r], writes=[lam_r])
            S.op("dve", lambda e: e.tensor_scalar(out=lam[:, 4:5], in0=lam[:, 4:5], scalar1=-lambda_init, scalar2=None,
                                                  op0=ALU.add), reads=[lam_r], writes=[lam_r])
            if self.sample:
                cs_t = sb("cs", [128, 2, LS], F32); cs_r = Res("dcs")
                S.dma("sp", cs_t[:], self.rope_in, writes=[cs_r])
            for hd in range(8):
                for m in range(2):
                    col = m * 1024 + hd * 128
                    wk, wk_r = self.load_cols(Wq, 2048 + col)
                    if self.sample:
                        S.dma("pool", kT[:, m, 0:256], self.kc1T_in[col:col + 128, :], writes=[kT_r])
                    for t in range(self.TT):
                        ps, ps_r = self.psum()
                        self.proj_fm(wk, wk_r, t, ps, ps_r)
                        dst = kT[:, m, koff + t * 512: koff + (t + 1) * 512]
                        if self.sample:
                            self.rope_evac(ps, ps_r, dst, kT_r, t, cs_t, cs_r)
                        else:
                            S.op("act", lambda e, ps=ps: e.copy(out=stage[:], in_=ps[:]),
                                 reads=[ps_r], writes=[stage_r])
                            S.op("dve", lambda e, dst=dst: e.tensor_copy(out=dst, in_=stage[:]),
                                 reads=[stage_r], writes=[kT_r])
                            S.dma("sp", self.k1T_out[col:col + 128, t * 512:(t + 1) * 512], stage[:],
                                  reads=[stage_r], store=True)
                    wq, wq_r = self.load_cols(Wq, col)
                    for t in range(self.TT):
                        ps, ps_r = self.psum()
                        self.proj_fm(wq, wq_r, t, ps, ps_r)
                        dst = qT[:, m, t * 512:(t + 1) * 512]
                        if self.sample:
                            self.rope_evac(ps, ps_r, dst, qT_r, t, cs_t, cs_r)
                        else:
                            S.op("act", lambda e, ps=ps, dst=dst: e.copy(out=dst, in_=ps[:]),
                                 reads=[ps_r], writes=[qT_r])
                wv, wv_r = self.load_cols(Wq, 4096 + hd * 256, ncols=256)
                if self.sample:
                    S.dma("pool", V[:, 0:2, :],
                          self.vc1_in[:, hd * 256:(hd + 1) * 256].rearrange("(n p) d -> p n d", p=128), writes=[V_r])
                for nt in range(NT):
                    ps, ps_r = self.psum()
                    self.proj_tm(wv, wv_r, nt, ps, ps_r, 256)
                    if self.sample:
                        S.op("act", lambda e, ps=ps, nt=nt: e.copy(out=V[:, koff // 128 + nt, :], in_=ps[:, 0:256]),
                             reads=[ps_r], writes=[V_r])
                    else:
                        S.op("act", lambda e, ps=ps: e.copy(out=stage[:, 0:256], in_=ps[:, 0:256]),
                             reads=[ps_r], writes=[stage_r])
                        S.op("dve", lambda e, nt=nt: e.tensor_copy(out=V[:, koff // 128 + nt, :], in_=stage[:, 0:256]),
                             reads=[stage_r], writes=[V_r])
                        S.dma("sp", self.v1_out[nt * 128:(nt + 1) * 128, hd * 256:(hd + 1) * 256], stage[:, 0:256],
                              reads=[stage_r], store=True)
                for s in range(self.nseq):
                    k0 = s * L if not self.sample else 0
                    vparts = [V[:, (k0 // 128) + b, :] for b in range(NK // 128)]
                    for qb in range(L // 128):
                        q0 = s * L + qb * 128
                        pos = []
                        for m in range(2):
                            kparts = []
                            c = 0
                            while c < NK:
                                n = min(512, NK - c)
                                kparts.append((kT[:, m, k0 + c:k0 + c + n], n, None))
                                c += n
                            pos.append(self.attn_tile(qT[:, m, q0:q0 + 128], kparts, vparts, scale, None, Ws[m]))
                        st0, st1 = Ws[0]["st"], Ws[1]["st"]
                        S.op("dve", lambda e: e.tensor_tensor(out=st1[:, 7:8], in0=st1[:, 6:7], in1=lam[:, 4:5],
                                                              op=ALU.mult),
                             reads=[Ws[1]["st_r"], lam_r], writes=[Ws[1]["st_r"]])
                        (p0, p0_r), (p1, p1_r) = pos
                        S.op("dve", lambda e, p0=p0: e.tensor_scalar(out=o32[:], in0=p0[:, 0:256], scalar1=st0[:, 6:7],
                                                                     scalar2=None, op0=ALU.mult),
                             reads=[p0_r, Ws[0]["st_r"]], writes=[o32_r])
                        S.op("dve", lambda e, p1=p1: e.scalar_tensor_tensor(
                            out=o32[:], in0=p1[:, 0:256], scalar=st1[:, 7:8], in1=o32[:], op0=ALU.mult, op1=ALU.add),
                            reads=[p1_r, Ws[1]["st_r"], o32_r], writes=[o32_r])
                        S.op("dve", lambda e: e.memset(st0[:, 12:13], 0.0), writes=[Ws[0]["st_r"]])
                        S.op("act", lambda e: e.activation(out=junk[:], in_=o32[:], func=AF.Square,
                                                           accum_out=st0[:, 12:13]),
                             reads=[o32_r, Ws[0]["st_r"]], writes=[junk_r, Ws[0]["st_r"]])
                        S.op("dve", lambda e: e.tensor_scalar(out=st0[:, 13:14], in0=st0[:, 12:13], scalar1=1.0 / 256,
                                                              scalar2=EPS, op0=ALU.mult, op1=ALU.add),
                             reads=[Ws[0]["st_r"]], writes=[Ws[0]["st_r"]])
                        S.op("act", lambda e: e.sqrt(out=st0[:, 13:14], in_=st0[:, 13:14]),
                             reads=[Ws[0]["st_r"]], writes=[Ws[0]["st_r"]])
                        S.op("dve", lambda e: e.reciprocal(out=st0[:, 14:15], in_=st0[:, 13:14]),
                             reads=[Ws[0]["st_r"]], writes=[Ws[0]["st_r"]])
                        S.op("dve", lambda e: e.tensor_scalar(out=st0[:, 14:15], in0=st0[:, 14:15],
                                                              scalar1=1.0 - lambda_init, scalar2=None, op0=ALU.mult),
                             reads=[Ws[0]["st_r"]], writes=[Ws[0]["st_r"]])
                        S.op("dve", lambda e: e.scalar_tensor_tensor(
                            out=o_tm[:], in0=o32[:], scalar=st0[:, 14:15], in1=self.cs("subln"),
                            op0=ALU.mult, op1=ALU.mult), reads=[o32_r, Ws[0]["st_r"], self.cst_r], writes=[o_r])
                        for eh in range(2):
                            self.transpose_to(o_tm[:, eh * 128:(eh + 1) * 128], [o_r], oT[:, eh, q0:q0 + 128], oT_r)
                wos = [self.load_rows(Wo, hd * 256 + eh * 128) for eh in range(2)]
                for oc in range(DC):
                    for t in range(self.TT):
                        po, po_r = self.psum()
                        for eh in range(2):
                            wo, wo_r = wos[eh]
                            S.op("pe", lambda e, po=po, oc=oc, t=t, eh=eh, wo=wo: e.matmul(
                                po[:], wo[:, oc * 128:(oc + 1) * 128], oT[:, eh, t * 512:(t + 1) * 512],
                                start=(eh == 0), stop=(eh == 1)), reads=[wo_r, oT_r], writes=[po_r], inc=(eh == 1))
                        self.x_accum(po, po_r, self.modcol(i, 5, oc), self.mods_r, oc, t)
            S.barrier()

    def ssm_mixer(self, i):
        S = self.S
        j = i // 3
        Win = self.ssm_w_in[j]
        Wout = self.ssm_w_out[j]
        TP, NT, L, nseq = self.TP, self.NT, self.L, self.nseq
        nch = L // 128
        U = {0: self.cs("Ule"), 1: self.cs("Uge")}
        SLU = {0: self.cs("SL"), 1: self.cs("SU")}
        with contextlib.ExitStack() as ms:
            sb = lambda n, s, d: self.sb("ss_" + n, s, d, ms)
            ssq = sb("ssq", [128, NT, NG], F32); ssq_r = Res("sssq")
            S.op("dve", lambda e: e.memset(ssq[:].rearrange("p a b -> p (a b)"), 0.0), writes=[ssq_r])
            with contextlib.ExitStack() as gs:
                gb = lambda n, s, d: self.sb("sg_" + n, s, d, gs)
                z_tm = gb("z", [128, NT, 512], BF16); z_r = Res("sz")
                yf = gb("yf", [128, NT, 512], BF16); yf_r = Res("syf")
                xc = gb("xc", [128, 6, TP], BF16); xc_r = Res("sxc")
                pre = gb("pre", [128, TP], F32); pre_r = Res("spre")
                acc = self.rstd; acc_r = Res("sacc")
                dt = gb("dt", [128, NT, 16], F32); dt_r = Res("sdt")
                dtA = gb("dtA", [128, NT, 16], F32); dtA_r = Res("sdtA")
                eac = gb("eac", [128, NT, 16], F32); eac_r = Res("seac")
                cd = gb("cd", [128, NT, 16], F32); cd_r = Res("scd")
                abc = gb("abc", [128, 2, 128], F32); abc_r = Res("sabc")
                xdt = gb("xdt", [128, 512], BF16); xdt_r = Res("sxdt")
                xdtd = gb("xdtd", [128, 512], BF16); xdtd_r = Res("sxdtd")
                B_tm = gb("B_tm", [128, 128], BF16); Btm_r = Res("sBtm")
                mCB = gb("mCB", [128, 128], F32); mCB_r = Res("smCB")
                MT = gb("MT", [128, 8, 128], BF16); MT_r = Res("sMT")
                hT = gb("hT", [128, 512], F32); hT_r = Res("shT")
                hTb = gb("hTb", [128, 512], BF16); hTb_r = Res("shTb")
                y32 = gb("y32", [128, 512], F32); y32_r = Res("sy32")
                yz = gb("yz", [128, 512], BF16); yz_r = Res("syz")
                yzT_t = gb("yzT", [128, 4, 128], BF16); yzT = yzT_t[:]; yzT_r = Res("syzT")
                junk = gb("junk", [128, 512], BF16); junk_r = Res("sjunk")
                off, _ = self.lay["a_log"]
                S.op("act", lambda e: e.activation(out=abc[:, 0, :], in_=self.cst[:, off + j * 128: off + (j + 1) * 128],
                                                   func=AF.Exp), reads=[self.cst_r], writes=[abc_r])
                S.op("dve", lambda e: e.tensor_scalar(out=abc[:, 0, :], in0=abc[:, 0, :], scalar1=-1.0, scalar2=None,
                                                      op0=ALU.mult), reads=[abc_r], writes=[abc_r])
                offb, _ = self.lay["dt_bias"]
                offd, _ = self.lay["ssm_d"]
                offcw, _ = self.lay["conv_w"]
                offcb, _ = self.lay["conv_b"]
                offng, _ = self.lay["ssm_norm_g"]
                for g in range(NG):
                    for d_ in range(2):
                        wd, wd_r = self.load_cols(Win, DI + 6144 + d_ * 64 + g * 8, ncols=8)
                        for nt in range(NT):
                            ps, ps_r = self.psum()
                            self.proj_tm(wd, wd_r, nt, ps, ps_r, 8)
                            hsl = slice(d_ * 64 + g * 8, d_ * 64 + g * 8 + 8)
                            bsl = slice(offb + j * 128 + d_ * 64 + g * 8, offb + j * 128 + d_ * 64 + g * 8 + 8)
                            sc, sc_r = self.scratch32()
                            S.op("dve", lambda e, sc=sc, ps=ps, bsl=bsl: e.tensor_tensor(
                                out=sc[:, 0:8], in0=ps[:, 0:8], in1=self.cst[:, bsl], op=ALU.add),
                                reads=[ps_r, self.cst_r], writes=[sc_r])
                            S.op("act", lambda e, sc=sc: e.activation(out=sc[:, 0:8], in_=sc[:, 0:8], func=AF.Exp),
                                 reads=[sc_r], writes=[sc_r])
                            S.op("dve", lambda e, sc=sc: e.tensor_scalar(out=sc[:, 0:8], in0=sc[:, 0:8], scalar1=1.0,
                                                                         scalar2=None, op0=ALU.add),
                                 reads=[sc_r], writes=[sc_r])
                            S.op("act", lambda e, sc=sc, nt=nt, d_=d_: e.activation(
                                out=dt[:, nt, d_ * 8:(d_ + 1) * 8], in_=sc[:, 0:8], func=AF.Ln),
                                reads=[sc_r], writes=[dt_r])
                            S.op("dve", lambda e, nt=nt, d_=d_, hsl=hsl: e.tensor_tensor(
                                out=dtA[:, nt, d_ * 8:(d_ + 1) * 8], in0=dt[:, nt, d_ * 8:(d_ + 1) * 8],
                                in1=abc[:, 0, hsl], op=ALU.mult), reads=[dt_r, abc_r], writes=[dtA_r])
                    for nt in range(NT):
                        ps, ps_r = self.psum()
                        S.op("pe", lambda e, ps=ps, nt=nt: e.matmul(ps[:, 0:8], U[0], dtA[:, nt, 0:8],
                                                                    start=True, stop=True),
                             reads=[self.cst_r, dtA_r], writes=[ps_r])
                        S.op("pe", lambda e, ps=ps, nt=nt: e.matmul(ps[:, 8:16], U[1], dtA[:, nt, 8:16],
                                                                    start=True, stop=True),
                             reads=[self.cst_r, dtA_r], writes=[ps_r])
                        S.op("pe", lambda e, ps=ps, nt=nt: e.matmul(ps[:, 16:32], self.cs("ones"), dtA[:, nt, :],
                                                                    start=True, stop=True),
                             reads=[self.cst_r, dtA_r], writes=[ps_r])
                        S.op("act", lambda e, ps=ps, nt=nt: e.activation(out=eac[:, nt, :], in_=ps[:, 0:16], func=AF.Exp),
                             reads=[ps_r], writes=[eac_r])
                        S.op("act", lambda e, ps=ps, nt=nt: e.activation(out=cd[:, nt, :], in_=ps[:, 16:32], func=AF.Exp),
                             reads=[ps_r], writes=[cd_r])
                    cols = [DI + g * 512 + q * 128 for q in range(4)] + [DI + DI + g * 128, DI + DI + 1024 + g * 128]
                    for ci, col in enumerate(cols):
                        cc = (col - DI) // 128
                        wv, wv_r = self.load_cols(Win, col)
                        for t in range(self.TT):
                            ps, ps_r = self.psum()
                            self.proj_fm(wv, wv_r, t, ps, ps_r)
                            S.op("act", lambda e, ps=ps, t=t: e.copy(out=pre[:, t * 512:(t + 1) * 512], in_=ps[:]),
                                 reads=[ps_r], writes=[pre_r])
                        w0 = self.cst[:, offcw + (j * 3 + 0) * 48 + cc: offcw + (j * 3 + 0) * 48 + cc + 1]
                        w1 = self.cst[:, offcw + (j * 3 + 1) * 48 + cc: offcw + (j * 3 + 1) * 48 + cc + 1]
                        w2 = self.cst[:, offcw + (j * 3 + 2) * 48 + cc: offcw + (j * 3 + 2) * 48 + cc + 1]
                        cb = self.cst[:, offcb + j * 48 + cc: offcb + j * 48 + cc + 1]
                        S.op("dve", lambda e, w1=w1, cb=cb: e.tensor_scalar(
                            out=acc[:, 0:TP], in0=pre[:], scalar1=w1, scalar2=cb, op0=ALU.mult, op1=ALU.add),
                            reads=[pre_r, self.cst_r], writes=[acc_r])
                        for s in range(nseq):
                            a0, a1 = s * L, (s + 1) * L
                            S.op("dve", lambda e, w0=w0, a0=a0, a1=a1: e.scalar_tensor_tensor(
                                out=acc[:, a0 + 1:a1], in0=pre[:, a0:a1 - 1], scalar=w0, in1=acc[:, a0 + 1:a1],
                                op0=ALU.mult, op1=ALU.add), reads=[pre_r, acc_r, self.cst_r], writes=[acc_r])
                            S.op("dve", lambda e, w2=w2, a0=a0, a1=a1: e.scalar_tensor_tensor(
                                out=acc[:, a0:a1 - 1], in0=pre[:, a0 + 1:a1], scalar=w2, in1=acc[:, a0:a1 - 1],
                                op0=ALU.mult, op1=ALU.add), reads=[pre_r, acc_r, self.cst_r], writes=[acc_r])
                        S.op("act", lambda e, ci=ci: e.activation(out=xc[:, ci, :], in_=acc[:, 0:TP], func=AF.Silu),
                             reads=[acc_r], writes=[xc_r])
                    for half in range(2):
                        wz, wz_r = self.load_cols(Win, g * 512 + half * 256, ncols=256)
                        for nt in range(NT):
                            ps, ps_r = self.psum()
                            self.proj_tm(wz, wz_r, nt, ps, ps_r, 256)
                            S.op("act", lambda e, ps=ps, nt=nt, half=half: e.activation(
                                out=z_tm[:, nt, half * 256:(half + 1) * 256], in_=ps[:, 0:256], func=AF.Silu),
                                reads=[ps_r], writes=[z_r])
                    for s in range(nseq):
                        for d_ in range(2):
                            if self.sample:
                                S.dma("sp", hT[:], self.st_in[j, d_, :, g * 512:(g + 1) * 512], writes=[hT_r])
                            else:
                                S.op("dve", lambda e: e.memset(hT[:], 0.0), writes=[hT_r])
                            S.op("act", lambda e: e.copy(out=hTb[:], in_=hT[:]), reads=[hT_r], writes=[hTb_r])
                            order = range(nch) if d_ == 0 else range(nch - 1, -1, -1)
                            for c in order:
                                nt = s * nch + c
                                tk = slice(nt * 128, (nt + 1) * 128)
                                dsl = slice(d_ * 8, d_ * 8 + 8)
                                px, px_r = self.psum()
                                for q in range(4):
                                    S.op("pe", lambda e, q=q, px=px: e.matmul(
                                        px[:, q * 128:(q + 1) * 128], xc[:, q, tk], self.ident_bf[:],
                                        start=True, stop=True), reads=[xc_r, self.cbf_r], writes=[px_r], inc=(q == 3))
                                S.op("dve", lambda e, px=px: e.tensor_tensor(
                                    out=xdt[:].rearrange("p (h q) -> p h q", h=8),
                                    in0=px[:].rearrange("p (h q) -> p h q", h=8),
                                    in1=dt[:, nt, dsl].unsqueeze(2).to_broadcast([128, 8, 64]), op=ALU.mult),
                                    reads=[px_r, dt_r], writes=[xdt_r])
                                if d_ == 0:
                                    dsk = self.cst[:, offd + j * 64 + g * 8: offd + j * 64 + g * 8 + 8]
                                    S.op("dve", lambda e, px=px, dsk=dsk: e.tensor_tensor(
                                        out=y32[:].rearrange("p (h q) -> p h q", h=8),
                                        in0=px[:].rearrange("p (h q) -> p h q", h=8),
                                        in1=dsk.unsqueeze(2).to_broadcast([128, 8, 64]), op=ALU.mult),
                                        reads=[px_r, self.cst_r], writes=[y32_r])
                                else:
                                    S.op("act", lambda e: e.copy(out=y32[:], in_=yf[:, nt, :]),
                                         reads=[yf_r], writes=[y32_r])
                                pb, pb_r = self.psum()
                                S.op("pe", lambda e, pb=pb: e.matmul(pb[:, 0:128], xc[:, 4, tk], self.ident_bf[:],
                                                                     start=True, stop=True),
                                     reads=[xc_r, self.cbf_r], writes=[pb_r])
                                S.op("act", lambda e, pb=pb: e.copy(out=B_tm[:], in_=pb[:, 0:128]),
                                     reads=[pb_r], writes=[Btm_r])
                                pc, pc_r = self.psum()
                                S.op("pe", lambda e, pc=pc: e.matmul(pc[:, 0:128], xc[:, 4, tk], xc[:, 5, tk],
                                                                     start=True, stop=True),
                                     reads=[xc_r], writes=[pc_r])
                                S.op("dve", lambda e, pc=pc: e.tensor_tensor(out=mCB[:], in0=pc[:, 0:128], in1=U[d_],
                                                                             op=ALU.mult),
                                     reads=[pc_r, self.cst_r], writes=[mCB_r])
                                iend = 127 if d_ == 0 else 0
                                for quad in range(2):
                                    hs = slice(d_ * 8 + quad * 4, d_ * 8 + quad * 4 + 4)
                                    rseg, rseg_r = self.scratch32()
                                    Lt, Lt_r = self.scratch32()
                                    S.op("dve", lambda e, hs=hs, rseg=rseg: e.tensor_tensor(
                                        out=rseg[:].rearrange("p (h q) -> p h q", h=4),
                                        in0=U[d_].unsqueeze(1).to_broadcast([128, 4, 128]),
                                        in1=dtA[:, nt, hs].unsqueeze(2).to_broadcast([128, 4, 128]), op=ALU.mult),
                                        reads=[self.cst_r, dtA_r], writes=[rseg_r])
                                    pl, pl_r = self.psum()
                                    S.op("pe", lambda e, pl=pl, rseg=rseg: e.matmul(pl[:], SLU[d_], rseg[:], start=True, stop=True),
                                         reads=[self.cst_r, rseg_r], writes=[pl_r])
                                    S.op("act", lambda e, pl=pl, Lt=Lt: e.activation(out=Lt[:], in_=pl[:], func=AF.Exp),
                                         reads=[pl_r], writes=[Lt_r])
                                    S.op("dve", lambda e, quad=quad, Lt=Lt: e.tensor_tensor(
                                        out=MT[:, quad * 4:(quad + 1) * 4, :],
                                        in0=Lt[:].rearrange("p (h q) -> p h q", h=4),
                                        in1=mCB[:].unsqueeze(1).to_broadcast([128, 4, 128]), op=ALU.mult),
                                        reads=[Lt_r, mCB_r], writes=[MT_r])
                                    S.op("dve", lambda e, quad=quad, Lt=Lt: e.tensor_tensor(
                                        out=xdtd[:, quad * 256:(quad + 1) * 256].rearrange("p (h q) -> p h q", h=4),
                                        in0=xdt[:, quad * 256:(quad + 1) * 256].rearrange("p (h q) -> p h q", h=4),
                                        in1=Lt[:].rearrange("p (h q) -> p h q", h=4)[:, :, iend:iend + 1]
                                        .to_broadcast([128, 4, 64]), op=ALU.mult),
                                        reads=[xdt_r, Lt_r], writes=[xdtd_r])
                                py, py_r = self.psum()
                                for hh in range(8):
                                    S.op("pe", lambda e, hh=hh, py=py: e.matmul(
                                        py[:, hh * 64:(hh + 1) * 64], MT[:, hh, :], xdt[:, hh * 64:(hh + 1) * 64],
                                        start=True, stop=True), reads=[MT_r, xdt_r], writes=[py_r], inc=(hh == 7))
                                pf, pf_r = self.psum()
                                S.op("pe", lambda e, pf=pf: e.matmul(pf[:], xc[:, 5, tk], hTb[:], start=True, stop=True),
                                     reads=[xc_r, hTb_r], writes=[pf_r])
                                S.op("dve", lambda e, py=py: e.tensor_tensor(out=y32[:], in0=py[:], in1=y32[:], op=ALU.add),
                                     reads=[py_r, y32_r], writes=[y32_r])
                                sc, sc_r = self.scratch32()
                                S.op("dve", lambda e, pf=pf, sc=sc: e.tensor_tensor(
                                    out=sc[:].rearrange("p (h q) -> p h q", h=8),
                                    in0=pf[:].rearrange("p (h q) -> p h q", h=8),
                                    in1=eac[:, nt, dsl].unsqueeze(2).to_broadcast([128, 8, 64]), op=ALU.mult),
                                    reads=[pf_r, eac_r], writes=[sc_r])
                                if d_ == 0:
                                    S.op("dve", lambda e, sc=sc: e.tensor_tensor(out=yf[:, nt, :], in0=sc[:], in1=y32[:],
                                                                                op=ALU.add),
                                         reads=[sc_r, y32_r], writes=[yf_r])
                                else:
                                    S.op("dve", lambda e, sc=sc: e.tensor_tensor(out=y32[:], in0=sc[:], in1=y32[:],
                                                                                op=ALU.add),
                                         reads=[sc_r, y32_r], writes=[y32_r])
                                    S.op("dve", lambda e: e.tensor_tensor(out=yz[:], in0=y32[:], in1=z_tm[:, nt, :],
                                                                          op=ALU.mult),
                                         reads=[y32_r, z_r], writes=[yz_r])
                                    S.op("act", lambda e: e.activation(out=junk[:, 0:512], in_=yz[:], func=AF.Square,
                                                                       accum_out=ssq[:, nt, g:g + 1]),
                                         reads=[yz_r, ssq_r], writes=[junk_r, ssq_r])
                                    for q in range(4):
                                        kc = g * 4 + q
                                        self.transpose_to(yz[:, q * 128:(q + 1) * 128], [yz_r], yzT[:, q, :], yzT_r,
                                                          scale_ap=self.cst[:, offng + j * 32 + kc: offng + j * 32 + kc + 1],
                                                          scale_r=self.cst_r)
                                    S.dma("sp", self.yz_scr[g * 4:(g + 1) * 4, :, nt * 128:(nt + 1) * 128]
                                          .rearrange("q p t -> p q t"), yzT, reads=[yzT_r], store=True)
                                pS, pS_r = self.psum()
                                S.op("pe", lambda e, pS=pS: e.matmul(pS[:], B_tm[:], xdtd[:], start=True, stop=True),
                                     reads=[Btm_r, xdtd_r], writes=[pS_r])
                                S.op("dve", lambda e: e.tensor_tensor(
                                    out=hT[:].rearrange("p (h q) -> p h q", h=8),
                                    in0=hT[:].rearrange("p (h q) -> p h q", h=8),
                                    in1=cd[:, nt, dsl].unsqueeze(2).to_broadcast([128, 8, 64]), op=ALU.mult),
                                    reads=[hT_r, cd_r], writes=[hT_r])
                                S.op("dve", lambda e, pS=pS: e.tensor_tensor(out=hT[:], in0=hT[:], in1=pS[:], op=ALU.add),
                                     reads=[hT_r, pS_r], writes=[hT_r])
                                S.op("act", lambda e: e.copy(out=hTb[:], in_=hT[:]), reads=[hT_r], writes=[hTb_r])
                            if not self.sample:
                                S.dma("sp", self.st_out[j, d_, s, :, g * 512:(g + 1) * 512], hT[:],
                                      reads=[hT_r], store=True)
                S.barrier()
            with contextlib.ExitStack() as os_:
                ob = lambda n, s, d: self.sb("so_" + n, s, d, os_)
                yzt = ob("yzt", [128, 32, 512], BF16); yzt_r = Res("syzt")
                rs = ob("rs", [128, NT], F32); rs_r = Res("srs")
                dg = ob("dg", [128, 128], F32); dg_r = Res("sdg")
                S._wait(S.engs["sp"], list(S.store_toks.values()))
                S.op("dve", lambda e: e.reduce_sum(out=rs[:], in_=ssq[:], axis=AX.X), reads=[ssq_r], writes=[rs_r])
                S.op("dve", lambda e: e.tensor_scalar(out=rs[:], in0=rs[:], scalar1=1.0 / DI, scalar2=EPS,
                                                      op0=ALU.mult, op1=ALU.add), reads=[rs_r], writes=[rs_r])
                S.op("act", lambda e: e.sqrt(out=rs[:], in_=rs[:]), reads=[rs_r], writes=[rs_r])
                S.op("dve", lambda e: e.reciprocal(out=rs[:], in_=rs[:]), reads=[rs_r], writes=[rs_r])
                for nt in range(NT):
                    S.op("dve", lambda e, nt=nt: e.tensor_scalar(out=dg[:], in0=self.cs("ident"), scalar1=rs[:, nt:nt + 1],
                                                                 scalar2=None, op0=ALU.mult),
                         reads=[self.cst_r, rs_r], writes=[dg_r])
                    ps, ps_r = self.psum()
                    S.op("pe", lambda e, ps=ps: e.matmul(ps[:, 0:128], self.cs("ones"), dg[:], start=True, stop=True),
                         reads=[self.cst_r, dg_r], writes=[ps_r])
                    S.op("act", lambda e, ps=ps, nt=nt: e.copy(out=self.rstd[:, nt * 128:(nt + 1) * 128], in_=ps[:, 0:128]),
                         reads=[ps_r], writes=[self.rstd_r[nt // 4]])
                for t in range(self.TT):
                    S.dma("sp", yzt[:], self.yz_scr[:, :, t * 512:(t + 1) * 512].rearrange("k p t -> p k t"),
                          writes=[yzt_r])
                    for oc in range(DC):
                        wo, wo_r = self.load_cols(Wout, oc * 128, rows=DI)
                        po, po_r = self.psum()
                        for kc in range(32):
                            S.op("pe", lambda e, kc=kc, po=po, wo=wo: e.matmul(po[:], wo[:, kc, :], yzt[:, kc, :],
                                                                               start=(kc == 0), stop=(kc == 31)),
                                 reads=[wo_r, yzt_r], writes=[po_r], inc=(kc == 31))
                        sc, sc_r = self.scratch32()
                        S.op("dve", lambda e, po=po, sc=sc, oc=oc, t=t: e.scalar_tensor_tensor(
                            out=sc[:], in0=po[:], scalar=self.modcol(i, 5, oc), in1=self.rstd[:, t * 512:(t + 1) * 512],
                            op0=ALU.mult, op1=ALU.mult), reads=[po_r, self.mods_r, self.rstd_r[t]], writes=[sc_r])
                        S.op("dve", lambda e, sc=sc, oc=oc, t=t: e.tensor_tensor(
                            out=self.x[:, oc, t * 512:(t + 1) * 512], in0=sc[:], in1=self.x[:, oc, t * 512:(t + 1) * 512],
                            op=ALU.add), reads=[sc_r, self.x_r[oc][t]], writes=[self.x_r[oc][t]])
                S.barrier()


WEIGHT_KEYS = ("w_ada", "ffn1_w_in", "ffn2_w_in", "ffn1_w_out", "ffn2_w_out", "ssm_w_in", "ssm_w_out",
               "diff_w_qkv", "diff_w_out", "win_w_qkv", "win_w_out")


def make_in_maps(inp, cores=range(N_CORES)):
    consts = pack_consts(inp)
    bada = fm(inp["b_ada"].reshape(-1))
    rope = rope_tables()
    wm = win_mask()
    maps = []
    for core in cores:
        b = core // 4
        xs = np.concatenate([inp["x_prompt"][2 * core], inp["x_prompt"][2 * core + 1], inp["x_sample"][b]], axis=0)
        cv = np.stack([fm(inp["c_ctx"]), fm(inp["c"][b])], axis=2).reshape(128, DC * 2)
        st = np.stack([np.stack([inp[f"state_l{l}_{d}"][b].reshape(DI, 128).T for d in ("fwd", "bwd")])
                       for l in (0, 3)])
        m = {"xT": np.ascontiguousarray(xs.T), "cvec": np.ascontiguousarray(cv), "consts": consts,
             "b_ada_fm": bada, "rope_cs": rope, "wmask": wm, "st_in": np.ascontiguousarray(st),
             "kc1T": np.ascontiguousarray(inp["cache_l1_k"][b].reshape(256, D).T),
             "vc1": np.ascontiguousarray(inp["cache_l1_v"][b].reshape(256, D)),
             "kc2T": np.ascontiguousarray(inp["cache_l2_k"][b].reshape(256, 512).T),
             "vc2": np.ascontiguousarray(inp["cache_l2_v"][b].reshape(256, 512))}
        for k in WEIGHT_KEYS:
            m[k] = inp[k]
        maps.append(m)
    return maps


def assemble(results):
    B, S_ = 16, 256
    y_prompt = np.zeros((B, S_, D), np.float32)
    y_sample = np.zeros((2, LS, D), np.float32)
    st = [np.zeros((B, 64, 64, 128), np.float32) for _ in range(4)]
    k1 = np.zeros((B, S_, 2, 8, 128), np.float32)
    v1 = np.zeros((B, S_, 8, 256), np.float32)
    k2 = np.zeros((B, S_, 4, 128), np.float32)
    v2 = np.zeros((B, S_, 4, 128), np.float32)
    for core, r in enumerate(results):
        y = r["yT"].T
        y_prompt[2 * core] = y[0:256]
        y_prompt[2 * core + 1] = y[256:512]
        if core % 4 == 0:
            y_sample[core // 4] = y[512:]
        so = r["st_out"]
        for l in range(2):
            for d in range(2):
                for s in range(2):
                    st[l * 2 + d][2 * core + s] = so[l, d, s].T.reshape(64, 64, 128)
        k1t = r["k1T"].T
        k2t = r["k2T"].T
        for s in range(2):
            k1[2 * core + s] = k1t[s * 256:(s + 1) * 256].reshape(256, 2, 8, 128)
            v1[2 * core + s] = r["v1"][s * 256:(s + 1) * 256].reshape(256, 8, 256)
            k2[2 * core + s] = k2t[s * 256:(s + 1) * 256].reshape(256, 4, 128)
            v2[2 * core + s] = r["v2"][s * 256:(s + 1) * 256].reshape(256, 4, 128)
    return (y_prompt, y_sample, st[0], st[1], k1, v1, k2, v2, st[2], st[3])


def kernel(**inp):
    inp = {k: np.asarray(v) for k, v in inp.items()}
    nc = Builder().build()
    maps = make_in_maps(inp)
    res = run_bass_kernel_spmd(nc, maps, core_ids=list(range(N_CORES)))
    return assemble(res.results)
```

```python
import contextlib
import math
import os
import numpy as np
import concourse.bass as bass
import concourse.mybir as mybir
from concourse.bass_utils import run_bass_kernel_spmd

F32 = mybir.dt.float32
BF16 = mybir.dt.bfloat16
AF = mybir.ActivationFunctionType
ALU = mybir.AluOpType
AX = mybir.AxisListType

D = 2048
DC = 16
NP_SEQ = 2
LP = 256
LS = 1024
T = NP_SEQ * LP + LS
NTT = T // 512
DEPTH = 4
D_FF = 5632
FC = D_FF // 128
N_MOD = 9
EPS = 1e-6
N_CORES = 8

SYNC_ENGS = set(os.environ.get('KSYNC', 'pe,act,dve,pool,sp').split(','))
KSTOP = int(os.environ.get('KSTOP', '9'))
KPART = os.environ.get('KPART', 'kv')
KSKIP = os.environ.get('KSKIP', '')


class Res:
    __slots__ = ("name", "w", "r", "dsem", "dval")

    def __init__(self, name):
        self.name = name
        self.w = None
        self.r = {}
        self.dsem = None
        self.dval = 0


class Eng:
    def __init__(self, name, handle, sem):
        self.name = name
        self.h = handle
        self.sem = sem
        self.count = 0
        self.waited = {}
        self.pend_r = []
        self.pend_w = []


class Sched:
    def __init__(self, nc, es):
        self.nc = nc
        self.es = es
        self.engs = {}
        for name, h in (("pe", nc.tensor), ("act", nc.scalar), ("dve", nc.vector),
                        ("pool", nc.gpsimd), ("sp", nc.sync)):
            sem = es.enter_context(nc.semaphore("sem_" + name))
            self.engs[name] = Eng(name, h, sem)
        self.sem_ids = {}
        self.store_toks = {}
        self.n_ops = 0

    def _wait(self, E, deps):
        for (sem, val) in deps:
            if sem is E.sem and E.name not in SYNC_ENGS:
                continue
            k = id(sem)
            if E.waited.get(k, 0) >= val:
                continue
            E.h.wait_ge(sem, val)
            E.waited[k] = val

    @staticmethod
    def _deps(reads, writes):
        deps = []
        for r in reads:
            if r.w is not None:
                deps.append(r.w)
        for w in writes:
            if w.w is not None:
                deps.append(w.w)
            for sem_k, (sem, val) in w.r.items():
                deps.append((sem, val))
        return deps

    def op(self, eng, fn, reads=(), writes=(), inc=True):
        E = self.engs[eng]
        self._wait(E, self._deps(reads, writes))
        ins = fn(E.h)
        self.n_ops += 1
        E.pend_r.extend(reads)
        E.pend_w.extend(writes)
        if inc:
            E.count += 1
            ins.then_inc(E.sem, 1)
            tok = (E.sem, E.count)
            for r in E.pend_r:
                r.r[id(E.sem)] = tok
            for w in E.pend_w:
                w.w = tok
                w.r = {}
            E.pend_r = []
            E.pend_w = []
        return ins

    def dma(self, eng, out, in_, reads=(), writes=(), store=False):
        E = self.engs[eng]
        self._wait(E, self._deps(reads, writes))
        res = (list(writes) + list(reads))[0]
        if res.dsem is None:
            if res.name not in self.sem_ids:
                self.sem_ids[res.name] = [self.es.enter_context(self.nc.semaphore("dsem_" + res.name)), 0]
            res.dsem, res.dval = self.sem_ids[res.name]
        res.dval += 16
        self.sem_ids[res.name][1] = res.dval
        E.h.dma_start(out=out, in_=in_).then_inc(res.dsem, 16)
        self.n_ops += 1
        tok = (res.dsem, res.dval)
        for r in reads:
            r.r[id(res.dsem)] = tok
        for w in writes:
            w.w = tok
            w.r = {}
        if store:
            self.store_toks[id(res.dsem)] = tok

    def barrier(self):
        toks = [(E.sem, E.count) for E in self.engs.values() if E.count > 0]
        toks += list(self.store_toks.values())
        for E in self.engs.values():
            assert not E.pend_r and not E.pend_w, "pending ops at barrier"
            self._wait(E, toks)

    def finish(self):
        E = self.engs["sp"]
        self._wait(E, list(self.store_toks.values()))
        toks = [(e.sem, e.count) for e in self.engs.values() if e.count > 0]
        self._wait(E, toks)


DI = 4096
NG = 8
N_SSM = 2
SSM_IN = 10368
QBL = 128
CONST_SPEC = (("norm_g", DEPTH * 3 * DC), ("final_g", DC), ("ident", 128), ("ones", 128),
              ("Ule", 128), ("Uge", 128), ("SL", 128), ("SU", 128), ("RT", 128),
              ("conv_w", N_SSM * 3 * 48), ("conv_b", N_SSM * 48), ("ssm_norm_g", N_SSM * 32),
              ("dt_bias", N_SSM * 128), ("a_log", N_SSM * 128), ("ssm_d", N_SSM * 64),
              ("subln", 256), ("lam", 512), ("sink", 16))


def fm(vec):
    v = np.asarray(vec, np.float32).reshape(-1, 128)
    return np.ascontiguousarray(v.T)


def bc(vec):
    v = np.asarray(vec, np.float32).reshape(1, -1)
    return np.ascontiguousarray(np.broadcast_to(v, (128, v.shape[1])))


def const_layout():
    lay = {}
    n = 0
    for name, w in CONST_SPEC:
        lay[name] = (n, w)
        n += w
    return lay, n


def pack_consts(inp):
    k = np.arange(128)
    parts = {
        "norm_g": fm(inp["norm_g"].reshape(-1)),
        "final_g": fm(inp["final_norm_g"]),
        "ident": np.eye(128, dtype=np.float32),
        "ones": np.ones((128, 128), np.float32),
        "Ule": (k[:, None] <= k[None, :]).astype(np.float32),
        "Uge": (k[:, None] >= k[None, :]).astype(np.float32),
        "SL": (k[:, None] > k[None, :]).astype(np.float32),
        "SU": (k[:, None] < k[None, :]).astype(np.float32),
    }
    R = np.zeros((128, 128), np.float32)
    for d in range(128):
        if (d // 32) % 2 == 0:
            R[d, d + 32] = -1.0
        else:
            R[d, d - 32] = 1.0
    parts["RT"] = np.ascontiguousarray(R.T)
    parts["conv_w"] = fm(inp["ssm_conv_w"].reshape(-1))
    parts["conv_b"] = fm(inp["ssm_conv_b"].reshape(-1))
    parts["ssm_norm_g"] = fm(inp["ssm_norm_g"].reshape(-1))
    parts["dt_bias"] = bc(inp["ssm_dt_bias"].reshape(-1))
    parts["a_log"] = bc(inp["ssm_a_log"].reshape(-1))
    parts["ssm_d"] = bc(inp["ssm_d"].reshape(-1))
    parts["subln"] = bc(inp["diff_subln_g"].reshape(-1))
    parts["lam"] = bc(inp["diff_lambda"].reshape(-1))
    parts["sink"] = bc(inp["win_sink"].reshape(-1))
    lay, n = const_layout()
    arrs = []
    for name, w in CONST_SPEC:
        a = parts[name]
        assert a.shape == (128, w), (name, a.shape, w)
        arrs.append(a)
    return np.ascontiguousarray(np.concatenate(arrs, axis=1))


def rope_tables():
    L, GW, nf = LS, 64, 32
    rows = L // GW
    row = np.repeat(np.arange(rows, dtype=np.float32), GW)
    col = np.tile(np.arange(GW, dtype=np.float32), rows)
    inv = (np.float32(10000.0) ** (-np.arange(nf, dtype=np.float32) / np.float32(nf))).astype(np.float32)
    ar = (row[:, None] * inv).astype(np.float32)
    ac = (col[:, None] * inv).astype(np.float32)
    ang = np.concatenate([ar, ar, ac, ac], axis=1)
    cs = np.stack([np.cos(ang).T, np.sin(ang).T], axis=1).astype(np.float32)
    return np.ascontiguousarray(cs)


def win_mask():
    qi = np.arange(128)[:, None]
    kj = np.arange(384)[None, :]
    ok = np.abs(qi + 128 - kj) <= 128
    return np.where(ok, 0.0, -30000.0).astype(np.float32)


TPM = 1024


class Builder:
    def __init__(self, layers=None, passes=("B", "A"), ffn=True, mix=True):
        self.layers = list(range(DEPTH)) if layers is None else layers
        self.pass_names = passes
        self.do_ffn = ffn
        self.do_mix = mix
        self.nc = bass.Bass("TRN2", target_bir_lowering=False)

    def dram_in(self, name, shape, dt=F32):
        return self.nc.dram_tensor(name, list(shape), dt, kind="ExternalInput").ap()

    def dram_out(self, name, shape, dt=F32):
        return self.nc.dram_tensor(name, list(shape), dt, kind="ExternalOutput").ap()

    def sb(self, name, shape, dt, es=None):
        self.uid = getattr(self, "uid", 0) + 1
        return (es or self.es).enter_context(self.nc.sbuf_tensor(f"{name}_{self.uid}", list(shape), dt))

    def psum(self):
        i = self.ps_next
        self.ps_next = (i + 1) % 8
        return self.ps_t[i], self.ps_r[i]

    def wslot(self):
        i = self.w_next
        self.w_next = (i + 1) % self.NW
        return self.w_t[i], self.w_r[i]

    def load_cols(self, W, col0, ncols=128, rows=D):
        t, r = self.wslot()
        kc = rows // 128
        assert kc * ncols <= 4096
        view = t[:, 0:kc * ncols].rearrange("p (c n) -> p c n", c=kc)
        src = W[:, col0:col0 + ncols].rearrange("(c p) n -> p c n", p=128)
        self.S.dma("pool", view, src, writes=[r])
        return view, r

    def load_rows(self, W, row0, ncols=D, col0=0):
        t, r = self.wslot()
        view = t[:, 0:ncols]
        self.S.dma("pool", view, W[row0:row0 + 128, col0:col0 + ncols], writes=[r])
        return view, r

    def scratch32(self):
        i = self.sc_next
        self.sc_next = (i + 1) % self.NSC
        return self.sc32[i], self.sc32_r[i]

    def cs(self, name, a=0, b=None):
        off, w = self.lay[name]
        if b is None:
            b = w
        return self.cst[:, off + a:off + b]

    def build(self):
        nc = self.nc
        with contextlib.ExitStack() as es:
            self.es = es
            self.S = S = Sched(nc, es)
            self.declare_io()
            self.alloc()
            self.load_consts()
            self.modulation_all()
            for pn in self.pass_names:
                self.set_pass(pn)
                self.load_x()
                for i in self.layers:
                    self.layer(i)
                self.final()
                S.barrier()
            S.finish()
        return nc

    def set_pass(self, pn):
        if pn == "A":
            self.tok0, self.TP, self.nseq, self.L, self.sample, self.v = 0, 512, 2, 256, False, 0
        else:
            self.tok0, self.TP, self.nseq, self.L, self.sample, self.v = 512, 1024, 1, 1024, True, 1
        self.TT = self.TP // 512
        self.NT = self.TP // 128

    def declare_io(self):
        lay, ncst = const_layout()
        self.lay = lay
        di = self.dram_in
        self.xT_in = di("xT", [D, T])
        self.cvec_in = di("cvec", [128, DC * 2])
        self.consts_in = di("consts", [128, ncst])
        self.bada_in = di("b_ada_fm", [128, DEPTH * N_MOD * DC])
        self.rope_in = di("rope_cs", [128, 2, LS])
        self.wmask_in = di("wmask", [128, 384])
        self.st_in = di("st_in", [2, 2, 128, DI])
        self.kc1T_in = di("kc1T", [D, 256])
        self.vc1_in = di("vc1", [256, D])
        self.kc2T_in = di("kc2T", [512, 256])
        self.vc2_in = di("vc2", [256, 512])
        self.w_ada = di("w_ada", [DEPTH, D, N_MOD * D])
        if self.do_ffn:
            self.ffn_w_in = [di("ffn1_w_in", [DEPTH, D, 2 * D_FF]), di("ffn2_w_in", [DEPTH, D, 2 * D_FF])]
            self.ffn_w_out = [di("ffn1_w_out", [DEPTH, D_FF, D]), di("ffn2_w_out", [DEPTH, D_FF, D])]
        kinds = {i % 3 for i in self.layers} if self.do_mix else set()
        if 0 in kinds:
            self.ssm_w_in = di("ssm_w_in", [N_SSM, D, SSM_IN])
            self.ssm_w_out = di("ssm_w_out", [N_SSM, DI, D])
        if 1 in kinds:
            self.diff_w_qkv = di("diff_w_qkv", [1, D, 6144])
            self.diff_w_out = di("diff_w_out", [1, D, D])
        if 2 in kinds:
            self.win_w_qkv = di("win_w_qkv", [1, D, 3072])
            self.win_w_out = di("win_w_out", [1, D, D])
        do = self.dram_out
        self.yT_out = do("yT", [D, T])
        self.st_out = do("st_out", [2, 2, 2, 128, DI])
        self.k1T_out = do("k1T", [D, 512])
        self.v1_out = do("v1", [512, D])
        self.k2T_out = do("k2T", [512, 512])
        self.v2_out = do("v2", [512, 512])
        self.yz_scr = self.nc.dram_tensor("yz_scr", [32, 128, TPM], BF16, kind="Internal").ap()
        self.wscr = [[self.nc.dram_tensor(f"wscr_{l}_{w}", [(FC // 2) * 3, 128, 4096], BF16, kind="Internal").ap()
                      for w in range(2)] for l in range(DEPTH)]
        self.reuse = ("B" in self.pass_names and "A" in self.pass_names and
                      self.pass_names.index("B") < self.pass_names.index("A"))

    def alloc(self):
        nc = self.nc
        lay, ncst = const_layout()
        self.x = self.sb("x", [128, DC, TPM], F32)
        self.x_r = [[Res(f"x{c}_{t}") for t in range(2)] for c in range(DC)]
        self.h = self.sb("h", [128, DC, TPM], BF16)
        self.h_r = [[Res(f"h{c}_{t}") for t in range(2)] for c in range(DC)]
        self.cst = self.sb("cst", [128, ncst], F32)
        self.cst_r = Res("cst")
        self.ident_bf = self.sb("ident_bf", [128, 128], BF16)
        self.ones_bf = self.sb("ones_bf", [128, 128], BF16)
        self.RT_bf = self.sb("RT_bf", [128, 128], BF16)
        self.cbf_r = Res("cbf")
        self.mods = self.sb("mods", [128, DEPTH * N_MOD * DC, 2], F32)
        self.mods_r = Res("mods")
        self.ab = self.sb("ab", [128, 3, DC], F32)
        self.ab_r = Res("ab")
        self.rstd = self.sb("rstd", [128, TPM], F32)
        self.rstd_r = [Res(f"rstd{t}") for t in range(2)]
        self.ps_t = [self.es.enter_context(nc.psum_tensor(f"ps{i}", [128, 512], F32)) for i in range(8)]
        self.ps_r = [Res(f"ps{i}") for i in range(8)]
        self.ps_next = 0
        self.NW = 4
        self.w_t = [self.sb(f"w{i}", [128, 4096], BF16) for i in range(self.NW)]
        self.w_r = [Res(f"w{i}") for i in range(self.NW)]
        self.w_next = 0
        self.NSC = 3
        self.sc32 = [self.sb(f"sc32_{i}", [128, 512], F32) for i in range(self.NSC)]
        self.sc32_r = [Res(f"sc32_{i}") for i in range(self.NSC)]
        self.sc_next = 0
        self.sq = [self.sb(f"sq{i}", [128, 512], BF16) for i in range(2)]
        self.sq_r = [Res(f"sq{i}") for i in range(2)]
        self.sq_next = 0

    def load_consts(self):
        S = self.S
        S.dma("sp", self.cst[:], self.consts_in, writes=[self.cst_r])
        for dst, name in ((self.ident_bf, "ident"), (self.ones_bf, "ones"), (self.RT_bf, "RT")):
            S.op("dve", lambda e, dst=dst, name=name: e.tensor_copy(out=dst[:], in_=self.cs(name)),
                 reads=[self.cst_r], writes=[self.cbf_r])

    def load_x(self):
        S = self.S
        xin = self.xT_in.rearrange("(c p) t -> p c t", p=128)
        for c in range(DC):
            for t in range(self.TT):
                S.dma("sp", self.x[:, c, t * 512:(t + 1) * 512],
                      xin[:, c, self.tok0 + t * 512:self.tok0 + (t + 1) * 512], writes=[self.x_r[c][t]])

    def modulation_all(self):
        S = self.S
        NCC = N_MOD * DC
        with contextlib.ExitStack() as ms:
            cv = self.sb("cv", [128, DC * 2], F32, ms)
            cv_r = Res("cv")
            csil = self.sb("csil", [128, DC, 2], BF16, ms)
            csil_r = Res("csil")
            bada = self.sb("bada", [128, DEPTH * NCC], F32, ms)
            bada_r = Res("bada")
            S.dma("sp", cv[:], self.cvec_in, writes=[cv_r])
            S.dma("sp", bada[:], self.bada_in, writes=[bada_r])
            S.op("act", lambda e: e.activation(out=csil[:].rearrange("p c v -> p (c v)"), in_=cv[:], func=AF.Silu),
                 reads=[cv_r], writes=[csil_r])
            for i in self.layers:
                W = self.w_ada[i]
                ps, ps_r = self.psum()
                for cc in range(NCC):
                    if cc % 2 == 0:
                        wv, wr = self.load_cols(W, cc * 128, ncols=256)
                    hf = (cc % 2) * 128
                    for k in range(DC):
                        S.op("pe", lambda e, k=k, wv=wv, cc=cc, ps=ps, hf=hf: e.matmul(
                            ps[:, cc * 2:cc * 2 + 2], wv[:, k, hf:hf + 128], csil[:, k, :],
                            start=(k == 0), stop=(k == DC - 1)),
                            reads=[wr, csil_r], writes=[ps_r], inc=(k == DC - 1))
                S.op("dve", lambda e, i=i, ps=ps: e.tensor_tensor(
                    out=self.mods[:, i * NCC:(i + 1) * NCC, :],
                    in0=ps[:, 0:2 * NCC].rearrange("p (c v) -> p c v", v=2),
                    in1=bada[:, i * NCC:(i + 1) * NCC].unsqueeze(2).to_broadcast([128, NCC, 2]), op=ALU.add),
                    reads=[ps_r, bada_r], writes=[self.mods_r])
            S.barrier()

    def modvec(self, i, j):
        b = (i * N_MOD + j) * DC
        return self.mods[:, b:b + DC, self.v]

    def modcol(self, i, j, c):
        b = (i * N_MOD + j) * DC + c
        return self.mods[:, b, self.v:self.v + 1]

    def rms_stats(self):
        S = self.S
        for t in range(self.TT):
            ts = slice(t * 512, (t + 1) * 512)
            ps, ps_r = self.psum()
            for c in range(DC):
                i = self.sq_next
                self.sq_next = (i + 1) % 2
                sq, sq_r = self.sq[i], self.sq_r[i]
                S.op("act", lambda e, c=c, ts=ts, sq=sq: e.activation(out=sq[:], in_=self.x[:, c, ts],
                                                                      func=AF.Square),
                     reads=[self.x_r[c][t]], writes=[sq_r])
                S.op("pe", lambda e, c=c, sq=sq, ps=ps: e.matmul(ps[:], self.ones_bf[:], sq[:],
                                                                 start=(c == 0), stop=(c == DC - 1)),
                     reads=[sq_r, self.cbf_r], writes=[ps_r], inc=True)
            S.op("dve", lambda e, ts=ts, ps=ps: e.tensor_scalar(
                out=self.rstd[:, ts], in0=ps[:], scalar1=1.0 / D, scalar2=EPS, op0=ALU.mult, op1=ALU.add),
                reads=[ps_r], writes=[self.rstd_r[t]])
            S.op("act", lambda e, ts=ts: e.sqrt(out=self.rstd[:, ts], in_=self.rstd[:, ts]),
                 reads=[self.rstd_r[t]], writes=[self.rstd_r[t]])
            S.op("dve", lambda e, ts=ts: e.reciprocal(out=self.rstd[:, ts], in_=self.rstd[:, ts]),
                 reads=[self.rstd_r[t]], writes=[self.rstd_r[t]])

    def modnorm(self, i, sub, j_shift, j_scale):
        S = self.S
        off, _ = self.lay["norm_g"]
        g = self.cst[:, off + (i * 3 + sub) * DC: off + (i * 3 + sub + 1) * DC]
        S.op("dve", lambda e: e.scalar_tensor_tensor(
            out=self.ab[:, 0, :], in0=self.modvec(i, j_scale), scalar=1.0, in1=g, op0=ALU.add, op1=ALU.mult),
            reads=[self.mods_r, self.cst_r], writes=[self.ab_r])
        self.rms_stats()
        for t in range(self.TT):
            ts = slice(t * 512, (t + 1) * 512)
            for c in range(DC):
                sc, sc_r = self.scratch32()
                S.op("dve", lambda e, c=c, ts=ts, sc=sc: e.scalar_tensor_tensor(
                    out=sc[:], in0=self.x[:, c, ts], scalar=self.ab[:, 0, c:c + 1], in1=self.rstd[:, ts],
                    op0=ALU.mult, op1=ALU.mult),
                    reads=[self.x_r[c][t], self.ab_r, self.rstd_r[t]], writes=[sc_r])
                S.op("act", lambda e, c=c, ts=ts, sc=sc: e.activation(
                    out=self.h[:, c, ts], in_=sc[:], func=AF.Identity,
                    bias=self.modcol(i, j_shift, c), scale=1.0),
                    reads=[sc_r, self.mods_r], writes=[self.h_r[c][t]])

    def ffn(self, i, which, j_gate):
        S = self.S
        W_in = self.ffn_w_in[which][i]
        W_out = self.ffn_w_out[which][i]
        S.op("dve", lambda e: e.tensor_scalar(out=self.ab[:, 2, :], in0=self.modvec(i, j_gate), scalar1=0.5,
                                              scalar2=None, op0=ALU.mult), reads=[self.mods_r], writes=[self.ab_r])
        NX = 5
        with contextlib.ExitStack() as fs:
            extra_t = [self.sb(f"wx{k}", [128, 4096], BF16, fs) for k in range(NX)]
            extra_r = [Res(f"wx{k}") for k in range(NX)]
            a_t = [self.sb(f"fa{k}", [128, 2, TPM], BF16, fs) for k in range(2)]
            a_r = [[[Res(f"fa{k}_{j}_{t}") for t in range(2)] for j in range(2)] for k in range(2)]
            save = (self.w_t, self.w_r, self.NW, self.w_next)
            self.w_t = self.w_t + extra_t
            self.w_r = self.w_r + extra_r
            self.NW = len(self.w_t)
            for f in range(FC // 2):
                sbase = f * 3
                wscr = self.wscr[i][which]
                if self.reuse and not self.sample:
                    tiles = []
                    for kind in range(3):
                        wt_, w_r = self.wslot()
                        S.dma("sp", wt_[:, 0:4096], wscr[sbase + kind], writes=[w_r])
                        tiles.append((wt_, w_r))
                    (tg, wg_r), (tu, wu_r), (to, wo_r) = tiles
                    wg = tg[:, 0:4096].rearrange("p (c n) -> p c n", c=DC)
                    wu = tu[:, 0:4096].rearrange("p (c n) -> p c n", c=DC)
                    wo = to[:, 0:4096].rearrange("p (c n) -> p c n", c=2)
                else:
                    wg, wg_r = self.load_cols(W_in, f * 256, ncols=256)
                    wu, wu_r = self.load_cols(W_in, D_FF + f * 256, ncols=256)
                    wt_, wo_r = self.wslot()
                    wo = wt_[:, 0:2 * D].rearrange("p (c n) -> p c n", c=2)
                    S.dma("pool", wo, W_out[f * 256:(f + 1) * 256, :].rearrange("(c p) n -> p c n", p=128),
                          writes=[wo_r])
                    if self.reuse:
                        for kind, (wv_, wr_) in enumerate(((wg, wg_r), (wu, wu_r), (wo, wo_r))):
                            flat = wv_.rearrange("p c n -> p (c n)")
                            S.dma("sp", wscr[sbase + kind], flat, reads=[wr_], store=True)
                ai = f % 2
                a = a_t[ai]
                for j in range(2):
                    for t in range(self.TT):
                        ts = slice(t * 512, (t + 1) * 512)
                        pg, pg_r = self.psum()
                        for k in range(DC):
                            S.op("pe", lambda e, k=k: e.matmul(
                                pg[:], wg[:, k, j * 128:(j + 1) * 128], self.h[:, k, ts], start=(k == 0),
                                stop=(k == DC - 1)),
                                reads=[wg_r, self.h_r[k][t]], writes=[pg_r], inc=(k == DC - 1))
                        pu, pu_r = self.psum()
                        for k in range(DC):
                            S.op("pe", lambda e, k=k: e.matmul(
                                pu[:], wu[:, k, j * 128:(j + 1) * 128], self.h[:, k, ts], start=(k == 0),
                                stop=(k == DC - 1)),
                                reads=[wu_r, self.h_r[k][t]], writes=[pu_r], inc=(k == DC - 1))
                        sc, sc_r = self.scratch32()
                        S.op("act", lambda e: e.activation(out=sc[:], in_=pg[:], func=AF.Silu),
                             reads=[pg_r], writes=[sc_r])
                        S.op("dve", lambda e: e.tensor_tensor(out=a[:, j, ts], in0=sc[:], in1=pu[:], op=ALU.mult),
                             reads=[sc_r, pu_r], writes=[a_r[ai][j][t]])
                for oc in range(DC):
                    for t in range(self.TT):
                        ts = slice(t * 512, (t + 1) * 512)
                        po, po_r = self.psum()
                        for j in range(2):
                            S.op("pe", lambda e, j=j: e.matmul(
                                po[:], wo[:, j, oc * 128:(oc + 1) * 128], a[:, j, ts], start=(j == 0), stop=(j == 1)),
                                reads=[wo_r, a_r[ai][j][t]], writes=[po_r], inc=(j == 1))
                        self.x_accum(po, po_r, self.ab[:, 2, oc:oc + 1], self.ab_r, oc, t)
            S.barrier()
            self.w_t, self.w_r, self.NW, self.w_next = save

    def x_accum(self, po, po_r, scal, scal_r, oc, t, n=512, off=0):
        ts = slice(t * 512 + off, t * 512 + off + n)
        self.S.op("dve", lambda e: e.scalar_tensor_tensor(
            out=self.x[:, oc, ts], in0=po[:, 0:n], scalar=scal, in1=self.x[:, oc, ts],
            op0=ALU.mult, op1=ALU.add),
            reads=[po_r, scal_r, self.x_r[oc][t]], writes=[self.x_r[oc][t]])

    def layer(self, i):
        if self.do_ffn:
            self.modnorm(i, 0, 0, 1)
            self.ffn(i, 0, 2)
        if self.do_mix:
            self.modnorm(i, 1, 3, 4)
            self.S.barrier()
            m = i % 3
            if m == 0:
                self.ssm_mixer(i)
            elif m == 1:
                self.diff_mixer(i)
            else:
                self.win_mixer(i)
            self.S.barrier()
        if self.do_ffn:
            self.modnorm(i, 2, 6, 7)
            self.ffn(i, 1, 8)

    def final(self):
        S = self.S
        self.rms_stats()
        off, _ = self.lay["final_g"]
        yout = self.yT_out.rearrange("(c p) t -> p c t", p=128)
        for t in range(self.TT):
            ts = slice(t * 512, (t + 1) * 512)
            for c in range(DC):
                sc, sc_r = self.scratch32()
                S.op("dve", lambda e, c=c, ts=ts, sc=sc: e.scalar_tensor_tensor(
                    out=sc[:], in0=self.x[:, c, ts], scalar=self.cst[:, off + c:off + c + 1],
                    in1=self.rstd[:, ts], op0=ALU.mult, op1=ALU.mult),
                    reads=[self.x_r[c][t], self.cst_r, self.rstd_r[t]], writes=[sc_r])
                S.dma("sp", yout[:, c, self.tok0 + t * 512:self.tok0 + (t + 1) * 512], sc[:],
                      reads=[sc_r], store=True)

    def proj_fm(self, wv, wr, t, ps, ps_r, ncol=128, wcol0=0):
        ts = slice(t * 512, (t + 1) * 512)
        for k in range(DC):
            self.S.op("pe", lambda e, k=k: e.matmul(ps[0:ncol, :], wv[:, k, wcol0:wcol0 + ncol], self.h[:, k, ts],
                                                    start=(k == 0), stop=(k == DC - 1)),
                      reads=[wr, self.h_r[k][t]], writes=[ps_r], inc=(k == DC - 1))

    def proj_tm(self, wv, wr, nt, ps, ps_r, ncol, pcol0=0, wcol0=0):
        t = nt // 4
        tk = slice(nt * 128, (nt + 1) * 128)
        for k in range(DC):
            self.S.op("pe", lambda e, k=k: e.matmul(ps[:, pcol0:pcol0 + ncol], self.h[:, k, tk],
                                                    wv[:, k, wcol0:wcol0 + ncol],
                                                    start=(k == 0), stop=(k == DC - 1)),
                      reads=[wr, self.h_r[k][t]], writes=[ps_r], inc=(k == DC - 1))

    def rope_evac(self, ps, ps_r, dst, dst_r, t, cs_t, cs_r):
        S = self.S
        ts = slice(t * 512, (t + 1) * 512)
        xb = self.sq[self.sq_next]
        xb_r = self.sq_r[self.sq_next]
        self.sq_next = (self.sq_next + 1) % 2
        S.op("dve", lambda e: e.tensor_copy(out=xb[:], in_=ps[:]), reads=[ps_r], writes=[xb_r])
        pr, pr_r = self.psum()
        S.op("pe", lambda e: e.matmul(pr[:], self.RT_bf[:], xb[:], start=True, stop=True),
             reads=[xb_r, self.cbf_r], writes=[pr_r])
        s1, s1_r = self.scratch32()
        s2, s2_r = self.scratch32()
        S.op("dve", lambda e: e.tensor_tensor(out=s1[:], in0=ps[:], in1=cs_t[:, 0, ts], op=ALU.mult),
             reads=[ps_r, cs_r], writes=[s1_r])
        S.op("dve", lambda e: e.tensor_tensor(out=s2[:], in0=pr[:], in1=cs_t[:, 1, ts], op=ALU.mult),
             reads=[pr_r, cs_r], writes=[s2_r])
        S.op("dve", lambda e: e.tensor_tensor(out=dst, in0=s1[:], in1=s2[:], op=ALU.add),
             reads=[s1_r, s2_r], writes=[dst_r])

    def attn_tile(self, qT_ap, kparts, vparts, scale, sink_ap, W):
        S = self.S
        P, P_r = W["P"], W["P_r"]
        PT, PT_r = W["PT"], W["PT_r"]
        st, st_r = W["st"], W["st_r"]
        dv = W["dv"]
        srcs = []
        col = 0
        for j, (kT_ap, n, mask_ap) in enumerate(kparts):
            ps, ps_r = self.psum()
            S.op("pe", lambda e, ps=ps, kT_ap=kT_ap, n=n: e.matmul(ps[:, 0:n], qT_ap, kT_ap, start=True, stop=True),
                 reads=W["q_reads"] + W["k_reads"], writes=[ps_r])
            if mask_ap is not None:
                sc, sc_r = self.scratch32()
                S.op("dve", lambda e, sc=sc, ps=ps, n=n, mask_ap=mask_ap: e.tensor_tensor(
                    out=sc[:, 0:n], in0=ps[:, 0:n], in1=mask_ap, op=ALU.add),
                    reads=[ps_r, W["mask_r"]], writes=[sc_r])
                src, src_r = sc, sc_r
            else:
                sc, sc_r = self.scratch32()
                S.op("dve", lambda e, sc=sc, ps=ps, n=n: e.tensor_copy(out=sc[:, 0:n], in_=ps[:, 0:n]),
                     reads=[ps_r], writes=[sc_r])
                src, src_r = sc, sc_r
            S.op("dve", lambda e, src=src, n=n, j=j: e.reduce_max(out=st[:, j:j + 1], in_=src[:, 0:n], axis=AX.X),
                 reads=[src_r], writes=[st_r])
            srcs.append((src, src_r, n, col))
            col += n
        nk = col
        for j in range(1, len(kparts)):
            S.op("dve", lambda e, j=j: e.tensor_tensor(out=st[:, 0:1], in0=st[:, 0:1], in1=st[:, j:j + 1], op=ALU.max),
                 reads=[st_r], writes=[st_r])
        if sink_ap is None:
            S.op("dve", lambda e: e.tensor_scalar(out=st[:, 4:5], in0=st[:, 0:1], scalar1=-scale, scalar2=None,
                                                  op0=ALU.mult), reads=[st_r], writes=[st_r])
        else:
            S.op("dve", lambda e: e.tensor_scalar(out=st[:, 4:5], in0=st[:, 0:1], scalar1=-scale,
                                                  scalar2=W["nsink_ap"], op0=ALU.mult, op1=ALU.min),
                 reads=[st_r, W["nsink_r"]], writes=[st_r])
        S.op("dve", lambda e: e.memset(st[:, 8:8 + len(kparts) + 1], 0.0), writes=[st_r])
        for j, (src, src_r, n, c0) in enumerate(srcs):
            S.op("act", lambda e, src=src, n=n, c0=c0, j=j: e.activation(
                out=P[:, c0:c0 + n], in_=src[:, 0:n], func=AF.Exp, bias=st[:, 4:5], scale=scale,
                accum_out=st[:, 8 + j:9 + j]),
                reads=[src_r, st_r], writes=[P_r, st_r])
        ns = len(kparts)
        if sink_ap is not None:
            S.op("act", lambda e: e.activation(out=st[:, 8 + ns:9 + ns], in_=sink_ap, func=AF.Exp,
                                               bias=st[:, 4:5], scale=1.0),
                 reads=[st_r, W["nsink_r"]], writes=[st_r])
            ns += 1
        S.op("dve", lambda e: e.reduce_sum(out=st[:, 5:6], in_=st[:, 8:8 + ns], axis=AX.X),
             reads=[st_r], writes=[st_r])
        S.op("dve", lambda e: e.reciprocal(out=st[:, 6:7], in_=st[:, 5:6]), reads=[st_r], writes=[st_r])
        nkt = nk // 128
        for b0 in range(0, nkt, 4):
            nb = min(4, nkt - b0)
            pt, pt_r = self.psum()
            for jj in range(nb):
                kt = b0 + jj
                S.op("pe", lambda e, pt=pt, jj=jj, kt=kt: e.matmul(
                    pt[:, jj * 128:(jj + 1) * 128], P[:, kt * 128:(kt + 1) * 128], self.ident_bf[:],
                    start=True, stop=True), reads=[P_r, self.cbf_r], writes=[pt_r], inc=(jj == nb - 1))
            S.op("act", lambda e, pt=pt, b0=b0, nb=nb: e.copy(
                out=PT[:, b0 * 128:(b0 + nb) * 128], in_=pt[:, 0:nb * 128]),
                reads=[pt_r], writes=[PT_r])
        po, po_r = self.psum()
        for kt in range(nkt):
            S.op("pe", lambda e, kt=kt: e.matmul(po[:, 0:dv], PT[:, kt * 128:(kt + 1) * 128], vparts[kt],
                                                 start=(kt == 0), stop=(kt == nkt - 1)),
                 reads=[PT_r] + W["v_reads"], writes=[po_r], inc=(kt == nkt - 1))
        return po, po_r

    def transpose_to(self, src_ap, src_reads, dst_ap, dst_r, ncols=128, scale_ap=None, scale_r=None):
        S = self.S
        pt, pt_r = self.psum()
        S.op("pe", lambda e: e.matmul(pt[0:ncols, 0:128], src_ap, self.ident_bf[:], start=True, stop=True),
             reads=list(src_reads) + [self.cbf_r], writes=[pt_r])
        if scale_ap is None:
            S.op("act", lambda e: e.copy(out=dst_ap, in_=pt[0:ncols, 0:128]), reads=[pt_r], writes=[dst_r])
        else:
            S.op("dve", lambda e: e.tensor_scalar(out=dst_ap, in0=pt[0:ncols, 0:128], scalar1=scale_ap, scalar2=None,
                                                  op0=ALU.mult), reads=[pt_r, scale_r], writes=[dst_r])

    def win_mixer(self, i):
        S = self.S
        Wq = self.win_w_qkv[0]
        Wo = self.win_w_out[0]
        TP, NT, L = self.TP, self.NT, self.L
        scale = 128 ** -0.5
        koff = 256 if self.sample else 0
        with contextlib.ExitStack() as ms:
            sb = lambda n, s, d: self.sb("wn_" + n, s, d, ms)
            kT = sb("kT", [128, koff + TP], BF16); kT_r = Res("wkT")
            V = sb("V", [128, (koff + TP) // 128, 128], BF16); V_r = Res("wV")
            qT = sb("qT", [128, TP], BF16); qT_r = Res("wqT")
            oT = sb("oT", [128, TP], BF16); oT_r = Res("woT")
            W = {"P": sb("P", [128, 640], BF16), "P_r": Res("wP"), "PT": sb("PT", [128, 640], BF16),
                 "PT_r": Res("wPT"), "st": sb("st", [128, 16], F32), "st_r": Res("wst"), "dv": 128,
                 "q_reads": [qT_r], "k_reads": [kT_r], "v_reads": [V_r]}
            o_tm = sb("o_tm", [128, 128], BF16); o_r = Res("wo_tm")
            nsink = sb("nsink", [128, 16], F32); nsink_r = Res("wnsink")
            W["nsink_r"] = nsink_r
            stage = sb("stage", [128, 512], F32); stage_r = Res("wstage")
            S.op("dve", lambda e: e.tensor_scalar(out=nsink[:], in0=self.cs("sink"), scalar1=-1.0, scalar2=None,
                                                  op0=ALU.mult), reads=[self.cst_r], writes=[nsink_r])
            if self.sample:
                cs_t = sb("cs", [128, 2, LS], F32); cs_r = Res("wcs")
                mask = sb("mask", [128, 384], F32); mask_r = Res("wmask")
                W["mask_r"] = mask_r
                S.dma("sp", cs_t[:], self.rope_in, writes=[cs_r])
                S.dma("sp", mask[:], self.wmask_in, writes=[mask_r])
            for g in range(4):
                if KSTOP < 1:
                    break
                wk, wk_r = self.load_cols(Wq, 2048 + g * 128)
                if self.sample:
                    S.dma("pool", kT[:, 0:256], self.kc2T_in[g * 128:(g + 1) * 128, :], writes=[kT_r])
                for t in range(self.TT if 'k' in KPART else 0):
                    ps, ps_r = self.psum()
                    self.proj_fm(wk, wk_r, t, ps, ps_r)
                    dst = kT[:, koff + t * 512: koff + (t + 1) * 512]
                    if self.sample:
                        self.rope_evac(ps, ps_r, dst, kT_r, t, cs_t, cs_r)
                    else:
                        S.op("act", lambda e, ps=ps: e.copy(out=stage[:], in_=ps[:]), reads=[ps_r], writes=[stage_r])
                        S.op("dve", lambda e, dst=dst: e.tensor_copy(out=dst, in_=stage[:]),
                             reads=[stage_r], writes=[kT_r])
                        if 's' not in KSKIP:
                            S.dma("sp", self.k2T_out[g * 128:(g + 1) * 128, t * 512:(t + 1) * 512], stage[:],
                                  reads=[stage_r], store=True)
                wv, wv_r = self.load_cols(Wq, 2560 + g * 128)
                if self.sample:
                    S.dma("pool", V[:, 0:2, :],
                          self.vc2_in[:, g * 128:(g + 1) * 128].rearrange("(n p) d -> p n d", p=128), writes=[V_r])
                for nt in range(NT if 'v' in KPART else 0):
                    ps, ps_r = self.psum()
                    self.proj_tm(wv, wv_r, nt, ps, ps_r, 128)
                    if self.sample:
                        S.op("act", lambda e, ps=ps, nt=nt: e.copy(out=V[:, koff // 128 + nt, :], in_=ps[:, 0:128]),
                             reads=[ps_r], writes=[V_r])
                    else:
                        S.op("act", lambda e, ps=ps: e.copy(out=stage[:, 0:128], in_=ps[:, 0:128]),
                             reads=[ps_r], writes=[stage_r])
                        S.op("dve", lambda e, nt=nt: e.tensor_copy(out=V[:, koff // 128 + nt, :], in_=stage[:, 0:128]),
                             reads=[stage_r], writes=[V_r])
                        S.dma("sp", self.v2_out[nt * 128:(nt + 1) * 128, g * 128:(g + 1) * 128], stage[:, 0:128],
                              reads=[stage_r], store=True)
                for r in range(4):
                    if KSTOP < 2:
                        break
                    hd = g * 4 + r
                    wq, wq_r = self.load_cols(Wq, hd * 128)
                    W["nsink_ap"] = nsink[:, hd:hd + 1]
                    for t in range(self.TT):
                        ps, ps_r = self.psum()
                        self.proj_fm(wq, wq_r, t, ps, ps_r)
                        dst = qT[:, t * 512:(t + 1) * 512]
                        if self.sample:
                            self.rope_evac(ps, ps_r, dst, qT_r, t, cs_t, cs_r)
                        else:
                            S.op("act", lambda e, ps=ps, dst=dst: e.copy(out=dst, in_=ps[:]),
                                 reads=[ps_r], writes=[qT_r])
                    for s in range(self.nseq):
                        if KSTOP < 3:
                            break
                        for qb in range(L // 128):
                            q0 = s * L + qb * 128
                            if self.sample:
                                b_lo, b_hi = max(qb - 1, 0), min(qb + 1, L // 128 - 1)
                                nb = (b_hi - b_lo + 1) * 128
                                m0 = (b_lo - (qb - 1)) * 128
                                kparts = [(kT[:, 0:256], 256, None),
                                          (kT[:, 256 + b_lo * 128: 256 + b_lo * 128 + nb], nb, mask[:, m0:m0 + nb])]
                                vparts = [V[:, 0, :], V[:, 1, :]] + [V[:, 2 + b, :] for b in range(b_lo, b_hi + 1)]
                            else:
                                kparts = [(kT[:, s * L:(s + 1) * L], L, None)]
                                vparts = [V[:, s * (L // 128) + b, :] for b in range(L // 128)]
                            po, po_r = self.attn_tile(qT[:, q0:q0 + 128], kparts, vparts, scale,
                                                      self.cs("sink", hd, hd + 1), W)
                            S.op("dve", lambda e, po=po: e.tensor_scalar(
                                out=o_tm[:], in0=po[:, 0:128], scalar1=W["st"][:, 6:7], scalar2=None, op0=ALU.mult),
                                reads=[po_r, W["st_r"]], writes=[o_r])
                            self.transpose_to(o_tm[:], [o_r], oT[:, q0:q0 + 128], oT_r)
                    if KSTOP < 4:
                        continue
                    wo, wo_r = self.load_rows(Wo, hd * 128)
                    for oc in range(DC):
                        for t in range(self.TT):
                            po, po_r = self.psum()
                            S.op("pe", lambda e, po=po, oc=oc, t=t: e.matmul(
                                po[:], wo[:, oc * 128:(oc + 1) * 128], oT[:, t * 512:(t + 1) * 512],
                                start=True, stop=True), reads=[wo_r, oT_r], writes=[po_r])
                            self.x_accum(po, po_r, self.modcol(i, 5, oc), self.mods_r, oc, t)
            S.barrier()

    def diff_mixer(self, i):
        S = self.S
        Wq = self.diff_w_qkv[0]
        Wo = self.diff_w_out[0]
        TP, NT, L = self.TP, self.NT, self.L
        scale = 128 ** -0.5
        lambda_init = 0.8 - 0.6 * math.exp(-0.3 * i)
        koff = 256 if self.sample else 0
        NK = koff + L
        with contextlib.ExitStack() as ms:
            sb = lambda n, s, d: self.sb("df_" + n, s, d, ms)
            kT = sb("kT", [128, 2, koff + TP], BF16); kT_r = Res("dkT")
            V = sb("V", [128, (koff + TP) // 128, 256], BF16); V_r = Res("dV")
            qT = sb("qT", [128, 2, TP], BF16); qT_r = Res("dqT")
            oT = sb("oT", [128, 2, TP], BF16); oT_r = Res("doT")
            Ws = []
            for m in range(2):
                Ws.append({"P": sb(f"P{m}", [128, NK], BF16), "P_r": Res(f"dP{m}"),
                           "PT": sb(f"PT{m}", [128, NK], BF16), "PT_r": Res(f"dPT{m}"),
                           "st": sb(f"st{m}", [128, 16], F32), "st_r": Res(f"dst{m}"), "dv": 256,
                           "q_reads": [qT_r], "k_reads": [kT_r], "v_reads": [V_r]})
            o32 = sb("o32", [128, 256], F32); o32_r = Res("do32")
            o_tm = sb("o_tm", [128, 256], BF16); o_r = Res("do_tm")
            lam = sb("lam", [128, 8], F32); lam_r = Res("dlam")
            junk = sb("junk", [128, 256], F32); junk_r = Res("djunk")
            stage = sb("stage", [128, 512], F32); stage_r = Res("dstage")
            lv = self.cs("lam")
            S.op("dve", lambda e: e.tensor_tensor(out=junk[:, 0:128], in0=lv[:, 0:128], in1=lv[:, 128:256], op=ALU.mult),
                 reads=[self.cst_r], writes=[junk_r])
            S.op("dve", lambda e: e.reduce_sum(out=lam[:, 0:1], in_=junk[:, 0:128], axis=AX.X),
                 reads=[junk_r], writes=[lam_r])
            S.op("dve", lambda e: e.tensor_tensor(out=junk[:, 128:256], in0=lv[:, 256:384], in1=lv[:, 384:512],
                                                  op=ALU.mult), reads=[self.cst_r], writes=[junk_r])
            S.op("dve", lambda e: e.reduce_sum(out=lam[:, 1:2], in_=junk[:, 128:256], axis=AX.X),
                 reads=[junk_r], writes=[lam_r])
            S.op("act", lambda e: e.activation(out=lam[:, 2:4], in_=lam[:, 0:2], func=AF.Exp),
                 reads=[lam_r], writes=[lam_r])
            S.op("dve", lambda e: e.tensor_tensor(out=lam[:, 4:5], in0=lam[:, 3:4], in1=lam[:, 2:3], op=ALU.subtract),
                 reads=[lam_r], writes=[lam_r])
            S.op("dve", lambda e: e.tensor_scalar(out=lam[:, 4:5], in0=lam[:, 4:5], scalar1=-lambda_init, scalar2=None,
                                                  op0=ALU.add), reads=[lam_r], writes=[lam_r])
            if self.sample:
                cs_t = sb("cs", [128, 2, LS], F32); cs_r = Res("dcs")
                S.dma("sp", cs_t[:], self.rope_in, writes=[cs_r])
            for hd in range(8):
                for m in range(2):
                    col = m * 1024 + hd * 128
                    wk, wk_r = self.load_cols(Wq, 2048 + col)
                    if self.sample:
                        S.dma("pool", kT[:, m, 0:256], self.kc1T_in[col:col + 128, :], writes=[kT_r])
                    for t in range(self.TT):
                        ps, ps_r = self.psum()
                        self.proj_fm(wk, wk_r, t, ps, ps_r)
                        dst = kT[:, m, koff + t * 512: koff + (t + 1) * 512]
                        if self.sample:
                            self.rope_evac(ps, ps_r, dst, kT_r, t, cs_t, cs_r)
                        else:
                            S.op("act", lambda e, ps=ps: e.copy(out=stage[:], in_=ps[:]),
                                 reads=[ps_r], writes=[stage_r])
                            S.op("dve", lambda e, dst=dst: e.tensor_copy(out=dst, in_=stage[:]),
                                 reads=[stage_r], writes=[kT_r])
                            S.dma("sp", self.k1T_out[col:col + 128, t * 512:(t + 1) * 512], stage[:],
                                  reads=[stage_r], store=True)
                    wq, wq_r = self.load_cols(Wq, col)
                    for t in range(self.TT):
                        ps, ps_r = self.psum()
                        self.proj_fm(wq, wq_r, t, ps, ps_r)
                        dst = qT[:, m, t * 512:(t + 1) * 512]
                        if self.sample:
                            self.rope_evac(ps, ps_r, dst, qT_r, t, cs_t, cs_r)
                        else:
                            S.op("act", lambda e, ps=ps, dst=dst: e.copy(out=dst, in_=ps[:]),
                                 reads=[ps_r], writes=[qT_r])
                wv, wv_r = self.load_cols(Wq, 4096 + hd * 256, ncols=256)
                if self.sample:
                    S.dma("pool", V[:, 0:2, :],
                          self.vc1_in[:, hd * 256:(hd + 1) * 256].rearrange("(n p) d -> p n d", p=128), writes=[V_r])
                for nt in range(NT):
                    ps, ps_r = self.psum()
                    self.proj_tm(wv, wv_r, nt, ps, ps_r, 256)
                    if self.sample:
                        S.op("act", lambda e, ps=ps, nt=nt: e.copy(out=V[:, koff // 128 + nt, :], in_=ps[:, 0:256]),
                             reads=[ps_r], writes=[V_r])
                    else:
                        S.op("act", lambda e, ps=ps: e.copy(out=stage[:, 0:256], in_=ps[:, 0:256]),
                             reads=[ps_r], writes=[stage_r])
                        S.op("dve", lambda e, nt=nt: e.tensor_copy(out=V[:, koff // 128 + nt, :], in_=stage[:, 0:256]),
                             reads=[stage_r], writes=[V_r])
                        S.dma("sp", self.v1_out[nt * 128:(nt + 1) * 128, hd * 256:(hd + 1) * 256], stage[:, 0:256],
                              reads=[stage_r], store=True)
                for s in range(self.nseq):
                    k0 = s * L if not self.sample else 0
                    vparts = [V[:, (k0 // 128) + b, :] for b in range(NK // 128)]
                    for qb in range(L // 128):
                        q0 = s * L + qb * 128
                        pos = []
                        for m in range(2):
                            kparts = []
                            c = 0
                            while c < NK:
                                n = min(512, NK - c)
                                kparts.append((kT[:, m, k0 + c:k0 + c + n], n, None))
                                c += n
                            pos.append(self.attn_tile(qT[:, m, q0:q0 + 128], kparts, vparts, scale, None, Ws[m]))
                        st0, st1 = Ws[0]["st"], Ws[1]["st"]
                        S.op("dve", lambda e: e.tensor_tensor(out=st1[:, 7:8], in0=st1[:, 6:7], in1=lam[:, 4:5],
                                                              op=ALU.mult),
                             reads=[Ws[1]["st_r"], lam_r], writes=[Ws[1]["st_r"]])
                        (p0, p0_r), (p1, p1_r) = pos
                        S.op("dve", lambda e, p0=p0: e.tensor_scalar(out=o32[:], in0=p0[:, 0:256], scalar1=st0[:, 6:7],
                                                                     scalar2=None, op0=ALU.mult),
                             reads=[p0_r, Ws[0]["st_r"]], writes=[o32_r])
                        S.op("dve", lambda e, p1=p1: e.scalar_tensor_tensor(
                            out=o32[:], in0=p1[:, 0:256], scalar=st1[:, 7:8], in1=o32[:], op0=ALU.mult, op1=ALU.add),
                            reads=[p1_r, Ws[1]["st_r"], o32_r], writes=[o32_r])
                        S.op("dve", lambda e: e.memset(st0[:, 12:13], 0.0), writes=[Ws[0]["st_r"]])
                        S.op("act", lambda e: e.activation(out=junk[:], in_=o32[:], func=AF.Square,
                                                           accum_out=st0[:, 12:13]),
                             reads=[o32_r, Ws[0]["st_r"]], writes=[junk_r, Ws[0]["st_r"]])
                        S.op("dve", lambda e: e.tensor_scalar(out=st0[:, 13:14], in0=st0[:, 12:13], scalar1=1.0 / 256,
                                                              scalar2=EPS, op0=ALU.mult, op1=ALU.add),
                             reads=[Ws[0]["st_r"]], writes=[Ws[0]["st_r"]])
                        S.op("act", lambda e: e.sqrt(out=st0[:, 13:14], in_=st0[:, 13:14]),
                             reads=[Ws[0]["st_r"]], writes=[Ws[0]["st_r"]])
                        S.op("dve", lambda e: e.reciprocal(out=st0[:, 14:15], in_=st0[:, 13:14]),
                             reads=[Ws[0]["st_r"]], writes=[Ws[0]["st_r"]])
                        S.op("dve", lambda e: e.tensor_scalar(out=st0[:, 14:15], in0=st0[:, 14:15],
                                                              scalar1=1.0 - lambda_init, scalar2=None, op0=ALU.mult),
                             reads=[Ws[0]["st_r"]], writes=[Ws[0]["st_r"]])
                        S.op("dve", lambda e: e.scalar_tensor_tensor(
                            out=o_tm[:], in0=o32[:], scalar=st0[:, 14:15], in1=self.cs("subln"),
                            op0=ALU.mult, op1=ALU.mult), reads=[o32_r, Ws[0]["st_r"], self.cst_r], writes=[o_r])
                        for eh in range(2):
                            self.transpose_to(o_tm[:, eh * 128:(eh + 1) * 128], [o_r], oT[:, eh, q0:q0 + 128], oT_r)
                wos = [self.load_rows(Wo, hd * 256 + eh * 128) for eh in range(2)]
                for oc in range(DC):
                    for t in range(self.TT):
                        po, po_r = self.psum()
                        for eh in range(2):
                            wo, wo_r = wos[eh]
                            S.op("pe", lambda e, po=po, oc=oc, t=t, eh=eh, wo=wo: e.matmul(
                                po[:], wo[:, oc * 128:(oc + 1) * 128], oT[:, eh, t * 512:(t + 1) * 512],
                                start=(eh == 0), stop=(eh == 1)), reads=[wo_r, oT_r], writes=[po_r], inc=(eh == 1))
                        self.x_accum(po, po_r, self.modcol(i, 5, oc), self.mods_r, oc, t)
            S.barrier()

    def ssm_mixer(self, i):
        S = self.S
        j = i // 3
        Win = self.ssm_w_in[j]
        Wout = self.ssm_w_out[j]
        TP, NT, L, nseq = self.TP, self.NT, self.L, self.nseq
        nch = L // 128
        U = {0: self.cs("Ule"), 1: self.cs("Uge")}
        SLU = {0: self.cs("SL"), 1: self.cs("SU")}
        with contextlib.ExitStack() as ms:
            sb = lambda n, s, d: self.sb("ss_" + n, s, d, ms)
            ssq = sb("ssq", [128, NT, NG], F32); ssq_r = Res("sssq")
            S.op("dve", lambda e: e.memset(ssq[:].rearrange("p a b -> p (a b)"), 0.0), writes=[ssq_r])
            with contextlib.ExitStack() as gs:
                gb = lambda n, s, d: self.sb("sg_" + n, s, d, gs)
                z_tm = gb("z", [128, NT, 512], BF16); z_r = Res("sz")
                yf = gb("yf", [128, NT, 512], BF16); yf_r = Res("syf")
                xc = gb("xc", [128, 6, TP], BF16); xc_r = Res("sxc")
                pre = gb("pre", [128, TP], F32); pre_r = Res("spre")
                acc = self.rstd; acc_r = Res("sacc")
                dt = gb("dt", [128, NT, 16], F32); dt_r = Res("sdt")
                dtA = gb("dtA", [128, NT, 16], F32); dtA_r = Res("sdtA")
                eac = gb("eac", [128, NT, 16], F32); eac_r = Res("seac")
                cd = gb("cd", [128, NT, 16], F32); cd_r = Res("scd")
                abc = gb("abc", [128, 2, 128], F32); abc_r = Res("sabc")
                xdt = gb("xdt", [128, 512], BF16); xdt_r = Res("sxdt")
                xdtd = gb("xdtd", [128, 512], BF16); xdtd_r = Res("sxdtd")
                B_tm = gb("B_tm", [128, 128], BF16); Btm_r = Res("sBtm")
                mCB = gb("mCB", [128, 128], F32); mCB_r = Res("smCB")
                MT = gb("MT", [128, 8, 128], BF16); MT_r = Res("sMT")
                hT = gb("hT", [128, 512], F32); hT_r = Res("shT")
                hTb = gb("hTb", [128, 512], BF16); hTb_r = Res("shTb")
                y32 = gb("y32", [128, 512], F32); y32_r = Res("sy32")
                yz = gb("yz", [128, 512], BF16); yz_r = Res("syz")
                yzT_t = gb("yzT", [128, 4, 128], BF16); yzT = yzT_t[:]; yzT_r = Res("syzT")
                junk = gb("junk", [128, 512], BF16); junk_r = Res("sjunk")
                off, _ = self.lay["a_log"]
                S.op("act", lambda e: e.activation(out=abc[:, 0, :], in_=self.cst[:, off + j * 128: off + (j + 1) * 128],
                                                   func=AF.Exp), reads=[self.cst_r], writes=[abc_r])
                S.op("dve", lambda e: e.tensor_scalar(out=abc[:, 0, :], in0=abc[:, 0, :], scalar1=-1.0, scalar2=None,
                                                      op0=ALU.mult), reads=[abc_r], writes=[abc_r])
                offb, _ = self.lay["dt_bias"]
                offd, _ = self.lay["ssm_d"]
                offcw, _ = self.lay["conv_w"]
                offcb, _ = self.lay["conv_b"]
                offng, _ = self.lay["ssm_norm_g"]
                for g in range(NG):
                    for d_ in range(2):
                        wd, wd_r = self.load_cols(Win, DI + 6144 + d_ * 64 + g * 8, ncols=8)
                        for nt in range(NT):
                            ps, ps_r = self.psum()
                            self.proj_tm(wd, wd_r, nt, ps, ps_r, 8)
                            hsl = slice(d_ * 64 + g * 8, d_ * 64 + g * 8 + 8)
                            bsl = slice(offb + j * 128 + d_ * 64 + g * 8, offb + j * 128 + d_ * 64 + g * 8 + 8)
                            sc, sc_r = self.scratch32()
                            S.op("dve", lambda e, sc=sc, ps=ps, bsl=bsl: e.tensor_tensor(
                                out=sc[:, 0:8], in0=ps[:, 0:8], in1=self.cst[:, bsl], op=ALU.add),
                                reads=[ps_r, self.cst_r], writes=[sc_r])
                            S.op("act", lambda e, sc=sc: e.activation(out=sc[:, 0:8], in_=sc[:, 0:8], func=AF.Exp),
                                 reads=[sc_r], writes=[sc_r])
                            S.op("dve", lambda e, sc=sc: e.tensor_scalar(out=sc[:, 0:8], in0=sc[:, 0:8], scalar1=1.0,
                                                                         scalar2=None, op0=ALU.add),
                                 reads=[sc_r], writes=[sc_r])
                            S.op("act", lambda e, sc=sc, nt=nt, d_=d_: e.activation(
                                out=dt[:, nt, d_ * 8:(d_ + 1) * 8], in_=sc[:, 0:8], func=AF.Ln),
                                reads=[sc_r], writes=[dt_r])
                            S.op("dve", lambda e, nt=nt, d_=d_, hsl=hsl: e.tensor_tensor(
                                out=dtA[:, nt, d_ * 8:(d_ + 1) * 8], in0=dt[:, nt, d_ * 8:(d_ + 1) * 8],
                                in1=abc[:, 0, hsl], op=ALU.mult), reads=[dt_r, abc_r], writes=[dtA_r])
                    for nt in range(NT):
                        ps, ps_r = self.psum()
                        S.op("pe", lambda e, ps=ps, nt=nt: e.matmul(ps[:, 0:8], U[0], dtA[:, nt, 0:8],
                                                                    start=True, stop=True),
                             reads=[self.cst_r, dtA_r], writes=[ps_r])
                        S.op("pe", lambda e, ps=ps, nt=nt: e.matmul(ps[:, 8:16], U[1], dtA[:, nt, 8:16],
                                                                    start=True, stop=True),
                             reads=[self.cst_r, dtA_r], writes=[ps_r])
                        S.op("pe", lambda e, ps=ps, nt=nt: e.matmul(ps[:, 16:32], self.cs("ones"), dtA[:, nt, :],
                                                                    start=True, stop=True),
                             reads=[self.cst_r, dtA_r], writes=[ps_r])
                        S.op("act", lambda e, ps=ps, nt=nt: e.activation(out=eac[:, nt, :], in_=ps[:, 0:16], func=AF.Exp),
                             reads=[ps_r], writes=[eac_r])
                        S.op("act", lambda e, ps=ps, nt=nt: e.activation(out=cd[:, nt, :], in_=ps[:, 16:32], func=AF.Exp),
                             reads=[ps_r], writes=[cd_r])
                    cols = [DI + g * 512 + q * 128 for q in range(4)] + [DI + DI + g * 128, DI + DI + 1024 + g * 128]
                    for ci, col in enumerate(cols):
                        cc = (col - DI) // 128
                        wv, wv_r = self.load_cols(Win, col)
                        for t in range(self.TT):
                            ps, ps_r = self.psum()
                            self.proj_fm(wv, wv_r, t, ps, ps_r)
                            S.op("act", lambda e, ps=ps, t=t: e.copy(out=pre[:, t * 512:(t + 1) * 512], in_=ps[:]),
                                 reads=[ps_r], writes=[pre_r])
                        w0 = self.cst[:, offcw + (j * 3 + 0) * 48 + cc: offcw + (j * 3 + 0) * 48 + cc + 1]
                        w1 = self.cst[:, offcw + (j * 3 + 1) * 48 + cc: offcw + (j * 3 + 1) * 48 + cc + 1]
                        w2 = self.cst[:, offcw + (j * 3 + 2) * 48 + cc: offcw + (j * 3 + 2) * 48 + cc + 1]
                        cb = self.cst[:, offcb + j * 48 + cc: offcb + j * 48 + cc + 1]
                        S.op("dve", lambda e, w1=w1, cb=cb: e.tensor_scalar(
                            out=acc[:, 0:TP], in0=pre[:], scalar1=w1, scalar2=cb, op0=ALU.mult, op1=ALU.add),
                            reads=[pre_r, self.cst_r], writes=[acc_r])
                        for s in range(nseq):
                            a0, a1 = s * L, (s + 1) * L
                            S.op("dve", lambda e, w0=w0, a0=a0, a1=a1: e.scalar_tensor_tensor(
                                out=acc[:, a0 + 1:a1], in0=pre[:, a0:a1 - 1], scalar=w0, in1=acc[:, a0 + 1:a1],
                                op0=ALU.mult, op1=ALU.add), reads=[pre_r, acc_r, self.cst_r], writes=[acc_r])
                            S.op("dve", lambda e, w2=w2, a0=a0, a1=a1: e.scalar_tensor_tensor(
                                out=acc[:, a0:a1 - 1], in0=pre[:, a0 + 1:a1], scalar=w2, in1=acc[:, a0:a1 - 1],
                                op0=ALU.mult, op1=ALU.add), reads=[pre_r, acc_r, self.cst_r], writes=[acc_r])
                        S.op("act", lambda e, ci=ci: e.activation(out=xc[:, ci, :], in_=acc[:, 0:TP], func=AF.Silu),
                             reads=[acc_r], writes=[xc_r])
                    for half in range(2):
                        wz, wz_r = self.load_cols(Win, g * 512 + half * 256, ncols=256)
                        for nt in range(NT):
                            ps, ps_r = self.psum()
                            self.proj_tm(wz, wz_r, nt, ps, ps_r, 256)
                            S.op("act", lambda e, ps=ps, nt=nt, half=half: e.activation(
                                out=z_tm[:, nt, half * 256:(half + 1) * 256], in_=ps[:, 0:256], func=AF.Silu),
                                reads=[ps_r], writes=[z_r])
                    for s in range(nseq):
                        for d_ in range(2):
                            if self.sample:
                                S.dma("sp", hT[:], self.st_in[j, d_, :, g * 512:(g + 1) * 512], writes=[hT_r])
                            else:
                                S.op("dve", lambda e: e.memset(hT[:], 0.0), writes=[hT_r])
                            S.op("act", lambda e: e.copy(out=hTb[:], in_=hT[:]), reads=[hT_r], writes=[hTb_r])
                            order = range(nch) if d_ == 0 else range(nch - 1, -1, -1)
                            for c in order:
                                nt = s * nch + c
                                tk = slice(nt * 128, (nt + 1) * 128)
                                dsl = slice(d_ * 8, d_ * 8 + 8)
                                px, px_r = self.psum()
                                for q in range(4):
                                    S.op("pe", lambda e, q=q, px=px: e.matmul(
                                        px[:, q * 128:(q + 1) * 128], xc[:, q, tk], self.ident_bf[:],
                                        start=True, stop=True), reads=[xc_r, self.cbf_r], writes=[px_r], inc=(q == 3))
                                S.op("dve", lambda e, px=px: e.tensor_tensor(
                                    out=xdt[:].rearrange("p (h q) -> p h q", h=8),
                                    in0=px[:].rearrange("p (h q) -> p h q", h=8),
                                    in1=dt[:, nt, dsl].unsqueeze(2).to_broadcast([128, 8, 64]), op=ALU.mult),
                                    reads=[px_r, dt_r], writes=[xdt_r])
                                if d_ == 0:
                                    dsk = self.cst[:, offd + j * 64 + g * 8: offd + j * 64 + g * 8 + 8]
                                    S.op("dve", lambda e, px=px, dsk=dsk: e.tensor_tensor(
                                        out=y32[:].rearrange("p (h q) -> p h q", h=8),
                                        in0=px[:].rearrange("p (h q) -> p h q", h=8),
                                        in1=dsk.unsqueeze(2).to_broadcast([128, 8, 64]), op=ALU.mult),
                                        reads=[px_r, self.cst_r], writes=[y32_r])
                                else:
                                    S.op("act", lambda e: e.copy(out=y32[:], in_=yf[:, nt, :]),
                                         reads=[yf_r], writes=[y32_r])
                                pb, pb_r = self.psum()
                                S.op("pe", lambda e, pb=pb: e.matmul(pb[:, 0:128], xc[:, 4, tk], self.ident_bf[:],
                                                                     start=True, stop=True),
                                     reads=[xc_r, self.cbf_r], writes=[pb_r])
                                S.op("act", lambda e, pb=pb: e.copy(out=B_tm[:], in_=pb[:, 0:128]),
                                     reads=[pb_r], writes=[Btm_r])
                                pc, pc_r = self.psum()
                                S.op("pe", lambda e, pc=pc: e.matmul(pc[:, 0:128], xc[:, 4, tk], xc[:, 5, tk],
                                                                     start=True, stop=True),
                                     reads=[xc_r], writes=[pc_r])
                                S.op("dve", lambda e, pc=pc: e.tensor_tensor(out=mCB[:], in0=pc[:, 0:128], in1=U[d_],
                                                                             op=ALU.mult),
                                     reads=[pc_r, self.cst_r], writes=[mCB_r])
                                iend = 127 if d_ == 0 else 0
                                for quad in range(2):
                                    hs = slice(d_ * 8 + quad * 4, d_ * 8 + quad * 4 + 4)
                                    rseg, rseg_r = self.scratch32()
                                    Lt, Lt_r = self.scratch32()
                                    S.op("dve", lambda e, hs=hs, rseg=rseg: e.tensor_tensor(
                                        out=rseg[:].rearrange("p (h q) -> p h q", h=4),
                                        in0=U[d_].unsqueeze(1).to_broadcast([128, 4, 128]),
                                        in1=dtA[:, nt, hs].unsqueeze(2).to_broadcast([128, 4, 128]), op=ALU.mult),
                                        reads=[self.cst_r, dtA_r], writes=[rseg_r])
                                    pl, pl_r = self.psum()
                                    S.op("pe", lambda e, pl=pl, rseg=rseg: e.matmul(pl[:], SLU[d_], rseg[:], start=True, stop=True),
                                         reads=[self.cst_r, rseg_r], writes=[pl_r])
                                    S.op("act", lambda e, pl=pl, Lt=Lt: e.activation(out=Lt[:], in_=pl[:], func=AF.Exp),
                                         reads=[pl_r], writes=[Lt_r])
                                    S.op("dve", lambda e, quad=quad, Lt=Lt: e.tensor_tensor(
                                        out=MT[:, quad * 4:(quad + 1) * 4, :],
                                        in0=Lt[:].rearrange("p (h q) -> p h q", h=4),
                                        in1=mCB[:].unsqueeze(1).to_broadcast([128, 4, 128]), op=ALU.mult),
                                        reads=[Lt_r, mCB_r], writes=[MT_r])
                                    S.op("dve", lambda e, quad=quad, Lt=Lt: e.tensor_tensor(
                                        out=xdtd[:, quad * 256:(quad + 1) * 256].rearrange("p (h q) -> p h q", h=4),
                                        in0=xdt[:, quad * 256:(quad + 1) * 256].rearrange("p (h q) -> p h q", h=4),
                                        in1=Lt[:].rearrange("p (h q) -> p h q", h=4)[:, :, iend:iend + 1]
                                        .to_broadcast([128, 4, 64]), op=ALU.mult),
                                        reads=[xdt_r, Lt_r], writes=[xdtd_r])
                                py, py_r = self.psum()
                                for hh in range(8):
                                    S.op("pe", lambda e, hh=hh, py=py: e.matmul(
                                        py[:, hh * 64:(hh + 1) * 64], MT[:, hh, :], xdt[:, hh * 64:(hh + 1) * 64],
                                        start=True, stop=True), reads=[MT_r, xdt_r], writes=[py_r], inc=(hh == 7))
                                pf, pf_r = self.psum()
                                S.op("pe", lambda e, pf=pf: e.matmul(pf[:], xc[:, 5, tk], hTb[:], start=True, stop=True),
                                     reads=[xc_r, hTb_r], writes=[pf_r])
                                S.op("dve", lambda e, py=py: e.tensor_tensor(out=y32[:], in0=py[:], in1=y32[:], op=ALU.add),
                                     reads=[py_r, y32_r], writes=[y32_r])
                                sc, sc_r = self.scratch32()
                                S.op("dve", lambda e, pf=pf, sc=sc: e.tensor_tensor(
                                    out=sc[:].rearrange("p (h q) -> p h q", h=8),
                                    in0=pf[:].rearrange("p (h q) -> p h q", h=8),
                                    in1=eac[:, nt, dsl].unsqueeze(2).to_broadcast([128, 8, 64]), op=ALU.mult),
                                    reads=[pf_r, eac_r], writes=[sc_r])
                                if d_ == 0:
                                    S.op("dve", lambda e, sc=sc: e.tensor_tensor(out=yf[:, nt, :], in0=sc[:], in1=y32[:],
                                                                                op=ALU.add),
                                         reads=[sc_r, y32_r], writes=[yf_r])
                                else:
                                    S.op("dve", lambda e, sc=sc: e.tensor_tensor(out=y32[:], in0=sc[:], in1=y32[:],
                                                                                op=ALU.add),
                                         reads=[sc_r, y32_r], writes=[y32_r])
                                    S.op("dve", lambda e: e.tensor_tensor(out=yz[:], in0=y32[:], in1=z_tm[:, nt, :],
                                                                          op=ALU.mult),
                                         reads=[y32_r, z_r], writes=[yz_r])
                                    S.op("act", lambda e: e.activation(out=junk[:, 0:512], in_=yz[:], func=AF.Square,
                                                                       accum_out=ssq[:, nt, g:g + 1]),
                                         reads=[yz_r, ssq_r], writes=[junk_r, ssq_r])
                                    for q in range(4):
                                        kc = g * 4 + q
                                        self.transpose_to(yz[:, q * 128:(q + 1) * 128], [yz_r], yzT[:, q, :], yzT_r,
                                                          scale_ap=self.cst[:, offng + j * 32 + kc: offng + j * 32 + kc + 1],
                                                          scale_r=self.cst_r)
                                    S.dma("sp", self.yz_scr[g * 4:(g + 1) * 4, :, nt * 128:(nt + 1) * 128]
                                          .rearrange("q p t -> p q t"), yzT, reads=[yzT_r], store=True)
                                pS, pS_r = self.psum()
                                S.op("pe", lambda e, pS=pS: e.matmul(pS[:], B_tm[:], xdtd[:], start=True, stop=True),
                                     reads=[Btm_r, xdtd_r], writes=[pS_r])
                                S.op("dve", lambda e: e.tensor_tensor(
                                    out=hT[:].rearrange("p (h q) -> p h q", h=8),
                                    in0=hT[:].rearrange("p (h q) -> p h q", h=8),
                                    in1=cd[:, nt, dsl].unsqueeze(2).to_broadcast([128, 8, 64]), op=ALU.mult),
                                    reads=[hT_r, cd_r], writes=[hT_r])
                                S.op("dve", lambda e, pS=pS: e.tensor_tensor(out=hT[:], in0=hT[:], in1=pS[:], op=ALU.add),
                                     reads=[hT_r, pS_r], writes=[hT_r])
                                S.op("act", lambda e: e.copy(out=hTb[:], in_=hT[:]), reads=[hT_r], writes=[hTb_r])
                            if not self.sample:
                                S.dma("sp", self.st_out[j, d_, s, :, g * 512:(g + 1) * 512], hT[:],
                                      reads=[hT_r], store=True)
                S.barrier()
            with contextlib.ExitStack() as os_:
                ob = lambda n, s, d: self.sb("so_" + n, s, d, os_)
                yzt = ob("yzt", [128, 32, 512], BF16); yzt_r = Res("syzt")
                rs = ob("rs", [128, NT], F32); rs_r = Res("srs")
                dg = ob("dg", [128, 128], F32); dg_r = Res("sdg")
                S._wait(S.engs["sp"], list(S.store_toks.values()))
                S.op("dve", lambda e: e.reduce_sum(out=rs[:], in_=ssq[:], axis=AX.X), reads=[ssq_r], writes=[rs_r])
                S.op("dve", lambda e: e.tensor_scalar(out=rs[:], in0=rs[:], scalar1=1.0 / DI, scalar2=EPS,
                                                      op0=ALU.mult, op1=ALU.add), reads=[rs_r], writes=[rs_r])
                S.op("act", lambda e: e.sqrt(out=rs[:], in_=rs[:]), reads=[rs_r], writes=[rs_r])
                S.op("dve", lambda e: e.reciprocal(out=rs[:], in_=rs[:]), reads=[rs_r], writes=[rs_r])
                for nt in range(NT):
                    S.op("dve", lambda e, nt=nt: e.tensor_scalar(out=dg[:], in0=self.cs("ident"), scalar1=rs[:, nt:nt + 1],
                                                                 scalar2=None, op0=ALU.mult),
                         reads=[self.cst_r, rs_r], writes=[dg_r])
                    ps, ps_r = self.psum()
                    S.op("pe", lambda e, ps=ps: e.matmul(ps[:, 0:128], self.cs("ones"), dg[:], start=True, stop=True),
                         reads=[self.cst_r, dg_r], writes=[ps_r])
                    S.op("act", lambda e, ps=ps, nt=nt: e.copy(out=self.rstd[:, nt * 128:(nt + 1) * 128], in_=ps[:, 0:128]),
                         reads=[ps_r], writes=[self.rstd_r[nt // 4]])
                for t in range(self.TT):
                    S.dma("sp", yzt[:], self.yz_scr[:, :, t * 512:(t + 1) * 512].rearrange("k p t -> p k t"),
                          writes=[yzt_r])
                    for oc in range(DC):
                        wo, wo_r = self.load_cols(Wout, oc * 128, rows=DI)
                        po, po_r = self.psum()
                        for kc in range(32):
                            S.op("pe", lambda e, kc=kc, po=po, wo=wo: e.matmul(po[:], wo[:, kc, :], yzt[:, kc, :],
                                                                               start=(kc == 0), stop=(kc == 31)),
                                 reads=[wo_r, yzt_r], writes=[po_r], inc=(kc == 31))
                        sc, sc_r = self.scratch32()
                        S.op("dve", lambda e, po=po, sc=sc, oc=oc, t=t: e.scalar_tensor_tensor(
                            out=sc[:], in0=po[:], scalar=self.modcol(i, 5, oc), in1=self.rstd[:, t * 512:(t + 1) * 512],
                            op0=ALU.mult, op1=ALU.mult), reads=[po_r, self.mods_r, self.rstd_r[t]], writes=[sc_r])
                        S.op("dve", lambda e, sc=sc, oc=oc, t=t: e.tensor_tensor(
                            out=self.x[:, oc, t * 512:(t + 1) * 512], in0=sc[:], in1=self.x[:, oc, t * 512:(t + 1) * 512],
                            op=ALU.add), reads=[sc_r, self.x_r[oc][t]], writes=[self.x_r[oc][t]])
                S.barrier()


WEIGHT_KEYS = ("w_ada", "ffn1_w_in", "ffn2_w_in", "ffn1_w_out", "ffn2_w_out", "ssm_w_in", "ssm_w_out",
               "diff_w_qkv", "diff_w_out", "win_w_qkv", "win_w_out")


def make_in_maps(inp, cores=range(N_CORES)):
    consts = pack_consts(inp)
    bada = fm(inp["b_ada"].reshape(-1))
    rope = rope_tables()
    wm = win_mask()
    maps = []
    for core in cores:
        b = core // 4
        xs = np.concatenate([inp["x_prompt"][2 * core], inp["x_prompt"][2 * core + 1], inp["x_sample"][b]], axis=0)
        cv = np.stack([fm(inp["c_ctx"]), fm(inp["c"][b])], axis=2).reshape(128, DC * 2)
        st = np.stack([np.stack([inp[f"state_l{l}_{d}"][b].reshape(DI, 128).T for d in ("fwd", "bwd")])
                       for l in (0, 3)])
        m = {"xT": np.ascontiguousarray(xs.T), "cvec": np.ascontiguousarray(cv), "consts": consts,
             "b_ada_fm": bada, "rope_cs": rope, "wmask": wm, "st_in": np.ascontiguousarray(st),
             "kc1T": np.ascontiguousarray(inp["cache_l1_k"][b].reshape(256, D).T),
             "vc1": np.ascontiguousarray(inp["cache_l1_v"][b].reshape(256, D)),
             "kc2T": np.ascontiguousarray(inp["cache_l2_k"][b].reshape(256, 512).T),
             "vc2": np.ascontiguousarray(inp["cache_l2_v"][b].reshape(256, 512))}
        for k in WEIGHT_KEYS:
            m[k] = inp[k]
        maps.append(m)
    return maps


def assemble(results):
    B, S_ = 16, 256
    y_prompt = np.zeros((B, S_, D), np.float32)
    y_sample = np.zeros((2, LS, D), np.float32)
    st = [np.zeros((B, 64, 64, 128), np.float32) for _ in range(4)]
    k1 = np.zeros((B, S_, 2, 8, 128), np.float32)
    v1 = np.zeros((B, S_, 8, 256), np.float32)
    k2 = np.zeros((B, S_, 4, 128), np.float32)
    v2 = np.zeros((B, S_, 4, 128), np.float32)
    for core, r in enumerate(results):
        y = r["yT"].T
        y_prompt[2 * core] = y[0:256]
        y_prompt[2 * core + 1] = y[256:512]
        if core % 4 == 0:
            y_sample[core // 4] = y[512:]
        so = r["st_out"]
        for l in range(2):
            for d in range(2):
                for s in range(2):
                    st[l * 2 + d][2 * core + s] = so[l, d, s].T.reshape(64, 64, 128)
        k1t = r["k1T"].T
        k2t = r["k2T"].T
        for s in range(2):
            k1[2 * core + s] = k1t[s * 256:(s + 1) * 256].reshape(256, 2, 8, 128)
            v1[2 * core + s] = r["v1"][s * 256:(s + 1) * 256].reshape(256, 8, 256)
            k2[2 * core + s] = k2t[s * 256:(s + 1) * 256].reshape(256, 4, 128)
            v2[2 * core + s] = r["v2"][s * 256:(s + 1) * 256].reshape(256, 4, 128)
    return (y_prompt, y_sample, st[0], st[1], k1, v1, k2, v2, st[2], st[3])


def kernel(**inp):
    inp = {k: np.asarray(v) for k, v in inp.items()}
    nc = Builder().build()
    maps = make_in_maps(inp)
    res = run_bass_kernel_spmd(nc, maps, core_ids=list(range(N_CORES)))
    return assemble(res.results)
```

```python
import contextlib
import math
import os
import numpy as np
import concourse.bass as bass
import concourse.mybir as mybir
from concourse.bass_utils import run_bass_kernel_spmd

F32 = mybir.dt.float32
BF16 = mybir.dt.bfloat16
AF = mybir.ActivationFunctionType
ALU = mybir.AluOpType
AX = mybir.AxisListType

D = 2048
DC = 16
NP_SEQ = 2
LP = 256
LS = 1024
T = NP_SEQ * LP + LS
NTT = T // 512
DEPTH = 4
D_FF = 5632
FC = D_FF // 128
N_MOD = 9
EPS = 1e-6
N_CORES = 8

SYNC_ENGS = set(os.environ.get('KSYNC', 'pe,act,dve,pool,sp').split(','))
KSTOP = int(os.environ.get('KSTOP', '9'))
KPART = os.environ.get('KPART', 'kv')
KSKIP = os.environ.get('KSKIP', '')


class Res:
    __slots__ = ("name", "w", "r", "dsem", "dval")

    def __init__(self, name):
        self.name = name
        self.w = None
        self.r = {}
        self.dsem = None
        self.dval = 0


class Eng:
    def __init__(self, name, handle, sem):
        self.name = name
        self.h = handle
        self.sem = sem
        self.count = 0
        self.waited = {}
        self.pend_r = []
        self.pend_w = []


class Sched:
    def __init__(self, nc, es):
        self.nc = nc
        self.es = es
        self.engs = {}
        for name, h in (("pe", nc.tensor), ("act", nc.scalar), ("dve", nc.vector),
                        ("pool", nc.gpsimd), ("sp", nc.sync)):
            sem = es.enter_context(nc.semaphore("sem_" + name))
            self.engs[name] = Eng(name, h, sem)
        self.sem_ids = {}
        self.store_toks = {}
        self.n_ops = 0

    def _wait(self, E, deps):
        for (sem, val) in deps:
            if sem is E.sem and E.name not in SYNC_ENGS:
                continue
            k = id(sem)
            if E.waited.get(k, 0) >= val:
                continue
            E.h.wait_ge(sem, val)
            E.waited[k] = val

    @staticmethod
    def _deps(reads, writes):
        deps = []
        for r in reads:
            if r.w is not None:
                deps.append(r.w)
        for w in writes:
            if w.w is not None:
                deps.append(w.w)
            for sem_k, (sem, val) in w.r.items():
                deps.append((sem, val))
        return deps

    def op(self, eng, fn, reads=(), writes=(), inc=True):
        E = self.engs[eng]
        self._wait(E, self._deps(reads, writes))
        ins = fn(E.h)
        self.n_ops += 1
        E.pend_r.extend(reads)
        E.pend_w.extend(writes)
        if inc:
            E.count += 1
            ins.then_inc(E.sem, 1)
            tok = (E.sem, E.count)
            for r in E.pend_r:
                r.r[id(E.sem)] = tok
            for w in E.pend_w:
                w.w = tok
                w.r = {}
            E.pend_r = []
            E.pend_w = []
        return ins

    def dma(self, eng, out, in_, reads=(), writes=(), store=False):
        E = self.engs[eng]
        self._wait(E, self._deps(reads, writes))
        res = (list(writes) + list(reads))[0]
        if res.dsem is None:
            if res.name not in self.sem_ids:
                self.sem_ids[res.name] = [self.es.enter_context(self.nc.semaphore("dsem_" + res.name)), 0]
            res.dsem, res.dval = self.sem_ids[res.name]
        res.dval += 16
        self.sem_ids[res.name][1] = res.dval
        E.h.dma_start(out=out, in_=in_).then_inc(res.dsem, 16)
        self.n_ops += 1
        tok = (res.dsem, res.dval)
        for r in reads:
            r.r[id(res.dsem)] = tok
        for w in writes:
            w.w = tok
            w.r = {}
        if store:
            self.store_toks[id(res.dsem)] = tok

    def barrier(self):
        toks = [(E.sem, E.count) for E in self.engs.values() if E.count > 0]
        toks += list(self.store_toks.values())
        for E in self.engs.values():
            assert not E.pend_r and not E.pend_w, "pending ops at barrier"
            self._wait(E, toks)

    def finish(self):
        E = self.engs["sp"]
        self._wait(E, list(self.store_toks.values()))
        toks = [(e.sem, e.count) for e in self.engs.values() if e.count > 0]
        self._wait(E, toks)


DI = 4096
NG = 8
N_SSM = 2
SSM_IN = 10368
QBL = 128
CONST_SPEC = (("norm_g", DEPTH * 3 * DC), ("final_g", DC), ("ident", 128), ("ones", 128),
              ("Ule", 128), ("Uge", 128), ("SL", 128), ("SU", 128), ("RT", 128),
              ("conv_w", N_SSM * 3 * 48), ("conv_b", N_SSM * 48), ("ssm_norm_g", N_SSM * 32),
              ("dt_bias", N_SSM * 128), ("a_log", N_SSM * 128), ("ssm_d", N_SSM * 64),
              ("subln", 256), ("lam", 512), ("sink", 16))


def fm(vec):
    v = np.asarray(vec, np.float32).reshape(-1, 128)
    return np.ascontiguousarray(v.T)


def bc(vec):
    v = np.asarray(vec, np.float32).reshape(1, -1)
    return np.ascontiguousarray(np.broadcast_to(v, (128, v.shape[1])))


def const_layout():
    lay = {}
    n = 0
    for name, w in CONST_SPEC:
        lay[name] = (n, w)
        n += w
    return lay, n


def pack_consts(inp):
    k = np.arange(128)
    parts = {
        "norm_g": fm(inp["norm_g"].reshape(-1)),
        "final_g": fm(inp["final_norm_g"]),
        "ident": np.eye(128, dtype=np.float32),
        "ones": np.ones((128, 128), np.float32),
        "Ule": (k[:, None] <= k[None, :]).astype(np.float32),
        "Uge": (k[:, None] >= k[None, :]).astype(np.float32),
        "SL": (k[:, None] > k[None, :]).astype(np.float32),
        "SU": (k[:, None] < k[None, :]).astype(np.float32),
    }
    R = np.zeros((128, 128), np.float32)
    for d in range(128):
        if (d // 32) % 2 == 0:
            R[d, d + 32] = -1.0
        else:
            R[d, d - 32] = 1.0
    parts["RT"] = np.ascontiguousarray(R.T)
    parts["conv_w"] = fm(inp["ssm_conv_w"].reshape(-1))
    parts["conv_b"] = fm(inp["ssm_conv_b"].reshape(-1))
    parts["ssm_norm_g"] = fm(inp["ssm_norm_g"].reshape(-1))
    parts["dt_bias"] = bc(inp["ssm_dt_bias"].reshape(-1))
    parts["a_log"] = bc(inp["ssm_a_log"].reshape(-1))
    parts["ssm_d"] = bc(inp["ssm_d"].reshape(-1))
    parts["subln"] = bc(inp["diff_subln_g"].reshape(-1))
    parts["lam"] = bc(inp["diff_lambda"].reshape(-1))
    parts["sink"] = bc(inp["win_sink"].reshape(-1))
    lay, n = const_layout()
    arrs = []
    for name, w in CONST_SPEC:
        a = parts[name]
        assert a.shape == (128, w), (name, a.shape, w)
        arrs.append(a)
    return np.ascontiguousarray(np.concatenate(arrs, axis=1))


def rope_tables():
    L, GW, nf = LS, 64, 32
    rows = L // GW
    row = np.repeat(np.arange(rows, dtype=np.float32), GW)
    col = np.tile(np.arange(GW, dtype=np.float32), rows)
    inv = (np.float32(10000.0) ** (-np.arange(nf, dtype=np.float32) / np.float32(nf))).astype(np.float32)
    ar = (row[:, None] * inv).astype(np.float32)
    ac = (col[:, None] * inv).astype(np.float32)
    ang = np.concatenate([ar, ar, ac, ac], axis=1)
    cs = np.stack([np.cos(ang).T, np.sin(ang).T], axis=1).astype(np.float32)
    return np.ascontiguousarray(cs)


def win_mask():
    qi = np.arange(128)[:, None]
    kj = np.arange(384)[None, :]
    ok = np.abs(qi + 128 - kj) <= 128
    return np.where(ok, 0.0, -30000.0).astype(np.float32)


TPM = 1024


class Builder:
    def __init__(self, layers=None, passes=("B", "A"), ffn=True, mix=True):
        self.layers = list(range(DEPTH)) if layers is None else layers
        self.pass_names = passes
        self.do_ffn = ffn
        self.do_mix = mix
        self.nc = bass.Bass("TRN2", target_bir_lowering=False)

    def dram_in(self, name, shape, dt=F32):
        return self.nc.dram_tensor(name, list(shape), dt, kind="ExternalInput").ap()

    def dram_out(self, name, shape, dt=F32):
        return self.nc.dram_tensor(name, list(shape), dt, kind="ExternalOutput").ap()

    def sb(self, name, shape, dt, es=None):
        self.uid = getattr(self, "uid", 0) + 1
        return (es or self.es).enter_context(self.nc.sbuf_tensor(f"{name}_{self.uid}", list(shape), dt))

    def psum(self):
        i = self.ps_next
        self.ps_next = (i + 1) % 8
        return self.ps_t[i], self.ps_r[i]

    def wslot(self):
        i = self.w_next
        self.w_next = (i + 1) % self.NW
        return self.w_t[i], self.w_r[i]

    def _scr_tile(self, key):
        if key not in self.mx_keys:
            n = len(self.mx_keys)
            if n // 128 >= len(self.mx_scr):
                self.mx_scr.append(self.nc.dram_tensor(f"mxs{len(self.mx_scr)}", [128, 128, 4096], BF16,
                                                       kind="Internal").ap())
            self.mx_keys[key] = n
            fresh = True
        else:
            fresh = False
        n = self.mx_keys[key]
        return self.mx_scr[n // 128][n % 128], fresh

    def _load(self, t, r, view, src, nused, key):
        S = self.S
        if key is None or not self.reuse:
            S.dma("pool", view, src, writes=[r])
            return
        if self.sample:
            S.dma("pool", view, src, writes=[r])
            tile, fresh = self._scr_tile(key)
            if fresh:
                S.dma("sp", tile[:, 0:nused], t[:, 0:nused], reads=[r], store=True)
        else:
            tile, fresh = self._scr_tile(key)
            assert not fresh, key
            S.dma("sp", t[:, 0:nused], tile[:, 0:nused], writes=[r])

    def load_cols(self, W, col0, ncols=128, rows=D, key=None):
        t, r = self.wslot()
        kc = rows // 128
        assert kc * ncols <= 4096
        view = t[:, 0:kc * ncols].rearrange("p (c n) -> p c n", c=kc)
        src = W[:, col0:col0 + ncols].rearrange("(c p) n -> p c n", p=128)
        self._load(t, r, view, src, kc * ncols, None if key is None else (key, "c", col0, ncols))
        return view, r

    def load_rows(self, W, row0, ncols=D, col0=0, key=None):
        t, r = self.wslot()
        view = t[:, 0:ncols]
        self._load(t, r, view, W[row0:row0 + 128, col0:col0 + ncols], ncols,
                   None if key is None else (key, "r", row0, ncols))
        return view, r

    def scratch32(self):
        i = self.sc_next
        self.sc_next = (i + 1) % self.NSC
        return self.sc32[i], self.sc32_r[i]

    def cs(self, name, a=0, b=None):
        off, w = self.lay[name]
        if b is None:
            b = w
        return self.cst[:, off + a:off + b]

    def build(self):
        nc = self.nc
        with contextlib.ExitStack() as es:
            self.es = es
            self.S = S = Sched(nc, es)
            self.declare_io()
            self.alloc()
            self.load_consts()
            self.modulation_all()
            for pn in self.pass_names:
                self.set_pass(pn)
                self.load_x()
                for i in self.layers:
                    self.layer(i)
                self.final()
                S.barrier()
            S.finish()
        return nc

    def set_pass(self, pn):
        if pn == "A":
            self.tok0, self.TP, self.nseq, self.L, self.sample, self.v = 0, 512, 2, 256, False, 0
        else:
            self.tok0, self.TP, self.nseq, self.L, self.sample, self.v = 512, 1024, 1, 1024, True, 1
        self.TT = self.TP // 512
        self.NT = self.TP // 128

    def declare_io(self):
        lay, ncst = const_layout()
        self.lay = lay
        di = self.dram_in
        self.xT_in = di("xT", [D, T])
        self.cvec_in = di("cvec", [128, DC * 2])
        self.consts_in = di("consts", [128, ncst])
        self.bada_in = di("b_ada_fm", [128, DEPTH * N_MOD * DC])
        self.rope_in = di("rope_cs", [128, 2, LS])
        self.wmask_in = di("wmask", [128, 384])
        self.st_in = di("st_in", [2, 2, 128, DI])
        self.kc1T_in = di("kc1T", [D, 256])
        self.vc1_in = di("vc1", [256, D])
        self.kc2T_in = di("kc2T", [512, 256])
        self.vc2_in = di("vc2", [256, 512])
        self.w_ada = di("w_ada", [DEPTH, D, N_MOD * D])
        if self.do_ffn:
            self.ffn_w_in = [di("ffn1_w_in", [DEPTH, D, 2 * D_FF]), di("ffn2_w_in", [DEPTH, D, 2 * D_FF])]
            self.ffn_w_out = [di("ffn1_w_out", [DEPTH, D_FF, D]), di("ffn2_w_out", [DEPTH, D_FF, D])]
        kinds = {i % 3 for i in self.layers} if self.do_mix else set()
        if 0 in kinds:
            self.ssm_w_in = di("ssm_w_in", [N_SSM, D, SSM_IN])
            self.ssm_w_out = di("ssm_w_out", [N_SSM, DI, D])
        if 1 in kinds:
            self.diff_w_qkv = di("diff_w_qkv", [1, D, 6144])
            self.diff_w_out = di("diff_w_out", [1, D, D])
        if 2 in kinds:
            self.win_w_qkv = di("win_w_qkv", [1, D, 3072])
            self.win_w_out = di("win_w_out", [1, D, D])
        do = self.dram_out
        self.yT_out = do("yT", [D, T])
        self.st_out = do("st_out", [2, 2, 2, 128, DI])
        self.k1T_out = do("k1T", [D, 512])
        self.v1_out = do("v1", [512, D])
        self.k2T_out = do("k2T", [512, 512])
        self.v2_out = do("v2", [512, 512])
        self.yz_scr = self.nc.dram_tensor("yz_scr", [32, 128, TPM], BF16, kind="Internal").ap()
        self.wscr = [[self.nc.dram_tensor(f"wscr_{l}_{w}", [(FC // 2) * 3, 128, 4096], BF16, kind="Internal").ap()
                      for w in range(2)] for l in range(DEPTH)]
        self.mx_keys = {}
        self.mx_scr = []
        self.reuse = ("B" in self.pass_names and "A" in self.pass_names and
                      self.pass_names.index("B") < self.pass_names.index("A"))

    def alloc(self):
        nc = self.nc
        lay, ncst = const_layout()
        self.x = self.sb("x", [128, DC, TPM], F32)
        self.x_r = [[Res(f"x{c}_{t}") for t in range(2)] for c in range(DC)]
        self.h = self.sb("h", [128, DC, TPM], BF16)
        self.h_r = [[Res(f"h{c}_{t}") for t in range(2)] for c in range(DC)]
        self.cst = self.sb("cst", [128, ncst], F32)
        self.cst_r = Res("cst")
        self.ident_bf = self.sb("ident_bf", [128, 128], BF16)
        self.ones_bf = self.sb("ones_bf", [128, 128], BF16)
        self.RT_bf = self.sb("RT_bf", [128, 128], BF16)
        self.cbf_r = Res("cbf")
        self.mods = self.sb("mods", [128, DEPTH * N_MOD * DC, 2], F32)
        self.mods_r = Res("mods")
        self.ab = self.sb("ab", [128, 3, DC], F32)
        self.ab_r = Res("ab")
        self.rstd = self.sb("rstd", [128, TPM], F32)
        self.rstd_r = [Res(f"rstd{t}") for t in range(2)]
        self.ps_t = [self.es.enter_context(nc.psum_tensor(f"ps{i}", [128, 512], F32)) for i in range(8)]
        self.ps_r = [Res(f"ps{i}") for i in range(8)]
        self.ps_next = 0
        self.NW = 4
        self.w_t = [self.sb(f"w{i}", [128, 4096], BF16) for i in range(self.NW)]
        self.w_r = [Res(f"w{i}") for i in range(self.NW)]
        self.w_next = 0
        self.NSC = 3
        self.sc32 = [self.sb(f"sc32_{i}", [128, 512], F32) for i in range(self.NSC)]
        self.sc32_r = [Res(f"sc32_{i}") for i in range(self.NSC)]
        self.sc_next = 0
        self.sq = [self.sb(f"sq{i}", [128, 512], BF16) for i in range(2)]
        self.sq_r = [Res(f"sq{i}") for i in range(2)]
        self.sq_next = 0

    def load_consts(self):
        S = self.S
        S.dma("sp", self.cst[:], self.consts_in, writes=[self.cst_r])
        for dst, name in ((self.ident_bf, "ident"), (self.ones_bf, "ones"), (self.RT_bf, "RT")):
            S.op("dve", lambda e, dst=dst, name=name: e.tensor_copy(out=dst[:], in_=self.cs(name)),
                 reads=[self.cst_r], writes=[self.cbf_r])

    def load_x(self):
        S = self.S
        xin = self.xT_in.rearrange("(c p) t -> p c t", p=128)
        for c in range(DC):
            for t in range(self.TT):
                S.dma("sp", self.x[:, c, t * 512:(t + 1) * 512],
                      xin[:, c, self.tok0 + t * 512:self.tok0 + (t + 1) * 512], writes=[self.x_r[c][t]])

    def modulation_all(self):
        S = self.S
        NCC = N_MOD * DC
        with contextlib.ExitStack() as ms:
            cv = self.sb("cv", [128, DC * 2], F32, ms)
            cv_r = Res("cv")
            csil = self.sb("csil", [128, DC, 2], BF16, ms)
            csil_r = Res("csil")
            bada = self.sb("bada", [128, DEPTH * NCC], F32, ms)
            bada_r = Res("bada")
            S.dma("sp", cv[:], self.cvec_in, writes=[cv_r])
            S.dma("sp", bada[:], self.bada_in, writes=[bada_r])
            S.op("act", lambda e: e.activation(out=csil[:].rearrange("p c v -> p (c v)"), in_=cv[:], func=AF.Silu),
                 reads=[cv_r], writes=[csil_r])
            for i in self.layers:
                W = self.w_ada[i]
                ps, ps_r = self.psum()
                for cc in range(NCC):
                    if cc % 2 == 0:
                        wv, wr = self.load_cols(W, cc * 128, ncols=256)
                    hf = (cc % 2) * 128
                    for k in range(DC):
                        S.op("pe", lambda e, k=k, wv=wv, cc=cc, ps=ps, hf=hf: e.matmul(
                            ps[:, cc * 2:cc * 2 + 2], wv[:, k, hf:hf + 128], csil[:, k, :],
                            start=(k == 0), stop=(k == DC - 1)),
                            reads=[wr, csil_r], writes=[ps_r], inc=(k == DC - 1))
                S.op("dve", lambda e, i=i, ps=ps: e.tensor_tensor(
                    out=self.mods[:, i * NCC:(i + 1) * NCC, :],
                    in0=ps[:, 0:2 * NCC].rearrange("p (c v) -> p c v", v=2),
                    in1=bada[:, i * NCC:(i + 1) * NCC].unsqueeze(2).to_broadcast([128, NCC, 2]), op=ALU.add),
                    reads=[ps_r, bada_r], writes=[self.mods_r])
            S.barrier()

    def modvec(self, i, j):
        b = (i * N_MOD + j) * DC
        return self.mods[:, b:b + DC, self.v]

    def modcol(self, i, j, c):
        b = (i * N_MOD + j) * DC + c
        return self.mods[:, b, self.v:self.v + 1]

    def rms_stats(self):
        S = self.S
        for t in range(self.TT):
            ts = slice(t * 512, (t + 1) * 512)
            ps, ps_r = self.psum()
            for c in range(DC):
                i = self.sq_next
                self.sq_next = (i + 1) % 2
                sq, sq_r = self.sq[i], self.sq_r[i]
                S.op("act", lambda e, c=c, ts=ts, sq=sq: e.activation(out=sq[:], in_=self.x[:, c, ts],
                                                                      func=AF.Square),
                     reads=[self.x_r[c][t]], writes=[sq_r])
                S.op("pe", lambda e, c=c, sq=sq, ps=ps: e.matmul(ps[:], self.ones_bf[:], sq[:],
                                                                 start=(c == 0), stop=(c == DC - 1)),
                     reads=[sq_r, self.cbf_r], writes=[ps_r], inc=True)
            S.op("dve", lambda e, ts=ts, ps=ps: e.tensor_scalar(
                out=self.rstd[:, ts], in0=ps[:], scalar1=1.0 / D, scalar2=EPS, op0=ALU.mult, op1=ALU.add),
                reads=[ps_r], writes=[self.rstd_r[t]])
            S.op("act", lambda e, ts=ts: e.sqrt(out=self.rstd[:, ts], in_=self.rstd[:, ts]),
                 reads=[self.rstd_r[t]], writes=[self.rstd_r[t]])
            S.op("dve", lambda e, ts=ts: e.reciprocal(out=self.rstd[:, ts], in_=self.rstd[:, ts]),
                 reads=[self.rstd_r[t]], writes=[self.rstd_r[t]])

    def modnorm(self, i, sub, j_shift, j_scale):
        S = self.S
        off, _ = self.lay["norm_g"]
        g = self.cst[:, off + (i * 3 + sub) * DC: off + (i * 3 + sub + 1) * DC]
        S.op("dve", lambda e: e.scalar_tensor_tensor(
            out=self.ab[:, 0, :], in0=self.modvec(i, j_scale), scalar=1.0, in1=g, op0=ALU.add, op1=ALU.mult),
            reads=[self.mods_r, self.cst_r], writes=[self.ab_r])
        self.rms_stats()
        for t in range(self.TT):
            ts = slice(t * 512, (t + 1) * 512)
            for c in range(DC):
                sc, sc_r = self.scratch32()
                S.op("dve", lambda e, c=c, ts=ts, sc=sc: e.scalar_tensor_tensor(
                    out=sc[:], in0=self.x[:, c, ts], scalar=self.ab[:, 0, c:c + 1], in1=self.rstd[:, ts],
                    op0=ALU.mult, op1=ALU.mult),
                    reads=[self.x_r[c][t], self.ab_r, self.rstd_r[t]], writes=[sc_r])
                S.op("act", lambda e, c=c, ts=ts, sc=sc: e.activation(
                    out=self.h[:, c, ts], in_=sc[:], func=AF.Identity,
                    bias=self.modcol(i, j_shift, c), scale=1.0),
                    reads=[sc_r, self.mods_r], writes=[self.h_r[c][t]])

    def ffn(self, i, which, j_gate):
        S = self.S
        W_in = self.ffn_w_in[which][i]
        W_out = self.ffn_w_out[which][i]
        S.op("dve", lambda e: e.tensor_scalar(out=self.ab[:, 2, :], in0=self.modvec(i, j_gate), scalar1=0.5,
                                              scalar2=None, op0=ALU.mult), reads=[self.mods_r], writes=[self.ab_r])
        NX = 5
        with contextlib.ExitStack() as fs:
            extra_t = [self.sb(f"wx{k}", [128, 4096], BF16, fs) for k in range(NX)]
            extra_r = [Res(f"wx{k}") for k in range(NX)]
            a_t = [self.sb(f"fa{k}", [128, 2, TPM], BF16, fs) for k in range(2)]
            a_r = [[[Res(f"fa{k}_{j}_{t}") for t in range(2)] for j in range(2)] for k in range(2)]
            save = (self.w_t, self.w_r, self.NW, self.w_next)
            self.w_t = self.w_t + extra_t
            self.w_r = self.w_r + extra_r
            self.NW = len(self.w_t)
            for f in range(FC // 2):
                sbase = f * 3
                wscr = self.wscr[i][which]
                if self.reuse and not self.sample:
                    tiles = []
                    for kind in range(3):
                        wt_, w_r = self.wslot()
                        S.dma("sp", wt_[:, 0:4096], wscr[sbase + kind], writes=[w_r])
                        tiles.append((wt_, w_r))
                    (tg, wg_r), (tu, wu_r), (to, wo_r) = tiles
                    wg = tg[:, 0:4096].rearrange("p (c n) -> p c n", c=DC)
                    wu = tu[:, 0:4096].rearrange("p (c n) -> p c n", c=DC)
                    wo = to[:, 0:4096].rearrange("p (c n) -> p c n", c=2)
                else:
                    wg, wg_r = self.load_cols(W_in, f * 256, ncols=256)
                    wu, wu_r = self.load_cols(W_in, D_FF + f * 256, ncols=256)
                    wt_, wo_r = self.wslot()
                    wo = wt_[:, 0:2 * D].rearrange("p (c n) -> p c n", c=2)
                    S.dma("pool", wo, W_out[f * 256:(f + 1) * 256, :].rearrange("(c p) n -> p c n", p=128),
                          writes=[wo_r])
                    if self.reuse:
                        for kind, (wv_, wr_) in enumerate(((wg, wg_r), (wu, wu_r), (wo, wo_r))):
                            flat = wv_.rearrange("p c n -> p (c n)")
                            S.dma("sp", wscr[sbase + kind], flat, reads=[wr_], store=True)
                ai = f % 2
                a = a_t[ai]
                for j in range(2):
                    for t in range(self.TT):
                        ts = slice(t * 512, (t + 1) * 512)
                        pg, pg_r = self.psum()
                        for k in range(DC):
                            S.op("pe", lambda e, k=k: e.matmul(
                                pg[:], wg[:, k, j * 128:(j + 1) * 128], self.h[:, k, ts], start=(k == 0),
                                stop=(k == DC - 1)),
                                reads=[wg_r, self.h_r[k][t]], writes=[pg_r], inc=(k == DC - 1))
                        pu, pu_r = self.psum()
                        for k in range(DC):
                            S.op("pe", lambda e, k=k: e.matmul(
                                pu[:], wu[:, k, j * 128:(j + 1) * 128], self.h[:, k, ts], start=(k == 0),
                                stop=(k == DC - 1)),
                                reads=[wu_r, self.h_r[k][t]], writes=[pu_r], inc=(k == DC - 1))
                        sc, sc_r = self.scratch32()
                        S.op("act", lambda e: e.activation(out=sc[:], in_=pg[:], func=AF.Silu),
                             reads=[pg_r], writes=[sc_r])
                        S.op("dve", lambda e: e.tensor_tensor(out=a[:, j, ts], in0=sc[:], in1=pu[:], op=ALU.mult),
                             reads=[sc_r, pu_r], writes=[a_r[ai][j][t]])
                for oc in range(DC):
                    for t in range(self.TT):
                        ts = slice(t * 512, (t + 1) * 512)
                        po, po_r = self.psum()
                        for j in range(2):
                            S.op("pe", lambda e, j=j: e.matmul(
                                po[:], wo[:, j, oc * 128:(oc + 1) * 128], a[:, j, ts], start=(j == 0), stop=(j == 1)),
                                reads=[wo_r, a_r[ai][j][t]], writes=[po_r], inc=(j == 1))
                        self.x_accum(po, po_r, self.ab[:, 2, oc:oc + 1], self.ab_r, oc, t)
            S.barrier()
            self.w_t, self.w_r, self.NW, self.w_next = save

    def x_accum(self, po, po_r, scal, scal_r, oc, t, n=512, off=0):
        ts = slice(t * 512 + off, t * 512 + off + n)
        self.S.op("dve", lambda e: e.scalar_tensor_tensor(
            out=self.x[:, oc, ts], in0=po[:, 0:n], scalar=scal, in1=self.x[:, oc, ts],
            op0=ALU.mult, op1=ALU.add),
            reads=[po_r, scal_r, self.x_r[oc][t]], writes=[self.x_r[oc][t]])

    def layer(self, i):
        if self.do_ffn:
            self.modnorm(i, 0, 0, 1)
            self.ffn(i, 0, 2)
        if self.do_mix:
            self.modnorm(i, 1, 3, 4)
            self.S.barrier()
            m = i % 3
            if m == 0:
                self.ssm_mixer(i)
            elif m == 1:
                self.diff_mixer(i)
            else:
                self.win_mixer(i)
            self.S.barrier()
        if self.do_ffn:
            self.modnorm(i, 2, 6, 7)
            self.ffn(i, 1, 8)

    def final(self):
        S = self.S
        self.rms_stats()
        off, _ = self.lay["final_g"]
        yout = self.yT_out.rearrange("(c p) t -> p c t", p=128)
        for t in range(self.TT):
            ts = slice(t * 512, (t + 1) * 512)
            for c in range(DC):
                sc, sc_r = self.scratch32()
                S.op("dve", lambda e, c=c, ts=ts, sc=sc: e.scalar_tensor_tensor(
                    out=sc[:], in0=self.x[:, c, ts], scalar=self.cst[:, off + c:off + c + 1],
                    in1=self.rstd[:, ts], op0=ALU.mult, op1=ALU.mult),
                    reads=[self.x_r[c][t], self.cst_r, self.rstd_r[t]], writes=[sc_r])
                S.dma("sp", yout[:, c, self.tok0 + t * 512:self.tok0 + (t + 1) * 512], sc[:],
                      reads=[sc_r], store=True)

    def proj_fm(self, wv, wr, t, ps, ps_r, ncol=128, wcol0=0):
        ts = slice(t * 512, (t + 1) * 512)
        for k in range(DC):
            self.S.op("pe", lambda e, k=k: e.matmul(ps[0:ncol, :], wv[:, k, wcol0:wcol0 + ncol], self.h[:, k, ts],
                                                    start=(k == 0), stop=(k == DC - 1)),
                      reads=[wr, self.h_r[k][t]], writes=[ps_r], inc=(k == DC - 1))

    def proj_tm(self, wv, wr, nt, ps, ps_r, ncol, pcol0=0, wcol0=0):
        t = nt // 4
        tk = slice(nt * 128, (nt + 1) * 128)
        for k in range(DC):
            self.S.op("pe", lambda e, k=k: e.matmul(ps[:, pcol0:pcol0 + ncol], self.h[:, k, tk],
                                                    wv[:, k, wcol0:wcol0 + ncol],
                                                    start=(k == 0), stop=(k == DC - 1)),
                      reads=[wr, self.h_r[k][t]], writes=[ps_r], inc=(k == DC - 1))

    def rope_evac(self, ps, ps_r, dst, dst_r, t, cs_t, cs_r):
        S = self.S
        ts = slice(t * 512, (t + 1) * 512)
        xb = self.sq[self.sq_next]
        xb_r = self.sq_r[self.sq_next]
        self.sq_next = (self.sq_next + 1) % 2
        S.op("dve", lambda e: e.tensor_copy(out=xb[:], in_=ps[:]), reads=[ps_r], writes=[xb_r])
        pr, pr_r = self.psum()
        S.op("pe", lambda e: e.matmul(pr[:], self.RT_bf[:], xb[:], start=True, stop=True),
             reads=[xb_r, self.cbf_r], writes=[pr_r])
        s1, s1_r = self.scratch32()
        s2, s2_r = self.scratch32()
        S.op("dve", lambda e: e.tensor_tensor(out=s1[:], in0=ps[:], in1=cs_t[:, 0, ts], op=ALU.mult),
             reads=[ps_r, cs_r], writes=[s1_r])
        S.op("dve", lambda e: e.tensor_tensor(out=s2[:], in0=pr[:], in1=cs_t[:, 1, ts], op=ALU.mult),
             reads=[pr_r, cs_r], writes=[s2_r])
        S.op("dve", lambda e: e.tensor_tensor(out=dst, in0=s1[:], in1=s2[:], op=ALU.add),
             reads=[s1_r, s2_r], writes=[dst_r])

    def attn_tile(self, qT_ap, kparts, vparts, scale, sink_ap, W):
        S = self.S
        P, P_r = W["P"], W["P_r"]
        PT, PT_r = W["PT"], W["PT_r"]
        st, st_r = W["st"], W["st_r"]
        dv = W["dv"]
        srcs = []
        col = 0
        for j, (kT_ap, n, mask_ap) in enumerate(kparts):
            ps, ps_r = self.psum()
            S.op("pe", lambda e, ps=ps, kT_ap=kT_ap, n=n: e.matmul(ps[:, 0:n], qT_ap, kT_ap, start=True, stop=True),
                 reads=W["q_reads"] + W["k_reads"], writes=[ps_r])
            if mask_ap is not None:
                sc, sc_r = self.scratch32()
                S.op("dve", lambda e, sc=sc, ps=ps, n=n, mask_ap=mask_ap: e.tensor_tensor(
                    out=sc[:, 0:n], in0=ps[:, 0:n], in1=mask_ap, op=ALU.add),
                    reads=[ps_r, W["mask_r"]], writes=[sc_r])
                src, src_r = sc, sc_r
            else:
                sc, sc_r = self.scratch32()
                S.op("dve", lambda e, sc=sc, ps=ps, n=n: e.tensor_copy(out=sc[:, 0:n], in_=ps[:, 0:n]),
                     reads=[ps_r], writes=[sc_r])
                src, src_r = sc, sc_r
            S.op("dve", lambda e, src=src, n=n, j=j: e.reduce_max(out=st[:, j:j + 1], in_=src[:, 0:n], axis=AX.X),
                 reads=[src_r], writes=[st_r])
            srcs.append((src, src_r, n, col))
            col += n
        nk = col
        for j in range(1, len(kparts)):
            S.op("dve", lambda e, j=j: e.tensor_tensor(out=st[:, 0:1], in0=st[:, 0:1], in1=st[:, j:j + 1], op=ALU.max),
                 reads=[st_r], writes=[st_r])
        if sink_ap is None:
            S.op("dve", lambda e: e.tensor_scalar(out=st[:, 4:5], in0=st[:, 0:1], scalar1=-scale, scalar2=None,
                                                  op0=ALU.mult), reads=[st_r], writes=[st_r])
        else:
            S.op("dve", lambda e: e.tensor_scalar(out=st[:, 4:5], in0=st[:, 0:1], scalar1=-scale,
                                                  scalar2=W["nsink_ap"], op0=ALU.mult, op1=ALU.min),
                 reads=[st_r, W["nsink_r"]], writes=[st_r])
        S.op("dve", lambda e: e.memset(st[:, 8:8 + len(kparts) + 1], 0.0), writes=[st_r])
        for j, (src, src_r, n, c0) in enumerate(srcs):
            S.op("act", lambda e, src=src, n=n, c0=c0, j=j: e.activation(
                out=P[:, c0:c0 + n], in_=src[:, 0:n], func=AF.Exp, bias=st[:, 4:5], scale=scale,
                accum_out=st[:, 8 + j:9 + j]),
                reads=[src_r, st_r], writes=[P_r, st_r])
        ns = len(kparts)
        if sink_ap is not None:
            S.op("act", lambda e: e.activation(out=st[:, 8 + ns:9 + ns], in_=sink_ap, func=AF.Exp,
                                               bias=st[:, 4:5], scale=1.0),
                 reads=[st_r, W["nsink_r"]], writes=[st_r])
            ns += 1
        S.op("dve", lambda e: e.reduce_sum(out=st[:, 5:6], in_=st[:, 8:8 + ns], axis=AX.X),
             reads=[st_r], writes=[st_r])
        S.op("dve", lambda e: e.reciprocal(out=st[:, 6:7], in_=st[:, 5:6]), reads=[st_r], writes=[st_r])
        nkt = nk // 128
        for b0 in range(0, nkt, 4):
            nb = min(4, nkt - b0)
            pt, pt_r = self.psum()
            for jj in range(nb):
                kt = b0 + jj
                S.op("pe", lambda e, pt=pt, jj=jj, kt=kt: e.matmul(
                    pt[:, jj * 128:(jj + 1) * 128], P[:, kt * 128:(kt + 1) * 128], self.ident_bf[:],
                    start=True, stop=True), reads=[P_r, self.cbf_r], writes=[pt_r], inc=(jj == nb - 1))
            S.op("act", lambda e, pt=pt, b0=b0, nb=nb: e.copy(
                out=PT[:, b0 * 128:(b0 + nb) * 128], in_=pt[:, 0:nb * 128]),
                reads=[pt_r], writes=[PT_r])
        po, po_r = self.psum()
        for kt in range(nkt):
            S.op("pe", lambda e, kt=kt: e.matmul(po[:, 0:dv], PT[:, kt * 128:(kt + 1) * 128], vparts[kt],
                                                 start=(kt == 0), stop=(kt == nkt - 1)),
                 reads=[PT_r] + W["v_reads"], writes=[po_r], inc=(kt == nkt - 1))
        return po, po_r

    def transpose_to(self, src_ap, src_reads, dst_ap, dst_r, ncols=128, scale_ap=None, scale_r=None):
        S = self.S
        pt, pt_r = self.psum()
        S.op("pe", lambda e: e.matmul(pt[0:ncols, 0:128], src_ap, self.ident_bf[:], start=True, stop=True),
             reads=list(src_reads) + [self.cbf_r], writes=[pt_r])
        if scale_ap is None:
            S.op("act", lambda e: e.copy(out=dst_ap, in_=pt[0:ncols, 0:128]), reads=[pt_r], writes=[dst_r])
        else:
            S.op("dve", lambda e: e.tensor_scalar(out=dst_ap, in0=pt[0:ncols, 0:128], scalar1=scale_ap, scalar2=None,
                                                  op0=ALU.mult), reads=[pt_r, scale_r], writes=[dst_r])

    def win_mixer(self, i):
        S = self.S
        Wq = self.win_w_qkv[0]
        Wo = self.win_w_out[0]
        TP, NT, L = self.TP, self.NT, self.L
        scale = 128 ** -0.5
        koff = 256 if self.sample else 0
        with contextlib.ExitStack() as ms:
            sb = lambda n, s, d: self.sb("wn_" + n, s, d, ms)
            kT = sb("kT", [128, koff + TP], BF16); kT_r = Res("wkT")
            V = sb("V", [128, (koff + TP) // 128, 128], BF16); V_r = Res("wV")
            qT = sb("qT", [128, TP], BF16); qT_r = Res("wqT")
            oT = sb("oT", [128, TP], BF16); oT_r = Res("woT")
            W = {"P": sb("P", [128, 640], BF16), "P_r": Res("wP"), "PT": sb("PT", [128, 640], BF16),
                 "PT_r": Res("wPT"), "st": sb("st", [128, 16], F32), "st_r": Res("wst"), "dv": 128,
                 "q_reads": [qT_r], "k_reads": [kT_r], "v_reads": [V_r]}
            o_tm = sb("o_tm", [128, 128], BF16); o_r = Res("wo_tm")
            nsink = sb("nsink", [128, 16], F32); nsink_r = Res("wnsink")
            W["nsink_r"] = nsink_r
            stage = sb("stage", [128, 512], F32); stage_r = Res("wstage")
            S.op("dve", lambda e: e.tensor_scalar(out=nsink[:], in0=self.cs("sink"), scalar1=-1.0, scalar2=None,
                                                  op0=ALU.mult), reads=[self.cst_r], writes=[nsink_r])
            if self.sample:
                cs_t = sb("cs", [128, 2, LS], F32); cs_r = Res("wcs")
                mask = sb("mask", [128, 384], F32); mask_r = Res("wmask")
                W["mask_r"] = mask_r
                S.dma("sp", cs_t[:], self.rope_in, writes=[cs_r])
                S.dma("sp", mask[:], self.wmask_in, writes=[mask_r])
            for g in range(4):
                if KSTOP < 1:
                    break
                wk, wk_r = self.load_cols(Wq, 2048 + g * 128, key="win_qkv")
                if self.sample:
                    S.dma("pool", kT[:, 0:256], self.kc2T_in[g * 128:(g + 1) * 128, :], writes=[kT_r])
                for t in range(self.TT if 'k' in KPART else 0):
                    ps, ps_r = self.psum()
                    self.proj_fm(wk, wk_r, t, ps, ps_r)
                    dst = kT[:, koff + t * 512: koff + (t + 1) * 512]
                    if self.sample:
                        self.rope_evac(ps, ps_r, dst, kT_r, t, cs_t, cs_r)
                    else:
                        S.op("act", lambda e, ps=ps: e.copy(out=stage[:], in_=ps[:]), reads=[ps_r], writes=[stage_r])
                        S.op("dve", lambda e, dst=dst: e.tensor_copy(out=dst, in_=stage[:]),
                             reads=[stage_r], writes=[kT_r])
                        if 's' not in KSKIP:
                            S.dma("sp", self.k2T_out[g * 128:(g + 1) * 128, t * 512:(t + 1) * 512], stage[:],
                                  reads=[stage_r], store=True)
                wv, wv_r = self.load_cols(Wq, 2560 + g * 128, key="win_qkv")
                if self.sample:
                    S.dma("pool", V[:, 0:2, :],
                          self.vc2_in[:, g * 128:(g + 1) * 128].rearrange("(n p) d -> p n d", p=128), writes=[V_r])
                for nt in range(NT if 'v' in KPART else 0):
                    ps, ps_r = self.psum()
                    self.proj_tm(wv, wv_r, nt, ps, ps_r, 128)
                    if self.sample:
                        S.op("act", lambda e, ps=ps, nt=nt: e.copy(out=V[:, koff // 128 + nt, :], in_=ps[:, 0:128]),
                             reads=[ps_r], writes=[V_r])
                    else:
                        S.op("act", lambda e, ps=ps: e.copy(out=stage[:, 0:128], in_=ps[:, 0:128]),
                             reads=[ps_r], writes=[stage_r])
                        S.op("dve", lambda e, nt=nt: e.tensor_copy(out=V[:, koff // 128 + nt, :], in_=stage[:, 0:128]),
                             reads=[stage_r], writes=[V_r])
                        S.dma("sp", self.v2_out[nt * 128:(nt + 1) * 128, g * 128:(g + 1) * 128], stage[:, 0:128],
                              reads=[stage_r], store=True)
                for r in range(4):
                    if KSTOP < 2:
                        break
                    hd = g * 4 + r
                    wq, wq_r = self.load_cols(Wq, hd * 128, key="win_qkv")
                    W["nsink_ap"] = nsink[:, hd:hd + 1]
                    for t in range(self.TT):
                        ps, ps_r = self.psum()
                        self.proj_fm(wq, wq_r, t, ps, ps_r)
                        dst = qT[:, t * 512:(t + 1) * 512]
                        if self.sample:
                            self.rope_evac(ps, ps_r, dst, qT_r, t, cs_t, cs_r)
                        else:
                            S.op("act", lambda e, ps=ps, dst=dst: e.copy(out=dst, in_=ps[:]),
                                 reads=[ps_r], writes=[qT_r])
                    for s in range(self.nseq):
                        if KSTOP < 3:
                            break
                        for qb in range(L // 128):
                            q0 = s * L + qb * 128
                            if self.sample:
                                b_lo, b_hi = max(qb - 1, 0), min(qb + 1, L // 128 - 1)
                                nb = (b_hi - b_lo + 1) * 128
                                m0 = (b_lo - (qb - 1)) * 128
                                kparts = [(kT[:, 0:256], 256, None),
                                          (kT[:, 256 + b_lo * 128: 256 + b_lo * 128 + nb], nb, mask[:, m0:m0 + nb])]
                                vparts = [V[:, 0, :], V[:, 1, :]] + [V[:, 2 + b, :] for b in range(b_lo, b_hi + 1)]
                            else:
                                kparts = [(kT[:, s * L:(s + 1) * L], L, None)]
                                vparts = [V[:, s * (L // 128) + b, :] for b in range(L // 128)]
                            po, po_r = self.attn_tile(qT[:, q0:q0 + 128], kparts, vparts, scale,
                                                      self.cs("sink", hd, hd + 1), W)
                            S.op("dve", lambda e, po=po: e.tensor_scalar(
                                out=o_tm[:], in0=po[:, 0:128], scalar1=W["st"][:, 6:7], scalar2=None, op0=ALU.mult),
                                reads=[po_r, W["st_r"]], writes=[o_r])
                            self.transpose_to(o_tm[:], [o_r], oT[:, q0:q0 + 128], oT_r)
                    if KSTOP < 4:
                        continue
                    wo, wo_r = self.load_rows(Wo, hd * 128, key="win_out")
                    for oc in range(DC):
                        for t in range(self.TT):
                            po, po_r = self.psum()
                            S.op("pe", lambda e, po=po, oc=oc, t=t: e.matmul(
                                po[:], wo[:, oc * 128:(oc + 1) * 128], oT[:, t * 512:(t + 1) * 512],
                                start=True, stop=True), reads=[wo_r, oT_r], writes=[po_r])
                            self.x_accum(po, po_r, self.modcol(i, 5, oc), self.mods_r, oc, t)
            S.barrier()

    def diff_mixer(self, i):
        S = self.S
        Wq = self.diff_w_qkv[0]
        Wo = self.diff_w_out[0]
        TP, NT, L = self.TP, self.NT, self.L
        scale = 128 ** -0.5
        lambda_init = 0.8 - 0.6 * math.exp(-0.3 * i)
        koff = 256 if self.sample else 0
        NK = koff + L
        with contextlib.ExitStack() as ms:
            sb = lambda n, s, d: self.sb("df_" + n, s, d, ms)
            kT = sb("kT", [128, 2, koff + TP], BF16); kT_r = Res("dkT")
            V = sb("V", [128, (koff + TP) // 128, 256], BF16); V_r = Res("dV")
            qT = sb("qT", [128, 2, TP], BF16); qT_r = Res("dqT")
            oT = sb("oT", [128, 2, TP], BF16); oT_r = Res("doT")
            Ws = []
            for m in range(2):
                Ws.append({"P": sb(f"P{m}", [128, NK], BF16), "P_r": Res(f"dP{m}"),
                           "PT": sb(f"PT{m}", [128, NK], BF16), "PT_r": Res(f"dPT{m}"),
                           "st": sb(f"st{m}", [128, 16], F32), "st_r": Res(f"dst{m}"), "dv": 256,
                           "q_reads": [qT_r], "k_reads": [kT_r], "v_reads": [V_r]})
            o32 = sb("o32", [128, 256], F32); o32_r = Res("do32")
            o_tm = sb("o_tm", [128, 256], BF16); o_r = Res("do_tm")
            lam = sb("lam", [128, 8], F32); lam_r = Res("dlam")
            junk = sb("junk", [128, 256], F32); junk_r = Res("djunk")
            stage = sb("stage", [128, 512], F32); stage_r = Res("dstage")
            lv = self.cs("lam")
            S.op("dve", lambda e: e.tensor_tensor(out=junk[:, 0:128], in0=lv[:, 0:128], in1=lv[:, 128:256], op=ALU.mult),
                 reads=[self.cst_r], writes=[junk_r])
            S.op("dve", lambda e: e.reduce_sum(out=lam[:, 0:1], in_=junk[:, 0:128], axis=AX.X),
                 reads=[junk_r], writes=[lam_r])
            S.op("dve", lambda e: e.tensor_tensor(out=junk[:, 128:256], in0=lv[:, 256:384], in1=lv[:, 384:512],
                                                  op=ALU.mult), reads=[self.cst_r], writes=[junk_r])
            S.op("dve", lambda e: e.reduce_sum(out=lam[:, 1:2], in_=junk[:, 128:256], axis=AX.X),
                 reads=[junk_r], writes=[lam_r])
            S.op("act", lambda e: e.activation(out=lam[:, 2:4], in_=lam[:, 0:2], func=AF.Exp),
                 reads=[lam_r], writes=[lam_r])
            S.op("dve", lambda e: e.tensor_tensor(out=lam[:, 4:5], in0=lam[:, 3:4], in1=lam[:, 2:3], op=ALU.subtract),
                 reads=[lam_r], writes=[lam_r])
            S.op("dve", lambda e: e.tensor_scalar(out=lam[:, 4:5], in0=lam[:, 4:5], scalar1=-lambda_init, scalar2=None,
                                                  op0=ALU.add), reads=[lam_r], writes=[lam_r])
            if self.sample:
                cs_t = sb("cs", [128, 2, LS], F32); cs_r = Res("dcs")
                S.dma("sp", cs_t[:], self.rope_in, writes=[cs_r])
            for hd in range(8):
                for m in range(2):
                    col = m * 1024 + hd * 128
                    wk, wk_r = self.load_cols(Wq, 2048 + col, key="diff_qkv")
                    if self.sample:
                        S.dma("pool", kT[:, m, 0:256], self.kc1T_in[col:col + 128, :], writes=[kT_r])
                    for t in range(self.TT):
                        ps, ps_r = self.psum()
                        self.proj_fm(wk, wk_r, t, ps, ps_r)
                        dst = kT[:, m, koff + t * 512: koff + (t + 1) * 512]
                        if self.sample:
                            self.rope_evac(ps, ps_r, dst, kT_r, t, cs_t, cs_r)
                        else:
                            S.op("act", lambda e, ps=ps: e.copy(out=stage[:], in_=ps[:]),
                                 reads=[ps_r], writes=[stage_r])
                            S.op("dve", lambda e, dst=dst: e.tensor_copy(out=dst, in_=stage[:]),
                                 reads=[stage_r], writes=[kT_r])
                            S.dma("sp", self.k1T_out[col:col + 128, t * 512:(t + 1) * 512], stage[:],
                                  reads=[stage_r], store=True)
                    wq, wq_r = self.load_cols(Wq, col, key="diff_qkv")
                    for t in range(self.TT):
                        ps, ps_r = self.psum()
                        self.proj_fm(wq, wq_r, t, ps, ps_r)
                        dst = qT[:, m, t * 512:(t + 1) * 512]
                        if self.sample:
                            self.rope_evac(ps, ps_r, dst, qT_r, t, cs_t, cs_r)
                        else:
                            S.op("act", lambda e, ps=ps, dst=dst: e.copy(out=dst, in_=ps[:]),
                                 reads=[ps_r], writes=[qT_r])
                wv, wv_r = self.load_cols(Wq, 4096 + hd * 256, ncols=256, key="diff_qkv")
                if self.sample:
                    S.dma("pool", V[:, 0:2, :],
                          self.vc1_in[:, hd * 256:(hd + 1) * 256].rearrange("(n p) d -> p n d", p=128), writes=[V_r])
                for nt in range(NT):
                    ps, ps_r = self.psum()
                    self.proj_tm(wv, wv_r, nt, ps, ps_r, 256)
                    if self.sample:
                        S.op("act", lambda e, ps=ps, nt=nt: e.copy(out=V[:, koff // 128 + nt, :], in_=ps[:, 0:256]),
                             reads=[ps_r], writes=[V_r])
                    else:
                        S.op("act", lambda e, ps=ps: e.copy(out=stage[:, 0:256], in_=ps[:, 0:256]),
                             reads=[ps_r], writes=[stage_r])
                        S.op("dve", lambda e, nt=nt: e.tensor_copy(out=V[:, koff // 128 + nt, :], in_=stage[:, 0:256]),
                             reads=[stage_r], writes=[V_r])
                        S.dma("sp", self.v1_out[nt * 128:(nt + 1) * 128, hd * 256:(hd + 1) * 256], stage[:, 0:256],
                              reads=[stage_r], store=True)
                for s in range(self.nseq):
                    k0 = s * L if not self.sample else 0
                    vparts = [V[:, (k0 // 128) + b, :] for b in range(NK // 128)]
                    for qb in range(L // 128):
                        q0 = s * L + qb * 128
                        pos = []
                        for m in range(2):
                            kparts = []
                            c = 0
                            while c < NK:
                                n = min(512, NK - c)
                                kparts.append((kT[:, m, k0 + c:k0 + c + n], n, None))
                                c += n
                            pos.append(self.attn_tile(qT[:, m, q0:q0 + 128], kparts, vparts, scale, None, Ws[m]))
                        st0, st1 = Ws[0]["st"], Ws[1]["st"]
                        S.op("dve", lambda e: e.tensor_tensor(out=st1[:, 7:8], in0=st1[:, 6:7], in1=lam[:, 4:5],
                                                              op=ALU.mult),
                             reads=[Ws[1]["st_r"], lam_r], writes=[Ws[1]["st_r"]])
                        (p0, p0_r), (p1, p1_r) = pos
                        S.op("dve", lambda e, p0=p0: e.tensor_scalar(out=o32[:], in0=p0[:, 0:256], scalar1=st0[:, 6:7],
                                                                     scalar2=None, op0=ALU.mult),
                             reads=[p0_r, Ws[0]["st_r"]], writes=[o32_r])
                        S.op("dve", lambda e, p1=p1: e.scalar_tensor_tensor(
                            out=o32[:], in0=p1[:, 0:256], scalar=st1[:, 7:8], in1=o32[:], op0=ALU.mult, op1=ALU.add),
                            reads=[p1_r, Ws[1]["st_r"], o32_r], writes=[o32_r])
                        S.op("dve", lambda e: e.memset(st0[:, 12:13], 0.0), writes=[Ws[0]["st_r"]])
                        S.op("act", lambda e: e.activation(out=junk[:], in_=o32[:], func=AF.Square,
                                                           accum_out=st0[:, 12:13]),
                             reads=[o32_r, Ws[0]["st_r"]], writes=[junk_r, Ws[0]["st_r"]])
                        S.op("dve", lambda e: e.tensor_scalar(out=st0[:, 13:14], in0=st0[:, 12:13], scalar1=1.0 / 256,
                                                              scalar2=EPS, op0=ALU.mult, op1=ALU.add),
                             reads=[Ws[0]["st_r"]], writes=[Ws[0]["st_r"]])
                        S.op("act", lambda e: e.sqrt(out=st0[:, 13:14], in_=st0[:, 13:14]),
                             reads=[Ws[0]["st_r"]], writes=[Ws[0]["st_r"]])
                        S.op("dve", lambda e: e.reciprocal(out=st0[:, 14:15], in_=st0[:, 13:14]),
                             reads=[Ws[0]["st_r"]], writes=[Ws[0]["st_r"]])
                        S.op("dve", lambda e: e.tensor_scalar(out=st0[:, 14:15], in0=st0[:, 14:15],
                                                              scalar1=1.0 - lambda_init, scalar2=None, op0=ALU.mult),
                             reads=[Ws[0]["st_r"]], writes=[Ws[0]["st_r"]])
                        S.op("dve", lambda e: e.scalar_tensor_tensor(
                            out=o_tm[:], in0=o32[:], scalar=st0[:, 14:15], in1=self.cs("subln"),
                            op0=ALU.mult, op1=ALU.mult), reads=[o32_r, Ws[0]["st_r"], self.cst_r], writes=[o_r])
                        for eh in range(2):
                            self.transpose_to(o_tm[:, eh * 128:(eh + 1) * 128], [o_r], oT[:, eh, q0:q0 + 128], oT_r)
                wos = [self.load_rows(Wo, hd * 256 + eh * 128, key="diff_out") for eh in range(2)]
                for oc in range(DC):
                    for t in range(self.TT):
                        po, po_r = self.psum()
                        for eh in range(2):
                            wo, wo_r = wos[eh]
                            S.op("pe", lambda e, po=po, oc=oc, t=t, eh=eh, wo=wo: e.matmul(
                                po[:], wo[:, oc * 128:(oc + 1) * 128], oT[:, eh, t * 512:(t + 1) * 512],
                                start=(eh == 0), stop=(eh == 1)), reads=[wo_r, oT_r], writes=[po_r], inc=(eh == 1))
                        self.x_accum(po, po_r, self.modcol(i, 5, oc), self.mods_r, oc, t)
            S.barrier()

    def ssm_mixer(self, i):
        S = self.S
        j = i // 3
        Win = self.ssm_w_in[j]
        Wout = self.ssm_w_out[j]
        TP, NT, L, nseq = self.TP, self.NT, self.L, self.nseq
        nch = L // 128
        U = {0: self.cs("Ule"), 1: self.cs("Uge")}
        SLU = {0: self.cs("SL"), 1: self.cs("SU")}
        with contextlib.ExitStack() as ms:
            sb = lambda n, s, d: self.sb("ss_" + n, s, d, ms)
            ssq = sb("ssq", [128, NT, NG], F32); ssq_r = Res("sssq")
            S.op("dve", lambda e: e.memset(ssq[:].rearrange("p a b -> p (a b)"), 0.0), writes=[ssq_r])
            with contextlib.ExitStack() as gs:
                gb = lambda n, s, d: self.sb("sg_" + n, s, d, gs)
                z_tm = gb("z", [128, NT, 512], BF16); z_r = Res("sz")
                yf = gb("yf", [128, NT, 512], BF16); yf_r = Res("syf")
                xc = gb("xc", [128, 6, TP], BF16); xc_r = Res("sxc")
                pre = gb("pre", [128, TP], F32); pre_r = Res("spre")
                acc = self.rstd; acc_r = Res("sacc")
                dt = gb("dt", [128, NT, 16], F32); dt_r = Res("sdt")
                dtA = gb("dtA", [128, NT, 16], F32); dtA_r = Res("sdtA")
                eac = gb("eac", [128, NT, 16], F32); eac_r = Res("seac")
                cd = gb("cd", [128, NT, 16], F32); cd_r = Res("scd")
                abc = gb("abc", [128, 2, 128], F32); abc_r = Res("sabc")
                xdt = gb("xdt", [128, 512], BF16); xdt_r = Res("sxdt")
                xdtd = gb("xdtd", [128, 512], BF16); xdtd_r = Res("sxdtd")
                B_tm = gb("B_tm", [128, 128], BF16); Btm_r = Res("sBtm")
                mCB = gb("mCB", [128, 128], F32); mCB_r = Res("smCB")
                MT = gb("MT", [128, 8, 128], BF16); MT_r = Res("sMT")
                hT = gb("hT", [128, 512], F32); hT_r = Res("shT")
                hTb = gb("hTb", [128, 512], BF16); hTb_r = Res("shTb")
                y32 = gb("y32", [128, 512], F32); y32_r = Res("sy32")
                yz = gb("yz", [128, 512], BF16); yz_r = Res("syz")
                yzT_t = gb("yzT", [128, 4, 128], BF16); yzT = yzT_t[:]; yzT_r = Res("syzT")
                junk = gb("junk", [128, 512], BF16); junk_r = Res("sjunk")
                off, _ = self.lay["a_log"]
                S.op("act", lambda e: e.activation(out=abc[:, 0, :], in_=self.cst[:, off + j * 128: off + (j + 1) * 128],
                                                   func=AF.Exp), reads=[self.cst_r], writes=[abc_r])
                S.op("dve", lambda e: e.tensor_scalar(out=abc[:, 0, :], in0=abc[:, 0, :], scalar1=-1.0, scalar2=None,
                                                      op0=ALU.mult), reads=[abc_r], writes=[abc_r])
                offb, _ = self.lay["dt_bias"]
                offd, _ = self.lay["ssm_d"]
                offcw, _ = self.lay["conv_w"]
                offcb, _ = self.lay["conv_b"]
                offng, _ = self.lay["ssm_norm_g"]
                for g in range(NG):
                    for d_ in range(2):
                        wd, wd_r = self.load_cols(Win, DI + 6144 + d_ * 64 + g * 8, ncols=8, key=f"ssm_in{j}")
                        for nt in range(NT):
                            ps, ps_r = self.psum()
                            self.proj_tm(wd, wd_r, nt, ps, ps_r, 8)
                            hsl = slice(d_ * 64 + g * 8, d_ * 64 + g * 8 + 8)
                            bsl = slice(offb + j * 128 + d_ * 64 + g * 8, offb + j * 128 + d_ * 64 + g * 8 + 8)
                            sc, sc_r = self.scratch32()
                            S.op("dve", lambda e, sc=sc, ps=ps, bsl=bsl: e.tensor_tensor(
                                out=sc[:, 0:8], in0=ps[:, 0:8], in1=self.cst[:, bsl], op=ALU.add),
                                reads=[ps_r, self.cst_r], writes=[sc_r])
                            S.op("act", lambda e, sc=sc: e.activation(out=sc[:, 0:8], in_=sc[:, 0:8], func=AF.Exp),
                                 reads=[sc_r], writes=[sc_r])
                            S.op("dve", lambda e, sc=sc: e.tensor_scalar(out=sc[:, 0:8], in0=sc[:, 0:8], scalar1=1.0,
                                                                         scalar2=None, op0=ALU.add),
                                 reads=[sc_r], writes=[sc_r])
                            S.op("act", lambda e, sc=sc, nt=nt, d_=d_: e.activation(
                                out=dt[:, nt, d_ * 8:(d_ + 1) * 8], in_=sc[:, 0:8], func=AF.Ln),
                                reads=[sc_r], writes=[dt_r])
                            S.op("dve", lambda e, nt=nt, d_=d_, hsl=hsl: e.tensor_tensor(
                                out=dtA[:, nt, d_ * 8:(d_ + 1) * 8], in0=dt[:, nt, d_ * 8:(d_ + 1) * 8],
                                in1=abc[:, 0, hsl], op=ALU.mult), reads=[dt_r, abc_r], writes=[dtA_r])
                    for nt in range(NT):
                        ps, ps_r = self.psum()
                        S.op("pe", lambda e, ps=ps, nt=nt: e.matmul(ps[:, 0:8], U[0], dtA[:, nt, 0:8],
                                                                    start=True, stop=True),
                             reads=[self.cst_r, dtA_r], writes=[ps_r])
                        S.op("pe", lambda e, ps=ps, nt=nt: e.matmul(ps[:, 8:16], U[1], dtA[:, nt, 8:16],
                                                                    start=True, stop=True),
                             reads=[self.cst_r, dtA_r], writes=[ps_r])
                        S.op("pe", lambda e, ps=ps, nt=nt: e.matmul(ps[:, 16:32], self.cs("ones"), dtA[:, nt, :],
                                                                    start=True, stop=True),
                             reads=[self.cst_r, dtA_r], writes=[ps_r])
                        S.op("act", lambda e, ps=ps, nt=nt: e.activation(out=eac[:, nt, :], in_=ps[:, 0:16], func=AF.Exp),
                             reads=[ps_r], writes=[eac_r])
                        S.op("act", lambda e, ps=ps, nt=nt: e.activation(out=cd[:, nt, :], in_=ps[:, 16:32], func=AF.Exp),
                             reads=[ps_r], writes=[cd_r])
                    cols = [DI + g * 512 + q * 128 for q in range(4)] + [DI + DI + g * 128, DI + DI + 1024 + g * 128]
                    for ci, col in enumerate(cols):
                        cc = (col - DI) // 128
                        wv, wv_r = self.load_cols(Win, col, key=f"ssm_in{j}")
                        for t in range(self.TT):
                            ps, ps_r = self.psum()
                            self.proj_fm(wv, wv_r, t, ps, ps_r)
                            S.op("act", lambda e, ps=ps, t=t: e.copy(out=pre[:, t * 512:(t + 1) * 512], in_=ps[:]),
                                 reads=[ps_r], writes=[pre_r])
                        w0 = self.cst[:, offcw + (j * 3 + 0) * 48 + cc: offcw + (j * 3 + 0) * 48 + cc + 1]
                        w1 = self.cst[:, offcw + (j * 3 + 1) * 48 + cc: offcw + (j * 3 + 1) * 48 + cc + 1]
                        w2 = self.cst[:, offcw + (j * 3 + 2) * 48 + cc: offcw + (j * 3 + 2) * 48 + cc + 1]
                        cb = self.cst[:, offcb + j * 48 + cc: offcb + j * 48 + cc + 1]
                        S.op("dve", lambda e, w1=w1, cb=cb: e.tensor_scalar(
                            out=acc[:, 0:TP], in0=pre[:], scalar1=w1, scalar2=cb, op0=ALU.mult, op1=ALU.add),
                            reads=[pre_r, self.cst_r], writes=[acc_r])
                        for s in range(nseq):
                            a0, a1 = s * L, (s + 1) * L
                            S.op("dve", lambda e, w0=w0, a0=a0, a1=a1: e.scalar_tensor_tensor(
                                out=acc[:, a0 + 1:a1], in0=pre[:, a0:a1 - 1], scalar=w0, in1=acc[:, a0 + 1:a1],
                                op0=ALU.mult, op1=ALU.add), reads=[pre_r, acc_r, self.cst_r], writes=[acc_r])
                            S.op("dve", lambda e, w2=w2, a0=a0, a1=a1: e.scalar_tensor_tensor(
                                out=acc[:, a0:a1 - 1], in0=pre[:, a0 + 1:a1], scalar=w2, in1=acc[:, a0:a1 - 1],
                                op0=ALU.mult, op1=ALU.add), reads=[pre_r, acc_r, self.cst_r], writes=[acc_r])
                        S.op("act", lambda e, ci=ci: e.activation(out=xc[:, ci, :], in_=acc[:, 0:TP], func=AF.Silu),
                             reads=[acc_r], writes=[xc_r])
                    for half in range(2):
                        wz, wz_r = self.load_cols(Win, g * 512 + half * 256, ncols=256, key=f"ssm_in{j}")
                        for nt in range(NT):
                            ps, ps_r = self.psum()
                            self.proj_tm(wz, wz_r, nt, ps, ps_r, 256)
                            S.op("act", lambda e, ps=ps, nt=nt, half=half: e.activation(
                                out=z_tm[:, nt, half * 256:(half + 1) * 256], in_=ps[:, 0:256], func=AF.Silu),
                                reads=[ps_r], writes=[z_r])
                    for s in range(nseq):
                        for d_ in range(2):
                            if self.sample:
                                S.dma("sp", hT[:], self.st_in[j, d_, :, g * 512:(g + 1) * 512], writes=[hT_r])
                            else:
                                S.op("dve", lambda e: e.memset(hT[:], 0.0), writes=[hT_r])
                            S.op("act", lambda e: e.copy(out=hTb[:], in_=hT[:]), reads=[hT_r], writes=[hTb_r])
                            order = range(nch) if d_ == 0 else range(nch - 1, -1, -1)
                            for c in order:
                                nt = s * nch + c
                                tk = slice(nt * 128, (nt + 1) * 128)
                                dsl = slice(d_ * 8, d_ * 8 + 8)
                                px, px_r = self.psum()
                                for q in range(4):
                                    S.op("pe", lambda e, q=q, px=px: e.matmul(
                                        px[:, q * 128:(q + 1) * 128], xc[:, q, tk], self.ident_bf[:],
                                        start=True, stop=True), reads=[xc_r, self.cbf_r], writes=[px_r], inc=(q == 3))
                                S.op("dve", lambda e, px=px: e.tensor_tensor(
                                    out=xdt[:].rearrange("p (h q) -> p h q", h=8),
                                    in0=px[:].rearrange("p (h q) -> p h q", h=8),
                                    in1=dt[:, nt, dsl].unsqueeze(2).to_broadcast([128, 8, 64]), op=ALU.mult),
                                    reads=[px_r, dt_r], writes=[xdt_r])
                                if d_ == 0:
                                    dsk = self.cst[:, offd + j * 64 + g * 8: offd + j * 64 + g * 8 + 8]
                                    S.op("dve", lambda e, px=px, dsk=dsk: e.tensor_tensor(
                                        out=y32[:].rearrange("p (h q) -> p h q", h=8),
                                        in0=px[:].rearrange("p (h q) -> p h q", h=8),
                                        in1=dsk.unsqueeze(2).to_broadcast([128, 8, 64]), op=ALU.mult),
                                        reads=[px_r, self.cst_r], writes=[y32_r])
                                else:
                                    S.op("act", lambda e: e.copy(out=y32[:], in_=yf[:, nt, :]),
                                         reads=[yf_r], writes=[y32_r])
                                pb, pb_r = self.psum()
                                S.op("pe", lambda e, pb=pb: e.matmul(pb[:, 0:128], xc[:, 4, tk], self.ident_bf[:],
                                                                     start=True, stop=True),
                                     reads=[xc_r, self.cbf_r], writes=[pb_r])
                                S.op("act", lambda e, pb=pb: e.copy(out=B_tm[:], in_=pb[:, 0:128]),
                                     reads=[pb_r], writes=[Btm_r])
                                pc, pc_r = self.psum()
                                S.op("pe", lambda e, pc=pc: e.matmul(pc[:, 0:128], xc[:, 4, tk], xc[:, 5, tk],
                                                                     start=True, stop=True),
                                     reads=[xc_r], writes=[pc_r])
                                S.op("dve", lambda e, pc=pc: e.tensor_tensor(out=mCB[:], in0=pc[:, 0:128], in1=U[d_],
                                                                             op=ALU.mult),
                                     reads=[pc_r, self.cst_r], writes=[mCB_r])
                                iend = 127 if d_ == 0 else 0
                                for quad in range(2):
                                    hs = slice(d_ * 8 + quad * 4, d_ * 8 + quad * 4 + 4)
                                    rseg, rseg_r = self.scratch32()
                                    Lt, Lt_r = self.scratch32()
                                    S.op("dve", lambda e, hs=hs, rseg=rseg: e.tensor_tensor(
                                        out=rseg[:].rearrange("p (h q) -> p h q", h=4),
                                        in0=U[d_].unsqueeze(1).to_broadcast([128, 4, 128]),
                                        in1=dtA[:, nt, hs].unsqueeze(2).to_broadcast([128, 4, 128]), op=ALU.mult),
                                        reads=[self.cst_r, dtA_r], writes=[rseg_r])
                                    pl, pl_r = self.psum()
                                    S.op("pe", lambda e, pl=pl, rseg=rseg: e.matmul(pl[:], SLU[d_], rseg[:], start=True, stop=True),
                                         reads=[self.cst_r, rseg_r], writes=[pl_r])
                                    S.op("act", lambda e, pl=pl, Lt=Lt: e.activation(out=Lt[:], in_=pl[:], func=AF.Exp),
                                         reads=[pl_r], writes=[Lt_r])
                                    S.op("dve", lambda e, quad=quad, Lt=Lt: e.tensor_tensor(
                                        out=MT[:, quad * 4:(quad + 1) * 4, :],
                                        in0=Lt[:].rearrange("p (h q) -> p h q", h=4),
                                        in1=mCB[:].unsqueeze(1).to_broadcast([128, 4, 128]), op=ALU.mult),
                                        reads=[Lt_r, mCB_r], writes=[MT_r])
                                    S.op("dve", lambda e, quad=quad, Lt=Lt: e.tensor_tensor(
                                        out=xdtd[:, quad * 256:(quad + 1) * 256].rearrange("p (h q) -> p h q", h=4),
                                        in0=xdt[:, quad * 256:(quad + 1) * 256].rearrange("p (h q) -> p h q", h=4),
                                        in1=Lt[:].rearrange("p (h q) -> p h q", h=4)[:, :, iend:iend + 1]
                                        .to_broadcast([128, 4, 64]), op=ALU.mult),
                                        reads=[xdt_r, Lt_r], writes=[xdtd_r])
                                py, py_r = self.psum()
                                for hh in range(8):
                                    S.op("pe", lambda e, hh=hh, py=py: e.matmul(
                                        py[:, hh * 64:(hh + 1) * 64], MT[:, hh, :], xdt[:, hh * 64:(hh + 1) * 64],
                                        start=True, stop=True), reads=[MT_r, xdt_r], writes=[py_r], inc=(hh == 7))
                                pf, pf_r = self.psum()
                                S.op("pe", lambda e, pf=pf: e.matmul(pf[:], xc[:, 5, tk], hTb[:], start=True, stop=True),
                                     reads=[xc_r, hTb_r], writes=[pf_r])
                                S.op("dve", lambda e, py=py: e.tensor_tensor(out=y32[:], in0=py[:], in1=y32[:], op=ALU.add),
                                     reads=[py_r, y32_r], writes=[y32_r])
                                sc, sc_r = self.scratch32()
                                S.op("dve", lambda e, pf=pf, sc=sc: e.tensor_tensor(
                                    out=sc[:].rearrange("p (h q) -> p h q", h=8),
                                    in0=pf[:].rearrange("p (h q) -> p h q", h=8),
                                    in1=eac[:, nt, dsl].unsqueeze(2).to_broadcast([128, 8, 64]), op=ALU.mult),
                                    reads=[pf_r, eac_r], writes=[sc_r])
                                if d_ == 0:
                                    S.op("dve", lambda e, sc=sc: e.tensor_tensor(out=yf[:, nt, :], in0=sc[:], in1=y32[:],
                                                                                op=ALU.add),
                                         reads=[sc_r, y32_r], writes=[yf_r])
                                else:
                                    S.op("dve", lambda e, sc=sc: e.tensor_tensor(out=y32[:], in0=sc[:], in1=y32[:],
                                                                                op=ALU.add),
                                         reads=[sc_r, y32_r], writes=[y32_r])
                                    S.op("dve", lambda e: e.tensor_tensor(out=yz[:], in0=y32[:], in1=z_tm[:, nt, :],
                                                                          op=ALU.mult),
                                         reads=[y32_r, z_r], writes=[yz_r])
                                    S.op("act", lambda e: e.activation(out=junk[:, 0:512], in_=yz[:], func=AF.Square,
                                                                       accum_out=ssq[:, nt, g:g + 1]),
                                         reads=[yz_r, ssq_r], writes=[junk_r, ssq_r])
                                    for q in range(4):
                                        kc = g * 4 + q
                                        self.transpose_to(yz[:, q * 128:(q + 1) * 128], [yz_r], yzT[:, q, :], yzT_r,
                                                          scale_ap=self.cst[:, offng + j * 32 + kc: offng + j * 32 + kc + 1],
                                                          scale_r=self.cst_r)
                                    S.dma("sp", self.yz_scr[g * 4:(g + 1) * 4, :, nt * 128:(nt + 1) * 128]
                                          .rearrange("q p t -> p q t"), yzT, reads=[yzT_r], store=True)
                                pS, pS_r = self.psum()
                                S.op("pe", lambda e, pS=pS: e.matmul(pS[:], B_tm[:], xdtd[:], start=True, stop=True),
                                     reads=[Btm_r, xdtd_r], writes=[pS_r])
                                S.op("dve", lambda e: e.tensor_tensor(
                                    out=hT[:].rearrange("p (h q) -> p h q", h=8),
                                    in0=hT[:].rearrange("p (h q) -> p h q", h=8),
                                    in1=cd[:, nt, dsl].unsqueeze(2).to_broadcast([128, 8, 64]), op=ALU.mult),
                                    reads=[hT_r, cd_r], writes=[hT_r])
                                S.op("dve", lambda e, pS=pS: e.tensor_tensor(out=hT[:], in0=hT[:], in1=pS[:], op=ALU.add),
                                     reads=[hT_r, pS_r], writes=[hT_r])
                                S.op("act", lambda e: e.copy(out=hTb[:], in_=hT[:]), reads=[hT_r], writes=[hTb_r])
                            if not self.sample:
                                S.dma("sp", self.st_out[j, d_, s, :, g * 512:(g + 1) * 512], hT[:],
                                      reads=[hT_r], store=True)
                S.barrier()
            with contextlib.ExitStack() as os_:
                ob = lambda n, s, d: self.sb("so_" + n, s, d, os_)
                yzt = ob("yzt", [128, 32, 512], BF16); yzt_r = Res("syzt")
                rs = ob("rs", [128, NT], F32); rs_r = Res("srs")
                dg = ob("dg", [128, 128], F32); dg_r = Res("sdg")
                S._wait(S.engs["sp"], list(S.store_toks.values()))
                S.op("dve", lambda e: e.reduce_sum(out=rs[:], in_=ssq[:], axis=AX.X), reads=[ssq_r], writes=[rs_r])
                S.op("dve", lambda e: e.tensor_scalar(out=rs[:], in0=rs[:], scalar1=1.0 / DI, scalar2=EPS,
                                                      op0=ALU.mult, op1=ALU.add), reads=[rs_r], writes=[rs_r])
                S.op("act", lambda e: e.sqrt(out=rs[:], in_=rs[:]), reads=[rs_r], writes=[rs_r])
                S.op("dve", lambda e: e.reciprocal(out=rs[:], in_=rs[:]), reads=[rs_r], writes=[rs_r])
                for nt in range(NT):
                    S.op("dve", lambda e, nt=nt: e.tensor_scalar(out=dg[:], in0=self.cs("ident"), scalar1=rs[:, nt:nt + 1],
                                                                 scalar2=None, op0=ALU.mult),
                         reads=[self.cst_r, rs_r], writes=[dg_r])
                    ps, ps_r = self.psum()
                    S.op("pe", lambda e, ps=ps: e.matmul(ps[:, 0:128], self.cs("ones"), dg[:], start=True, stop=True),
                         reads=[self.cst_r, dg_r], writes=[ps_r])
                    S.op("act", lambda e, ps=ps, nt=nt: e.copy(out=self.rstd[:, nt * 128:(nt + 1) * 128], in_=ps[:, 0:128]),
                         reads=[ps_r], writes=[self.rstd_r[nt // 4]])
                for t in range(self.TT):
                    S.dma("sp", yzt[:], self.yz_scr[:, :, t * 512:(t + 1) * 512].rearrange("k p t -> p k t"),
                          writes=[yzt_r])
                    for oc in range(DC):
                        wo, wo_r = self.load_cols(Wout, oc * 128, rows=DI, key=f"ssm_out{j}")
                        po, po_r = self.psum()
                        for kc in range(32):
                            S.op("pe", lambda e, kc=kc, po=po, wo=wo: e.matmul(po[:], wo[:, kc, :], yzt[:, kc, :],
                                                                               start=(kc == 0), stop=(kc == 31)),
                                 reads=[wo_r, yzt_r], writes=[po_r], inc=(kc == 31))
                        sc, sc_r = self.scratch32()
                        S.op("dve", lambda e, po=po, sc=sc, oc=oc, t=t: e.scalar_tensor_tensor(
                            out=sc[:], in0=po[:], scalar=self.modcol(i, 5, oc), in1=self.rstd[:, t * 512:(t + 1) * 512],
                            op0=ALU.mult, op1=ALU.mult), reads=[po_r, self.mods_r, self.rstd_r[t]], writes=[sc_r])
                        S.op("dve", lambda e, sc=sc, oc=oc, t=t: e.tensor_tensor(
                            out=self.x[:, oc, t * 512:(t + 1) * 512], in0=sc[:], in1=self.x[:, oc, t * 512:(t + 1) * 512],
                            op=ALU.add), reads=[sc_r, self.x_r[oc][t]], writes=[self.x_r[oc][t]])
                S.barrier()


WEIGHT_KEYS = ("w_ada", "ffn1_w_in", "ffn2_w_in", "ffn1_w_out", "ffn2_w_out", "ssm_w_in", "ssm_w_out",
               "diff_w_qkv", "diff_w_out", "win_w_qkv", "win_w_out")


def make_in_maps(inp, cores=range(N_CORES)):
    consts = pack_consts(inp)
    bada = fm(inp["b_ada"].reshape(-1))
    rope = rope_tables()
    wm = win_mask()
    maps = []
    for core in cores:
        b = core // 4
        xs = np.concatenate([inp["x_prompt"][2 * core], inp["x_prompt"][2 * core + 1], inp["x_sample"][b]], axis=0)
        cv = np.stack([fm(inp["c_ctx"]), fm(inp["c"][b])], axis=2).reshape(128, DC * 2)
        st = np.stack([np.stack([inp[f"state_l{l}_{d}"][b].reshape(DI, 128).T for d in ("fwd", "bwd")])
                       for l in (0, 3)])
        m = {"xT": np.ascontiguousarray(xs.T), "cvec": np.ascontiguousarray(cv), "consts": consts,
             "b_ada_fm": bada, "rope_cs": rope, "wmask": wm, "st_in": np.ascontiguousarray(st),
             "kc1T": np.ascontiguousarray(inp["cache_l1_k"][b].reshape(256, D).T),
             "vc1": np.ascontiguousarray(inp["cache_l1_v"][b].reshape(256, D)),
             "kc2T": np.ascontiguousarray(inp["cache_l2_k"][b].reshape(256, 512).T),
             "vc2": np.ascontiguousarray(inp["cache_l2_v"][b].reshape(256, 512))}
        for k in WEIGHT_KEYS:
            m[k] = inp[k]
        maps.append(m)
    return maps


def assemble(results):
    B, S_ = 16, 256
    y_prompt = np.zeros((B, S_, D), np.float32)
    y_sample = np.zeros((2, LS, D), np.float32)
    st = [np.zeros((B, 64, 64, 128), np.float32) for _ in range(4)]
    k1 = np.zeros((B, S_, 2, 8, 128), np.float32)
    v1 = np.zeros((B, S_, 8, 256), np.float32)
    k2 = np.zeros((B, S_, 4, 128), np.float32)
    v2 = np.zeros((B, S_, 4, 128), np.float32)
    for core, r in enumerate(results):
        y = r["yT"].T
        y_prompt[2 * core] = y[0:256]
        y_prompt[2 * core + 1] = y[256:512]
        if core % 4 == 0:
            y_sample[core // 4] = y[512:]
        so = r["st_out"]
        for l in range(2):
            for d in range(2):
                for s in range(2):
                    st[l * 2 + d][2 * core + s] = so[l, d, s].T.reshape(64, 64, 128)
        k1t = r["k1T"].T
        k2t = r["k2T"].T
        for s in range(2):
            k1[2 * core + s] = k1t[s * 256:(s + 1) * 256].reshape(256, 2, 8, 128)
            v1[2 * core + s] = r["v1"][s * 256:(s + 1) * 256].reshape(256, 8, 256)
            k2[2 * core + s] = k2t[s * 256:(s + 1) * 256].reshape(256, 4, 128)
            v2[2 * core + s] = r["v2"][s * 256:(s + 1) * 256].reshape(256, 4, 128)
    return (y_prompt, y_sample, st[0], st[1], k1, v1, k2, v2, st[2], st[3])


def kernel(**inp):
    inp = {k: np.asarray(v) for k, v in inp.items()}
    nc = Builder().build()
    maps = make_in_maps(inp)
    res = run_bass_kernel_spmd(nc, maps, core_ids=list(range(N_CORES)))
    return assemble(res.results)
```

```python
import contextlib
import math
import os
import numpy as np
import concourse.bass as bass
import concourse.mybir as mybir
from concourse.bass_utils import run_bass_kernel_spmd

F32 = mybir.dt.float32
BF16 = mybir.dt.bfloat16
AF = mybir.ActivationFunctionType
ALU = mybir.AluOpType
AX = mybir.AxisListType

D = 2048
DC = 16
NP_SEQ = 2
LP = 256
LS = 1024
T = NP_SEQ * LP + LS
NTT = T // 512
DEPTH = 4
D_FF = 5632
FC = D_FF // 128
N_MOD = 9
EPS = 1e-6
N_CORES = 8

SYNC_ENGS = set(os.environ.get('KSYNC', 'pe,act,dve,pool,sp').split(','))
KSTOP = int(os.environ.get('KSTOP', '9'))
KPART = os.environ.get('KPART', 'kv')
KSKIP = os.environ.get('KSKIP', '')


class Res:
    __slots__ = ("name", "w", "r", "dsem", "dval")

    def __init__(self, name):
        self.name = name
        self.w = None
        self.r = {}
        self.dsem = None
        self.dval = 0


class Eng:
    def __init__(self, name, handle, sem):
        self.name = name
        self.h = handle
        self.sem = sem
        self.count = 0
        self.waited = {}
        self.pend_r = []
        self.pend_w = []


class Sched:
    def __init__(self, nc, es):
        self.nc = nc
        self.es = es
        self.engs = {}
        for name, h in (("pe", nc.tensor), ("act", nc.scalar), ("dve", nc.vector),
                        ("pool", nc.gpsimd), ("sp", nc.sync)):
            sem = es.enter_context(nc.semaphore("sem_" + name))
            self.engs[name] = Eng(name, h, sem)
        self.sem_ids = {}
        self.store_toks = {}
        self.n_ops = 0

    def _wait(self, E, deps):
        for (sem, val) in deps:
            if sem is E.sem and E.name not in SYNC_ENGS:
                continue
            k = id(sem)
            if E.waited.get(k, 0) >= val:
                continue
            E.h.wait_ge(sem, val)
            E.waited[k] = val

    @staticmethod
    def _deps(reads, writes):
        deps = []
        for r in reads:
            if r.w is not None:
                deps.append(r.w)
        for w in writes:
            if w.w is not None:
                deps.append(w.w)
            for sem_k, (sem, val) in w.r.items():
                deps.append((sem, val))
        return deps

    def op(self, eng, fn, reads=(), writes=(), inc=True):
        E = self.engs[eng]
        self._wait(E, self._deps(reads, writes))
        ins = fn(E.h)
        self.n_ops += 1
        E.pend_r.extend(reads)
        E.pend_w.extend(writes)
        if inc:
            E.count += 1
            ins.then_inc(E.sem, 1)
            tok = (E.sem, E.count)
            for r in E.pend_r:
                r.r[id(E.sem)] = tok
            for w in E.pend_w:
                w.w = tok
                w.r = {}
            E.pend_r = []
            E.pend_w = []
        return ins

    def dma(self, eng, out, in_, reads=(), writes=(), store=False):
        E = self.engs[eng]
        self._wait(E, self._deps(reads, writes))
        res = (list(writes) + list(reads))[0]
        if res.dsem is None:
            if res.name not in self.sem_ids:
                self.sem_ids[res.name] = [self.es.enter_context(self.nc.semaphore("dsem_" + res.name)), 0]
            res.dsem, res.dval = self.sem_ids[res.name]
        res.dval += 16
        self.sem_ids[res.name][1] = res.dval
        E.h.dma_start(out=out, in_=in_).then_inc(res.dsem, 16)
        self.n_ops += 1
        tok = (res.dsem, res.dval)
        for r in reads:
            r.r[id(res.dsem)] = tok
        for w in writes:
            w.w = tok
            w.r = {}
        if store:
            self.store_toks[id(res.dsem)] = tok

    def barrier(self):
        toks = [(E.sem, E.count) for E in self.engs.values() if E.count > 0]
        toks += list(self.store_toks.values())
        for E in self.engs.values():
            assert not E.pend_r and not E.pend_w, "pending ops at barrier"
            self._wait(E, toks)

    def finish(self):
        E = self.engs["sp"]
        self._wait(E, list(self.store_toks.values()))
        toks = [(e.sem, e.count) for e in self.engs.values() if e.count > 0]
        self._wait(E, toks)


DI = 4096
NG = 8
N_SSM = 2
SSM_IN = 10368
QBL = 128
CONST_SPEC = (("norm_g", DEPTH * 3 * DC), ("final_g", DC), ("ident", 128), ("ones", 128),
              ("Ule", 128), ("Uge", 128), ("SL", 128), ("SU", 128), ("RT", 128),
              ("conv_w", N_SSM * 3 * 48), ("conv_b", N_SSM * 48), ("ssm_norm_g", N_SSM * 32),
              ("dt_bias", N_SSM * 128), ("a_log", N_SSM * 128), ("ssm_d", N_SSM * 64),
              ("subln", 256), ("lam", 512), ("sink", 16))


def fm(vec):
    v = np.asarray(vec, np.float32).reshape(-1, 128)
    return np.ascontiguousarray(v.T)


def bc(vec):
    v = np.asarray(vec, np.float32).reshape(1, -1)
    return np.ascontiguousarray(np.broadcast_to(v, (128, v.shape[1])))


def const_layout():
    lay = {}
    n = 0
    for name, w in CONST_SPEC:
        lay[name] = (n, w)
        n += w
    return lay, n


def pack_consts(inp):
    k = np.arange(128)
    parts = {
        "norm_g": fm(inp["norm_g"].reshape(-1)),
        "final_g": fm(inp["final_norm_g"]),
        "ident": np.eye(128, dtype=np.float32),
        "ones": np.ones((128, 128), np.float32),
        "Ule": (k[:, None] <= k[None, :]).astype(np.float32),
        "Uge": (k[:, None] >= k[None, :]).astype(np.float32),
        "SL": (k[:, None] > k[None, :]).astype(np.float32),
        "SU": (k[:, None] < k[None, :]).astype(np.float32),
    }
    R = np.zeros((128, 128), np.float32)
    for d in range(128):
        if (d // 32) % 2 == 0:
            R[d, d + 32] = -1.0
        else:
            R[d, d - 32] = 1.0
    parts["RT"] = np.ascontiguousarray(R.T)
    parts["conv_w"] = fm(inp["ssm_conv_w"].reshape(-1))
    parts["conv_b"] = fm(inp["ssm_conv_b"].reshape(-1))
    parts["ssm_norm_g"] = fm(inp["ssm_norm_g"].reshape(-1))
    parts["dt_bias"] = bc(inp["ssm_dt_bias"].reshape(-1))
    parts["a_log"] = bc(inp["ssm_a_log"].reshape(-1))
    parts["ssm_d"] = bc(inp["ssm_d"].reshape(-1))
    parts["subln"] = bc(inp["diff_subln_g"].reshape(-1))
    parts["lam"] = bc(inp["diff_lambda"].reshape(-1))
    parts["sink"] = bc(inp["win_sink"].reshape(-1))
    lay, n = const_layout()
    arrs = []
    for name, w in CONST_SPEC:
        a = parts[name]
        assert a.shape == (128, w), (name, a.shape, w)
        arrs.append(a)
    return np.ascontiguousarray(np.concatenate(arrs, axis=1))


def rope_tables():
    L, GW, nf = LS, 64, 32
    rows = L // GW
    row = np.repeat(np.arange(rows, dtype=np.float32), GW)
    col = np.tile(np.arange(GW, dtype=np.float32), rows)
    inv = (np.float32(10000.0) ** (-np.arange(nf, dtype=np.float32) / np.float32(nf))).astype(np.float32)
    ar = (row[:, None] * inv).astype(np.float32)
    ac = (col[:, None] * inv).astype(np.float32)
    ang = np.concatenate([ar, ar, ac, ac], axis=1)
    cs = np.stack([np.cos(ang).T, np.sin(ang).T], axis=1).astype(np.float32)
    return np.ascontiguousarray(cs)


def win_mask():
    qi = np.arange(128)[:, None]
    kj = np.arange(384)[None, :]
    ok = np.abs(qi + 128 - kj) <= 128
    return np.where(ok, 0.0, -30000.0).astype(np.float32)


TPM = 1024


class Builder:
    def __init__(self, layers=None, passes=("B", "A"), ffn=True, mix=True):
        self.layers = list(range(DEPTH)) if layers is None else layers
        self.pass_names = passes
        self.do_ffn = ffn
        self.do_mix = mix
        self.nc = bass.Bass("TRN2", target_bir_lowering=False)

    def dram_in(self, name, shape, dt=F32):
        return self.nc.dram_tensor(name, list(shape), dt, kind="ExternalInput").ap()

    def dram_out(self, name, shape, dt=F32):
        return self.nc.dram_tensor(name, list(shape), dt, kind="ExternalOutput").ap()

    def sb(self, name, shape, dt, es=None):
        self.uid = getattr(self, "uid", 0) + 1
        return (es or self.es).enter_context(self.nc.sbuf_tensor(f"{name}_{self.uid}", list(shape), dt))

    def psum(self):
        i = self.ps_next
        if i == self.ps_skip:
            i = (i + 1) % 8
        self.ps_next = (i + 1) % 8
        return self.ps_t[i], self.ps_r[i]

    def wslot(self):
        i = self.w_next
        self.w_next = (i + 1) % self.NW
        return self.w_t[i], self.w_r[i]

    def _scr_tile(self, key):
        if key not in self.mx_keys:
            n = len(self.mx_keys)
            if n // 128 >= len(self.mx_scr):
                self.mx_scr.append(self.nc.dram_tensor(f"mxs{len(self.mx_scr)}", [128, 128, 4096], BF16,
                                                       kind="Internal").ap())
            self.mx_keys[key] = n
            fresh = True
        else:
            fresh = False
        n = self.mx_keys[key]
        return self.mx_scr[n // 128][n % 128], fresh

    def _load(self, t, r, view, src, nused, key):
        S = self.S
        if key is None or not self.reuse:
            S.dma("pool", view, src, writes=[r])
            return
        if self.sample:
            S.dma("pool", view, src, writes=[r])
            tile, fresh = self._scr_tile(key)
            if fresh:
                S.dma("sp", tile[:, 0:nused], t[:, 0:nused], reads=[r], store=True)
        else:
            tile, fresh = self._scr_tile(key)
            assert not fresh, key
            S.dma("sp", t[:, 0:nused], tile[:, 0:nused], writes=[r])

    def load_cols(self, W, col0, ncols=128, rows=D, key=None):
        t, r = self.wslot()
        kc = rows // 128
        assert kc * ncols <= 4096
        view = t[:, 0:kc * ncols].rearrange("p (c n) -> p c n", c=kc)
        src = W[:, col0:col0 + ncols].rearrange("(c p) n -> p c n", p=128)
        self._load(t, r, view, src, kc * ncols, None if key is None else (key, "c", col0, ncols))
        return view, r

    def load_rows(self, W, row0, ncols=D, col0=0, key=None):
        t, r = self.wslot()
        view = t[:, 0:ncols]
        self._load(t, r, view, W[row0:row0 + 128, col0:col0 + ncols], ncols,
                   None if key is None else (key, "r", row0, ncols))
        return view, r

    def scratch32(self):
        i = self.sc_next
        self.sc_next = (i + 1) % self.NSC
        return self.sc32[i], self.sc32_r[i]

    def cs(self, name, a=0, b=None):
        off, w = self.lay[name]
        if b is None:
            b = w
        return self.cst[:, off + a:off + b]

    def build(self):
        nc = self.nc
        with contextlib.ExitStack() as es:
            self.es = es
            self.S = S = Sched(nc, es)
            self.declare_io()
            self.alloc()
            self.load_consts()
            self.modulation_all()
            for pn in self.pass_names:
                self.set_pass(pn)
                self.load_x()
                for i in self.layers:
                    self.layer(i)
                self.final()
                S.barrier()
            S.finish()
        return nc

    def set_pass(self, pn):
        if pn == "A":
            self.tok0, self.TP, self.nseq, self.L, self.sample, self.v = 0, 512, 2, 256, False, 0
        else:
            self.tok0, self.TP, self.nseq, self.L, self.sample, self.v = 512, 1024, 1, 1024, True, 1
        self.TT = self.TP // 512
        self.NT = self.TP // 128

    def declare_io(self):
        lay, ncst = const_layout()
        self.lay = lay
        di = self.dram_in
        self.xT_in = di("xT", [D, T])
        self.cvec_in = di("cvec", [128, DC * 2])
        self.consts_in = di("consts", [128, ncst])
        self.bada_in = di("b_ada_fm", [128, DEPTH * N_MOD * DC])
        self.rope_in = di("rope_cs", [128, 2, LS])
        self.wmask_in = di("wmask", [128, 384])
        self.st_in = di("st_in", [2, 2, 128, DI])
        self.kc1T_in = di("kc1T", [D, 256])
        self.vc1_in = di("vc1", [256, D])
        self.kc2T_in = di("kc2T", [512, 256])
        self.vc2_in = di("vc2", [256, 512])
        self.w_ada = di("w_ada", [DEPTH, D, N_MOD * D])
        if self.do_ffn:
            self.ffn_w_in = [di("ffn1_w_in", [DEPTH, D, 2 * D_FF]), di("ffn2_w_in", [DEPTH, D, 2 * D_FF])]
            self.ffn_w_out = [di("ffn1_w_out", [DEPTH, D_FF, D]), di("ffn2_w_out", [DEPTH, D_FF, D])]
        kinds = {i % 3 for i in self.layers} if self.do_mix else set()
        if 0 in kinds:
            self.ssm_w_in = di("ssm_w_in", [N_SSM, D, SSM_IN])
            self.ssm_w_out = di("ssm_w_out", [N_SSM, DI, D])
        if 1 in kinds:
            self.diff_w_qkv = di("diff_w_qkv", [1, D, 6144])
            self.diff_w_out = di("diff_w_out", [1, D, D])
        if 2 in kinds:
            self.win_w_qkv = di("win_w_qkv", [1, D, 3072])
            self.win_w_out = di("win_w_out", [1, D, D])
        do = self.dram_out
        self.yT_out = do("yT", [D, T])
        self.st_out = do("st_out", [2, 2, 2, 128, DI])
        self.k1T_out = do("k1T", [D, 512])
        self.v1_out = do("v1", [512, D])
        self.k2T_out = do("k2T", [512, 512])
        self.v2_out = do("v2", [512, 512])
        self.yz_scr = self.nc.dram_tensor("yz_scr", [32, 128, TPM], BF16, kind="Internal").ap()
        self.wscr = [[self.nc.dram_tensor(f"wscr_{l}_{w}", [(FC // 2) * 3, 128, 4096], BF16, kind="Internal").ap()
                      for w in range(2)] for l in range(DEPTH)]
        self.mx_keys = {}
        self.mx_scr = []
        self.reuse = ("B" in self.pass_names and "A" in self.pass_names and
                      self.pass_names.index("B") < self.pass_names.index("A"))

    def alloc(self):
        nc = self.nc
        lay, ncst = const_layout()
        self.x = self.sb("x", [128, DC, TPM], F32)
        self.x_r = [[Res(f"x{c}_{t}") for t in range(2)] for c in range(DC)]
        self.h = self.sb("h", [128, DC, TPM], BF16)
        self.h_r = [[Res(f"h{c}_{t}") for t in range(2)] for c in range(DC)]
        self.cst = self.sb("cst", [128, ncst], F32)
        self.cst_r = Res("cst")
        self.ident_bf = self.sb("ident_bf", [128, 128], BF16)
        self.ones_bf = self.sb("ones_bf", [128, 128], BF16)
        self.RT_bf = self.sb("RT_bf", [128, 128], BF16)
        self.cbf_r = Res("cbf")
        self.mods = self.sb("mods", [128, DEPTH * N_MOD * DC, 2], F32)
        self.mods_r = Res("mods")
        self.ab = self.sb("ab", [128, 3, DC], F32)
        self.ab_r = Res("ab")
        self.rstd = self.sb("rstd", [128, TPM], F32)
        self.rstd_r = [Res(f"rstd{t}") for t in range(2)]
        self.ps_t = [self.es.enter_context(nc.psum_tensor(f"ps{i}", [128, 512], F32)) for i in range(8)]
        self.ps_r = [Res(f"ps{i}") for i in range(8)]
        self.ps_next = 0
        self.ps_skip = -1
        self.csil = self.sb("csil", [128, DC, 2], BF16)
        self.csil_r = Res("csil")
        self.NW = 4
        self.w_t = [self.sb(f"w{i}", [128, 4096], BF16) for i in range(self.NW)]
        self.w_r = [Res(f"w{i}") for i in range(self.NW)]
        self.w_next = 0
        self.NSC = 3
        self.sc32 = [self.sb(f"sc32_{i}", [128, 512], F32) for i in range(self.NSC)]
        self.sc32_r = [Res(f"sc32_{i}") for i in range(self.NSC)]
        self.sc_next = 0
        self.sq = [self.sb(f"sq{i}", [128, 512], BF16) for i in range(2)]
        self.sq_r = [Res(f"sq{i}") for i in range(2)]
        self.sq_next = 0

    def load_consts(self):
        S = self.S
        S.dma("sp", self.cst[:], self.consts_in, writes=[self.cst_r])
        for dst, name in ((self.ident_bf, "ident"), (self.ones_bf, "ones"), (self.RT_bf, "RT")):
            S.op("dve", lambda e, dst=dst, name=name: e.tensor_copy(out=dst[:], in_=self.cs(name)),
                 reads=[self.cst_r], writes=[self.cbf_r])

    def load_x(self):
        S = self.S
        xin = self.xT_in.rearrange("(c p) t -> p c t", p=128)
        for c in range(DC):
            for t in range(self.TT):
                S.dma("sp", self.x[:, c, t * 512:(t + 1) * 512],
                      xin[:, c, self.tok0 + t * 512:self.tok0 + (t + 1) * 512], writes=[self.x_r[c][t]])

    def modulation_all(self):
        S = self.S
        NCC = N_MOD * DC
        with contextlib.ExitStack() as ms:
            cv = self.sb("cv", [128, DC * 2], F32, ms)
            cv_r = Res("cv")
            csil, csil_r = self.csil, self.csil_r
            bada = self.sb("bada", [128, DEPTH * NCC], F32, ms)
            bada_r = Res("bada")
            S.dma("sp", cv[:], self.cvec_in, writes=[cv_r])
            S.dma("sp", bada[:], self.bada_in, writes=[bada_r])
            S.op("act", lambda e: e.activation(out=csil[:].rearrange("p c v -> p (c v)"), in_=cv[:], func=AF.Silu),
                 reads=[cv_r], writes=[csil_r])
            first_pass_sample = (self.pass_names[0] == "B") and self.do_ffn
            for i in (self.layers[:1] if first_pass_sample else self.layers):
                W = self.w_ada[i]
                ps, ps_r = self.psum()
                for cc in range(NCC):
                    if cc % 2 == 0:
                        wv, wr = self.load_cols(W, cc * 128, ncols=256)
                    hf = (cc % 2) * 128
                    for k in range(DC):
                        S.op("pe", lambda e, k=k, wv=wv, cc=cc, ps=ps, hf=hf: e.matmul(
                            ps[:, cc * 2:cc * 2 + 2], wv[:, k, hf:hf + 128], csil[:, k, :],
                            start=(k == 0), stop=(k == DC - 1)),
                            reads=[wr, csil_r], writes=[ps_r], inc=(k == DC - 1))
                S.op("dve", lambda e, i=i, ps=ps: e.tensor_tensor(
                    out=self.mods[:, i * NCC:(i + 1) * NCC, :],
                    in0=ps[:, 0:2 * NCC].rearrange("p (c v) -> p c v", v=2),
                    in1=bada[:, i * NCC:(i + 1) * NCC].unsqueeze(2).to_broadcast([128, NCC, 2]), op=ALU.add),
                    reads=[ps_r, bada_r], writes=[self.mods_r])
            S.barrier()

    def modvec(self, i, j):
        b = (i * N_MOD + j) * DC
        return self.mods[:, b:b + DC, self.v]

    def modcol(self, i, j, c):
        b = (i * N_MOD + j) * DC + c
        return self.mods[:, b, self.v:self.v + 1]

    def rms_stats(self):
        S = self.S
        for t in range(self.TT):
            ts = slice(t * 512, (t + 1) * 512)
            ps, ps_r = self.psum()
            for c in range(DC):
                i = self.sq_next
                self.sq_next = (i + 1) % 2
                sq, sq_r = self.sq[i], self.sq_r[i]
                S.op("act", lambda e, c=c, ts=ts, sq=sq: e.activation(out=sq[:], in_=self.x[:, c, ts],
                                                                      func=AF.Square),
                     reads=[self.x_r[c][t]], writes=[sq_r])
                S.op("pe", lambda e, c=c, sq=sq, ps=ps: e.matmul(ps[:], self.ones_bf[:], sq[:],
                                                                 start=(c == 0), stop=(c == DC - 1)),
                     reads=[sq_r, self.cbf_r], writes=[ps_r], inc=True)
            S.op("dve", lambda e, ts=ts, ps=ps: e.tensor_scalar(
                out=self.rstd[:, ts], in0=ps[:], scalar1=1.0 / D, scalar2=EPS, op0=ALU.mult, op1=ALU.add),
                reads=[ps_r], writes=[self.rstd_r[t]])
            S.op("act", lambda e, ts=ts: e.sqrt(out=self.rstd[:, ts], in_=self.rstd[:, ts]),
                 reads=[self.rstd_r[t]], writes=[self.rstd_r[t]])
            S.op("dve", lambda e, ts=ts: e.reciprocal(out=self.rstd[:, ts], in_=self.rstd[:, ts]),
                 reads=[self.rstd_r[t]], writes=[self.rstd_r[t]])

    def modnorm(self, i, sub, j_shift, j_scale):
        S = self.S
        off, _ = self.lay["norm_g"]
        g = self.cst[:, off + (i * 3 + sub) * DC: off + (i * 3 + sub + 1) * DC]
        S.op("dve", lambda e: e.scalar_tensor_tensor(
            out=self.ab[:, 0, :], in0=self.modvec(i, j_scale), scalar=1.0, in1=g, op0=ALU.add, op1=ALU.mult),
            reads=[self.mods_r, self.cst_r], writes=[self.ab_r])
        self.rms_stats()
        for t in range(self.TT):
            ts = slice(t * 512, (t + 1) * 512)
            for c in range(DC):
                sc, sc_r = self.scratch32()
                S.op("dve", lambda e, c=c, ts=ts, sc=sc: e.scalar_tensor_tensor(
                    out=sc[:], in0=self.x[:, c, ts], scalar=self.ab[:, 0, c:c + 1], in1=self.rstd[:, ts],
                    op0=ALU.mult, op1=ALU.mult),
                    reads=[self.x_r[c][t], self.ab_r, self.rstd_r[t]], writes=[sc_r])
                S.op("act", lambda e, c=c, ts=ts, sc=sc: e.activation(
                    out=self.h[:, c, ts], in_=sc[:], func=AF.Identity,
                    bias=self.modcol(i, j_shift, c), scale=1.0),
                    reads=[sc_r, self.mods_r], writes=[self.h_r[c][t]])

    def ffn(self, i, which, j_gate):
        S = self.S
        W_in = self.ffn_w_in[which][i]
        W_out = self.ffn_w_out[which][i]
        S.op("dve", lambda e: e.tensor_scalar(out=self.ab[:, 2, :], in0=self.modvec(i, j_gate), scalar1=0.5,
                                              scalar2=None, op0=ALU.mult), reads=[self.mods_r], writes=[self.ab_r])
        NX = 5
        with contextlib.ExitStack() as fs:
            extra_t = [self.sb(f"wx{k}", [128, 4096], BF16, fs) for k in range(NX)]
            extra_r = [Res(f"wx{k}") for k in range(NX)]
            a_t = [self.sb(f"fa{k}", [128, 2, TPM], BF16, fs) for k in range(2)]
            a_r = [[[Res(f"fa{k}_{j}_{t}") for t in range(2)] for j in range(2)] for k in range(2)]
            save = (self.w_t, self.w_r, self.NW, self.w_next)
            self.w_t = self.w_t + extra_t
            self.w_r = self.w_r + extra_r
            self.NW = len(self.w_t)
            NCC = N_MOD * DC
            nl = None
            if which == 0 and self.sample and self.pass_names[0] == "B":
                li = self.layers.index(i)
                if li + 1 < len(self.layers):
                    nl = self.layers[li + 1]
            if nl is not None:
                bch = self.sb("bch", [128, NCC], F32, fs)
                bch_r = Res("bch")
                S.dma("sp", bch[:], self.bada_in[:, nl * NCC:(nl + 1) * NCC], writes=[bch_r])
                psm, psm_r = self.psum()
                self.ps_skip = self.ps_t.index(psm)
                Wm = self.w_ada[nl]
                NPAIR = NCC // 2
            for f in range(FC // 2):
                if nl is not None:
                    for pr_ in range(f * NPAIR // (FC // 2), (f + 1) * NPAIR // (FC // 2)):
                        wm, wm_r = self.load_cols(Wm, pr_ * 256, ncols=256)
                        for hh in range(2):
                            cc = pr_ * 2 + hh
                            for k in range(DC):
                                S.op("pe", lambda e, k=k: e.matmul(
                                    psm[:, cc * 2:cc * 2 + 2], wm[:, k, hh * 128:(hh + 1) * 128], self.csil[:, k, :],
                                    start=(k == 0), stop=(k == DC - 1)),
                                    reads=[wm_r, self.csil_r], writes=[psm_r], inc=(k == DC - 1))
                sbase = f * 3
                wscr = self.wscr[i][which]
                if self.reuse and not self.sample:
                    tiles = []
                    for kind in range(3):
                        wt_, w_r = self.wslot()
                        S.dma("sp", wt_[:, 0:4096], wscr[sbase + kind], writes=[w_r])
                        tiles.append((wt_, w_r))
                    (tg, wg_r), (tu, wu_r), (to, wo_r) = tiles
                    wg = tg[:, 0:4096].rearrange("p (c n) -> p c n", c=DC)
                    wu = tu[:, 0:4096].rearrange("p (c n) -> p c n", c=DC)
                    wo = to[:, 0:4096].rearrange("p (c n) -> p c n", c=2)
                else:
                    wg, wg_r = self.load_cols(W_in, f * 256, ncols=256)
                    wu, wu_r = self.load_cols(W_in, D_FF + f * 256, ncols=256)
                    wt_, wo_r = self.wslot()
                    wo = wt_[:, 0:2 * D].rearrange("p (c n) -> p c n", c=2)
                    S.dma("pool", wo, W_out[f * 256:(f + 1) * 256, :].rearrange("(c p) n -> p c n", p=128),
                          writes=[wo_r])
                    if self.reuse:
                        for kind, (wv_, wr_) in enumerate(((wg, wg_r), (wu, wu_r), (wo, wo_r))):
                            flat = wv_.rearrange("p c n -> p (c n)")
                            S.dma("sp", wscr[sbase + kind], flat, reads=[wr_], store=True)
                ai = f % 2
                a = a_t[ai]
                for j in range(2):
                    for t in range(self.TT):
                        ts = slice(t * 512, (t + 1) * 512)
                        pg, pg_r = self.psum()
                        for k in range(DC):
                            S.op("pe", lambda e, k=k: e.matmul(
                                pg[:], wg[:, k, j * 128:(j + 1) * 128], self.h[:, k, ts], start=(k == 0),
                                stop=(k == DC - 1)),
                                reads=[wg_r, self.h_r[k][t]], writes=[pg_r], inc=(k == DC - 1))
                        pu, pu_r = self.psum()
                        for k in range(DC):
                            S.op("pe", lambda e, k=k: e.matmul(
                                pu[:], wu[:, k, j * 128:(j + 1) * 128], self.h[:, k, ts], start=(k == 0),
                                stop=(k == DC - 1)),
                                reads=[wu_r, self.h_r[k][t]], writes=[pu_r], inc=(k == DC - 1))
                        sc, sc_r = self.scratch32()
                        S.op("act", lambda e: e.activation(out=sc[:], in_=pg[:], func=AF.Silu),
                             reads=[pg_r], writes=[sc_r])
                        S.op("dve", lambda e: e.tensor_tensor(out=a[:, j, ts], in0=sc[:], in1=pu[:], op=ALU.mult),
                             reads=[sc_r, pu_r], writes=[a_r[ai][j][t]])
                for oc in range(DC):
                    for t in range(self.TT):
                        ts = slice(t * 512, (t + 1) * 512)
                        po, po_r = self.psum()
                        for j in range(2):
                            S.op("pe", lambda e, j=j: e.matmul(
                                po[:], wo[:, j, oc * 128:(oc + 1) * 128], a[:, j, ts], start=(j == 0), stop=(j == 1)),
                                reads=[wo_r, a_r[ai][j][t]], writes=[po_r], inc=(j == 1))
                        self.x_accum(po, po_r, self.ab[:, 2, oc:oc + 1], self.ab_r, oc, t)
            if nl is not None:
                S.op("dve", lambda e: e.tensor_tensor(
                    out=self.mods[:, nl * NCC:(nl + 1) * NCC, :],
                    in0=psm[:, 0:2 * NCC].rearrange("p (c v) -> p c v", v=2),
                    in1=bch[:].unsqueeze(2).to_broadcast([128, NCC, 2]), op=ALU.add),
                    reads=[psm_r, bch_r], writes=[self.mods_r])
                self.ps_skip = -1
            S.barrier()
            self.w_t, self.w_r, self.NW, self.w_next = save

    def x_accum(self, po, po_r, scal, scal_r, oc, t, n=512, off=0):
        ts = slice(t * 512 + off, t * 512 + off + n)
        self.S.op("dve", lambda e: e.scalar_tensor_tensor(
            out=self.x[:, oc, ts], in0=po[:, 0:n], scalar=scal, in1=self.x[:, oc, ts],
            op0=ALU.mult, op1=ALU.add),
            reads=[po_r, scal_r, self.x_r[oc][t]], writes=[self.x_r[oc][t]])

    def layer(self, i):
        if self.do_ffn:
            self.modnorm(i, 0, 0, 1)
            self.ffn(i, 0, 2)
        if self.do_mix:
            self.modnorm(i, 1, 3, 4)
            self.S.barrier()
            m = i % 3
            if m == 0:
                self.ssm_mixer(i)
            elif m == 1:
                self.diff_mixer(i)
            else:
                self.win_mixer(i)
            self.S.barrier()
        if self.do_ffn:
            self.modnorm(i, 2, 6, 7)
            self.ffn(i, 1, 8)

    def final(self):
        S = self.S
        self.rms_stats()
        off, _ = self.lay["final_g"]
        yout = self.yT_out.rearrange("(c p) t -> p c t", p=128)
        for t in range(self.TT):
            ts = slice(t * 512, (t + 1) * 512)
            for c in range(DC):
                sc, sc_r = self.scratch32()
                S.op("dve", lambda e, c=c, ts=ts, sc=sc: e.scalar_tensor_tensor(
                    out=sc[:], in0=self.x[:, c, ts], scalar=self.cst[:, off + c:off + c + 1],
                    in1=self.rstd[:, ts], op0=ALU.mult, op1=ALU.mult),
                    reads=[self.x_r[c][t], self.cst_r, self.rstd_r[t]], writes=[sc_r])
                S.dma("sp", yout[:, c, self.tok0 + t * 512:self.tok0 + (t + 1) * 512], sc[:],
                      reads=[sc_r], store=True)

    def proj_fm(self, wv, wr, t, ps, ps_r, ncol=128, wcol0=0):
        ts = slice(t * 512, (t + 1) * 512)
        for k in range(DC):
            self.S.op("pe", lambda e, k=k: e.matmul(ps[0:ncol, :], wv[:, k, wcol0:wcol0 + ncol], self.h[:, k, ts],
                                                    start=(k == 0), stop=(k == DC - 1)),
                      reads=[wr, self.h_r[k][t]], writes=[ps_r], inc=(k == DC - 1))

    def proj_tm(self, wv, wr, nt, ps, ps_r, ncol, pcol0=0, wcol0=0):
        t = nt // 4
        tk = slice(nt * 128, (nt + 1) * 128)
        for k in range(DC):
            self.S.op("pe", lambda e, k=k: e.matmul(ps[:, pcol0:pcol0 + ncol], self.h[:, k, tk],
                                                    wv[:, k, wcol0:wcol0 + ncol],
                                                    start=(k == 0), stop=(k == DC - 1)),
                      reads=[wr, self.h_r[k][t]], writes=[ps_r], inc=(k == DC - 1))

    def rope_evac(self, ps, ps_r, dst, dst_r, t, cs_t, cs_r):
        S = self.S
        ts = slice(t * 512, (t + 1) * 512)
        xb = self.sq[self.sq_next]
        xb_r = self.sq_r[self.sq_next]
        self.sq_next = (self.sq_next + 1) % 2
        S.op("dve", lambda e: e.tensor_copy(out=xb[:], in_=ps[:]), reads=[ps_r], writes=[xb_r])
        pr, pr_r = self.psum()
        S.op("pe", lambda e: e.matmul(pr[:], self.RT_bf[:], xb[:], start=True, stop=True),
             reads=[xb_r, self.cbf_r], writes=[pr_r])
        s1, s1_r = self.scratch32()
        s2, s2_r = self.scratch32()
        S.op("dve", lambda e: e.tensor_tensor(out=s1[:], in0=ps[:], in1=cs_t[:, 0, ts], op=ALU.mult),
             reads=[ps_r, cs_r], writes=[s1_r])
        S.op("dve", lambda e: e.tensor_tensor(out=s2[:], in0=pr[:], in1=cs_t[:, 1, ts], op=ALU.mult),
             reads=[pr_r, cs_r], writes=[s2_r])
        S.op("dve", lambda e: e.tensor_tensor(out=dst, in0=s1[:], in1=s2[:], op=ALU.add),
             reads=[s1_r, s2_r], writes=[dst_r])

    def attn_tile(self, qT_ap, kparts, vparts, scale, sink_ap, W):
        S = self.S
        P, P_r = W["P"], W["P_r"]
        PT, PT_r = W["PT"], W["PT_r"]
        st, st_r = W["st"], W["st_r"]
        dv = W["dv"]
        srcs = []
        col = 0
        for j, (kT_ap, n, mask_ap) in enumerate(kparts):
            ps, ps_r = self.psum()
            S.op("pe", lambda e, ps=ps, kT_ap=kT_ap, n=n: e.matmul(ps[:, 0:n], qT_ap, kT_ap, start=True, stop=True),
                 reads=W["q_reads"] + W["k_reads"], writes=[ps_r])
            if mask_ap is not None:
                sc, sc_r = self.scratch32()
                S.op("dve", lambda e, sc=sc, ps=ps, n=n, mask_ap=mask_ap: e.tensor_tensor(
                    out=sc[:, 0:n], in0=ps[:, 0:n], in1=mask_ap, op=ALU.add),
                    reads=[ps_r, W["mask_r"]], writes=[sc_r])
                src, src_r = sc, sc_r
            else:
                sc, sc_r = self.scratch32()
                S.op("dve", lambda e, sc=sc, ps=ps, n=n: e.tensor_copy(out=sc[:, 0:n], in_=ps[:, 0:n]),
                     reads=[ps_r], writes=[sc_r])
                src, src_r = sc, sc_r
            S.op("dve", lambda e, src=src, n=n, j=j: e.reduce_max(out=st[:, j:j + 1], in_=src[:, 0:n], axis=AX.X),
                 reads=[src_r], writes=[st_r])
            srcs.append((src, src_r, n, col))
            col += n
        nk = col
        for j in range(1, len(kparts)):
            S.op("dve", lambda e, j=j: e.tensor_tensor(out=st[:, 0:1], in0=st[:, 0:1], in1=st[:, j:j + 1], op=ALU.max),
                 reads=[st_r], writes=[st_r])
        if sink_ap is None:
            S.op("dve", lambda e: e.tensor_scalar(out=st[:, 4:5], in0=st[:, 0:1], scalar1=-scale, scalar2=None,
                                                  op0=ALU.mult), reads=[st_r], writes=[st_r])
        else:
            S.op("dve", lambda e: e.tensor_scalar(out=st[:, 4:5], in0=st[:, 0:1], scalar1=-scale,
                                                  scalar2=W["nsink_ap"], op0=ALU.mult, op1=ALU.min),
                 reads=[st_r, W["nsink_r"]], writes=[st_r])
        S.op("dve", lambda e: e.memset(st[:, 8:8 + len(kparts) + 1], 0.0), writes=[st_r])
        for j, (src, src_r, n, c0) in enumerate(srcs):
            S.op("act", lambda e, src=src, n=n, c0=c0, j=j: e.activation(
                out=P[:, c0:c0 + n], in_=src[:, 0:n], func=AF.Exp, bias=st[:, 4:5], scale=scale,
                accum_out=st[:, 8 + j:9 + j]),
                reads=[src_r, st_r], writes=[P_r, st_r])
        ns = len(kparts)
        if sink_ap is not None:
            S.op("act", lambda e: e.activation(out=st[:, 8 + ns:9 + ns], in_=sink_ap, func=AF.Exp,
                                               bias=st[:, 4:5], scale=1.0),
                 reads=[st_r, W["nsink_r"]], writes=[st_r])
            ns += 1
        S.op("dve", lambda e: e.reduce_sum(out=st[:, 5:6], in_=st[:, 8:8 + ns], axis=AX.X),
             reads=[st_r], writes=[st_r])
        S.op("dve", lambda e: e.reciprocal(out=st[:, 6:7], in_=st[:, 5:6]), reads=[st_r], writes=[st_r])
        nkt = nk // 128
        for b0 in range(0, nkt, 4):
            nb = min(4, nkt - b0)
            pt, pt_r = self.psum()
            for jj in range(nb):
                kt = b0 + jj
                S.op("pe", lambda e, pt=pt, jj=jj, kt=kt: e.matmul(
                    pt[:, jj * 128:(jj + 1) * 128], P[:, kt * 128:(kt + 1) * 128], self.ident_bf[:],
                    start=True, stop=True), reads=[P_r, self.cbf_r], writes=[pt_r], inc=(jj == nb - 1))
            S.op("act", lambda e, pt=pt, b0=b0, nb=nb: e.copy(
                out=PT[:, b0 * 128:(b0 + nb) * 128], in_=pt[:, 0:nb * 128]),
                reads=[pt_r], writes=[PT_r])
        po, po_r = self.psum()
        for kt in range(nkt):
            S.op("pe", lambda e, kt=kt: e.matmul(po[:, 0:dv], PT[:, kt * 128:(kt + 1) * 128], vparts[kt],
                                                 start=(kt == 0), stop=(kt == nkt - 1)),
                 reads=[PT_r] + W["v_reads"], writes=[po_r], inc=(kt == nkt - 1))
        return po, po_r

    def transpose_to(self, src_ap, src_reads, dst_ap, dst_r, ncols=128, scale_ap=None, scale_r=None):
        S = self.S
        pt, pt_r = self.psum()
        S.op("pe", lambda e: e.matmul(pt[0:ncols, 0:128], src_ap, self.ident_bf[:], start=True, stop=True),
             reads=list(src_reads) + [self.cbf_r], writes=[pt_r])
        if scale_ap is None:
            S.op("act", lambda e: e.copy(out=dst_ap, in_=pt[0:ncols, 0:128]), reads=[pt_r], writes=[dst_r])
        else:
            S.op("dve", lambda e: e.tensor_scalar(out=dst_ap, in0=pt[0:ncols, 0:128], scalar1=scale_ap, scalar2=None,
                                                  op0=ALU.mult), reads=[pt_r, scale_r], writes=[dst_r])

    def win_mixer(self, i):
        S = self.S
        Wq = self.win_w_qkv[0]
        Wo = self.win_w_out[0]
        TP, NT, L = self.TP, self.NT, self.L
        scale = 128 ** -0.5
        koff = 256 if self.sample else 0
        with contextlib.ExitStack() as ms:
            sb = lambda n, s, d: self.sb("wn_" + n, s, d, ms)
            kT = sb("kT", [128, koff + TP], BF16); kT_r = Res("wkT")
            V = sb("V", [128, (koff + TP) // 128, 128], BF16); V_r = Res("wV")
            qT = sb("qT", [128, TP], BF16); qT_r = Res("wqT")
            oT = sb("oT", [128, TP], BF16); oT_r = Res("woT")
            W = {"P": sb("P", [128, 640], BF16), "P_r": Res("wP"), "PT": sb("PT", [128, 640], BF16),
                 "PT_r": Res("wPT"), "st": sb("st", [128, 16], F32), "st_r": Res("wst"), "dv": 128,
                 "q_reads": [qT_r], "k_reads": [kT_r], "v_reads": [V_r]}
            o_tm = sb("o_tm", [128, 128], BF16); o_r = Res("wo_tm")
            nsink = sb("nsink", [128, 16], F32); nsink_r = Res("wnsink")
            W["nsink_r"] = nsink_r
            stage = sb("stage", [128, 512], F32); stage_r = Res("wstage")
            S.op("dve", lambda e: e.tensor_scalar(out=nsink[:], in0=self.cs("sink"), scalar1=-1.0, scalar2=None,
                                                  op0=ALU.mult), reads=[self.cst_r], writes=[nsink_r])
            if self.sample:
                cs_t = sb("cs", [128, 2, LS], F32); cs_r = Res("wcs")
                mask = sb("mask", [128, 384], F32); mask_r = Res("wmask")
                W["mask_r"] = mask_r
                S.dma("sp", cs_t[:], self.rope_in, writes=[cs_r])
                S.dma("sp", mask[:], self.wmask_in, writes=[mask_r])
            for g in range(4):
                if KSTOP < 1:
                    break
                wk, wk_r = self.load_cols(Wq, 2048 + g * 128, key="win_qkv")
                if self.sample:
                    S.dma("pool", kT[:, 0:256], self.kc2T_in[g * 128:(g + 1) * 128, :], writes=[kT_r])
                for t in range(self.TT if 'k' in KPART else 0):
                    ps, ps_r = self.psum()
                    self.proj_fm(wk, wk_r, t, ps, ps_r)
                    dst = kT[:, koff + t * 512: koff + (t + 1) * 512]
                    if self.sample:
                        self.rope_evac(ps, ps_r, dst, kT_r, t, cs_t, cs_r)
                    else:
                        S.op("act", lambda e, ps=ps: e.copy(out=stage[:], in_=ps[:]), reads=[ps_r], writes=[stage_r])
                        S.op("dve", lambda e, dst=dst: e.tensor_copy(out=dst, in_=stage[:]),
                             reads=[stage_r], writes=[kT_r])
                        if 's' not in KSKIP:
                            S.dma("sp", self.k2T_out[g * 128:(g + 1) * 128, t * 512:(t + 1) * 512], stage[:],
                                  reads=[stage_r], store=True)
                wv, wv_r = self.load_cols(Wq, 2560 + g * 128, key="win_qkv")
                if self.sample:
                    S.dma("pool", V[:, 0:2, :],
                          self.vc2_in[:, g * 128:(g + 1) * 128].rearrange("(n p) d -> p n d", p=128), writes=[V_r])
                for nt in range(NT if 'v' in KPART else 0):
                    ps, ps_r = self.psum()
                    self.proj_tm(wv, wv_r, nt, ps, ps_r, 128)
                    if self.sample:
                        S.op("act", lambda e, ps=ps, nt=nt: e.copy(out=V[:, koff // 128 + nt, :], in_=ps[:, 0:128]),
                             reads=[ps_r], writes=[V_r])
                    else:
                        S.op("act", lambda e, ps=ps: e.copy(out=stage[:, 0:128], in_=ps[:, 0:128]),
                             reads=[ps_r], writes=[stage_r])
                        S.op("dve", lambda e, nt=nt: e.tensor_copy(out=V[:, koff // 128 + nt, :], in_=stage[:, 0:128]),
                             reads=[stage_r], writes=[V_r])
                        S.dma("sp", self.v2_out[nt * 128:(nt + 1) * 128, g * 128:(g + 1) * 128], stage[:, 0:128],
                              reads=[stage_r], store=True)
                for r in range(4):
                    if KSTOP < 2:
                        break
                    hd = g * 4 + r
                    wq, wq_r = self.load_cols(Wq, hd * 128, key="win_qkv")
                    W["nsink_ap"] = nsink[:, hd:hd + 1]
                    for t in range(self.TT):
                        ps, ps_r = self.psum()
                        self.proj_fm(wq, wq_r, t, ps, ps_r)
                        dst = qT[:, t * 512:(t + 1) * 512]
                        if self.sample:
                            self.rope_evac(ps, ps_r, dst, qT_r, t, cs_t, cs_r)
                        else:
                            S.op("act", lambda e, ps=ps, dst=dst: e.copy(out=dst, in_=ps[:]),
                                 reads=[ps_r], writes=[qT_r])
                    for s in range(self.nseq):
                        if KSTOP < 3:
                            break
                        for qb in range(L // 128):
                            q0 = s * L + qb * 128
                            if self.sample:
                                b_lo, b_hi = max(qb - 1, 0), min(qb + 1, L // 128 - 1)
                                nb = (b_hi - b_lo + 1) * 128
                                m0 = (b_lo - (qb - 1)) * 128
                                kparts = [(kT[:, 0:256], 256, None),
                                          (kT[:, 256 + b_lo * 128: 256 + b_lo * 128 + nb], nb, mask[:, m0:m0 + nb])]
                                vparts = [V[:, 0, :], V[:, 1, :]] + [V[:, 2 + b, :] for b in range(b_lo, b_hi + 1)]
                            else:
                                kparts = [(kT[:, s * L:(s + 1) * L], L, None)]
                                vparts = [V[:, s * (L // 128) + b, :] for b in range(L // 128)]
                            po, po_r = self.attn_tile(qT[:, q0:q0 + 128], kparts, vparts, scale,
                                                      self.cs("sink", hd, hd + 1), W)
                            S.op("dve", lambda e, po=po: e.tensor_scalar(
                                out=o_tm[:], in0=po[:, 0:128], scalar1=W["st"][:, 6:7], scalar2=None, op0=ALU.mult),
                                reads=[po_r, W["st_r"]], writes=[o_r])
                            self.transpose_to(o_tm[:], [o_r], oT[:, q0:q0 + 128], oT_r)
                    if KSTOP < 4:
                        continue
                    wo, wo_r = self.load_rows(Wo, hd * 128, key="win_out")
                    for oc in range(DC):
                        for t in range(self.TT):
                            po, po_r = self.psum()
                            S.op("pe", lambda e, po=po, oc=oc, t=t: e.matmul(
                                po[:], wo[:, oc * 128:(oc + 1) * 128], oT[:, t * 512:(t + 1) * 512],
                                start=True, stop=True), reads=[wo_r, oT_r], writes=[po_r])
                            self.x_accum(po, po_r, self.modcol(i, 5, oc), self.mods_r, oc, t)
            S.barrier()

    def diff_mixer(self, i):
        S = self.S
        Wq = self.diff_w_qkv[0]
        Wo = self.diff_w_out[0]
        TP, NT, L = self.TP, self.NT, self.L
        scale = 128 ** -0.5
        lambda_init = 0.8 - 0.6 * math.exp(-0.3 * i)
        koff = 256 if self.sample else 0
        NK = koff + L
        with contextlib.ExitStack() as ms:
            sb = lambda n, s, d: self.sb("df_" + n, s, d, ms)
            kT = sb("kT", [128, 2, koff + TP], BF16); kT_r = Res("dkT")
            V = sb("V", [128, (koff + TP) // 128, 256], BF16); V_r = Res("dV")
            qT = sb("qT", [128, 2, TP], BF16); qT_r = Res("dqT")
            oT = sb("oT", [128, 2, TP], BF16); oT_r = Res("doT")
            Ws = []
            for m in range(2):
                Ws.append({"P": sb(f"P{m}", [128, NK], BF16), "P_r": Res(f"dP{m}"),
                           "PT": sb(f"PT{m}", [128, NK], BF16), "PT_r": Res(f"dPT{m}"),
                           "st": sb(f"st{m}", [128, 16], F32), "st_r": Res(f"dst{m}"), "dv": 256,
                           "q_reads": [qT_r], "k_reads": [kT_r], "v_reads": [V_r]})
            o32 = sb("o32", [128, 256], F32); o32_r = Res("do32")
            o_tm = sb("o_tm", [128, 256], BF16); o_r = Res("do_tm")
            lam = sb("lam", [128, 8], F32); lam_r = Res("dlam")
            junk = sb("junk", [128, 256], F32); junk_r = Res("djunk")
            stage = sb("stage", [128, 512], F32); stage_r = Res("dstage")
            lv = self.cs("lam")
            S.op("dve", lambda e: e.tensor_tensor(out=junk[:, 0:128], in0=lv[:, 0:128], in1=lv[:, 128:256], op=ALU.mult),
                 reads=[self.cst_r], writes=[junk_r])
            S.op("dve", lambda e: e.reduce_sum(out=lam[:, 0:1], in_=junk[:, 0:128], axis=AX.X),
                 reads=[junk_r], writes=[lam_r])
            S.op("dve", lambda e: e.tensor_tensor(out=junk[:, 128:256], in0=lv[:, 256:384], in1=lv[:, 384:512],
                                                  op=ALU.mult), reads=[self.cst_r], writes=[junk_r])
            S.op("dve", lambda e: e.reduce_sum(out=lam[:, 1:2], in_=junk[:, 128:256], axis=AX.X),
                 reads=[junk_r], writes=[lam_r])
            S.op("act", lambda e: e.activation(out=lam[:, 2:4], in_=lam[:, 0:2], func=AF.Exp),
                 reads=[lam_r], writes=[lam_r])
            S.op("dve", lambda e: e.tensor_tensor(out=lam[:, 4:5], in0=lam[:, 3:4], in1=lam[:, 2:3], op=ALU.subtract),
                 reads=[lam_r], writes=[lam_r])
            S.op("dve", lambda e: e.tensor_scalar(out=lam[:, 4:5], in0=lam[:, 4:5], scalar1=-lambda_init, scalar2=None,
                                                  op0=ALU.add), reads=[lam_r], writes=[lam_r])
            if self.sample:
                cs_t = sb("cs", [128, 2, LS], F32); cs_r = Res("dcs")
                S.dma("sp", cs_t[:], self.rope_in, writes=[cs_r])
            for hd in range(8):
                for m in range(2):
                    col = m * 1024 + hd * 128
                    wk, wk_r = self.load_cols(Wq, 2048 + col, key="diff_qkv")
                    if self.sample:
                        S.dma("pool", kT[:, m, 0:256], self.kc1T_in[col:col + 128, :], writes=[kT_r])
                    for t in range(self.TT):
                        ps, ps_r = self.psum()
                        self.proj_fm(wk, wk_r, t, ps, ps_r)
                        dst = kT[:, m, koff + t * 512: koff + (t + 1) * 512]
                        if self.sample:
                            self.rope_evac(ps, ps_r, dst, kT_r, t, cs_t, cs_r)
                        else:
                            S.op("act", lambda e, ps=ps: e.copy(out=stage[:], in_=ps[:]),
                                 reads=[ps_r], writes=[stage_r])
                            S.op("dve", lambda e, dst=dst: e.tensor_copy(out=dst, in_=stage[:]),
                                 reads=[stage_r], writes=[kT_r])
                            S.dma("sp", self.k1T_out[col:col + 128, t * 512:(t + 1) * 512], stage[:],
                                  reads=[stage_r], store=True)
                    wq, wq_r = self.load_cols(Wq, col, key="diff_qkv")
                    for t in range(self.TT):
                        ps, ps_r = self.psum()
                        self.proj_fm(wq, wq_r, t, ps, ps_r)
                        dst = qT[:, m, t * 512:(t + 1) * 512]
                        if self.sample:
                            self.rope_evac(ps, ps_r, dst, qT_r, t, cs_t, cs_r)
                        else:
                            S.op("act", lambda e, ps=ps, dst=dst: e.copy(out=dst, in_=ps[:]),
                                 reads=[ps_r], writes=[qT_r])
                wv, wv_r = self.load_cols(Wq, 4096 + hd * 256, ncols=256, key="diff_qkv")
                if self.sample:
                    S.dma("pool", V[:, 0:2, :],
                          self.vc1_in[:, hd * 256:(hd + 1) * 256].rearrange("(n p) d -> p n d", p=128), writes=[V_r])
                for nt in range(NT):
                    ps, ps_r = self.psum()
                    self.proj_tm(wv, wv_r, nt, ps, ps_r, 256)
                    if self.sample:
                        S.op("act", lambda e, ps=ps, nt=nt: e.copy(out=V[:, koff // 128 + nt, :], in_=ps[:, 0:256]),
                             reads=[ps_r], writes=[V_r])
                    else:
                        S.op("act", lambda e, ps=ps: e.copy(out=stage[:, 0:256], in_=ps[:, 0:256]),
                             reads=[ps_r], writes=[stage_r])
                        S.op("dve", lambda e, nt=nt: e.tensor_copy(out=V[:, koff // 128 + nt, :], in_=stage[:, 0:256]),
                             reads=[stage_r], writes=[V_r])
                        S.dma("sp", self.v1_out[nt * 128:(nt + 1) * 128, hd * 256:(hd + 1) * 256], stage[:, 0:256],
                              reads=[stage_r], store=True)
                for s in range(self.nseq):
                    k0 = s * L if not self.sample else 0
                    vparts = [V[:, (k0 // 128) + b, :] for b in range(NK // 128)]
                    for qb in range(L // 128):
                        q0 = s * L + qb * 128
                        pos = []
                        for m in range(2):
                            kparts = []
                            c = 0
                            while c < NK:
                                n = min(512, NK - c)
                                kparts.append((kT[:, m, k0 + c:k0 + c + n], n, None))
                                c += n
                            pos.append(self.attn_tile(qT[:, m, q0:q0 + 128], kparts, vparts, scale, None, Ws[m]))
                        st0, st1 = Ws[0]["st"], Ws[1]["st"]
                        S.op("dve", lambda e: e.tensor_tensor(out=st1[:, 7:8], in0=st1[:, 6:7], in1=lam[:, 4:5],
                                                              op=ALU.mult),
                             reads=[Ws[1]["st_r"], lam_r], writes=[Ws[1]["st_r"]])
                        (p0, p0_r), (p1, p1_r) = pos
                        S.op("dve", lambda e, p0=p0: e.tensor_scalar(out=o32[:], in0=p0[:, 0:256], scalar1=st0[:, 6:7],
                                                                     scalar2=None, op0=ALU.mult),
                             reads=[p0_r, Ws[0]["st_r"]], writes=[o32_r])
                        S.op("dve", lambda e, p1=p1: e.scalar_tensor_tensor(
                            out=o32[:], in0=p1[:, 0:256], scalar=st1[:, 7:8], in1=o32[:], op0=ALU.mult, op1=ALU.add),
                            reads=[p1_r, Ws[1]["st_r"], o32_r], writes=[o32_r])
                        S.op("dve", lambda e: e.memset(st0[:, 12:13], 0.0), writes=[Ws[0]["st_r"]])
                        S.op("act", lambda e: e.activation(out=junk[:], in_=o32[:], func=AF.Square,
                                                           accum_out=st0[:, 12:13]),
                             reads=[o32_r, Ws[0]["st_r"]], writes=[junk_r, Ws[0]["st_r"]])
                        S.op("dve", lambda e: e.tensor_scalar(out=st0[:, 13:14], in0=st0[:, 12:13], scalar1=1.0 / 256,
                                                              scalar2=EPS, op0=ALU.mult, op1=ALU.add),
                             reads=[Ws[0]["st_r"]], writes=[Ws[0]["st_r"]])
                        S.op("act", lambda e: e.sqrt(out=st0[:, 13:14], in_=st0[:, 13:14]),
                             reads=[Ws[0]["st_r"]], writes=[Ws[0]["st_r"]])
                        S.op("dve", lambda e: e.reciprocal(out=st0[:, 14:15], in_=st0[:, 13:14]),
                             reads=[Ws[0]["st_r"]], writes=[Ws[0]["st_r"]])
                        S.op("dve", lambda e: e.tensor_scalar(out=st0[:, 14:15], in0=st0[:, 14:15],
                                                              scalar1=1.0 - lambda_init, scalar2=None, op0=ALU.mult),
                             reads=[Ws[0]["st_r"]], writes=[Ws[0]["st_r"]])
                        S.op("dve", lambda e: e.scalar_tensor_tensor(
                            out=o_tm[:], in0=o32[:], scalar=st0[:, 14:15], in1=self.cs("subln"),
                            op0=ALU.mult, op1=ALU.mult), reads=[o32_r, Ws[0]["st_r"], self.cst_r], writes=[o_r])
                        for eh in range(2):
                            self.transpose_to(o_tm[:, eh * 128:(eh + 1) * 128], [o_r], oT[:, eh, q0:q0 + 128], oT_r)
                wos = [self.load_rows(Wo, hd * 256 + eh * 128, key="diff_out") for eh in range(2)]
                for oc in range(DC):
                    for t in range(self.TT):
                        po, po_r = self.psum()
                        for eh in range(2):
                            wo, wo_r = wos[eh]
                            S.op("pe", lambda e, po=po, oc=oc, t=t, eh=eh, wo=wo: e.matmul(
                                po[:], wo[:, oc * 128:(oc + 1) * 128], oT[:, eh, t * 512:(t + 1) * 512],
                                start=(eh == 0), stop=(eh == 1)), reads=[wo_r, oT_r], writes=[po_r], inc=(eh == 1))
                        self.x_accum(po, po_r, self.modcol(i, 5, oc), self.mods_r, oc, t)
            S.barrier()

    def ssm_mixer(self, i):
        S = self.S
        j = i // 3
        Win = self.ssm_w_in[j]
        Wout = self.ssm_w_out[j]
        TP, NT, L, nseq = self.TP, self.NT, self.L, self.nseq
        nch = L // 128
        U = {0: self.cs("Ule"), 1: self.cs("Uge")}
        SLU = {0: self.cs("SL"), 1: self.cs("SU")}
        with contextlib.ExitStack() as ms:
            sb = lambda n, s, d: self.sb("ss_" + n, s, d, ms)
            ssq = sb("ssq", [128, NT, NG], F32); ssq_r = Res("sssq")
            S.op("dve", lambda e: e.memset(ssq[:].rearrange("p a b -> p (a b)"), 0.0), writes=[ssq_r])
            with contextlib.ExitStack() as gs:
                gb = lambda n, s, d: self.sb("sg_" + n, s, d, gs)
                z_tm = gb("z", [128, NT, 512], BF16); z_r = Res("sz")
                yf = gb("yf", [128, NT, 512], BF16); yf_r = Res("syf")
                xc = gb("xc", [128, 6, TP], BF16); xc_r = Res("sxc")
                pre = gb("pre", [128, TP], F32); pre_r = Res("spre")
                acc = self.rstd; acc_r = Res("sacc")
                dt = gb("dt", [128, NT, 16], F32); dt_r = Res("sdt")
                dtA = gb("dtA", [128, NT, 16], F32); dtA_r = Res("sdtA")
                eac = gb("eac", [128, NT, 16], F32); eac_r = Res("seac")
                cd = gb("cd", [128, NT, 16], F32); cd_r = Res("scd")
                abc = gb("abc", [128, 2, 128], F32); abc_r = Res("sabc")
                xdt = gb("xdt", [128, 512], BF16); xdt_r = Res("sxdt")
                xdtd = gb("xdtd", [128, 512], BF16); xdtd_r = Res("sxdtd")
                B_tm = gb("B_tm", [128, 128], BF16); Btm_r = Res("sBtm")
                mCB = gb("mCB", [128, 128], F32); mCB_r = Res("smCB")
                MT = gb("MT", [128, 8, 128], BF16); MT_r = Res("sMT")
                hT = gb("hT", [128, 512], F32); hT_r = Res("shT")
                hTb = gb("hTb", [128, 512], BF16); hTb_r = Res("shTb")
                y32 = gb("y32", [128, 512], F32); y32_r = Res("sy32")
                yz = gb("yz", [128, 512], BF16); yz_r = Res("syz")
                yzT_t = gb("yzT", [128, 4, 128], BF16); yzT = yzT_t[:]; yzT_r = Res("syzT")
                junk = gb("junk", [128, 512], BF16); junk_r = Res("sjunk")
                off, _ = self.lay["a_log"]
                S.op("act", lambda e: e.activation(out=abc[:, 0, :], in_=self.cst[:, off + j * 128: off + (j + 1) * 128],
                                                   func=AF.Exp), reads=[self.cst_r], writes=[abc_r])
                S.op("dve", lambda e: e.tensor_scalar(out=abc[:, 0, :], in0=abc[:, 0, :], scalar1=-1.0, scalar2=None,
                                                      op0=ALU.mult), reads=[abc_r], writes=[abc_r])
                offb, _ = self.lay["dt_bias"]
                offd, _ = self.lay["ssm_d"]
                offcw, _ = self.lay["conv_w"]
                offcb, _ = self.lay["conv_b"]
                offng, _ = self.lay["ssm_norm_g"]
                for g in range(NG):
                    for d_ in range(2):
                        wd, wd_r = self.load_cols(Win, DI + 6144 + d_ * 64 + g * 8, ncols=8, key=f"ssm_in{j}")
                        for nt in range(NT):
                            ps, ps_r = self.psum()
                            self.proj_tm(wd, wd_r, nt, ps, ps_r, 8)
                            hsl = slice(d_ * 64 + g * 8, d_ * 64 + g * 8 + 8)
                            bsl = slice(offb + j * 128 + d_ * 64 + g * 8, offb + j * 128 + d_ * 64 + g * 8 + 8)
                            sc, sc_r = self.scratch32()
                            S.op("dve", lambda e, sc=sc, ps=ps, bsl=bsl: e.tensor_tensor(
                                out=sc[:, 0:8], in0=ps[:, 0:8], in1=self.cst[:, bsl], op=ALU.add),
                                reads=[ps_r, self.cst_r], writes=[sc_r])
                            S.op("act", lambda e, sc=sc: e.activation(out=sc[:, 0:8], in_=sc[:, 0:8], func=AF.Exp),
                                 reads=[sc_r], writes=[sc_r])
                            S.op("dve", lambda e, sc=sc: e.tensor_scalar(out=sc[:, 0:8], in0=sc[:, 0:8], scalar1=1.0,
                                                                         scalar2=None, op0=ALU.add),
                                 reads=[sc_r], writes=[sc_r])
                            S.op("act", lambda e, sc=sc, nt=nt, d_=d_: e.activation(
                                out=dt[:, nt, d_ * 8:(d_ + 1) * 8], in_=sc[:, 0:8], func=AF.Ln),
                                reads=[sc_r], writes=[dt_r])
                            S.op("dve", lambda e, nt=nt, d_=d_, hsl=hsl: e.tensor_tensor(
                                out=dtA[:, nt, d_ * 8:(d_ + 1) * 8], in0=dt[:, nt, d_ * 8:(d_ + 1) * 8],
                                in1=abc[:, 0, hsl], op=ALU.mult), reads=[dt_r, abc_r], writes=[dtA_r])
                    for nt in range(NT):
                        ps, ps_r = self.psum()
                        S.op("pe", lambda e, ps=ps, nt=nt: e.matmul(ps[:, 0:8], U[0], dtA[:, nt, 0:8],
                                                                    start=True, stop=True),
                             reads=[self.cst_r, dtA_r], writes=[ps_r])
                        S.op("pe", lambda e, ps=ps, nt=nt: e.matmul(ps[:, 8:16], U[1], dtA[:, nt, 8:16],
                                                                    start=True, stop=True),
                             reads=[self.cst_r, dtA_r], writes=[ps_r])
                        S.op("pe", lambda e, ps=ps, nt=nt: e.matmul(ps[:, 16:32], self.cs("ones"), dtA[:, nt, :],
                                                                    start=True, stop=True),
                             reads=[self.cst_r, dtA_r], writes=[ps_r])
                        S.op("act", lambda e, ps=ps, nt=nt: e.activation(out=eac[:, nt, :], in_=ps[:, 0:16], func=AF.Exp),
                             reads=[ps_r], writes=[eac_r])
                        S.op("act", lambda e, ps=ps, nt=nt: e.activation(out=cd[:, nt, :], in_=ps[:, 16:32], func=AF.Exp),
                             reads=[ps_r], writes=[cd_r])
                    cols = [DI + g * 512 + q * 128 for q in range(4)] + [DI + DI + g * 128, DI + DI + 1024 + g * 128]
                    for ci, col in enumerate(cols):
                        cc = (col - DI) // 128
                        wv, wv_r = self.load_cols(Win, col, key=f"ssm_in{j}")
                        for t in range(self.TT):
                            ps, ps_r = self.psum()
                            self.proj_fm(wv, wv_r, t, ps, ps_r)
                            S.op("act", lambda e, ps=ps, t=t: e.copy(out=pre[:, t * 512:(t + 1) * 512], in_=ps[:]),
                                 reads=[ps_r], writes=[pre_r])
                        w0 = self.cst[:, offcw + (j * 3 + 0) * 48 + cc: offcw + (j * 3 + 0) * 48 + cc + 1]
                        w1 = self.cst[:, offcw + (j * 3 + 1) * 48 + cc: offcw + (j * 3 + 1) * 48 + cc + 1]
                        w2 = self.cst[:, offcw + (j * 3 + 2) * 48 + cc: offcw + (j * 3 + 2) * 48 + cc + 1]
                        cb = self.cst[:, offcb + j * 48 + cc: offcb + j * 48 + cc + 1]
                        S.op("dve", lambda e, w1=w1, cb=cb: e.tensor_scalar(
                            out=acc[:, 0:TP], in0=pre[:], scalar1=w1, scalar2=cb, op0=ALU.mult, op1=ALU.add),
                            reads=[pre_r, self.cst_r], writes=[acc_r])
                        for s in range(nseq):
                            a0, a1 = s * L, (s + 1) * L
                            S.op("dve", lambda e, w0=w0, a0=a0, a1=a1: e.scalar_tensor_tensor(
                                out=acc[:, a0 + 1:a1], in0=pre[:, a0:a1 - 1], scalar=w0, in1=acc[:, a0 + 1:a1],
                                op0=ALU.mult, op1=ALU.add), reads=[pre_r, acc_r, self.cst_r], writes=[acc_r])
                            S.op("dve", lambda e, w2=w2, a0=a0, a1=a1: e.scalar_tensor_tensor(
                                out=acc[:, a0:a1 - 1], in0=pre[:, a0 + 1:a1], scalar=w2, in1=acc[:, a0:a1 - 1],
                                op0=ALU.mult, op1=ALU.add), reads=[pre_r, acc_r, self.cst_r], writes=[acc_r])
                        S.op("act", lambda e, ci=ci: e.activation(out=xc[:, ci, :], in_=acc[:, 0:TP], func=AF.Silu),
                             reads=[acc_r], writes=[xc_r])
                    for half in range(2):
                        wz, wz_r = self.load_cols(Win, g * 512 + half * 256, ncols=256, key=f"ssm_in{j}")
                        for nt in range(NT):
                            ps, ps_r = self.psum()
                            self.proj_tm(wz, wz_r, nt, ps, ps_r, 256)
                            S.op("act", lambda e, ps=ps, nt=nt, half=half: e.activation(
                                out=z_tm[:, nt, half * 256:(half + 1) * 256], in_=ps[:, 0:256], func=AF.Silu),
                                reads=[ps_r], writes=[z_r])
                    for s in range(nseq):
                        for d_ in range(2):
                            if self.sample:
                                S.dma("sp", hT[:], self.st_in[j, d_, :, g * 512:(g + 1) * 512], writes=[hT_r])
                            else:
                                S.op("dve", lambda e: e.memset(hT[:], 0.0), writes=[hT_r])
                            S.op("act", lambda e: e.copy(out=hTb[:], in_=hT[:]), reads=[hT_r], writes=[hTb_r])
                            order = range(nch) if d_ == 0 else range(nch - 1, -1, -1)
                            for c in order:
                                nt = s * nch + c
                                tk = slice(nt * 128, (nt + 1) * 128)
                                dsl = slice(d_ * 8, d_ * 8 + 8)
                                px, px_r = self.psum()
                                for q in range(4):
                                    S.op("pe", lambda e, q=q, px=px: e.matmul(
                                        px[:, q * 128:(q + 1) * 128], xc[:, q, tk], self.ident_bf[:],
                                        start=True, stop=True), reads=[xc_r, self.cbf_r], writes=[px_r], inc=(q == 3))
                                S.op("dve", lambda e, px=px: e.tensor_tensor(
                                    out=xdt[:].rearrange("p (h q) -> p h q", h=8),
                                    in0=px[:].rearrange("p (h q) -> p h q", h=8),
                                    in1=dt[:, nt, dsl].unsqueeze(2).to_broadcast([128, 8, 64]), op=ALU.mult),
                                    reads=[px_r, dt_r], writes=[xdt_r])
                                if d_ == 0:
                                    dsk = self.cst[:, offd + j * 64 + g * 8: offd + j * 64 + g * 8 + 8]
                                    S.op("dve", lambda e, px=px, dsk=dsk: e.tensor_tensor(
                                        out=y32[:].rearrange("p (h q) -> p h q", h=8),
                                        in0=px[:].rearrange("p (h q) -> p h q", h=8),
                                        in1=dsk.unsqueeze(2).to_broadcast([128, 8, 64]), op=ALU.mult),
                                        reads=[px_r, self.cst_r], writes=[y32_r])
                                else:
                                    S.op("act", lambda e: e.copy(out=y32[:], in_=yf[:, nt, :]),
                                         reads=[yf_r], writes=[y32_r])
                                pb, pb_r = self.psum()
                                S.op("pe", lambda e, pb=pb: e.matmul(pb[:, 0:128], xc[:, 4, tk], self.ident_bf[:],
                                                                     start=True, stop=True),
                                     reads=[xc_r, self.cbf_r], writes=[pb_r])
                                S.op("act", lambda e, pb=pb: e.copy(out=B_tm[:], in_=pb[:, 0:128]),
                                     reads=[pb_r], writes=[Btm_r])
                                pc, pc_r = self.psum()
                                S.op("pe", lambda e, pc=pc: e.matmul(pc[:, 0:128], xc[:, 4, tk], xc[:, 5, tk],
                                                                     start=True, stop=True),
                                     reads=[xc_r], writes=[pc_r])
                                S.op("dve", lambda e, pc=pc: e.tensor_tensor(out=mCB[:], in0=pc[:, 0:128], in1=U[d_],
                                                                             op=ALU.mult),
                                     reads=[pc_r, self.cst_r], writes=[mCB_r])
                                iend = 127 if d_ == 0 else 0
                                for quad in range(2):
                                    hs = slice(d_ * 8 + quad * 4, d_ * 8 + quad * 4 + 4)
                                    rseg, rseg_r = self.scratch32()
                                    Lt, Lt_r = self.scratch32()
                                    S.op("dve", lambda e, hs=hs, rseg=rseg: e.tensor_tensor(
                                        out=rseg[:].rearrange("p (h q) -> p h q", h=4),
                                        in0=U[d_].unsqueeze(1).to_broadcast([128, 4, 128]),
                                        in1=dtA[:, nt, hs].unsqueeze(2).to_broadcast([128, 4, 128]), op=ALU.mult),
                                        reads=[self.cst_r, dtA_r], writes=[rseg_r])
                                    pl, pl_r = self.psum()
                                    S.op("pe", lambda e, pl=pl, rseg=rseg: e.matmul(pl[:], SLU[d_], rseg[:], start=True, stop=True),
                                         reads=[self.cst_r, rseg_r], writes=[pl_r])
                                    S.op("act", lambda e, pl=pl, Lt=Lt: e.activation(out=Lt[:], in_=pl[:], func=AF.Exp),
                                         reads=[pl_r], writes=[Lt_r])
                                    S.op("dve", lambda e, quad=quad, Lt=Lt: e.tensor_tensor(
                                        out=MT[:, quad * 4:(quad + 1) * 4, :],
                                        in0=Lt[:].rearrange("p (h q) -> p h q", h=4),
                                        in1=mCB[:].unsqueeze(1).to_broadcast([128, 4, 128]), op=ALU.mult),
                                        reads=[Lt_r, mCB_r], writes=[MT_r])
                                    S.op("dve", lambda e, quad=quad, Lt=Lt: e.tensor_tensor(
                                        out=xdtd[:, quad * 256:(quad + 1) * 256].rearrange("p (h q) -> p h q", h=4),
                                        in0=xdt[:, quad * 256:(quad + 1) * 256].rearrange("p (h q) -> p h q", h=4),
                                        in1=Lt[:].rearrange("p (h q) -> p h q", h=4)[:, :, iend:iend + 1]
                                        .to_broadcast([128, 4, 64]), op=ALU.mult),
                                        reads=[xdt_r, Lt_r], writes=[xdtd_r])
                                py, py_r = self.psum()
                                for hh in range(8):
                                    S.op("pe", lambda e, hh=hh, py=py: e.matmul(
                                        py[:, hh * 64:(hh + 1) * 64], MT[:, hh, :], xdt[:, hh * 64:(hh + 1) * 64],
                                        start=True, stop=True), reads=[MT_r, xdt_r], writes=[py_r], inc=(hh == 7))
                                pf, pf_r = self.psum()
                                S.op("pe", lambda e, pf=pf: e.matmul(pf[:], xc[:, 5, tk], hTb[:], start=True, stop=True),
                                     reads=[xc_r, hTb_r], writes=[pf_r])
                                S.op("dve", lambda e, py=py: e.tensor_tensor(out=y32[:], in0=py[:], in1=y32[:], op=ALU.add),
                                     reads=[py_r, y32_r], writes=[y32_r])
                                sc, sc_r = self.scratch32()
                                S.op("dve", lambda e, pf=pf, sc=sc: e.tensor_tensor(
                                    out=sc[:].rearrange("p (h q) -> p h q", h=8),
                                    in0=pf[:].rearrange("p (h q) -> p h q", h=8),
                                    in1=eac[:, nt, dsl].unsqueeze(2).to_broadcast([128, 8, 64]), op=ALU.mult),
                                    reads=[pf_r, eac_r], writes=[sc_r])
                                if d_ == 0:
                                    S.op("dve", lambda e, sc=sc: e.tensor_tensor(out=yf[:, nt, :], in0=sc[:], in1=y32[:],
                                                                                op=ALU.add),
                                         reads=[sc_r, y32_r], writes=[yf_r])
                                else:
                                    S.op("dve", lambda e, sc=sc: e.tensor_tensor(out=y32[:], in0=sc[:], in1=y32[:],
                                                                                op=ALU.add),
                                         reads=[sc_r, y32_r], writes=[y32_r])
                                    S.op("dve", lambda e: e.tensor_tensor(out=yz[:], in0=y32[:], in1=z_tm[:, nt, :],
                                                                          op=ALU.mult),
                                         reads=[y32_r, z_r], writes=[yz_r])
                                    S.op("act", lambda e: e.activation(out=junk[:, 0:512], in_=yz[:], func=AF.Square,
                                                                       accum_out=ssq[:, nt, g:g + 1]),
                                         reads=[yz_r, ssq_r], writes=[junk_r, ssq_r])
                                    for q in range(4):
                                        kc = g * 4 + q
                                        self.transpose_to(yz[:, q * 128:(q + 1) * 128], [yz_r], yzT[:, q, :], yzT_r,
                                                          scale_ap=self.cst[:, offng + j * 32 + kc: offng + j * 32 + kc + 1],
                                                          scale_r=self.cst_r)
                                    S.dma("sp", self.yz_scr[g * 4:(g + 1) * 4, :, nt * 128:(nt + 1) * 128]
                                          .rearrange("q p t -> p q t"), yzT, reads=[yzT_r], store=True)
                                pS, pS_r = self.psum()
                                S.op("pe", lambda e, pS=pS: e.matmul(pS[:], B_tm[:], xdtd[:], start=True, stop=True),
                                     reads=[Btm_r, xdtd_r], writes=[pS_r])
                                S.op("dve", lambda e: e.tensor_tensor(
                                    out=hT[:].rearrange("p (h q) -> p h q", h=8),
                                    in0=hT[:].rearrange("p (h q) -> p h q", h=8),
                                    in1=cd[:, nt, dsl].unsqueeze(2).to_broadcast([128, 8, 64]), op=ALU.mult),
                                    reads=[hT_r, cd_r], writes=[hT_r])
                                S.op("dve", lambda e, pS=pS: e.tensor_tensor(out=hT[:], in0=hT[:], in1=pS[:], op=ALU.add),
                                     reads=[hT_r, pS_r], writes=[hT_r])
                                S.op("act", lambda e: e.copy(out=hTb[:], in_=hT[:]), reads=[hT_r], writes=[hTb_r])
                            if not self.sample:
                                S.dma("sp", self.st_out[j, d_, s, :, g * 512:(g + 1) * 512], hT[:],
                                      reads=[hT_r], store=True)
                S.barrier()
            with contextlib.ExitStack() as os_:
                ob = lambda n, s, d: self.sb("so_" + n, s, d, os_)
                yzt = ob("yzt", [128, 32, 512], BF16); yzt_r = Res("syzt")
                rs = ob("rs", [128, NT], F32); rs_r = Res("srs")
                dg = ob("dg", [128, 128], F32); dg_r = Res("sdg")
                S._wait(S.engs["sp"], list(S.store_toks.values()))
                S.op("dve", lambda e: e.reduce_sum(out=rs[:], in_=ssq[:], axis=AX.X), reads=[ssq_r], writes=[rs_r])
                S.op("dve", lambda e: e.tensor_scalar(out=rs[:], in0=rs[:], scalar1=1.0 / DI, scalar2=EPS,
                                                      op0=ALU.mult, op1=ALU.add), reads=[rs_r], writes=[rs_r])
                S.op("act", lambda e: e.sqrt(out=rs[:], in_=rs[:]), reads=[rs_r], writes=[rs_r])
                S.op("dve", lambda e: e.reciprocal(out=rs[:], in_=rs[:]), reads=[rs_r], writes=[rs_r])
                for nt in range(NT):
                    S.op("dve", lambda e, nt=nt: e.tensor_scalar(out=dg[:], in0=self.cs("ident"), scalar1=rs[:, nt:nt + 1],
                                                                 scalar2=None, op0=ALU.mult),
                         reads=[self.cst_r, rs_r], writes=[dg_r])
                    ps, ps_r = self.psum()
                    S.op("pe", lambda e, ps=ps: e.matmul(ps[:, 0:128], self.cs("ones"), dg[:], start=True, stop=True),
                         reads=[self.cst_r, dg_r], writes=[ps_r])
                    S.op("act", lambda e, ps=ps, nt=nt: e.copy(out=self.rstd[:, nt * 128:(nt + 1) * 128], in_=ps[:, 0:128]),
                         reads=[ps_r], writes=[self.rstd_r[nt // 4]])
                for t in range(self.TT):
                    S.dma("sp", yzt[:], self.yz_scr[:, :, t * 512:(t + 1) * 512].rearrange("k p t -> p k t"),
                          writes=[yzt_r])
                    for oc in range(DC):
                        wo, wo_r = self.load_cols(Wout, oc * 128, rows=DI, key=f"ssm_out{j}")
                        po, po_r = self.psum()
                        for kc in range(32):
                            S.op("pe", lambda e, kc=kc, po=po, wo=wo: e.matmul(po[:], wo[:, kc, :], yzt[:, kc, :],
                                                                               start=(kc == 0), stop=(kc == 31)),
                                 reads=[wo_r, yzt_r], writes=[po_r], inc=(kc == 31))
                        sc, sc_r = self.scratch32()
                        S.op("dve", lambda e, po=po, sc=sc, oc=oc, t=t: e.scalar_tensor_tensor(
                            out=sc[:], in0=po[:], scalar=self.modcol(i, 5, oc), in1=self.rstd[:, t * 512:(t + 1) * 512],
                            op0=ALU.mult, op1=ALU.mult), reads=[po_r, self.mods_r, self.rstd_r[t]], writes=[sc_r])
                        S.op("dve", lambda e, sc=sc, oc=oc, t=t: e.tensor_tensor(
                            out=self.x[:, oc, t * 512:(t + 1) * 512], in0=sc[:], in1=self.x[:, oc, t * 512:(t + 1) * 512],
                            op=ALU.add), reads=[sc_r, self.x_r[oc][t]], writes=[self.x_r[oc][t]])
                S.barrier()


WEIGHT_KEYS = ("w_ada", "ffn1_w_in", "ffn2_w_in", "ffn1_w_out", "ffn2_w_out", "ssm_w_in", "ssm_w_out",
               "diff_w_qkv", "diff_w_out", "win_w_qkv", "win_w_out")


def make_in_maps(inp, cores=range(N_CORES)):
    consts = pack_consts(inp)
    bada = fm(inp["b_ada"].reshape(-1))
    rope = rope_tables()
    wm = win_mask()
    maps = []
    for core in cores:
        b = core // 4
        xs = np.concatenate([inp["x_prompt"][2 * core], inp["x_prompt"][2 * core + 1], inp["x_sample"][b]], axis=0)
        cv = np.stack([fm(inp["c_ctx"]), fm(inp["c"][b])], axis=2).reshape(128, DC * 2)
        st = np.stack([np.stack([inp[f"state_l{l}_{d}"][b].reshape(DI, 128).T for d in ("fwd", "bwd")])
                       for l in (0, 3)])
        m = {"xT": np.ascontiguousarray(xs.T), "cvec": np.ascontiguousarray(cv), "consts": consts,
             "b_ada_fm": bada, "rope_cs": rope, "wmask": wm, "st_in": np.ascontiguousarray(st),
             "kc1T": np.ascontiguousarray(inp["cache_l1_k"][b].reshape(256, D).T),
             "vc1": np.ascontiguousarray(inp["cache_l1_v"][b].reshape(256, D)),
             "kc2T": np.ascontiguousarray(inp["cache_l2_k"][b].reshape(256, 512).T),
             "vc2": np.ascontiguousarray(inp["cache_l2_v"][b].reshape(256, 512))}
        for k in WEIGHT_KEYS:
            m[k] = inp[k]
        maps.append(m)
    return maps


def assemble(results):
    B, S_ = 16, 256
    y_prompt = np.zeros((B, S_, D), np.float32)
    y_sample = np.zeros((2, LS, D), np.float32)
    st = [np.zeros((B, 64, 64, 128), np.float32) for _ in range(4)]
    k1 = np.zeros((B, S_, 2, 8, 128), np.float32)
    v1 = np.zeros((B, S_, 8, 256), np.float32)
    k2 = np.zeros((B, S_, 4, 128), np.float32)
    v2 = np.zeros((B, S_, 4, 128), np.float32)
    for core, r in enumerate(results):
        y = r["yT"].T
        y_prompt[2 * core] = y[0:256]
        y_prompt[2 * core + 1] = y[256:512]
        if core % 4 == 0:
            y_sample[core // 4] = y[512:]
        so = r["st_out"]
        for l in range(2):
            for d in range(2):
                for s in range(2):
                    st[l * 2 + d][2 * core + s] = so[l, d, s].T.reshape(64, 64, 128)
        k1t = r["k1T"].T
        k2t = r["k2T"].T
        for s in range(2):
            k1[2 * core + s] = k1t[s * 256:(s + 1) * 256].reshape(256, 2, 8, 128)
            v1[2 * core + s] = r["v1"][s * 256:(s + 1) * 256].reshape(256, 8, 256)
            k2[2 * core + s] = k2t[s * 256:(s + 1) * 256].reshape(256, 4, 128)
            v2[2 * core + s] = r["v2"][s * 256:(s + 1) * 256].reshape(256, 4, 128)
    return (y_prompt, y_sample, st[0], st[1], k1, v1, k2, v2, st[2], st[3])


def kernel(**inp):
    inp = {k: np.asarray(v) for k, v in inp.items()}
    nc = Builder().build()
    maps = make_in_maps(inp)
    res = run_bass_kernel_spmd(nc, maps, core_ids=list(range(N_CORES)))
    return assemble(res.results)
```
